# Optimizing a Trainium2 kernel written in Bass

```python
import jax, jax.numpy as jnp
from jax import lax
import numpy as np

D_MODEL = 1024
BATCH = 4
SEQ = 8192
DEPTH = 2
DEC_BATCH = 128
DEC_SEQ = 4
PAST_LEN = 16384
PAGE_SIZE = 128

N_MIXERS = 2
N_LAYERS_A = (DEPTH + 1) // 2
N_LAYERS_B = DEPTH // 2
EPS = 1e-6

A_HEADS = 8
A_NOPE = 64
A_ROPE = 32
A_QK = A_NOPE + A_ROPE
A_V = 64
A_Q_LORA = 384
A_KV_LORA = 256
A_WIDTH = A_HEADS * A_V
A_IN = A_Q_LORA + A_KV_LORA + A_ROPE + A_WIDTH
ROPE_THETA = 10000.0
Q_BLOCK = 128

B_HEADS = 8
B_DK = 64
B_DV = 64
B_WIDTH = B_HEADS * B_DV
B_CONV = 4
B_CONV_CH = 2 * B_HEADS * B_DK + B_WIDTH
B_IN = B_CONV_CH + B_WIDTH + 2 * B_HEADS
CHUNK = 64

kernel_name = "hybrid_mla_gated_deltanet_step"


def rms_norm(x, g):
    xf = x.astype(jnp.float32)
    y = xf * lax.rsqrt(jnp.mean(xf * xf, -1, keepdims=True) + EPS)
    return (y * g.astype(jnp.float32)).astype(x.dtype)


def l2_norm(x):
    return x * lax.rsqrt(jnp.sum(x * x, -1, keepdims=True) + EPS)


def rope(x, pos):
    half = A_ROPE // 2
    inv = ROPE_THETA ** (-jnp.arange(half, dtype=jnp.float32) / half)
    ang = pos.astype(jnp.float32)[:, None] * inv[None, :]
    cos = jnp.cos(ang)[:, None, :]
    sin = jnp.sin(ang)[:, None, :]
    x1 = x[..., :half].astype(jnp.float32)
    x2 = x[..., half:].astype(jnp.float32)
    return jnp.concatenate([x1 * cos - x2 * sin, x2 * cos + x1 * sin], -1).astype(x.dtype)


def mla_project(x, pos, norm_g, w_in, g_qa, w_uq, g_kv, g_q):
    n, t, _ = x.shape
    h = rms_norm(x, norm_g)
    proj = h @ w_in
    o1 = A_Q_LORA
    o2 = o1 + A_KV_LORA
    o3 = o2 + A_ROPE
    q_a, c, k_pe, z = proj[..., :o1], proj[..., o1:o2], proj[..., o2:o3], proj[..., o3:]
    q = (rms_norm(q_a, g_qa) @ w_uq).reshape(n, t, A_HEADS, A_QK)
    q = jnp.concatenate([q[..., :A_NOPE], rope(q[..., A_NOPE:], pos)], -1)
    q = rms_norm(q, g_q)
    c = rms_norm(c, g_kv)
    k_pe = rope(k_pe[:, :, None, :], pos)[:, :, 0, :]
    return q, c, k_pe, z


def mla_keys(c, k_pe, w_uk, g_k):
    k_nope = jnp.einsum('...sc,chd->...shd', c, w_uk)
    k_pe_h = jnp.broadcast_to(k_pe[..., None, :], k_nope.shape[:-1] + (A_ROPE,))
    return rms_norm(jnp.concatenate([k_nope, k_pe_h], -1), g_k)


def mla_prompt_attention(q, k, v):
    b, s = q.shape[:2]
    nb = s // Q_BLOCK
    qb = q.reshape(b, nb, Q_BLOCK, A_HEADS, A_QK).transpose(1, 0, 2, 3, 4)
    kpos = jnp.arange(s)
    scale = A_QK ** -0.5

    def block(args):
        qi, i = args
        qpos = i * Q_BLOCK + jnp.arange(Q_BLOCK)
        sc = jnp.einsum('bqhd,bkhd->bhqk', qi, k).astype(jnp.float32) * scale
        sc = jnp.where(kpos[None, :] <= qpos[:, None], sc, -jnp.inf)
        p = jax.nn.softmax(sc, axis=-1).astype(v.dtype)
        return jnp.einsum('bhqk,bkhd->bqhd', p, v)

    o = lax.map(block, (qb, jnp.arange(nb)))
    return o.transpose(1, 0, 2, 3, 4).reshape(b, s, A_HEADS, A_V)


def mla_sample_attention(q, c_new, kpe_new, cache_latent, cache_krope, la, page_table, w_uk, w_uv, g_k):
    t = q.shape[1]
    past = page_table.shape[1] * PAGE_SIZE
    spos = jnp.arange(past + t)
    tpos = past + jnp.arange(t)
    mask = spos[None, :] <= tpos[:, None]
    scale = A_QK ** -0.5

    def one(args):
        qi, ci, kpi, pt = args
        c_all = jnp.concatenate([cache_latent[la, pt].reshape(past, A_KV_LORA).astype(ci.dtype), ci], 0)
        kpe_all = jnp.concatenate([cache_krope[la, pt].reshape(past, A_ROPE).astype(kpi.dtype), kpi], 0)
        k = mla_keys(c_all, kpe_all, w_uk, g_k)
        sc = jnp.einsum('thd,shd->hts', qi, k).astype(jnp.float32) * scale
        sc = jnp.where(mask[None], sc, -jnp.inf)
        p = jax.nn.softmax(sc, axis=-1).astype(c_all.dtype)
        o_lat = jnp.einsum('hts,sc->thc', p, c_all)
        return jnp.einsum('thc,chd->thd', o_lat, w_uv)

    return lax.map(one, (q, c_new, kpe_new, page_table))


def gated_out(x, o, z, w_o):
    n, t = x.shape[:2]
    return x + (o.reshape(n, t, -1) * jax.nn.silu(z)) @ w_o


def gdn_project(x, norm_g, w_in):
    h = rms_norm(x, norm_g)
    proj = h @ w_in
    o1 = B_CONV_CH
    o2 = o1 + B_WIDTH
    o3 = o2 + B_HEADS
    return proj[..., :o1], proj[..., o1:o2], proj[..., o2:o3], proj[..., o3:]


def gdn_features(ext, t, a, b, w_conv, a_log, dt_bias):
    n = ext.shape[0]
    conv = ext[:, 0:t] * w_conv[0]
    for j in range(1, B_CONV):
        conv = conv + ext[:, j:j + t] * w_conv[j]
    conv = jax.nn.silu(conv).astype(jnp.float32)
    hk = B_HEADS * B_DK
    q = l2_norm(conv[..., :hk].reshape(n, t, B_HEADS, B_DK)) * (B_DK ** -0.5)
    k = l2_norm(conv[..., hk:2 * hk].reshape(n, t, B_HEADS, B_DK))
    v = conv[..., 2 * hk:].reshape(n, t, B_HEADS, B_DV)
    g = -jnp.exp(a_log.astype(jnp.float32)) * jax.nn.softplus(a.astype(jnp.float32) + dt_bias.astype(jnp.float32))
    beta = jax.nn.sigmoid(b.astype(jnp.float32))
    return q, k, v, g, beta


def gated_delta_chunked(q, k, v, g, beta):
    b, s = q.shape[:2]
    n = s // CHUNK

    def chunks(x):
        return jnp.swapaxes(x.reshape((b, n, CHUNK) + x.shape[2:]), 2, 3)

    q, k, v, g, beta = chunks(q), chunks(k), chunks(v), chunks(g), chunks(beta)
    gc = jnp.cumsum(g, -1)
    idx = jnp.arange(CHUNK)
    strict = idx[:, None] > idx[None, :]
    incl = idx[:, None] >= idx[None, :]
    decay = jnp.exp(jnp.where(incl, gc[..., :, None] - gc[..., None, :], -jnp.inf))
    kb = k * beta[..., None]
    lmat = jnp.where(strict, jnp.einsum('...id,...jd->...ij', kb, k) * decay, 0.0)
    amat = lmat + jnp.eye(CHUNK, dtype=lmat.dtype)
    u = lax.linalg.triangular_solve(amat, v * beta[..., None], left_side=True, lower=True, unit_diagonal=True)
    w = lax.linalg.triangular_solve(amat, kb * jnp.exp(gc)[..., None], left_side=True, lower=True, unit_diagonal=True)
    intra = jnp.where(incl, jnp.einsum('...id,...jd->...ij', q, k) * decay, 0.0)

    def step(st, xs):
        qi, ki, ui, wi, gi, ai = xs
        v_new = ui - jnp.einsum('bhcd,bhde->bhce', wi, st)
        o = jnp.einsum('bhcd,bhde->bhce', qi * jnp.exp(gi)[..., None], st) + jnp.einsum('bhij,bhje->bhie', ai, v_new)
        glast = gi[..., -1]
        st = st * jnp.exp(glast)[..., None, None] + jnp.einsum(
            'bhcd,bhce->bhde', ki * jnp.exp(glast[..., None] - gi)[..., None], v_new)
        return st, o

    xs = tuple(jnp.moveaxis(t, 1, 0) for t in (q, k, u, w, gc, intra))
    st0 = jnp.zeros((b, B_HEADS, B_DK, B_DV), jnp.float32)
    st, o = lax.scan(step, st0, xs)
    o = o.transpose(1, 0, 3, 2, 4).reshape(b, s, B_HEADS, B_DV)
    return o, st


def gated_delta_recurrent(q, k, v, g, beta, st0):
    def step(st, xs):
        qt, kt, vt, gt, bt = xs
        st = st * jnp.exp(gt)[..., None, None]
        delta = (vt - jnp.einsum('nhd,nhde->nhe', kt, st)) * bt[..., None]
        st = st + jnp.einsum('nhd,nhe->nhde', kt, delta)
        return st, jnp.einsum('nhd,nhde->nhe', qt, st)

    xs = tuple(jnp.swapaxes(t, 0, 1) for t in (q, k, v, g, beta))
    st, o = lax.scan(step, st0, xs)
    return jnp.swapaxes(o, 0, 1), st


def gdn_out(x, o, z, g_o, w_o):
    n, t = x.shape[:2]
    on = rms_norm(o, g_o).astype(x.dtype)
    gated = on * jax.nn.silu(z.reshape(n, t, B_HEADS, B_DV))
    return x + gated.reshape(n, t, B_WIDTH) @ w_o


def setup_inputs(seed: int = 0) -> dict:
    key = jax.random.key(seed)
    ks = jax.random.split(key, 32)
    f32 = jnp.float32
    n_pages = PAST_LEN // PAGE_SIZE
    n_used = DEC_BATCH * n_pages
    n_pool = n_used + max(1, n_used // 4)

    def nrm(k, shape, scale):
        return jax.random.normal(k, shape, f32) * scale

    def gain(k, shape):
        return 1.0 + 0.02 * jax.random.normal(k, shape, f32)

    page_table = jax.random.permutation(ks[4], n_pool)[:n_used].reshape(DEC_BATCH, n_pages).astype(jnp.int32)
    return {
        "x_prompt": nrm(ks[0], (BATCH, SEQ, D_MODEL), 1.0),
        "x_sample": nrm(ks[1], (DEC_BATCH, DEC_SEQ, D_MODEL), 1.0),
        "cache_latent": nrm(ks[2], (N_LAYERS_A, n_pool, PAGE_SIZE, A_KV_LORA), 1.0),
        "cache_krope": nrm(ks[3], (N_LAYERS_A, n_pool, PAGE_SIZE, A_ROPE), 1.0),
        "page_table": page_table,
        "state_conv": nrm(ks[5], (N_LAYERS_B, DEC_BATCH, B_CONV - 1, B_CONV_CH), 1.0),
        "state_ssm": nrm(ks[6], (N_LAYERS_B, DEC_BATCH, B_HEADS, B_DK, B_DV), 0.3),
        "a_norm": gain(ks[7], (N_LAYERS_A, D_MODEL)),
        "a_w_in": nrm(ks[8], (N_LAYERS_A, D_MODEL, A_IN), D_MODEL ** -0.5),
        "a_g_qa": gain(ks[9], (N_LAYERS_A, A_Q_LORA)),
        "a_w_uq": nrm(ks[10], (N_LAYERS_A, A_Q_LORA, A_HEADS * A_QK), A_Q_LORA ** -0.5),
        "a_g_kv": gain(ks[11], (N_LAYERS_A, A_KV_LORA)),
        "a_w_uk": nrm(ks[12], (N_LAYERS_A, A_KV_LORA, A_HEADS, A_NOPE), A_KV_LORA ** -0.5),
        "a_w_uv": nrm(ks[13], (N_LAYERS_A, A_KV_LORA, A_HEADS, A_V), A_KV_LORA ** -0.5),
        "a_g_q": gain(ks[14], (N_LAYERS_A, A_QK)),
        "a_g_k": gain(ks[15], (N_LAYERS_A, A_QK)),
        "a_w_o": nrm(ks[16], (N_LAYERS_A, A_WIDTH, D_MODEL), A_WIDTH ** -0.5),
        "b_norm": gain(ks[17], (N_LAYERS_B, D_MODEL)),
        "b_w_in": nrm(ks[18], (N_LAYERS_B, D_MODEL, B_IN), D_MODEL ** -0.5),
        "b_w_conv": nrm(ks[19], (N_LAYERS_B, B_CONV, B_CONV_CH), B_CONV ** -0.5),
        "b_a_log": jnp.log(jax.random.uniform(ks[20], (N_LAYERS_B, B_HEADS), f32, 1.0, 16.0)),
        "b_dt_bias": nrm(ks[21], (N_LAYERS_B, B_HEADS), 0.1),
        "b_g_o": gain(ks[22], (N_LAYERS_B, B_DV)),
        "b_w_o": nrm(ks[23], (N_LAYERS_B, B_WIDTH, D_MODEL), B_WIDTH ** -0.5),
    }


def reference(x_prompt, x_sample, cache_latent, cache_krope, page_table, state_conv, state_ssm,
              a_norm, a_w_in, a_g_qa, a_w_uq, a_g_kv, a_w_uk, a_w_uv, a_g_q, a_g_k, a_w_o,
              b_norm, b_w_in, b_w_conv, b_a_log, b_dt_bias, b_g_o, b_w_o):
    past_len = page_table.shape[1] * PAGE_SIZE
    pos_p = jnp.arange(x_prompt.shape[1])
    pos_s = past_len + jnp.arange(x_sample.shape[1])
    xp, xs = x_prompt, x_sample
    lat_p, kpe_p, lat_s, kpe_s = [], [], [], []
    conv_p, ssm_p, conv_s, ssm_s = [], [], [], []
    for i in range(DEPTH):
        li = i // N_MIXERS
        if i % N_MIXERS == 0:
            wp = (a_norm[li], a_w_in[li], a_g_qa[li], a_w_uq[li], a_g_kv[li], a_g_q[li])
            q, c, kpe, z = mla_project(xp, pos_p, *wp)
            k = mla_keys(c, kpe, a_w_uk[li], a_g_k[li])
            v = jnp.einsum('bsc,chd->bshd', c, a_w_uv[li])
            o = mla_prompt_attention(q, k, v)
            xp = gated_out(xp, o, z, a_w_o[li])
            lat_p.append(c)
            kpe_p.append(kpe)
            q, c, kpe, z = mla_project(xs, pos_s, *wp)
            o = mla_sample_attention(q, c, kpe, cache_latent, cache_krope, li, page_table,
                                     a_w_uk[li], a_w_uv[li], a_g_k[li])
            xs = gated_out(xs, o, z, a_w_o[li])
            lat_s.append(c)
            kpe_s.append(kpe)
        else:
            qkv, z, a, b = gdn_project(xp, b_norm[li], b_w_in[li])
            t = xp.shape[1]
            ext = jnp.concatenate([jnp.zeros((xp.shape[0], B_CONV - 1, B_CONV_CH), qkv.dtype), qkv], 1)
            q, k, v, g, beta = gdn_features(ext, t, a, b, b_w_conv[li], b_a_log[li], b_dt_bias[li])
            o, st = gated_delta_chunked(q, k, v, g, beta)
            xp = gdn_out(xp, o, z, b_g_o[li], b_w_o[li])
            conv_p.append(ext[:, -(B_CONV - 1):])
            ssm_p.append(st.astype(xp.dtype))

            qkv, z, a, b = gdn_project(xs, b_norm[li], b_w_in[li])
            t = xs.shape[1]
            ext = jnp.concatenate([state_conv[li].astype(qkv.dtype), qkv], 1)
            q, k, v, g, beta = gdn_features(ext, t, a, b, b_w_conv[li], b_a_log[li], b_dt_bias[li])
            o, st = gated_delta_recurrent(q, k, v, g, beta, state_ssm[li].astype(jnp.float32))
            xs = gdn_out(xs, o, z, b_g_o[li], b_w_o[li])
            conv_s.append(ext[:, -(B_CONV - 1):])
            ssm_s.append(st.astype(xs.dtype))
    return (xp, xs, jnp.stack(lat_p), jnp.stack(kpe_p), jnp.stack(lat_s), jnp.stack(kpe_s),
            jnp.stack(conv_p), jnp.stack(ssm_p), jnp.stack(conv_s), jnp.stack(ssm_s))
```

```python
import numpy as np
import concourse.bass as bass
import concourse.mybir as mybir
from concourse.bass_utils import run_bass_kernel_spmd

F32 = mybir.dt.float32
BF16 = mybir.dt.bfloat16
I32 = mybir.dt.int32
U8 = mybir.dt.uint8
ALU = mybir.AluOpType
AF = mybir.ActivationFunctionType
AX = mybir.AxisListType

ENGS = ("pe", "act", "dve", "pool", "sp")
NDSEM = 16
EPS = 1e-6
NEG = -30000.0


class Buf:
    __slots__ = ("name", "w", "r", "excl")

    def __init__(self, name="", excl=False):
        self.name = name
        self.w = None
        self.r = []
        self.excl = excl


class T:
    def __init__(self, ap, name=""):
        self.ap = ap
        self.b = Buf(name)

    def __getitem__(self, k):
        return self.ap[k]


class Op:
    __slots__ = ("eng", "fn", "waits", "idx", "dma", "dsem", "dval", "signal")


def _b(x):
    return x.b if isinstance(x, T) else x


class Sched:
    def __init__(self, nc, self_sync=("act", "dve", "pool")):
        self.nc = nc
        self.ops = {e: [] for e in ENGS}
        self.seen = {e: {} for e in ENGS}
        self.ndma = {e: 0 for e in ENGS}
        self.self_sync = set(self_sync)
        self.last_dma = {}

    def _waits(self, eng, deps):
        waits = []
        seen = self.seen[eng]
        for d in deps:
            if d.dma:
                key = ("d", d.eng, d.dsem)
                val = d.dval
            else:
                if d.fn is None:
                    continue
                if d.eng == eng and eng not in self.self_sync:
                    continue
                key = ("e", d.eng)
                val = d.idx + 1
            if seen.get(key, 0) >= val:
                continue
            seen[key] = val
            waits.append(d)
            d.signal = True
        return waits

    def _mk(self, eng, fn, reads, writes, dma=False):
        if getattr(self, "cap", None) is not None:
            self.cap.append((eng, fn, list(reads), list(writes), dma))
            return None
        reads = [_b(x) for x in reads]
        writes = [_b(x) for x in writes]
        if eng != "pe":
            ex = [b for b in reads if b.excl and b not in writes]
            if ex:
                writes = writes + ex
                reads = [b for b in reads if not b.excl]
        op = Op()
        op.eng = eng
        op.fn = fn
        op.dma = dma
        op.signal = False
        op.idx = len(self.ops[eng])
        deps = []
        for b in reads:
            if b.w is not None:
                deps.append(b.w)
        for b in writes:
            if b.w is not None:
                deps.append(b.w)
            deps.extend(b.r)
        if dma:
            n = self.ndma[eng]
            self.ndma[eng] = n + 1
            op.dsem = n % NDSEM
            op.dval = 16 * (n // NDSEM + 1)
            op.signal = True
            prev = self.last_dma.get((eng, op.dsem))
            if prev is not None:
                deps.append(prev)
            self.last_dma[(eng, op.dsem)] = op
        op.waits = self._waits(eng, deps)
        self.ops[eng].append(op)
        for b in reads:
            b.r.append(op)
        for b in writes:
            b.w = op
            b.r = []
        return op

    def op(self, eng, fn, reads=(), writes=()):
        return self._mk(eng, fn, reads, writes)

    def begin_capture(self):
        self.cap = []

    def end_capture(self):
        c = self.cap
        self.cap = None
        return c

    def replay(self, streams):
        streams = [st for st in streams if st]
        pos = [0] * len(streams)
        total = sum(len(st) for st in streams)
        for _ in range(total):
            k = min(range(len(streams)), key=lambda i: (pos[i] / len(streams[i])) if pos[i] < len(streams[i]) else 2.0)
            a = streams[k][pos[k]]
            pos[k] += 1
            self._mk(*a)

    def dma(self, eng, out, in_, reads=(), writes=(), **kw):
        return self._mk(eng, lambda e: e.dma_start(out=out, in_=in_, **kw), reads, writes, dma=True)

    def dma_fn(self, eng, fn, reads=(), writes=()):
        return self._mk(eng, fn, reads, writes, dma=True)

    def barrier(self, engines=ENGS):
        last = []
        for e in ENGS:
            for o in reversed(self.ops[e]):
                if not o.dma and o.fn is not None:
                    last.append(o)
                    break
        deps = last + list(self.last_dma.values())
        for e in engines:
            op = Op()
            op.eng = e
            op.fn = None
            op.dma = False
            op.signal = False
            op.idx = len(self.ops[e])
            op.waits = self._waits(e, [d for d in deps if d.dma or d.eng != e])
            self.ops[e].append(op)

    def emit(self):
        nc = self.nc
        esem = {e: nc.alloc_semaphore(name=f"es_{e}") for e in ENGS}
        dsem = {e: [nc.alloc_semaphore(name=f"ds_{e}{i}") for i in range(NDSEM)]
                for e in ENGS if self.ndma[e] > 0}
        signum = {}
        for e in ENGS:
            n = 0
            for o in self.ops[e]:
                if o.dma or o.fn is None:
                    continue
                if o.signal:
                    n += 1
                    signum[id(o)] = n

        def run(e, eng):
            for o in self.ops[e]:
                for d in o.waits:
                    if d.dma:
                        eng.wait_ge(dsem[d.eng][d.dsem], d.dval)
                    else:
                        eng.wait_ge(esem[d.eng], signum[id(d)])
                if o.fn is None:
                    continue
                ins = o.fn(eng)
                if o.dma:
                    ins.then_inc(dsem[e][o.dsem], 16)
                elif o.signal:
                    ins.then_inc(esem[e], 1)

        with nc.Block() as block:
            @block.tensor
            def _(eng):
                run("pe", eng)

            @block.scalar
            def _(eng):
                run("act", eng)

            @block.vector
            def _(eng):
                run("dve", eng)

            @block.gpsimd
            def _(eng):
                run("pool", eng)

            @block.sync
            def _(eng):
                run("sp", eng)


class Arena:
    def __init__(self, base, nbytes):
        self.base = base
        self.nbytes = nbytes
        self.off = 0
        self.marks = []
        self.n = 0

    def alloc(self, shape_free, dtype, name=None):
        esz = {F32: 4, BF16: 2, I32: 4}[dtype]
        n = int(np.prod(shape_free))
        nb = (n * esz + 63) // 64 * 64
        assert self.off + nb <= self.nbytes, ("SBUF arena overflow", name, self.off, nb, self.nbytes)
        ap = self.base[:, self.off:self.off + n * esz].bitcast(dtype)
        self.off += nb
        if len(shape_free) > 1:
            names = " ".join(f"a{i}" for i in range(len(shape_free)))
            kw = {f"a{i}": int(s) for i, s in enumerate(shape_free)}
            ap = ap.rearrange(f"p ({names}) -> p {names}", **kw)
        self.n += 1
        return T(ap, name or f"t{self.n}")

    def mark(self):
        self.marks.append(self.off)

    def release(self):
        self.off = self.marks.pop()


class Rot:
    def __init__(self, items):
        self.items = items
        self.i = 0

    def next(self):
        t = self.items[self.i % len(self.items)]
        self.i += 1
        return t


def mm(out, lhsT, rhs, start=True, stop=True):
    return lambda e: e.matmul(out, lhsT=lhsT, rhs=rhs, start=start, stop=stop)


def trp(out, in_, ident):
    return lambda e: e.transpose(out=out, in_=in_, identity=ident)


def actf(out, in_, func, **kw):
    return lambda e: e.activation(out=out, in_=in_, func=func, **kw)


def tt(out, a, b, op):
    return lambda e: e.tensor_tensor(out=out, in0=a, in1=b, op=op)


def ts(out, a, s1, op0, s2=None, op1=None):
    if op1 is None:
        return lambda e: e.tensor_scalar(out=out, in0=a, scalar1=s1, scalar2=None, op0=op0)
    return lambda e: e.tensor_scalar(out=out, in0=a, scalar1=s1, scalar2=s2, op0=op0, op1=op1)


def stt(out, a, s, b, op0, op1):
    return lambda e: e.scalar_tensor_tensor(out=out, in0=a, scalar=s, in1=b, op0=op0, op1=op1)


def cp(out, in_):
    return lambda e: e.tensor_copy(out=out, in_=in_)


def acp(out, in_):
    return lambda e: e.copy(out=out, in_=in_)


def rsum(out, in_):
    return lambda e: e.reduce_sum(out=out, in_=in_, axis=AX.X)


def recip(out, in_):
    return lambda e: e.reciprocal(out=out, in_=in_)


def mset(ap, v):
    return lambda e: e.memset(ap, v)


def bc(ap, shape, axis):
    return ap.unsqueeze(axis).to_broadcast(list(shape))


class Cfg:
    def __init__(self, SEQ=8192, NSEQ=16, NPOOL=20480, debug=False, phases=None, cut=99):
        self.cut = cut
        self.SEQ = SEQ
        self.NSEQ = NSEQ
        self.NPOOL = NPOOL
        self.debug = debug
        self.phases = phases


def build(cfg):
    SEQ, NSEQ, NPOOL = cfg.SEQ, cfg.NSEQ, cfg.NPOOL
    NT = SEQ // 128
    NQ = SEQ // 512
    NS = NSEQ * 4
    nc = bass.Bass("TRN2", target_bir_lowering=False)
    S = Sched(nc)

    def din(name, shape, dt=F32):
        return nc.dram_tensor(name, list(shape), dt, kind="ExternalInput").ap()

    def dout(name, shape, dt=F32):
        return nc.dram_tensor(name, list(shape), dt, kind="ExternalOutput").ap()

    def dscr(name, shape, dt=F32):
        kind = "ExternalOutput" if cfg.debug else "Internal"
        return nc.dram_tensor(name, list(shape), dt, kind=kind).ap()

    x_p = din("x_p", [SEQ, 1024])
    x_s = din("x_s", [NS, 1024])
    cache_lat = din("cache_lat", [NPOOL, 128 * 256])
    cache_kr = din("cache_kr", [NPOOL, 128 * 32])
    ptab = din("ptab", [128, NSEQ], I32)
    st_conv = din("st_conv", [NSEQ, 3, 1536])
    st_ssm = din("st_ssm", [NSEQ * 8, 4096])
    a_norm = din("a_norm", [128, 8])
    a_w_in = din("a_w_in", [1024, 1184])
    a_g_qa = din("a_g_qa", [128, 3])
    a_w_uq = din("a_w_uq", [384, 768])
    a_g_kv = din("a_g_kv", [1, 256])
    a_w_uk = din("a_w_uk", [256, 512])
    a_w_uv = din("a_w_uv", [256, 512])
    a_g_q = din("a_g_q", [1, 96])
    a_g_k = din("a_g_k", [1, 96])
    a_w_o = din("a_w_o", [512, 1024])
    b_norm = din("b_norm", [128, 8])
    b_w_in = din("b_w_in", [1024, 2064])
    b_w_conv = din("b_w_conv", [4, 1536])
    b_w_convT = din("b_w_convT", [128, 48])
    PHn = NSEQ * 8
    b_w_convL = din("b_w_convL", [PHn, 4 * 192])
    st_convL = din("st_convL", [PHn, 3 * 192])
    b_alogL = din("b_alogL", [PHn, 1])
    b_dtbL = din("b_dtbL", [PHn, 1])
    b_a_log = din("b_a_log", [1, 8])
    b_dt_bias = din("b_dt_bias", [1, 8])
    b_g_o = din("b_g_o", [1, 64])
    b_w_o = din("b_w_o", [512, 1024])
    c_ident = din("c_ident", [128, 128])
    c_tri = din("c_tri", [128, 128])
    c_strict = din("c_strict", [128, 128])
    c_negm = din("c_negm", [128, 128])
    c_blk = din("c_blk", [128, 128])
    c_amask = din("c_amask", [128, 4, 512])
    c_smask = din("c_smask", [NS, NSEQ * 32])
    c_rope_p = din("c_rope_p", [SEQ, 64])
    c_rope_s = din("c_rope_s", [NS, 64])

    y_p = dout("y_p", [SEQ, 1024])
    y_s = dout("y_s", [NS, 1024])
    lat_p = dout("lat_p", [SEQ, 256])
    kr_p = dout("kr_p", [SEQ, 32])
    lat_s = dout("lat_s", [NS, 256])
    kr_s = dout("kr_s", [NS, 32])
    conv_p = dout("conv_p", [3, 1536])
    ssm_p = dout("ssm_p", [8, 64, 64])
    conv_s = dout("conv_s", [NSEQ * 8, 3 * 192])
    ssm_s = dout("ssm_s", [NSEQ * 8, 4096])

    gt1 = dscr("gt1", [2, 128, 2, SEQ], BF16)
    xp1 = dscr("xp1", [SEQ, 1024])
    gt2 = dscr("gt2", [2, 128, 2, SEQ], BF16)
    qs_scr = dscr("qs_scr", [NS, 1536])
    ab_scr = dscr("ab_scr", [NS, 16])
    os_scr = dscr("os_scr", [NSEQ * 8, 256])
    D_gt1, D_xp1, D_gt2 = Buf("gt1"), Buf("xp1"), Buf("gt2")
    D_qs, D_ab, D_os = Buf("qs"), Buf("ab"), Buf("os")
    D_out = Buf("outs")

    ARENA_BYTES = 204 * 1024
    sb = nc.alloc_sbuf_tensor("arena", [128, ARENA_BYTES], U8)
    ar = Arena(sb.ap(), ARENA_BYTES)
    PSD = [nc.alloc_psum_tensor(f"psd{i}", [128, 1024], F32).ap() for i in range(4)]
    PS = [T(PSD[i // 2][:, (i % 2) * 512:(i % 2 + 1) * 512], f"ps{i}") for i in range(8)]
    for p_ in PS:
        p_.b.excl = True

    def psb(i):
        return PS[i].ap.bitcast(BF16)

    dmaq = Rot(["sp", "pool"])

    ident_f = ar.alloc([128], F32, "ident_f")
    ident = ar.alloc([128], BF16, "ident")
    S.dma("sp", ident_f.ap, c_ident, writes=[ident_f])
    S.op("dve", cp(ident.ap, ident_f.ap), [ident_f], [ident])

    xs1 = ar.alloc([1024], F32, "xs1")
    ptab_sb = ar.alloc([NSEQ], I32, "ptab_sb")
    S.dma("sp", ptab_sb.ap, ptab, writes=[ptab_sb])

    def load_bf16_rows(dst, dst_slices, src_rows, gain, tmp, ncols):
        for kc, rows in enumerate(src_rows):
            S.dma(dmaq.next(), tmp[:, 0:ncols], rows, writes=[tmp])
            if gain is None:
                S.op("dve", cp(dst_slices[kc], tmp[:, 0:ncols]), [tmp], [dst])
            else:
                S.op("dve", ts(dst_slices[kc], tmp[:, 0:ncols], gain[0][:, kc:kc + 1], ALU.mult),
                     [tmp, gain[1]], [dst])

    def rms_rstd(x_ap, xT, n, rstd, junk, parts=128, eng_sq="act"):
        S.op("act", actf(junk[0:parts, 0:n], x_ap, AF.Square, accum_out=rstd[0:parts, 1:2]),
             [xT], [rstd])
        S.op("act", actf(rstd[0:parts, 2:3], rstd[0:parts, 1:2], AF.Sqrt, scale=1.0 / n, bias=EPS),
             [rstd], [rstd])
        S.op("dve", recip(rstd[0:parts, 0:1], rstd[0:parts, 2:3]), [rstd], [rstd])

    def transpose_to(dstT, dst_ap_fn, src, src_ap_fn, nchunk, bank, parts=128, width=128, evac="act"):
        pb = psb(bank)
        for c in range(nchunk):
            S.op("pe", trp(pb[0:width, c * parts:(c + 1) * parts], src_ap_fn(c), ident[0:parts, 0:parts]),
                 [src, ident], [PS[bank]])
        fn = acp if evac == "act" else cp
        S.op(evac, fn(dst_ap_fn(), pb[0:width, 0:nchunk * parts]), [PS[bank]], [dstT])

    def rope(x_view, xT, cs, nh, parts, sw, t1):
        swv = sw[0:parts, 0:nh, :]
        t1v = t1[0:parts, 0:nh, :]
        S.op("pool", cp(swv[:, :, 0:16], x_view[:, :, 16:32]), [xT], [sw])
        S.op("pool", cp(swv[:, :, 16:32], x_view[:, :, 0:16]), [xT], [sw])
        S.op("dve", tt(t1v, swv, bc(cs[0:parts, 32:64], [parts, nh, 32], 1), ALU.mult), [sw, cs], [t1])
        S.op("dve", tt(x_view, x_view, bc(cs[0:parts, 0:32], [parts, nh, 32], 1), ALU.mult), [xT, cs], [xT])
        S.op("dve", tt(x_view, x_view, t1v, ALU.add), [xT, t1], [xT])

    ar.mark()
    an_sb = ar.alloc([8], F32, "an_sb")
    gqa_sb = ar.alloc([3], F32, "gqa_sb")
    S.dma("sp", an_sb.ap, a_norm, writes=[an_sb])
    S.dma("sp", gqa_sb.ap, a_g_qa, writes=[gqa_sb])
    W_in = ar.alloc([8, 1184], BF16, "W_in")
    W_uq = ar.alloc([3, 768], BF16, "W_uq")
    W_uk = ar.alloc([2, 512], BF16, "W_uk")
    W_uv = ar.alloc([2, 512], BF16, "W_uv")
    amask = ar.alloc([4, 512], BF16, "amask")
    gkv_b = ar.alloc([256], F32, "gkv_b")
    gg_b = ar.alloc([96], F32, "gg_b")
    gk_t = ar.alloc([96], F32, "gk_t")
    ar.mark()
    wtmp = ar.alloc([2064], F32, "wtmp")
    load_bf16_rows(W_in, [W_in[:, kc, :] for kc in range(8)],
                   [a_w_in[kc * 128:(kc + 1) * 128, :] for kc in range(8)], (an_sb, an_sb), wtmp, 1184)
    load_bf16_rows(W_uq, [W_uq[:, kc, :] for kc in range(3)],
                   [a_w_uq[kc * 128:(kc + 1) * 128, :] for kc in range(3)], (gqa_sb, gqa_sb), wtmp, 768)
    load_bf16_rows(W_uk, [W_uk[:, kc, :] for kc in range(2)],
                   [a_w_uk[kc * 128:(kc + 1) * 128, :] for kc in range(2)], None, wtmp, 512)
    load_bf16_rows(W_uv, [W_uv[:, kc, :] for kc in range(2)],
                   [a_w_uv[kc * 128:(kc + 1) * 128, :] for kc in range(2)], None, wtmp, 512)
    S.dma("sp", wtmp[:, 0:2048], c_amask.rearrange("p a q -> p (a q)"), writes=[wtmp])
    S.op("dve", cp(amask.ap.rearrange("p a q -> p (a q)"), wtmp[:, 0:2048]), [wtmp], [amask])
    S.barrier()
    ar.release()
    S.dma("sp", gkv_b.ap, a_g_kv.to_broadcast([128, 256]), writes=[gkv_b])
    S.dma("sp", gg_b.ap, a_g_q.to_broadcast([128, 96]), writes=[gg_b])
    S.dma("sp", gk_t.ap, a_g_k.to_broadcast([128, 96]), writes=[gk_t])
    S.op("dve", stt(gg_b.ap, gg_b.ap, float(96 ** -0.5), gk_t.ap, ALU.mult, ALU.mult), [gg_b, gk_t], [gg_b])

    junk = ar.alloc([1024], BF16, "junk")
    stat = Rot([ar.alloc([8], F32, f"stat{i}") for i in range(4)])

    def mla_kv_from_cn(cn_ap, cnT, kpe_ap, kpeT, parts, hsel, KTs, kt_cols, V_store_fn, banks, tmp, kcol0=0):
        col0, nh = hsel
        cnb, cT, sq, ssq, Kt = tmp
        bT, bKV, bK = banks
        S.op("act", acp(cnb[0:parts, :], cn_ap), [cnT], [cnb])
        pb = psb(bT)
        for c in range(2):
            S.op("pe", trp(pb[:, c * parts:(c + 1) * parts], cnb[0:parts, c * 128:(c + 1) * 128],
                           ident[0:parts, 0:parts]), [cnb, ident], [PS[bT]])
        S.op("act", acp(cT[:, :, 0:parts], pb[:, 0:2 * parts].rearrange("p (c t) -> p c t", c=2)),
             [PS[bT]], [cT])
        w = nh * 64
        for kc in range(2):
            S.op("pe", mm(PS[bKV][0:parts, 0:w], cT[:, kc, 0:parts], W_uk[:, kc, col0:col0 + w],
                          kc == 0, kc == 1), [cT, W_uk], [PS[bKV]])
        if V_store_fn is not None:
            for kc in range(2):
                S.op("pe", mm(PS[bKV][0:parts, 256:256 + w], cT[:, kc, 0:parts], W_uv[:, kc, col0:col0 + w],
                              kc == 0, kc == 1), [cT, W_uv], [PS[bKV]])
            V_store_fn(PS[bKV])
        S.op("act", actf(sq[0:parts, 0:w], PS[bKV][0:parts, 0:w], AF.Square), [PS[bKV]], [sq])
        S.op("dve", rsum(ssq[0:parts, 0:nh], sq[0:parts, 0:w].rearrange("p (h d) -> p h d", h=nh)), [sq], [ssq])
        S.op("act", actf(junk[0:parts, 0:32], kpe_ap, AF.Square, accum_out=ssq[0:parts, 8:9]), [kpeT], [junk, ssq])
        S.op("dve", ts(ssq[0:parts, 0:nh], ssq[0:parts, 0:nh], ssq[0:parts, 8:9], ALU.add), [ssq], [ssq])
        S.op("act", actf(ssq[0:parts, 0:nh], ssq[0:parts, 0:nh], AF.Sqrt, scale=1.0 / 96, bias=EPS), [ssq], [ssq])
        S.op("dve", recip(ssq[0:parts, 0:nh], ssq[0:parts, 0:nh]), [ssq], [ssq])
        S.op("dve", tt(Kt[0:parts, 0:nh, 0:64], PS[bKV][0:parts, 0:w].rearrange("p (h d) -> p h d", h=nh),
                       bc(ssq[0:parts, 0:nh], [parts, nh, 64], 2), ALU.mult), [PS[bKV], ssq], [Kt])
        S.op("dve", tt(Kt[0:parts, 0:nh, 64:96], bc(kpe_ap, [parts, nh, 32], 1),
                       bc(ssq[0:parts, 0:nh], [parts, nh, 32], 2), ALU.mult), [kpeT, ssq], [Kt])
        pk = psb(bK)
        for h in range(nh):
            S.op("pe", trp(pk[0:96, kcol0 + h * parts:kcol0 + (h + 1) * parts], Kt[0:parts, h, :], ident[0:parts, 0:parts]),
                 [Kt, ident], [PS[bK]])
        return pk

    def ckp_from_proj(ps_c, ps_k, psT, parts, cs, cn, kpe, st, sw, t1):
        rms_rstd(ps_c, psT, 256, st, junk, parts)
        S.op("dve", stt(cn[0:parts, :], ps_c, st[0:parts, 0:1], gkv_b[0:parts, :], ALU.mult, ALU.mult),
             [psT, st, gkv_b], [cn])
        S.op("act", acp(kpe[0:parts, :], ps_k), [psT], [kpe])
        rope(kpe[0:parts, :].rearrange("p (h d) -> p h d", h=1), kpe, cs, 1, parts, sw, t1)

    def phase_mla_prompt():
        ar.mark()
        KT = ar.alloc([4, SEQ], BF16, "KT")
        V1 = ar.alloc([NT, 2, 192], BF16, "V1")
        S.op("pool", mset(V1.ap, 1.0), [], [V1])
        xt = [ar.alloc([1024], F32, f"xt{i}") for i in range(2)]
        hb = [ar.alloc([1024], BF16, f"hb{i}") for i in range(2)]
        hT4 = ar.alloc([8, 512], BF16, "hT4")
        hTv = [T(hT4[:, :, i * 128:(i + 1) * 128], f"hTv{i}") for i in range(4)]
        cs_t = [ar.alloc([64], F32, f"cs{i}") for i in range(2)]
        sw = [ar.alloc([4, 32], F32, f"sw{i}") for i in range(2)]
        t1 = [ar.alloc([4, 32], F32, f"t1{i}") for i in range(2)]
        sq = [ar.alloc([384], F32, f"sq{i}") for i in range(2)]
        ssq = [ar.alloc([16], F32, f"ssq{i}") for i in range(2)]
        junk2 = [junk, junk]

        def load_h(tile, par, dstT, bank):
            x, h_ = xt[par], hb[par]
            S.dma("sp", x.ap, x_p[tile * 128:(tile + 1) * 128, :], writes=[x])
            st = stat.next()
            rms_rstd(x.ap, x, 1024, st, junk2[par])
            S.op("dve", ts(h_.ap, x.ap, st[:, 0:1], ALU.mult), [x, st], [h_])
            pb = psb(bank)
            for c in range(8):
                S.op("pe", trp(pb[:, c * 128:(c + 1) * 128], h_[:, c * 128:(c + 1) * 128], ident.ap),
                     [h_, ident], [PS[bank]])
            S.op("act", acp(dstT.ap, pb[:, 0:1024].rearrange("p (c t) -> p c t", c=8)), [PS[bank]], [dstT])

        for g in range(2):
            ar.mark()
            cn = [ar.alloc([256], F32, f"cn{i}") for i in range(2)]
            kpe = [ar.alloc([32], F32, f"kpe{i}") for i in range(2)]
            cnb = [ar.alloc([256], BF16, f"cnb{i}") for i in range(2)]
            cT = [ar.alloc([2, 128], BF16, f"cT{i}") for i in range(2)]
            Kt = [ar.alloc([4, 96], BF16, f"Kt{i}") for i in range(2)]
            streams = []
            for t in range(NT):
                par = t % 2
                B0 = 4 * par
                S.begin_capture()
                load_h(t, par, hTv[par], B0)
                for kc in range(8):
                    S.op("pe", mm(PS[B0 + 1][:, 0:288], hTv[par][:, kc, :], W_in[:, kc, 384:672], kc == 0, kc == 7),
                         [hTv[par], W_in], [PS[B0 + 1]])
                cs = cs_t[par]
                S.dma("pool", cs.ap, c_rope_p[t * 128:(t + 1) * 128, :], writes=[cs])
                cn_t, kpe_t, st = cn[par], kpe[par], stat.next()
                ckp_from_proj(PS[B0 + 1][:, 0:256], PS[B0 + 1][:, 256:288], PS[B0 + 1], 128, cs, cn_t, kpe_t, st,
                              sw[par], t1[par])
                if g == 0:
                    S.dma("pool", lat_p[t * 128:(t + 1) * 128, :], cn_t.ap, reads=[cn_t], writes=[D_out])
                    S.dma("pool", kr_p[t * 128:(t + 1) * 128, :], kpe_t.ap, reads=[kpe_t], writes=[D_out])

                def vstore(psT, t=t):
                    pv = psT[:, 256:512].rearrange("p (a b d) -> p a b d", a=2, b=2)
                    S.op("act", acp(V1[:, t, :, 0:64], pv[:, :, 0, :]), [psT], [V1])
                    S.op("act", acp(V1[:, t, :, 128:192], pv[:, :, 1, :]), [psT], [V1])

                pk = mla_kv_from_cn(cn_t.ap, cn_t, kpe_t.ap, kpe_t, 128, (g * 256, 4), KT, None, vstore,
                                    (B0 + 2, B0 + 3, B0 + 2), (cnb[par], cT[par], sq[par], ssq[par], Kt[par]),
                                    kcol0=256)
                S.op("act", acp(KT[0:96, :, t * 128:(t + 1) * 128],
                                pk[0:96, 256:768].rearrange("p (h t) -> p h t", h=4)), [PS[B0 + 2]], [KT])
                streams.append(S.end_capture())
                if len(streams) == 2:
                    S.replay(streams)
                    streams = []
            S.replay(streams)
            S.barrier()
            ar.release()
            ar.mark()
            qan = [ar.alloc([384], BF16, f"qan{i}") for i in range(2)]
            qaT = [ar.alloc([3, 128], BF16, f"qaT{i}") for i in range(2)]
            qs = [ar.alloc([4, 96], F32, f"qs{i}") for i in range(2)]
            qf = [ar.alloc([4, 96], BF16, f"qf{i}") for i in range(2)]
            QT = ar.alloc([4, 512], BF16, "QT")
            QTv = [T(QT[0:96, :, i * 128:(i + 1) * 128], f"QTv{i}") for i in range(4)]
            sz = ar.alloc([2, 512], F32, "sz")
            pT = Rot([ar.alloc([512], BF16, f"pT{i}") for i in range(3)])
            rcp = ar.alloc([512], F32, "rcp")
            tmpo = ar.alloc([512], F32, "tmpo")
            GTt = Rot([ar.alloc([2, 512], BF16, f"GTt{i}") for i in range(1)])
            for qi in range(NQ):
                streams = []
                for s_ in range(4):
                    tile = qi * 4 + s_
                    par = s_ % 2
                    B0 = 4 * par
                    S.begin_capture()
                    load_h(tile, par, hTv[s_], B0)
                    for kc in range(8):
                        S.op("pe", mm(PS[B0 + 1][:, 0:384], hTv[s_][:, kc, :], W_in[:, kc, 0:384], kc == 0, kc == 7),
                             [hTv[s_], W_in], [PS[B0 + 1]])
                    st = stat.next()
                    rms_rstd(PS[B0 + 1][:, 0:384], PS[B0 + 1], 384, st, junk2[par])
                    S.op("dve", ts(qan[par].ap, PS[B0 + 1][:, 0:384], st[:, 0:1], ALU.mult), [PS[B0 + 1], st], [qan[par]])
                    pa = psb(B0 + 2)
                    for c in range(3):
                        S.op("pe", trp(pa[:, c * 128:(c + 1) * 128], qan[par][:, c * 128:(c + 1) * 128], ident.ap),
                             [qan[par], ident], [PS[B0 + 2]])
                    S.op("dve", cp(qaT[par].ap, pa[:, 0:384].rearrange("p (c t) -> p c t", c=3)), [PS[B0 + 2]], [qaT[par]])
                    for kc in range(3):
                        S.op("pe", mm(PS[B0 + 3][:, 0:384], qaT[par][:, kc, :], W_uq[:, kc, g * 384:(g + 1) * 384],
                                      kc == 0, kc == 2), [qaT[par], W_uq], [PS[B0 + 3]])
                    q_, sq_, ssq_ = qs[par], sq[par], ssq[par]
                    S.op("act", acp(q_.ap.rearrange("p h d -> p (h d)"), PS[B0 + 3][:, 0:384]), [PS[B0 + 3]], [q_])
                    S.op("act", actf(sq_[:, 0:384], q_.ap.rearrange("p h d -> p (h d)"), AF.Square), [q_], [sq_])
                    S.op("dve", rsum(ssq_[:, 0:4], sq_[:, 0:384].rearrange("p (h d) -> p h d", h=4)), [sq_], [ssq_])
                    S.op("act", actf(ssq_[:, 0:4], ssq_[:, 0:4], AF.Sqrt, scale=1.0 / 96, bias=EPS), [ssq_], [ssq_])
                    S.op("dve", recip(ssq_[:, 0:4], ssq_[:, 0:4]), [ssq_], [ssq_])
                    cs = cs_t[par]
                    S.dma("pool", cs.ap, c_rope_p[tile * 128:(tile + 1) * 128, :], writes=[cs])
                    rope(q_[:, :, 64:96], q_, cs, 4, 128, sw[par], t1[par])
                    S.op("dve", tt(q_.ap, q_.ap, bc(ssq_[:, 0:4], [128, 4, 96], 2), ALU.mult), [q_, ssq_], [q_])
                    S.op("dve", tt(qf[par].ap, q_.ap, bc(gg_b.ap, [128, 4, 96], 1), ALU.mult), [q_, gg_b], [qf[par]])
                    pq = psb(B0 + 2)
                    for h in range(4):
                        S.op("pe", trp(pq[0:96, 384 + h * 128:384 + (h + 1) * 128], qf[par][:, h, :], ident.ap),
                             [qf[par], ident], [PS[B0 + 2]])
                    S.op("dve", cp(QTv[s_].ap, pq[0:96, 384:896].rearrange("p (h t) -> p h t", h=4)), [PS[B0 + 2]], [QTv[s_]])
                    streams.append(S.end_capture())
                    if len(streams) == 2:
                        S.replay(streams)
                        streams = []
                hall = [hTv[i] for i in range(4)]
                qall = [QTv[i] for i in range(4)]
                for c in range(2):
                    col = 672 + g * 256 + c * 128
                    for kc in range(8):
                        S.op("pe", mm(PS[4 + c].ap, W_in[:, kc, col:col + 128], hT4[:, kc, :], kc == 0, kc == 7),
                             [W_in] + hall, [PS[4 + c]])
                    S.op("act", actf(sz[:, c, :], PS[4 + c].ap, AF.Exp, scale=-1.0), [PS[4 + c]], [sz])
                    S.op("dve", ts(sz[:, c, :], sz[:, c, :], 1.0, ALU.add), [sz], [sz])
                    S.op("dve", recip(sz[:, c, :], sz[:, c, :]), [sz], [sz])
                    S.op("dve", tt(sz[:, c, :], PS[4 + c].ap, sz[:, c, :], ALU.mult), [PS[4 + c], sz], [sz])
                G = GTt.next()
                nk = 4 * qi + 4
                sbank = Rot([0, 1, 2, 3])
                units = [(h, kt) for h in range(4) for kt in range(nk)]
                pend = None

                def emit_pv(u):
                    h, kt, p = u
                    ob = 6 + (h % 2)
                    pr, par = h // 2, h % 2
                    lhs = V1[:, kt, pr, 0:128] if par == 0 else V1[:, kt, pr, 64:192]
                    S.op("pe", mm(PS[ob].ap, lhs, p.ap, kt == 0, kt == nk - 1), [V1, p], [PS[ob]])
                    if kt == nk - 1:
                        if par == 0:
                            o_r, s_r = slice(0, 64), slice(64, 128)
                        else:
                            o_r, s_r = slice(64, 128), slice(0, 64)
                        S.op("dve", recip(rcp[s_r, :], PS[ob][s_r, :]), [PS[ob]], [rcp])
                        S.op("dve", tt(tmpo[o_r, :], PS[ob][o_r, :], rcp[s_r, :], ALU.mult), [PS[ob], rcp], [tmpo])
                        S.op("dve", tt(G[o_r, pr, :], tmpo[o_r, :], sz[o_r, pr, :], ALU.mult), [tmpo, sz], [G])

                for (h, kt) in units:
                    b = sbank.next()
                    S.op("pe", mm(PS[b].ap, KT[0:96, h, kt * 128:(kt + 1) * 128], QT[0:96, h, :]),
                         [KT] + qall, [PS[b]])
                    p = pT.next()
                    S.op("act", actf(p.ap, PS[b].ap, AF.Exp), [PS[b]], [p])
                    if kt >= 4 * qi:
                        S.op("pool", tt(p.ap, p.ap, amask[:, kt - 4 * qi, :], ALU.mult), [p, amask], [p])
                    if pend is not None:
                        emit_pv(pend)
                    pend = (h, kt, p)
                emit_pv(pend)
                S.dma("sp", gt1[g, :, :, qi * 512:(qi + 1) * 512], G.ap, reads=[G], writes=[D_gt1])
            S.barrier()
            ar.release()
        S.barrier()
        ar.release()

    def phase_out_proj(src_x, D_src, gt, D_gt, Wo, dst, D_dst):
        ar.mark()
        xt = Rot([ar.alloc([1024], F32, f"xo{i}") for i in range(2)])
        gT = Rot([ar.alloc([4, 128], BF16, f"gT{i}") for i in range(2)])
        for t in range(NT):
            x, g_ = xt.next(), gT.next()
            S.dma("sp", x.ap, src_x[t * 128:(t + 1) * 128, :], reads=[D_src], writes=[x])
            for grp in range(2):
                S.dma("pool", g_[:, 2 * grp:2 * grp + 2, :], gt[grp, :, :, t * 128:(t + 1) * 128],
                      reads=[D_gt], writes=[g_])
            for half in range(2):
                b = 2 * (t % 2) + half
                for c in range(4):
                    S.op("pe", mm(PS[b].ap, g_[:, c, :], Wo[:, c, half * 512:(half + 1) * 512], c == 0, c == 3),
                         [g_, Wo], [PS[b]])
                S.op("dve", tt(x[:, half * 512:(half + 1) * 512], x[:, half * 512:(half + 1) * 512],
                               PS[b].ap, ALU.add), [x, PS[b]], [x])
            S.dma("sp", dst[t * 128:(t + 1) * 128, :], x.ap, reads=[x], writes=[D_dst])
        S.barrier()
        ar.release()


    def phase_mla_sample():
        ar.mark()
        P = NS
        SC = 16
        NCH = 128 // SC
        xs = ar.alloc([1024], F32, "xs")
        hb = ar.alloc([1024], BF16, "s_hb")
        hTs = ar.alloc([8, P], BF16, "hTs")
        cs = ar.alloc([64], F32, "s_cs")
        cn_s = ar.alloc([256], F32, "cn_s")
        kpe_s = ar.alloc([32], F32, "kpe_s")
        sw = ar.alloc([8, 32], F32, "s_sw")
        t1 = ar.alloc([8, 32], F32, "s_t1")
        qan = ar.alloc([384], BF16, "s_qan")
        qaT = ar.alloc([3, P], BF16, "s_qaT")
        qs = ar.alloc([8, 96], F32, "s_qs")
        qfb = ar.alloc([8, 96], BF16, "s_qfb")
        sq = Rot([ar.alloc([768], F32, f"s_sq{i}") for i in range(2)])
        ssq = Rot([ar.alloc([16], F32, f"s_ssq{i}") for i in range(2)])
        qnT = ar.alloc([8, P], BF16, "qnT")
        qpT = ar.alloc([8, P], BF16, "qpT")
        wukT = ar.alloc([8, 256], BF16, "wukT")
        qlatT = ar.alloc([2, NSEQ, 8, 4], BF16, "qlatT")
        qpeT = ar.alloc([NSEQ, 8, 4], BF16, "qpeT")
        smask = ar.alloc([NSEQ * 32], F32, "smask")
        pTn = ar.alloc([NSEQ * 32], BF16, "pTn")
        scn = ar.alloc([NSEQ * 32], F32, "scn")
        lbn = ar.alloc([289], BF16, "lbn")
        gl = Rot([ar.alloc([SC, 256], F32, f"gl{i}") for i in range(2)])
        gk = Rot([ar.alloc([SC, 32], F32, f"gk{i}") for i in range(2)])
        lb = Rot([ar.alloc([289], BF16, f"lb{i}") for i in range(3)])
        cT = Rot([ar.alloc([2, 128], BF16, f"s_cT{i}") for i in range(2)])
        kpT = Rot([ar.alloc([128], BF16, f"kpT{i}") for i in range(2)])
        sc = Rot([ar.alloc([32], F32, f"s_sc{i}") for i in range(2)])
        pT = Rot([ar.alloc([32], BF16, f"s_pT{i}") for i in range(3)])
        ol = ar.alloc([256], BF16, "ol")
        olr = ar.alloc([4], F32, "olr")
        olatT = ar.alloc([2, 8, P], BF16, "olatT")
        ez = ar.alloc([512], F32, "s_ez")
        z_sb = ar.alloc([512], F32, "s_zsb")
        gated = ar.alloc([512], BF16, "s_gated")
        gTs = ar.alloc([4, P], BF16, "gTs")
        for l_ in lb.items + [lbn]:
            S.op("pool", mset(l_[:, 256:257], 1.0), [], [l_])
        S.dma("sp", smask[0:P, :], c_smask, writes=[smask])
        S.dma("sp", cs[0:P, :], c_rope_s, writes=[cs])
        S.dma("sp", xs[0:P, :], x_s, writes=[xs])
        st = stat.next()
        rms_rstd(xs[0:P, :], xs, 1024, st, junk, P)
        S.op("dve", ts(hb[0:P, :], xs[0:P, :], st[0:P, 0:1], ALU.mult), [xs, st], [hb])
        pb = psb(0)
        for c in range(8):
            S.op("pe", trp(pb[:, c * P:(c + 1) * P], hb[0:P, c * 128:(c + 1) * 128], ident[0:P, 0:P]), [hb, ident], [PS[0]])
        S.op("act", acp(hTs.ap, pb[:, 0:8 * P].rearrange("p (c t) -> p c t", c=8)), [PS[0]], [hTs])
        for (bank, c0, w) in ((1, 384, 288), (2, 0, 384), (3, 672, 512)):
            for kc in range(8):
                S.op("pe", mm(PS[bank][0:P, 0:w], hTs[:, kc, :], W_in[:, kc, c0:c0 + w], kc == 0, kc == 7),
                     [hTs, W_in], [PS[bank]])
        st = stat.next()
        ckp_from_proj(PS[1][0:P, 0:256], PS[1][0:P, 256:288], PS[1], P, cs, cn_s, kpe_s, st, sw, t1)
        S.dma("sp", lat_s, cn_s[0:P, :], reads=[cn_s], writes=[D_out])
        S.dma("sp", kr_s, kpe_s[0:P, :], reads=[kpe_s], writes=[D_out])
        st = stat.next()
        rms_rstd(PS[2][0:P, 0:384], PS[2], 384, st, junk, P)
        S.op("dve", ts(qan[0:P, :], PS[2][0:P, 0:384], st[0:P, 0:1], ALU.mult), [PS[2], st], [qan])
        pa = psb(0)
        for c in range(3):
            S.op("pe", trp(pa[:, c * P:(c + 1) * P], qan[0:P, c * 128:(c + 1) * 128], ident[0:P, 0:P]), [qan, ident], [PS[0]])
        S.op("dve", cp(qaT.ap, pa[:, 0:3 * P].rearrange("p (c t) -> p c t", c=3)), [PS[0]], [qaT])
        for (bank, c0, w) in ((4, 0, 384), (5, 384, 384)):
            for kc in range(3):
                S.op("pe", mm(PS[bank][0:P, 0:w], qaT[:, kc, :], W_uq[:, kc, c0:c0 + w], kc == 0, kc == 2), [qaT, W_uq], [PS[bank]])
            S.op("act", acp(qs[0:P, c0 // 96:c0 // 96 + 4, :].rearrange("p h d -> p (h d)"), PS[bank][0:P, 0:w]), [PS[bank]], [qs])
        sq_, ssq_ = sq.next(), ssq.next()
        S.op("act", actf(sq_[0:P, 0:768], qs[0:P].rearrange("p h d -> p (h d)"), AF.Square), [qs], [sq_])
        S.op("dve", rsum(ssq_[0:P, 0:8], sq_[0:P, 0:768].rearrange("p (h d) -> p h d", h=8)), [sq_], [ssq_])
        S.op("act", actf(ssq_[0:P, 0:8], ssq_[0:P, 0:8], AF.Sqrt, scale=1.0 / 96, bias=EPS), [ssq_], [ssq_])
        S.op("dve", recip(ssq_[0:P, 0:8], ssq_[0:P, 0:8]), [ssq_], [ssq_])
        rope(qs[0:P, :, 64:96], qs, cs, 8, P, sw, t1)
        S.op("dve", tt(qs[0:P], qs[0:P], bc(ssq_[0:P, 0:8], [P, 8, 96], 2), ALU.mult), [qs, ssq_], [qs])
        S.op("dve", tt(qfb[0:P], qs[0:P], bc(gg_b[0:P, :], [P, 8, 96], 1), ALU.mult), [qs, gg_b], [qfb])
        pq = psb(0)
        for h in range(8):
            S.op("pe", trp(pq[0:64, h * P:(h + 1) * P], qfb[0:P, h, 0:64], ident[0:P, 0:P]), [qfb, ident], [PS[0]])
        S.op("dve", cp(qnT[0:64], pq[0:64, 0:8 * P].rearrange("p (h t) -> p h t", h=8)), [PS[0]], [qnT])
        for h in range(8):
            S.op("pe", trp(pq[0:32, h * P:(h + 1) * P], qfb[0:P, h, 64:96], ident[0:P, 0:P]), [qfb, ident], [PS[0]])
        S.op("dve", cp(qpT[0:32], pq[0:32, 0:8 * P].rearrange("p (h t) -> p h t", h=8)), [PS[0]], [qpT])
        S.op("dve", cp(qpeT[0:32].rearrange("p n h t -> p h n t"),
                       qpT[0:32].rearrange("p h (n t) -> p h n t", t=4)), [qpT], [qpeT])
        for kc in range(2):
            pw = psb(4 + kc)
            for h in range(8):
                S.op("pe", trp(pw[0:64, h * 128:(h + 1) * 128], W_uk[:, kc, h * 64:(h + 1) * 64], ident.ap), [W_uk, ident], [PS[4 + kc]])
            S.op("act", acp(wukT[0:64, :, kc * 128:(kc + 1) * 128], pw[0:64, 0:1024].rearrange("p (h c) -> p h c", h=8)),
                 [PS[4 + kc]], [wukT])
        for ck in range(2):
            for h in range(8):
                S.op("pe", mm(PS[4 + ck][:, h * P:(h + 1) * P], wukT[0:64, h, ck * 128:(ck + 1) * 128], qnT[0:64, h, :]),
                     [wukT, qnT], [PS[4 + ck]])
            S.op("act", acp(qlatT[:, ck].rearrange("p n h t -> p h n t"),
                            PS[4 + ck][:, 0:8 * P].rearrange("p (h n t) -> p h n t", h=8, t=4)), [PS[4 + ck]], [qlatT])

        def tile_scores(lbt, parts, kp_ap, kpT_src, bank_t, bank_k, bank_s, rhs_lat, rhs_pe, ncols):
            pbt = psb(bank_t)
            cT_, kpT_ = cT.next(), kpT.next()
            for c in range(2):
                S.op("pe", trp(pbt[:, c * parts:(c + 1) * parts], lbt[0:parts, c * 128:(c + 1) * 128], ident[0:parts, 0:parts]),
                     [lbt, ident], [PS[bank_t]])
            S.op("pe", trp(pbt[0:32, 256:256 + parts], lbt[0:parts, 257:289], ident[0:parts, 0:parts]), [lbt, ident], [PS[bank_t]])
            S.op("act", acp(cT_[:, :, 0:parts], pbt[:, 0:2 * parts].rearrange("p (c t) -> p c t", c=2)), [PS[bank_t]], [cT_])
            S.op("dve", cp(kpT_[0:32, 0:parts], pbt[0:32, 256:256 + parts]), [PS[bank_t]], [kpT_])
            for kc in range(2):
                S.op("pe", mm(PS[bank_k][0:parts, :], cT_[:, kc, 0:parts], W_uk[:, kc, :], kc == 0, kc == 1), [cT_, W_uk], [PS[bank_k]])
            sq_, ssq_ = sq.next(), ssq.next()
            S.op("act", actf(sq_[0:parts, 0:512], PS[bank_k][0:parts, :], AF.Square), [PS[bank_k]], [sq_])
            S.op("dve", rsum(ssq_[0:parts, 0:8], sq_[0:parts, 0:512].rearrange("p (h d) -> p h d", h=8)), [sq_], [ssq_])
            S.op("act", actf(sq_[0:parts, 512:544], kp_ap, AF.Square, accum_out=ssq_[0:parts, 8:9]), [kpT_src], [sq_, ssq_])
            S.op("dve", ts(ssq_[0:parts, 0:8], ssq_[0:parts, 0:8], ssq_[0:parts, 8:9], ALU.add), [ssq_], [ssq_])
            S.op("act", actf(ssq_[0:parts, 0:8], ssq_[0:parts, 0:8], AF.Sqrt, scale=1.0 / 96, bias=EPS), [ssq_], [ssq_])
            S.op("dve", recip(ssq_[0:parts, 0:8], ssq_[0:parts, 0:8]), [ssq_], [ssq_])
            for kc in range(2):
                S.op("pe", mm(PS[bank_s][0:parts, 0:ncols], cT_[:, kc, 0:parts], rhs_lat(kc), kc == 0, False),
                     [cT_, qlatT], [PS[bank_s]])
            S.op("pe", mm(PS[bank_s][0:parts, 0:ncols], kpT_[0:32, 0:parts], rhs_pe, False, True), [kpT_, qpeT], [PS[bank_s]])
            return ssq_

        S.op("act", acp(lbn[0:P, 0:256], cn_s[0:P, :]), [cn_s], [lbn])
        S.op("act", acp(lbn[0:P, 257:289], kpe_s[0:P, :]), [kpe_s], [lbn])
        NC_ = NSEQ * 32
        r_ = tile_scores(lbn, P, kpe_s[0:P, :], kpe_s, 0, 1, 2,
                         lambda kc: qlatT[:, kc].rearrange("p n h t -> p (n h t)"),
                         qpeT[0:32].rearrange("p n h t -> p (n h t)"), NC_)
        S.op("dve", tt(scn[0:P, :].rearrange("p (n h t) -> p n h t", h=8, t=4),
                       PS[2][0:P, 0:NC_].rearrange("p (n h t) -> p n h t", h=8, t=4),
                       r_[0:P, 0:8].unsqueeze(1).unsqueeze(3).to_broadcast([P, NSEQ, 8, 4]), ALU.mult), [PS[2], r_], [scn])
        S.op("act", actf(scn[0:P, :], scn[0:P, :], AF.Exp), [scn], [scn])
        S.op("dve", tt(pTn[0:P, :], scn[0:P, :], smask[0:P, :], ALU.mult), [scn, smask], [pTn])

        S.op("act", acp(z_sb[0:P, :], PS[3][0:P, :]), [PS[3]], [z_sb])
        lbc = Rot([ar.alloc([SC, 289], BF16, f"lbc{i}") for i in range(2)])
        for l_ in lbc.items:
            S.op("pool", mset(l_[:, :, 256:257], 1.0), [], [l_])
        cT2 = Rot([ar.alloc([2, 2, 128], BF16, f"cT2_{i}") for i in range(2)])
        kpT2 = Rot([ar.alloc([2, 128], BF16, f"kpT2_{i}") for i in range(2)])
        sq2 = Rot([ar.alloc([1024], F32, f"sq2_{i}") for i in range(2)])
        sqk = Rot([ar.alloc([SC, 32], F32, f"sqk{i}") for i in range(2)])
        ssqc = Rot([ar.alloc([SC, 8], F32, f"ssqc{i}") for i in range(2)])
        sspc = Rot([ar.alloc([SC], F32, f"sspc{i}") for i in range(2)])
        scr = Rot([ar.alloc([SC, 32], F32, f"scr{i}") for i in range(2)])
        pTc = Rot([ar.alloc([SC, 32], BF16, f"pTc{i}") for i in range(2)])
        OLB = 7
        nb = 0

        def stage_b(n, ch, lc_, ssq_, ssp_, scr_, first_of_seq):
            p_ = pTc.next()
            S.op("dve", tt(ssq_.ap, ssq_.ap, bc(ssp_.ap, [128, SC, 8], 2), ALU.add), [ssq_, ssp_], [ssq_])
            S.op("act", actf(ssq_.ap, ssq_.ap, AF.Sqrt, scale=1.0 / 96, bias=EPS), [ssq_], [ssq_])
            S.op("dve", recip(ssq_.ap, ssq_.ap), [ssq_], [ssq_])
            S.op("dve", tt(scr_.ap.rearrange("p s (h t) -> p s h t", t=4), scr_.ap.rearrange("p s (h t) -> p s h t", t=4),
                           bc(ssq_.ap, [128, SC, 8, 4], 3), ALU.mult), [scr_, ssq_], [scr_])
            S.op("act", actf(p_.ap, scr_.ap, AF.Exp), [scr_], [p_])
            if first_of_seq:
                S.op("pe", mm(PS[OLB][0:32, 0:257], pTn[0:P, n * 32:(n + 1) * 32], lbn[0:P, 0:257], True, False),
                     [pTn, lbn], [PS[OLB]])
            for i in range(SC):
                last = (ch == NCH - 1 and i == SC - 1)
                S.op("pe", mm(PS[OLB][0:32, 0:257], p_[:, i, :], lc_[:, i, 0:257], False, last), [p_, lc_], [PS[OLB]])

        def seq_epilogue(n):
            OL = OLB
            S.op("dve", recip(olr[0:32, 0:1], PS[OL][0:32, 256:257]), [PS[OL]], [olr])
            S.op("dve", ts(ol[0:32, :], PS[OL][0:32, 0:256], olr[0:32, 0:1], ALU.mult), [PS[OL], olr], [ol])
            po = psb(6)
            for ck in range(2):
                S.op("pe", trp(po[:, 512 + ck * 32:512 + (ck + 1) * 32], ol[0:32, ck * 128:(ck + 1) * 128], ident[0:32, 0:32]),
                     [ol, ident], [PS[6]])
            S.op("act", acp(olatT[:, :, :, n * 4:(n + 1) * 4],
                            po[:, 512:576].rearrange("p (c h t) -> p c h t", c=2, t=4)), [PS[6]], [olatT])

        pending = None
        for n in range(NSEQ):
            for ch in range(NCH):
                gl_, gk_ = gl.next(), gk.next()
                S.dma_fn("pool", lambda e, gl_=gl_, n=n, ch=ch: e.indirect_dma_start(
                    out=gl_.ap.rearrange("p s c -> p (s c)"), out_offset=None, in_=cache_lat,
                    in_offset=bass.IndirectOffsetOnAxis(ap=ptab_sb[:, n:n + 1], axis=0),
                    element_offset=ch * SC * 256), [ptab_sb], [gl_])
                S.dma_fn("pool", lambda e, gk_=gk_, n=n, ch=ch: e.indirect_dma_start(
                    out=gk_.ap.rearrange("p s c -> p (s c)"), out_offset=None, in_=cache_kr,
                    in_offset=bass.IndirectOffsetOnAxis(ap=ptab_sb[:, n:n + 1], axis=0),
                    element_offset=ch * SC * 32), [ptab_sb], [gk_])
                lc_, ssq_, ssp_, scr_, qk_ = lbc.next(), ssqc.next(), sspc.next(), scr.next(), sqk.next()
                S.op("pool", cp(lc_[:, :, 257:289], gk_.ap), [gk_], [lc_])
                S.op("act", actf(qk_.ap, gk_.ap, AF.Square), [gk_], [qk_])
                S.op("dve", rsum(ssp_.ap, qk_.ap), [qk_], [ssp_])
                for j in range(SC // 2):
                    sl = slice(2 * j, 2 * j + 2)
                    S.op("pool", cp(lc_[:, sl, 0:256], gl_[:, sl, :]), [gl_], [lc_])
                    bT = nb % 2
                    bK = (2, 3) if nb % 2 == 0 else (4, 5)
                    nb += 1
                    pbt = psb(bT)
                    c2, k2 = cT2.next(), kpT2.next()
                    for g_ in range(2):
                        for ck in range(2):
                            S.op("pe", trp(pbt[:, (g_ * 2 + ck) * 128:(g_ * 2 + ck + 1) * 128],
                                           lc_[:, 2 * j + g_, ck * 128:(ck + 1) * 128], ident.ap), [lc_, ident], [PS[bT]])
                        S.op("pe", trp(pbt[0:32, 512 + g_ * 128:512 + (g_ + 1) * 128], lc_[:, 2 * j + g_, 257:289], ident.ap),
                             [lc_, ident], [PS[bT]])
                    S.op("act", acp(c2.ap.rearrange("p g c t -> p (g c t)"), pbt[:, 0:512]), [PS[bT]], [c2])
                    S.op("dve", cp(k2[0:32].rearrange("p g t -> p (g t)"), pbt[0:32, 512:768]), [PS[bT]], [k2])
                    for g_ in range(2):
                        for kc in range(2):
                            S.op("pe", mm(PS[bK[g_]].ap, c2[:, g_, kc, :], W_uk[:, kc, :], kc == 0, kc == 1), [c2, W_uk], [PS[bK[g_]]])
                    for g_ in range(2):
                        for kc in range(2):
                            S.op("pe", mm(PS[6][:, g_ * 32:(g_ + 1) * 32], c2[:, g_, kc, :],
                                          qlatT[:, kc, n].rearrange("p h t -> p (h t)"), kc == 0, False), [c2, qlatT], [PS[6]])
                        S.op("pe", mm(PS[6][:, g_ * 32:(g_ + 1) * 32], k2[0:32, g_, :],
                                      qpeT[0:32, n].rearrange("p h t -> p (h t)"), False, True), [k2, qpeT], [PS[6]])
                    q2 = sq2.next()
                    S.op("act", actf(q2.ap, PSD[bK[0] // 2], AF.Square), [PS[bK[0]], PS[bK[1]]], [q2])
                    S.op("dve", rsum(ssq_[:, sl, :].rearrange("p g h -> p (g h)"), q2.ap.rearrange("p (x d) -> p x d", d=64)),
                         [q2], [ssq_])
                    S.op("dve", cp(scr_[:, sl, :].rearrange("p g x -> p (g x)"), PS[6][:, 0:64]), [PS[6]], [scr_])
                    if j == 1 and pending is not None:
                        stage_b(*pending)
                        if pending[-1] is False and pending[1] == NCH - 1:
                            pass
                        pending = None
                        if ch == 0 and n > 0:
                            seq_epilogue(n - 1)
                pending = (n, ch, lc_, ssq_, ssp_, scr_, ch == 0)
        stage_b(*pending)
        seq_epilogue(NSEQ - 1)
        for h in range(8):
            for ck in range(2):
                S.op("pe", mm(PS[1][0:P, h * 64:(h + 1) * 64], olatT[:, ck, h, :], W_uv[:, ck, h * 64:(h + 1) * 64], ck == 0, ck == 1),
                     [olatT, W_uv], [PS[1]])
        S.op("act", actf(ez[0:P, :], z_sb[0:P, :], AF.Exp, scale=-1.0), [z_sb], [ez])
        S.op("dve", ts(ez[0:P, :], ez[0:P, :], 1.0, ALU.add), [ez], [ez])
        S.op("dve", recip(ez[0:P, :], ez[0:P, :]), [ez], [ez])
        S.op("dve", tt(ez[0:P, :], z_sb[0:P, :], ez[0:P, :], ALU.mult), [z_sb, ez], [ez])
        S.op("dve", tt(gated[0:P, :], PS[1][0:P, :], ez[0:P, :], ALU.mult), [PS[1], ez], [gated])
        pg = psb(0)
        for c in range(4):
            S.op("pe", trp(pg[:, c * P:(c + 1) * P], gated[0:P, c * 128:(c + 1) * 128], ident[0:P, 0:P]), [gated, ident], [PS[0]])
        S.op("act", acp(gTs.ap, pg[:, 0:4 * P].rearrange("p (c t) -> p c t", c=4)), [PS[0]], [gTs])
        for half in range(2):
            for c in range(4):
                S.op("pe", mm(PS[4 + half][0:P, :], gTs[:, c, :], W_o[:, c, half * 512:(half + 1) * 512], c == 0, c == 3),
                     [gTs, W_o], [PS[4 + half]])
            S.op("dve", tt(xs1[0:P, half * 512:(half + 1) * 512], xs[0:P, half * 512:(half + 1) * 512],
                           PS[4 + half][0:P, :], ALU.add), [xs, PS[4 + half]], [xs1])
        if "s2" not in phases:
            S.dma("sp", y_s, xs1[0:P, :], reads=[xs1], writes=[D_out])
        S.barrier()
        ar.release()

    def load_gdn_weights():
        W = {}
        bn_sb = ar.alloc([8], F32, "bn_sb")
        S.dma("sp", bn_sb.ap, b_norm, writes=[bn_sb])
        wtmp2 = ar.alloc([2064], F32, "wtmp2")
        W["in"] = ar.alloc([8, 2064], BF16, "Wb_in")
        load_bf16_rows(W["in"], [W["in"][:, kc, :] for kc in range(8)],
                       [b_w_in[kc * 128:(kc + 1) * 128, :] for kc in range(8)], (bn_sb, bn_sb), wtmp2, 2064)
        W["o"] = ar.alloc([4, 1024], BF16, "Wb_o")
        load_bf16_rows(W["o"], [W["o"][:, kc, :] for kc in range(4)],
                       [b_w_o[kc * 128:(kc + 1) * 128, :] for kc in range(4)], None, wtmp2, 1024)
        W["convT"] = ar.alloc([12, 4], F32, "wconvT")
        S.dma("sp", W["convT"].ap.rearrange("p c j -> p (c j)"), b_w_convT, writes=[W["convT"]])
        W["negA"] = ar.alloc([8], F32, "negA")
        S.dma("sp", W["negA"].ap, b_a_log.to_broadcast([128, 8]), writes=[W["negA"]])
        S.op("act", actf(W["negA"].ap, W["negA"].ap, AF.Exp), [W["negA"]], [W["negA"]])
        S.op("dve", ts(W["negA"].ap, W["negA"].ap, -1.0, ALU.mult), [W["negA"]], [W["negA"]])
        W["dtb"] = ar.alloc([8], F32, "dtb")
        S.dma("sp", W["dtb"].ap, b_dt_bias.to_broadcast([128, 8]), writes=[W["dtb"]])
        W["go"] = ar.alloc([64], F32, "go")
        S.dma("sp", W["go"].ap, b_g_o.to_broadcast([128, 64]), writes=[W["go"]])
        for nm, src in (("tri", c_tri), ("strict", c_strict), ("negm", c_negm)):
            W[nm] = ar.alloc([128], F32, nm)
            S.dma("sp", W[nm].ap, src, writes=[W[nm]])
        W["ones"] = ar.alloc([128], F32, "ones_f")
        S.op("pool", mset(W["ones"].ap, 1.0), [], [W["ones"]])
        blk_f = ar.alloc([128], F32, "blk_f")
        S.dma("sp", blk_f.ap, c_blk, writes=[blk_f])
        W["blk"] = ar.alloc([128], BF16, "blk")
        S.op("dve", cp(W["blk"].ap, blk_f.ap), [blk_f], [W["blk"]])
        return W

    def gates(xa_ap, xb_ap, srcT, parts, nh, W, hcol, gt, bt, tmp):
        a1, a2 = tmp
        P = slice(0, parts)
        S.op("dve", tt(a1[P, 0:nh], xa_ap, W["dtb"][P, hcol:hcol + nh], ALU.add), [srcT, W["dtb"]], [a1])
        S.op("dve", stt(a2[P, 0:nh], a1[P, 0:nh], -1.0, a1[P, 0:nh], ALU.mult, ALU.max), [a1], [a2])
        S.op("act", actf(a2[P, 0:nh], a2[P, 0:nh], AF.Exp, scale=-1.0), [a2], [a2])
        S.op("act", actf(a2[P, 0:nh], a2[P, 0:nh], AF.Ln, bias=1.0), [a2], [a2])
        S.op("dve", stt(a1[P, 0:nh], a1[P, 0:nh], 0.0, a2[P, 0:nh], ALU.max, ALU.add), [a1, a2], [a1])
        S.op("dve", tt(gt[P, 0:nh], a1[P, 0:nh], W["negA"][P, hcol:hcol + nh], ALU.mult), [a1, W["negA"]], [gt])
        S.op("act", actf(a2[P, 0:nh], xb_ap, AF.Exp, scale=-1.0), [srcT], [a2])
        S.op("dve", ts(a2[P, 0:nh], a2[P, 0:nh], 1.0, ALU.add), [a2], [a2])
        S.op("dve", recip(bt[P, 0:nh], a2[P, 0:nh]), [a2], [bt])

    def phase_gdn_prompt(W):
        ar.mark()
        NTL = SEQ // 512
        xt = Rot([ar.alloc([1024], F32, f"gx{i}") for i in range(2)])
        hb = ar.alloc([1024], BF16, "ghb")
        hT4 = ar.alloc([8, 512], BF16, "ghT4")
        qk = [ar.alloc([6, 515], F32, f"qk{i}") for i in range(2)]
        cs = ar.alloc([6, 512], F32, "gcs")
        sqb = ar.alloc([512], BF16, "sqb")
        sd = ar.alloc([512], F32, "sd")
        QKn = ar.alloc([4, 512], BF16, "QKn")
        vb = ar.alloc([2, 512], BF16, "vb")
        KnZ = ar.alloc([2, 2, 512], BF16, "KnZ")
        SbZ = ar.alloc([2, 2, 64], BF16, "SbZ")
        S.op("pool", mset(KnZ.ap, 0.0), [], [KnZ])
        gt_, bt_ = ar.alloc([4], F32, "g_t"), ar.alloc([4], F32, "b_t")
        a1, a2 = ar.alloc([4], F32, "ga1"), ar.alloc([4], F32, "ga2")
        smp = [ar.alloc([32], F32, f"gsm{i}") for i in range(2)]
        smS = ar.alloc([8], F32, "gsmS")
        egp = [ar.alloc([2], F32, f"eg{i}") for i in range(2)]
        z_p = [ar.alloc([256], F32, f"z_p{i}") for i in range(2)]
        Z = ar.alloc([4, 128], F32, "Z")
        Dm = ar.alloc([4, 128], F32, "Dm")
        decay = ar.alloc([4, 128], F32, "decay")
        A1 = ar.alloc([4, 128], F32, "A1")
        bS = ar.alloc([4, 128], F32, "bS")
        Lb = ar.alloc([4, 128], BF16, "Lb")
        intra = ar.alloc([4, 128], BF16, "intra")
        intraTp = [ar.alloc([4, 128], BF16, f"intraT{i}") for i in range(2)]
        Nr = Rot([ar.alloc([4, 128], BF16, f"N{i}") for i in range(2)])
        Mr = Rot([ar.alloc([4, 128], BF16, f"M{i}") for i in range(2)])
        ImLT = ar.alloc([4, 128], BF16, "ImLT")
        Yr = Rot([ar.alloc([4, 128], BF16, f"Y{i}") for i in range(2)])
        Ywz = ar.alloc([2, 2, 128], BF16, "Ywz")
        ktzp = [ar.alloc([2, 2, 128], BF16, f"ktz{i}") for i in range(2)]
        S.op("pool", mset(Ywz.ap, 0.0), [], [Ywz])
        for k_ in ktzp:
            S.op("pool", mset(k_.ap, 0.0), [], [k_])
        ktok = ar.alloc([4, 64], F32, "ktok")
        u_p = [ar.alloc([4, 64], F32, f"u_sb{i}") for i in range(2)]
        wT_p = [ar.alloc([2, 128], BF16, f"wT_sb{i}") for i in range(2)]
        vn = ar.alloc([4, 64], BF16, "vn")
        o_sb = ar.alloc([4, 64], F32, "o_sb")
        o_t = ar.alloc([4, 64], F32, "o_t")
        St = ar.alloc([2, 64], F32, "St")
        St_t = ar.alloc([2, 64], F32, "St_t")
        Sb = ar.alloc([2, 64], BF16, "Sb")
        sgz = ar.alloc([256], F32, "sgz")
        gated = ar.alloc([256], BF16, "gated")
        G2 = Rot([ar.alloc([2, 512], BF16, f"G2{i}") for i in range(2)])
        ident_b = ident
        crow = ar.alloc([6, 128], F32, "crow")

        def load_h(tile, col):
            x = xt.next()
            S.dma("sp", x.ap, xp1[tile * 128:(tile + 1) * 128, :], reads=[D_xp1], writes=[x])
            st = stat.next()
            rms_rstd(x.ap, x, 1024, st, junk)
            S.op("dve", ts(hb.ap, x.ap, st[:, 0:1], ALU.mult), [x, st], [hb])
            pb = psb(0)
            for c in range(8):
                S.op("pe", trp(pb[:, c * 128:(c + 1) * 128], hb[:, c * 128:(c + 1) * 128], ident.ap),
                     [hb, ident], [PS[0]])
            S.op("act", acp(hT4[:, :, col:col + 128], pb[:, 0:1024].rearrange("p (c t) -> p c t", c=8)),
                 [PS[0]], [hT4])

        for g in range(2):
            S.op("pool", mset(St.ap, 0.0), [], [St])
            S.op("pool", mset(SbZ.ap, 0.0), [], [SbZ])
            gch = [2 * g, 2 * g + 1, 4 + 2 * g, 5 + 2 * g, 8 + 2 * g, 9 + 2 * g]
            for ti in range(NTL):
                cur, prev = qk[ti % 2], qk[(ti + 1) % 2]
                for s in range(4):
                    load_h(ti * 4 + s, s * 128)
                if ti == 0:
                    S.op("pool", mset(cur[:, :, 0:3], 0.0), [], [cur])
                else:
                    S.op("pool", cp(cur[:, :, 0:3], prev[:, :, 512:515]), [prev], [cur])
                for lc, gc_ in enumerate(gch):
                    b = 1 + (lc % 2)
                    for kc in range(8):
                        S.op("pe", mm(PS[b].ap, W["in"][:, kc, gc_ * 128:(gc_ + 1) * 128], hT4[:, kc, :],
                                      kc == 0, kc == 7), [W["in"], hT4], [PS[b]])
                    S.op("act", acp(cur[:, lc, 3:515], PS[b].ap), [PS[b]], [cur])
                    ce = "dve"
                    S.op(ce, ts(cs[:, lc, :], cur[:, lc, 0:512], W["convT"][:, gc_, 0:1], ALU.mult),
                         [cur, W["convT"]], [cs])
                    for j in range(1, 4):
                        S.op(ce, stt(cs[:, lc, :], cur[:, lc, j:j + 512], W["convT"][:, gc_, j:j + 1], cs[:, lc, :],
                                     ALU.mult, ALU.add), [cur, W["convT"], cs], [cs])
                    S.op("act", actf(cs[:, lc, :], cs[:, lc, :], AF.Silu), [cs], [cs])
                if ti == NTL - 1:
                    for lc, gc_ in enumerate(gch):
                        S.op("pe", trp(PS[3][0:3, 0:128], cur[:, lc, 512:515], ident_f.ap), [cur, ident_f], [PS[3]])
                        S.op("dve", cp(crow[0:3, lc, :], PS[3][0:3, 0:128]), [PS[3]], [crow])
                        S.dma("sp", conv_p[:, gc_ * 128:(gc_ + 1) * 128], crow[0:3, lc, :], reads=[crow], writes=[D_out])
                for lc in range(4 if cfg.cut >= 1 else 0):
                    S.op("act", actf(sqb.ap, cs[:, lc, :], AF.Square), [cs], [sqb])
                    S.op("pe", mm(PS[3].ap, W["blk"].ap, sqb.ap), [W["blk"], sqb], [PS[3]])
                    S.op("act", actf(sd.ap, PS[3].ap, AF.Sqrt, bias=EPS), [PS[3]], [sd])
                    S.op("dve", recip(sd.ap, sd.ap), [sd], [sd])
                    S.op("dve", stt(QKn[:, lc, :], cs[:, lc, :], 0.125 if lc < 2 else 1.0, sd.ap, ALU.mult, ALU.mult),
                         [cs, sd], [QKn])
                S.op("act", acp(vb.ap, cs[:, 4:6, :]), [cs], [vb])
                S.op("pool", cp(KnZ[0:64, :, 0, :], QKn[0:64, 2:4, :]), [QKn], [KnZ])
                S.op("pool", cp(KnZ[64:128, :, 1, :], QKn[64:128, 2:4, :]), [QKn], [KnZ])
                Gt = G2.next()
                def local(s):
                    CUT = cfg.cut
                    par = s % 2
                    sm, eg, ktz, intraT, u_sb, wT_sb = smp[par], egp[par], ktzp[par], intraTp[par], u_p[par], wT_p[par]
                    tc_ = slice(s * 128, (s + 1) * 128)
                    zc = 1536 + g * 256
                    for kc in range(8):
                        S.op("pe", mm(PS[0][:, 0:256], hT4[:, kc, tc_], W["in"][:, kc, zc:zc + 256], kc == 0, kc == 7),
                             [hT4, W["in"]], [PS[0]])
                    for kc in range(8):
                        S.op("pe", mm(PS[0][:, 256:272], hT4[:, kc, tc_], W["in"][:, kc, 2048:2064], kc == 0, kc == 7),
                             [hT4, W["in"]], [PS[0]])
                    S.op("act", acp(z_p[par].ap, PS[0][:, 0:256]), [PS[0]], [z_p[par]])
                    gates(PS[0][:, 256 + 4 * g:260 + 4 * g], PS[0][:, 264 + 4 * g:268 + 4 * g], PS[0], 128, 4, W,
                          4 * g, gt_, bt_, (a1, a2))
                    S.op("pe", mm(PS[4][:, 0:4], W["tri"].ap, gt_.ap), [W["tri"], gt_], [PS[4]])
                    S.op("dve", cp(sm[:, 0:4], PS[4][:, 0:4]), [PS[4]], [sm])
                    S.op("dve", tt(Z.ap, bc(gt_.ap, [128, 4, 128], 2), bc(W["tri"].ap, [128, 4, 128], 1), ALU.mult),
                         [gt_, W["tri"]], [Z])
                    S.op("pe", mm(PS[4].ap, W["ones"].ap, Z.ap.rearrange("p h j -> p (h j)")), [W["ones"], Z], [PS[4]])
                    gcrow = PS[4].ap.rearrange("p (h j) -> p h j", h=4)
                    S.op("dve", tt(Dm.ap, bc(sm[:, 0:4], [128, 4, 128], 2), gcrow, ALU.subtract), [sm, PS[4]], [Dm])
                    S.op("dve", stt(Dm.ap, Dm.ap, 0.0, bc(W["negm"].ap, [128, 4, 128], 1), ALU.min, ALU.add),
                         [Dm, W["negm"]], [Dm])
                    S.op("act", actf(decay.ap, Dm.ap, AF.Exp), [Dm], [decay])
                    S.op("act", actf(sm[:, 4:8], sm[:, 0:4], AF.Exp), [sm], [sm])
                    S.op("dve", tt(sm[:, 8:12], sm[:, 4:8], bt_.ap, ALU.mult), [sm, bt_], [sm])
                    S.op("dve", tt(sm[:, 20:24], gcrow[:, :, 127], sm[:, 0:4], ALU.subtract), [PS[4], sm], [sm])
                    S.op("act", actf(sm[:, 12:16], sm[:, 20:24], AF.Exp), [sm], [sm])
                    S.op("act", actf(eg[0:64, :], gcrow[0:64, 0::2, 127], AF.Exp), [PS[4]], [eg])
                    S.op("act", actf(eg[64:128, :], gcrow[64:128, 1::2, 127], AF.Exp), [PS[4]], [eg])
                    if CUT < 3:
                        return
                    pb7 = psb(7)
                    for c in range(2):
                        S.op("pe", trp(pb7[:, c * 128:(c + 1) * 128], QKn[:, 2 + c, tc_], ident.ap), [QKn, ident], [PS[7]])
                    for c in range(2):
                        S.op("pe", trp(pb7[:, 256 + c * 128:256 + (c + 1) * 128], vb[:, c, tc_], ident.ap),
                             [vb, ident], [PS[7]])
                    S.op("act", acp(ktok.ap.rearrange("p h d -> p (h d)"), pb7[:, 0:256]), [PS[7]], [ktok])
                    Y = Yr.next()
                    S.op("dve", tt(Y[:, :, 0:64], pb7[:, 256:512].rearrange("p (h d) -> p h d", h=4),
                                   bc(bt_.ap, [128, 4, 64], 2), ALU.mult), [PS[7], bt_], [Y])
                    S.op("dve", tt(Y[:, :, 64:128], ktok.ap, bc(sm[:, 8:12], [128, 4, 64], 2), ALU.mult), [ktok, sm], [Y])
                    kz = ktok.ap.rearrange("p (pr par) d -> p pr par d", par=2)
                    ekv = sm[:, 12:16].rearrange("p (pr par) -> p pr par", par=2)
                    for par in range(2):
                        S.op("dve", tt(ktz[:, :, par, par * 64:(par + 1) * 64], kz[:, :, par, :],
                                       bc(ekv[:, :, par], [128, 2, 64], 2), ALU.mult), [ktok, sm], [ktz])
                    if CUT < 3.2:
                        return
                    for h in range(4):
                        pr, par = h // 2, h % 2
                        kz_ = KnZ[:, pr, par, tc_]
                        S.op("pe", mm(PS[5][:, h * 128:(h + 1) * 128], kz_, QKn[:, 2 + pr, tc_]), [QKn, KnZ], [PS[5]])
                        S.op("pe", mm(PS[6][:, h * 128:(h + 1) * 128], QKn[:, pr, tc_], kz_), [QKn, KnZ], [PS[6]])
                    if CUT < 3.4:
                        return
                    S.op("dve", tt(A1.ap.rearrange("p h j -> p (h j)"), PS[5].ap, decay.ap.rearrange("p h j -> p (h j)"),
                                   ALU.mult), [PS[5], decay], [A1])
                    S.op("dve", tt(bS.ap, bc(bt_.ap, [128, 4, 128], 2), bc(W["strict"].ap, [128, 4, 128], 1), ALU.mult),
                         [bt_, W["strict"]], [bS])
                    S.op("dve", tt(Lb.ap, A1.ap, bS.ap, ALU.mult), [A1, bS], [Lb])
                    S.op("dve", tt(intra.ap.rearrange("p h j -> p (h j)"), PS[6].ap, decay.ap.rearrange("p h j -> p (h j)"),
                                   ALU.mult), [PS[6], decay], [intra])
                    if CUT < 3.6:
                        return
                    for h in range(4):
                        S.op("pe", trp(pb7[:, h * 128:(h + 1) * 128], intra[:, h, :], ident.ap), [intra, ident], [PS[7]])
                    S.op("act", acp(intraT.ap.rearrange("p h j -> p (h j)"), pb7[:, 0:512]), [PS[7]], [intraT])
                    for h in range(4):
                        S.op("pe", trp(pb7[:, 512 + h * 128:512 + (h + 1) * 128], Lb[:, h, :], ident.ap), [Lb, ident], [PS[7]])
                    if CUT < 3.8:
                        return
                    N = Nr.next()
                    S.op("act", acp(N.ap.rearrange("p h j -> p (h j)"), pb7[:, 512:1024]), [PS[7]], [N])
                    if CUT < 3.85:
                        return
                    S.op("dve", tt(ImLT.ap, bc(ident_b.ap, [128, 4, 128], 1), N.ap, ALU.subtract), [ident_b, N], [ImLT])
                    if CUT < 4:
                        return
                    M = Lb
                    for k in range(1, 7):
                        N2 = Nr.next()
                        for h in range(4):
                            S.op("pe", mm(PS[5][:, h * 128:(h + 1) * 128], M[:, h, :], N[:, h, :]), [M, N], [PS[5]])
                        if k < 6:
                            M2 = Mr.next()
                            for h in range(4):
                                S.op("pe", mm(PS[6][:, h * 128:(h + 1) * 128], N[:, h, :], M[:, h, :]), [M, N], [PS[6]])
                        S.op("act", acp(N2.ap.rearrange("p h j -> p (h j)"), PS[5].ap), [PS[5]], [N2])
                        if k < 6:
                            S.op("dve", cp(M2.ap.rearrange("p h j -> p (h j)"), PS[6].ap), [PS[6]], [M2])
                            M = M2
                        N = N2
                        for h in range(4):
                            S.op("pe", mm(PS[4][:, h * 128:(h + 1) * 128], N[:, h, :], Y[:, h, :]), [N, Y], [PS[4]])
                        Y2 = Yr.next()
                        S.op("dve", tt(Y2.ap.rearrange("p h j -> p (h j)"), PS[4].ap, Y.ap.rearrange("p h j -> p (h j)"),
                                       ALU.add), [PS[4], Y], [Y2])
                        Y = Y2
                    if CUT < 5:
                        return
                    yv = Y[:, :, 64:128].rearrange("p (pr par) d -> p pr par d", par=2)
                    for par in range(2):
                        S.op("pool", cp(Ywz[:, :, par, par * 64:(par + 1) * 64], yv[:, :, par, :]), [Y], [Ywz])
                    for h in range(4):
                        S.op("pe", mm(PS[1][:, h * 64:(h + 1) * 64], ImLT[:, h, :], Y[:, h, 0:64]), [ImLT, Y], [PS[1]])
                    for pr in range(2):
                        for par in range(2):
                            S.op("pe", mm(PS[1][:, 256 + pr * 128:256 + (pr + 1) * 128], Ywz[:, pr, par, :], ImLT[:, 2 * pr + par, :],
                                          par == 0, par == 1), [Ywz, ImLT], [PS[1]])
                    S.op("act", acp(u_sb.ap.rearrange("p h d -> p (h d)"), PS[1][:, 0:256]), [PS[1]], [u_sb])
                    S.op("dve", cp(wT_sb.ap.rearrange("p a t -> p (a t)"), PS[1][:, 256:512]), [PS[1]], [wT_sb])

                def scan(s):
                    CUT = cfg.cut
                    par_s = s % 2
                    sm, eg, ktz, intraT, u_sb, wT_sb = smp[par_s], egp[par_s], ktzp[par_s], intraTp[par_s], u_p[par_s], wT_p[par_s]
                    tc_ = slice(s * 128, (s + 1) * 128)
                    pb2 = psb(2)
                    for h in range(4):
                        pr, par = h // 2, h % 2
                        S.op("pe", mm(PS[2][:, h * 64:(h + 1) * 64], wT_sb[:, pr, :], SbZ[:, pr, par, :]),
                             [wT_sb, SbZ], [PS[2]])
                        S.op("pe", mm(PS[3][:, h * 64:(h + 1) * 64], QKn[:, pr, tc_], SbZ[:, pr, par, :]),
                             [QKn, SbZ], [PS[3]])
                    S.op("dve", tt(vn.ap.rearrange("p h d -> p (h d)"), u_sb.ap.rearrange("p h d -> p (h d)"),
                                   PS[2][:, 0:256], ALU.subtract), [u_sb, PS[2]], [vn])
                    for h in range(4):
                        S.op("pe", mm(PS[3][:, 256 + h * 64:256 + (h + 1) * 64], intraT[:, h, :], vn[:, h, :]),
                             [intraT, vn], [PS[3]])
                    for pr in range(2):
                        for par in range(2):
                            S.op("pe", mm(PS[2][:, 256 + pr * 64:256 + (pr + 1) * 64], ktz[:, pr, par, :], vn[:, 2 * pr + par, :],
                                          par == 0, par == 1), [ktz, vn], [PS[2]])
                    S.op("dve", tt(St_t.ap, St.ap, bc(eg.ap, [128, 2, 64], 2), ALU.mult), [St, eg], [St_t])
                    S.op("dve", tt(St.ap.rearrange("p a d -> p (a d)"), St_t.ap.rearrange("p a d -> p (a d)"),
                                   PS[2][:, 256:384], ALU.add), [St_t, PS[2]], [St])
                    S.op("act", acp(SbZ[0:64, :, 0, :], St[0:64, :, :]), [St], [SbZ])
                    S.op("act", acp(SbZ[64:128, :, 1, :], St[64:128, :, :]), [St], [SbZ])
                    S.op("dve", tt(o_t.ap, PS[3][:, 0:256].rearrange("p (h d) -> p h d", h=4),
                                   bc(sm[:, 4:8], [128, 4, 64], 2), ALU.mult), [PS[3], sm], [o_t])
                    S.op("dve", tt(o_sb.ap.rearrange("p h d -> p (h d)"), o_t.ap.rearrange("p h d -> p (h d)"),
                                   PS[3][:, 256:512], ALU.add), [o_t, PS[3]], [o_sb])
                    if CUT < 7:
                        return
                    S.op("act", actf(o_t.ap, o_sb.ap, AF.Square), [o_sb], [o_t])
                    S.op("dve", rsum(smS[:, 0:4], o_t.ap), [o_t], [smS])
                    S.op("act", actf(smS[:, 0:4], smS[:, 0:4], AF.Sqrt, scale=1.0 / 64, bias=EPS), [smS], [smS])
                    S.op("dve", recip(smS[:, 0:4], smS[:, 0:4]), [smS], [smS])
                    S.op("dve", tt(o_sb.ap, o_sb.ap, bc(smS[:, 0:4], [128, 4, 64], 2), ALU.mult), [o_sb, smS], [o_sb])
                    S.op("dve", tt(o_sb.ap, o_sb.ap, bc(W["go"].ap, [128, 4, 64], 1), ALU.mult), [o_sb, W["go"]], [o_sb])
                    S.op("act", actf(sgz.ap, z_p[par_s].ap, AF.Silu), [z_p[par_s]], [sgz])
                    S.op("dve", tt(gated.ap, o_sb.ap.rearrange("p h d -> p (h d)"), sgz.ap, ALU.mult), [o_sb, sgz], [gated])
                    for c in range(2):
                        S.op("pe", trp(pb2[:, 768 + c * 128:768 + (c + 1) * 128], gated[:, c * 128:(c + 1) * 128], ident.ap),
                             [gated, ident], [PS[2]])
                    S.op("act", acp(Gt[:, :, tc_], pb2[:, 768:1024].rearrange("p (c t) -> p c t", c=2)), [PS[2]], [Gt])
                local(0)
                for s in range(4):
                    streams = []
                    if s < 3:
                        S.begin_capture()
                        local(s + 1)
                        streams.append(S.end_capture())
                    S.begin_capture()
                    scan(s)
                    streams.append(S.end_capture())
                    S.replay(streams)
                S.dma("sp", gt2[g, :, :, ti * 512:(ti + 1) * 512], Gt.ap, reads=[Gt], writes=[D_gt2])
            if cfg.cut >= 6:
                S.dma("sp", ssm_p[4 * g:4 * g + 4].rearrange("(pr par) dk dv -> (par dk) pr dv", par=2), St.ap,
                      reads=[St], writes=[D_out])
        S.barrier()
        ar.release()


    def phase_gdn_sample(W):
        ar.mark()
        P, PH = NS, NSEQ * 8
        hb = ar.alloc([1024], BF16, "g_hb")
        hTs = ar.alloc([8, P], BF16, "g_hTs")
        qkv_sb = ar.alloc([1536], F32, "qkv_sb")
        z_sb = ar.alloc([512], F32, "z_sb")
        ab_sb = ar.alloc([16], F32, "ab_sb")
        E = ar.alloc([7, 3, 64], F32, "E")
        Wc = ar.alloc([4, 3, 64], F32, "Wc")
        AB = ar.alloc([4, 2], F32, "AB")
        alog = ar.alloc([1], F32, "alog")
        dtb = ar.alloc([1], F32, "dtbL")
        cv = ar.alloc([4, 3, 64], F32, "cv")
        cv2 = ar.alloc([4, 3, 64], F32, "cv2")
        nr = ar.alloc([4, 2], F32, "nr")
        qk = ar.alloc([4, 2, 64], F32, "qkn")
        g1, g2, gg_, eg, bet = [ar.alloc([4], F32, f"sg{i}") for i in range(5)]
        St = ar.alloc([64, 64], F32, "StS")
        tmp = ar.alloc([64, 64], F32, "tmpS")
        kS = ar.alloc([64], F32, "kS")
        dl = ar.alloc([64], F32, "dl")
        o_all = ar.alloc([4, 64], F32, "o_all")
        o_tok = ar.alloc([8, 64], F32, "o_tok")
        o_sq = ar.alloc([8, 64], F32, "o_sq")
        orr = ar.alloc([8], F32, "orr")
        gated = ar.alloc([512], BF16, "g_gated")
        gTs = ar.alloc([4, P], BF16, "g_gTs")
        yv = ar.alloc([1024], F32, "yv")
        st = stat.next()
        rms_rstd(xs1[0:P, :], xs1, 1024, st, junk, P)
        S.op("dve", ts(hb[0:P, :], xs1[0:P, :], st[0:P, 0:1], ALU.mult), [xs1, st], [hb])
        pb = psb(0)
        for c in range(8):
            S.op("pe", trp(pb[:, c * P:(c + 1) * P], hb[0:P, c * 128:(c + 1) * 128], ident[0:P, 0:P]), [hb, ident], [PS[0]])
        S.op("act", acp(hTs.ap, pb[:, 0:8 * P].rearrange("p (c t) -> p c t", c=8)), [PS[0]], [hTs])
        for (bank, c0, w, dst) in ((1, 0, 512, qkv_sb[0:P, 0:512]), (2, 512, 512, qkv_sb[0:P, 512:1024]),
                                   (3, 1024, 512, qkv_sb[0:P, 1024:1536]), (4, 1536, 512, z_sb[0:P, :]),
                                   (5, 2048, 16, ab_sb[0:P, :])):
            for kc in range(8):
                S.op("pe", mm(PS[bank][0:P, 0:w], hTs[:, kc, :], W["in"][:, kc, c0:c0 + w], kc == 0, kc == 7),
                     [hTs, W["in"]], [PS[bank]])
            dT = qkv_sb if bank <= 3 else (z_sb if bank == 4 else ab_sb)
            S.op("act", acp(dst, PS[bank][0:P, 0:w]), [PS[bank]], [dT])
        S.dma("sp", qs_scr, qkv_sb[0:P, :], reads=[qkv_sb], writes=[D_qs])
        S.dma("sp", ab_scr, ab_sb[0:P, :], reads=[ab_sb], writes=[D_ab])
        S.dma("pool", E[0:PH, 0:3].rearrange("p r s d -> p (r s d)"), st_convL, writes=[E])
        S.dma("pool", Wc[0:PH].rearrange("p j s d -> p (j s d)"), b_w_convL, writes=[Wc])
        S.dma("pool", alog[0:PH, :], b_alogL, writes=[alog])
        S.dma("pool", dtb[0:PH, :], b_dtbL, writes=[dtb])
        S.dma("pool", St[0:PH].rearrange("p k v -> p (k v)"), st_ssm, writes=[St])
        for n in range(NSEQ):
            q_ = dmaq.next()
            for t in range(4):
                S.dma(q_, E[n * 8:(n + 1) * 8, 3 + t, :, :],
                      qs_scr[n * 4 + t].rearrange("(s h d) -> h s d", s=3, h=8), reads=[D_qs], writes=[E])
            S.dma(q_, AB[n * 8:(n + 1) * 8, :, :], ab_scr[n * 4:(n + 1) * 4, :].rearrange("t (x h) -> h t x", x=2),
                  reads=[D_ab], writes=[AB], allow_slow_non_contiguous=True)
        S.dma("sp", conv_s, E[0:PH, 4:7].rearrange("p r s d -> p (r s d)"), reads=[E], writes=[D_out])
        H = slice(0, PH)
        S.op("dve", tt(cv[H], E[H, 0:4], bc(Wc[H, 0], [PH, 4, 3, 64], 1), ALU.mult), [E, Wc], [cv])
        for j_ in range(1, 4):
            S.op("dve", tt(cv2[H], E[H, j_:j_ + 4], bc(Wc[H, j_], [PH, 4, 3, 64], 1), ALU.mult), [E, Wc], [cv2])
            S.op("dve", tt(cv[H], cv[H], cv2[H], ALU.add), [cv, cv2], [cv])
        S.op("act", actf(cv[H], cv[H], AF.Silu), [cv], [cv])
        S.op("act", actf(cv2[H, :, 0:2, :], cv[H, :, 0:2, :], AF.Square), [cv], [cv2])
        S.op("dve", rsum(nr[H], cv2[H, :, 0:2, :]), [cv2], [nr])
        S.op("act", actf(nr[H], nr[H], AF.Sqrt, bias=EPS), [nr], [nr])
        S.op("dve", recip(nr[H], nr[H]), [nr], [nr])
        S.op("dve", tt(qk[H], cv[H, :, 0:2, :], bc(nr[H], [PH, 4, 2, 64], 3), ALU.mult), [cv, nr], [qk])
        S.op("dve", ts(qk[H, :, 0, :], qk[H, :, 0, :], 0.125, ALU.mult), [qk], [qk])
        S.op("dve", ts(g1[H], AB[H, :, 0], dtb[H, 0:1], ALU.add), [AB, dtb], [g1])
        S.op("dve", stt(g2[H], g1[H], -1.0, g1[H], ALU.mult, ALU.max), [g1], [g2])
        S.op("act", actf(g2[H], g2[H], AF.Exp, scale=-1.0), [g2], [g2])
        S.op("act", actf(g2[H], g2[H], AF.Ln, bias=1.0), [g2], [g2])
        S.op("dve", stt(g1[H], g1[H], 0.0, g2[H], ALU.max, ALU.add), [g1, g2], [g1])
        S.op("act", actf(alog[H], alog[H], AF.Exp), [alog], [alog])
        S.op("dve", ts(gg_[H], g1[H], alog[H, 0:1], ALU.mult, -1.0, ALU.mult), [g1, alog], [gg_])
        S.op("act", actf(eg[H], gg_[H], AF.Exp), [gg_], [eg])
        S.op("act", actf(g2[H], AB[H, :, 1], AF.Exp, scale=-1.0), [AB], [g2])
        S.op("dve", ts(g2[H], g2[H], 1.0, ALU.add), [g2], [g2])
        S.op("dve", recip(bet[H], g2[H]), [g2], [bet])
        for t in range(4):
            q_t, k_t, v_t = qk[H, t, 0, :], qk[H, t, 1, :], cv[H, t, 2, :]
            S.op("dve", ts(St[H], St[H], eg[H, t:t + 1], ALU.mult), [St, eg], [St])
            S.op("dve", tt(tmp[H], St[H], bc(k_t, [PH, 64, 64], 2), ALU.mult), [St, qk], [tmp])
            S.op("dve", rsum(kS[H], tmp[H].rearrange("p k v -> p v k")), [tmp], [kS])
            S.op("dve", tt(dl[H], v_t, kS[H], ALU.subtract), [cv, kS], [dl])
            S.op("dve", ts(dl[H], dl[H], bet[H, t:t + 1], ALU.mult), [dl, bet], [dl])
            S.op("dve", tt(tmp[H], bc(k_t, [PH, 64, 64], 2), bc(dl[H], [PH, 64, 64], 1), ALU.mult), [qk, dl], [tmp])
            S.op("dve", tt(St[H], St[H], tmp[H], ALU.add), [St, tmp], [St])
            S.op("dve", tt(tmp[H], St[H], bc(q_t, [PH, 64, 64], 2), ALU.mult), [St, qk], [tmp])
            S.op("dve", rsum(o_all[H, t, :], tmp[H].rearrange("p k v -> p v k")), [tmp], [o_all])
        S.dma("sp", ssm_s, St[0:PH].rearrange("p k v -> p (k v)"), reads=[St], writes=[D_out])
        S.dma("sp", os_scr, o_all[0:PH].rearrange("p t d -> p (t d)"), reads=[o_all], writes=[D_os])
        for n in range(NSEQ):
            S.dma(dmaq.next(), o_tok[n * 4:(n + 1) * 4], os_scr[n * 8:(n + 1) * 8, :].rearrange("h (t d) -> t h d", t=4),
                  reads=[D_os], writes=[o_tok])
        Pp = slice(0, P)
        S.op("act", actf(o_sq[Pp], o_tok[Pp], AF.Square), [o_tok], [o_sq])
        S.op("dve", rsum(orr[Pp], o_sq[Pp]), [o_sq], [orr])
        S.op("act", actf(orr[Pp], orr[Pp], AF.Sqrt, scale=1.0 / 64, bias=EPS), [orr], [orr])
        S.op("dve", recip(orr[Pp], orr[Pp]), [orr], [orr])
        S.op("dve", tt(o_tok[Pp], o_tok[Pp], bc(orr[Pp], [P, 8, 64], 2), ALU.mult), [o_tok, orr], [o_tok])
        S.op("dve", tt(o_tok[Pp], o_tok[Pp], bc(W["go"][Pp, :], [P, 8, 64], 1), ALU.mult), [o_tok, W["go"]], [o_tok])
        S.op("act", actf(z_sb[Pp, :], z_sb[Pp, :], AF.Silu), [z_sb], [z_sb])
        S.op("dve", tt(gated[Pp, :], o_tok[Pp].rearrange("p h d -> p (h d)"), z_sb[Pp, :], ALU.mult), [o_tok, z_sb], [gated])
        pg = psb(0)
        for c in range(4):
            S.op("pe", trp(pg[:, c * P:(c + 1) * P], gated[0:P, c * 128:(c + 1) * 128], ident[0:P, 0:P]), [gated, ident], [PS[0]])
        S.op("act", acp(gTs.ap, pg[:, 0:4 * P].rearrange("p (c t) -> p c t", c=4)), [PS[0]], [gTs])
        for half in range(2):
            for c in range(4):
                S.op("pe", mm(PS[1 + half][0:P, :], gTs[:, c, :], W["o"][:, c, half * 512:(half + 1) * 512], c == 0, c == 3),
                     [gTs, W["o"]], [PS[1 + half]])
            S.op("dve", tt(yv[0:P, half * 512:(half + 1) * 512], xs1[0:P, half * 512:(half + 1) * 512],
                           PS[1 + half][0:P, :], ALU.add), [xs1, PS[1 + half]], [yv])
        S.dma("sp", y_s, yv[0:P, :], reads=[yv], writes=[D_out])
        S.barrier()
        ar.release()

    phases = cfg.phases or ["mla_p", "p3", "s1", "gdn_p", "s2"]
    if "mla_p" in phases:
        phase_mla_prompt()
    W_o = ar.alloc([4, 1024], BF16, "W_o")
    ar.mark()
    wtmp = ar.alloc([2064], F32, "wtmp_o")
    load_bf16_rows(W_o, [W_o[:, kc, :] for kc in range(4)],
                   [a_w_o[kc * 128:(kc + 1) * 128, :] for kc in range(4)], None, wtmp, 1024)
    S.barrier()
    ar.release()
    if "p3" in phases:
        phase_out_proj(x_p, Buf(), gt1, D_gt1, W_o, xp1 if "gdn_p" in phases else y_p,
                       D_xp1 if "gdn_p" in phases else D_out)
    if "s1" in phases:
        phase_mla_sample()
    S.barrier()
    ar.release()
    if "gdn_p" in phases or "s2" in phases:
        ar.mark()
        junk = ar.alloc([1024], BF16, "junk2")
        stat = Rot([ar.alloc([8], F32, f"statb{i}") for i in range(4)])
        WG = load_gdn_weights()
        if "gdn_p" in phases:
            phase_gdn_prompt(WG)
            phase_out_proj(xp1, D_xp1, gt2, D_gt2, WG["o"], y_p, D_out)
        if "s2" in phases:
            phase_gdn_sample(WG)

    S.barrier(engines=("sp",))
    S.op("sp", lambda e: e.nop(), [D_out, D_xp1, D_gt1, D_gt2, D_qs, D_ab, D_os], [])
    S.emit()
    return nc


def host_consts(SEQ, NSEQ, past_len):
    i = np.arange(128)
    c = {}
    c["c_ident"] = np.eye(128, dtype=np.float32)
    c["c_tri"] = (i[:, None] <= i[None, :]).astype(np.float32)
    c["c_strict"] = (i[:, None] > i[None, :]).astype(np.float32)
    c["c_negm"] = np.where(i[:, None] >= i[None, :], 0.0, NEG).astype(np.float32)
    c["c_blk"] = ((i[:, None] // 64) == (i[None, :] // 64)).astype(np.float32)
    q = np.arange(512)
    am = np.zeros((128, 4, 512), np.float32)
    for j in range(4):
        am[:, j, :] = ((128 * j + i)[:, None] <= q[None, :])
    c["c_amask"] = am
    NS = NSEQ * 4
    sm = np.zeros((NS, NSEQ, 8, 4), np.float32)
    for n in range(NSEQ):
        for tk in range(4):
            for tq in range(4):
                if tk <= tq:
                    sm[n * 4 + tk, n, :, tq] = 1.0
    c["c_smask"] = sm.reshape(NS, NSEQ * 32)

    def rope_tab(pos):
        half = 16
        inv = (np.float32(10000.0) ** (-np.arange(half, dtype=np.float32) / np.float32(half))).astype(np.float32)
        ang = pos.astype(np.float32)[:, None] * inv[None, :]
        cos = np.cos(ang).astype(np.float32)
        sin = np.sin(ang).astype(np.float32)
        return np.concatenate([cos, cos, -sin, sin], axis=1).astype(np.float32)

    c["c_rope_p"] = rope_tab(np.arange(SEQ))
    c["c_rope_s"] = rope_tab(np.tile(past_len + np.arange(4), NSEQ))
    return c


def core_inputs(inp, core, n_cores, SEQ, NSEQ, consts):
    b = core % inp["x_prompt"].shape[0]
    f = np.ascontiguousarray
    m = dict(consts)
    m["x_p"] = f(inp["x_prompt"][b])
    m["x_s"] = f(inp["x_sample"][core * NSEQ:(core + 1) * NSEQ].reshape(NSEQ * 4, 1024))
    m["cache_lat"] = inp["cache_latent"][0].reshape(-1, 128 * 256)
    m["cache_kr"] = inp["cache_krope"][0].reshape(-1, 128 * 32)
    m["ptab"] = f(inp["page_table"][core * NSEQ:(core + 1) * NSEQ].T.astype(np.int32))
    m["st_conv"] = f(inp["state_conv"][0, core * NSEQ:(core + 1) * NSEQ])
    m["st_ssm"] = f(inp["state_ssm"][0, core * NSEQ:(core + 1) * NSEQ].reshape(NSEQ * 8, 4096))
    m["a_norm"] = f(inp["a_norm"][0].reshape(8, 128).T)
    m["a_w_in"] = inp["a_w_in"][0]
    m["a_g_qa"] = f(inp["a_g_qa"][0].reshape(3, 128).T)
    m["a_w_uq"] = inp["a_w_uq"][0]
    m["a_g_kv"] = inp["a_g_kv"][0].reshape(1, 256)
    m["a_w_uk"] = inp["a_w_uk"][0].reshape(256, 512)
    m["a_w_uv"] = inp["a_w_uv"][0].reshape(256, 512)
    m["a_g_q"] = inp["a_g_q"][0].reshape(1, 96)
    m["a_g_k"] = inp["a_g_k"][0].reshape(1, 96)
    m["a_w_o"] = inp["a_w_o"][0]
    m["b_norm"] = f(inp["b_norm"][0].reshape(8, 128).T)
    m["b_w_in"] = inp["b_w_in"][0]
    m["b_w_conv"] = inp["b_w_conv"][0]
    m["b_w_convT"] = f(inp["b_w_conv"][0].reshape(4, 12, 128).transpose(2, 1, 0).reshape(128, 48))
    sc_ = inp["state_conv"][0, core * NSEQ:(core + 1) * NSEQ]
    m["st_convL"] = f(sc_.reshape(NSEQ, 3, 3, 8, 64).transpose(0, 3, 1, 2, 4).reshape(NSEQ * 8, 576))
    wc_ = inp["b_w_conv"][0].reshape(4, 3, 8, 64).transpose(2, 0, 1, 3).reshape(8, 768)
    m["b_w_convL"] = f(np.tile(wc_, (NSEQ, 1)))
    m["b_alogL"] = f(np.tile(inp["b_a_log"][0].reshape(8, 1), (NSEQ, 1)))
    m["b_dtbL"] = f(np.tile(inp["b_dt_bias"][0].reshape(8, 1), (NSEQ, 1)))
    m["b_a_log"] = inp["b_a_log"][0].reshape(1, 8)
    m["b_dt_bias"] = inp["b_dt_bias"][0].reshape(1, 8)
    m["b_g_o"] = inp["b_g_o"][0].reshape(1, 64)
    m["b_w_o"] = inp["b_w_o"][0]
    return m


def run(inp, n_cores, cfg):
    inp = {k: np.asarray(v) for k, v in inp.items()}
    past_len = inp["page_table"].shape[1] * 128
    consts = host_consts(cfg.SEQ, cfg.NSEQ, past_len)
    nc = build(cfg)
    in_maps = [core_inputs(inp, c, n_cores, cfg.SEQ, cfg.NSEQ, consts) for c in range(n_cores)]
    res = run_bass_kernel_spmd(nc, in_maps, core_ids=list(range(n_cores)))
    return res.results


def fix_out(name, a, NSEQ):
    if name == "conv_s":
        return np.ascontiguousarray(a.reshape(NSEQ, 8, 3, 3, 64).transpose(0, 2, 3, 1, 4)).reshape(NSEQ, 3, 1536)
    return a


def kernel(**inputs):
    cfg = Cfg()
    r = run(inputs, 8, cfg)
    B = 4
    y_p = np.stack([r[b]["y_p"] for b in range(B)])
    y_s = np.concatenate([r[c]["y_s"].reshape(16, 4, 1024) for c in range(8)])
    lat_p = np.stack([r[b]["lat_p"] for b in range(B)])[None]
    kr_p = np.stack([r[b]["kr_p"] for b in range(B)])[None]
    lat_s = np.concatenate([r[c]["lat_s"].reshape(16, 4, 256) for c in range(8)])[None]
    kr_s = np.concatenate([r[c]["kr_s"].reshape(16, 4, 32) for c in range(8)])[None]
    conv_p = np.stack([r[b]["conv_p"] for b in range(B)])[None]
    ssm_p = np.stack([r[b]["ssm_p"] for b in range(B)])[None]
    conv_s = np.concatenate([fix_out("conv_s", r[c]["conv_s"], 16) for c in range(8)])[None]
    ssm_s = np.concatenate([r[c]["ssm_s"].reshape(16, 8, 64, 64) for c in range(8)])[None]
    return (y_p, y_s, lat_p, kr_p, lat_s, kr_s, conv_p, ssm_p, conv_s, ssm_s)
```

```python
import numpy as np
import concourse.bass as bass
import concourse.mybir as mybir
from concourse.bass_utils import run_bass_kernel_spmd

F32 = mybir.dt.float32
BF16 = mybir.dt.bfloat16
I32 = mybir.dt.int32
U8 = mybir.dt.uint8
ALU = mybir.AluOpType
AF = mybir.ActivationFunctionType
AX = mybir.AxisListType

ENGS = ("pe", "act", "dve", "pool", "sp")
NDSEM = 16
EPS = 1e-6
NEG = -30000.0


class Buf:
    __slots__ = ("name", "w", "r", "excl")

    def __init__(self, name="", excl=False):
        self.name = name
        self.w = None
        self.r = []
        self.excl = excl


class T:
    def __init__(self, ap, name=""):
        self.ap = ap
        self.b = Buf(name)

    def __getitem__(self, k):
        return self.ap[k]


class Op:
    __slots__ = ("eng", "fn", "waits", "idx", "dma", "dsem", "dval", "signal")


def _b(x):
    return x.b if isinstance(x, T) else x


class Sched:
    def __init__(self, nc, self_sync=("act", "dve", "pool")):
        self.nc = nc
        self.ops = {e: [] for e in ENGS}
        self.seen = {e: {} for e in ENGS}
        self.ndma = {e: 0 for e in ENGS}
        self.self_sync = set(self_sync)
        self.last_dma = {}

    def _waits(self, eng, deps):
        waits = []
        seen = self.seen[eng]
        for d in deps:
            if d.dma:
                key = ("d", d.eng, d.dsem)
                val = d.dval
            else:
                if d.fn is None:
                    continue
                if d.eng == eng and eng not in self.self_sync:
                    continue
                key = ("e", d.eng)
                val = d.idx + 1
            if seen.get(key, 0) >= val:
                continue
            seen[key] = val
            waits.append(d)
            d.signal = True
        return waits

    def _mk(self, eng, fn, reads, writes, dma=False):
        if getattr(self, "cap", None) is not None:
            self.cap.append((eng, fn, list(reads), list(writes), dma))
            return None
        reads = [_b(x) for x in reads]
        writes = [_b(x) for x in writes]
        if eng != "pe":
            ex = [b for b in reads if b.excl and b not in writes]
            if ex:
                writes = writes + ex
                reads = [b for b in reads if not b.excl]
        op = Op()
        op.eng = eng
        op.fn = fn
        op.dma = dma
        op.signal = False
        op.idx = len(self.ops[eng])
        deps = []
        for b in reads:
            if b.w is not None:
                deps.append(b.w)
        for b in writes:
            if b.w is not None:
                deps.append(b.w)
            deps.extend(b.r)
        if dma:
            n = self.ndma[eng]
            self.ndma[eng] = n + 1
            op.dsem = n % NDSEM
            op.dval = 16 * (n // NDSEM + 1)
            op.signal = True
            prev = self.last_dma.get((eng, op.dsem))
            if prev is not None:
                deps.append(prev)
            self.last_dma[(eng, op.dsem)] = op
        op.waits = self._waits(eng, deps)
        self.ops[eng].append(op)
        for b in reads:
            b.r.append(op)
        for b in writes:
            b.w = op
            b.r = []
        return op

    def op(self, eng, fn, reads=(), writes=()):
        return self._mk(eng, fn, reads, writes)

    def begin_capture(self):
        self.cap = []

    def end_capture(self):
        c = self.cap
        self.cap = None
        return c

    def replay(self, streams):
        streams = [st for st in streams if st]
        pos = [0] * len(streams)
        total = sum(len(st) for st in streams)
        for _ in range(total):
            k = min(range(len(streams)), key=lambda i: (pos[i] / len(streams[i])) if pos[i] < len(streams[i]) else 2.0)
            a = streams[k][pos[k]]
            pos[k] += 1
            self._mk(*a)

    def dma(self, eng, out, in_, reads=(), writes=(), **kw):
        return self._mk(eng, lambda e: e.dma_start(out=out, in_=in_, **kw), reads, writes, dma=True)

    def dma_fn(self, eng, fn, reads=(), writes=()):
        return self._mk(eng, fn, reads, writes, dma=True)

    def barrier(self, engines=ENGS):
        last = []
        for e in ENGS:
            for o in reversed(self.ops[e]):
                if not o.dma and o.fn is not None:
                    last.append(o)
                    break
        deps = last + list(self.last_dma.values())
        for e in engines:
            op = Op()
            op.eng = e
            op.fn = None
            op.dma = False
            op.signal = False
            op.idx = len(self.ops[e])
            op.waits = self._waits(e, [d for d in deps if d.dma or d.eng != e])
            self.ops[e].append(op)

    def emit(self):
        nc = self.nc
        esem = {e: nc.alloc_semaphore(name=f"es_{e}") for e in ENGS}
        dsem = {e: [nc.alloc_semaphore(name=f"ds_{e}{i}") for i in range(NDSEM)]
                for e in ENGS if self.ndma[e] > 0}
        signum = {}
        for e in ENGS:
            n = 0
            for o in self.ops[e]:
                if o.dma or o.fn is None:
                    continue
                if o.signal:
                    n += 1
                    signum[id(o)] = n

        def run(e, eng):
            for o in self.ops[e]:
                for d in o.waits:
                    if d.dma:
                        eng.wait_ge(dsem[d.eng][d.dsem], d.dval)
                    else:
                        eng.wait_ge(esem[d.eng], signum[id(d)])
                if o.fn is None:
                    continue
                ins = o.fn(eng)
                if o.dma:
                    ins.then_inc(dsem[e][o.dsem], 16)
                elif o.signal:
                    ins.then_inc(esem[e], 1)

        with nc.Block() as block:
            @block.tensor
            def _(eng):
                run("pe", eng)

            @block.scalar
            def _(eng):
                run("act", eng)

            @block.vector
            def _(eng):
                run("dve", eng)

            @block.gpsimd
            def _(eng):
                run("pool", eng)

            @block.sync
            def _(eng):
                run("sp", eng)


class Arena:
    def __init__(self, base, nbytes):
        self.base = base
        self.nbytes = nbytes
        self.off = 0
        self.marks = []
        self.n = 0

    def alloc(self, shape_free, dtype, name=None):
        esz = {F32: 4, BF16: 2, I32: 4}[dtype]
        n = int(np.prod(shape_free))
        nb = (n * esz + 63) // 64 * 64
        assert self.off + nb <= self.nbytes, ("SBUF arena overflow", name, self.off, nb, self.nbytes)
        ap = self.base[:, self.off:self.off + n * esz].bitcast(dtype)
        self.off += nb
        if len(shape_free) > 1:
            names = " ".join(f"a{i}" for i in range(len(shape_free)))
            kw = {f"a{i}": int(s) for i, s in enumerate(shape_free)}
            ap = ap.rearrange(f"p ({names}) -> p {names}", **kw)
        self.n += 1
        return T(ap, name or f"t{self.n}")

    def mark(self):
        self.marks.append(self.off)

    def release(self):
        self.off = self.marks.pop()


class Rot:
    def __init__(self, items):
        self.items = items
        self.i = 0

    def next(self):
        t = self.items[self.i % len(self.items)]
        self.i += 1
        return t


def mm(out, lhsT, rhs, start=True, stop=True):
    return lambda e: e.matmul(out, lhsT=lhsT, rhs=rhs, start=start, stop=stop)


def trp(out, in_, ident):
    return lambda e: e.transpose(out=out, in_=in_, identity=ident)


def actf(out, in_, func, **kw):
    return lambda e: e.activation(out=out, in_=in_, func=func, **kw)


def tt(out, a, b, op):
    return lambda e: e.tensor_tensor(out=out, in0=a, in1=b, op=op)


def ts(out, a, s1, op0, s2=None, op1=None):
    if op1 is None:
        return lambda e: e.tensor_scalar(out=out, in0=a, scalar1=s1, scalar2=None, op0=op0)
    return lambda e: e.tensor_scalar(out=out, in0=a, scalar1=s1, scalar2=s2, op0=op0, op1=op1)


def stt(out, a, s, b, op0, op1):
    return lambda e: e.scalar_tensor_tensor(out=out, in0=a, scalar=s, in1=b, op0=op0, op1=op1)


def cp(out, in_):
    return lambda e: e.tensor_copy(out=out, in_=in_)


def acp(out, in_):
    return lambda e: e.copy(out=out, in_=in_)


def rsum(out, in_):
    return lambda e: e.reduce_sum(out=out, in_=in_, axis=AX.X)


def recip(out, in_):
    return lambda e: e.reciprocal(out=out, in_=in_)


def mset(ap, v):
    return lambda e: e.memset(ap, v)


def bc(ap, shape, axis):
    return ap.unsqueeze(axis).to_broadcast(list(shape))


class Cfg:
    def __init__(self, SEQ=8192, NSEQ=16, NPOOL=20480, debug=False, phases=None, cut=99):
        self.cut = cut
        self.SEQ = SEQ
        self.NSEQ = NSEQ
        self.NPOOL = NPOOL
        self.debug = debug
        self.phases = phases


def build(cfg):
    SEQ, NSEQ, NPOOL = cfg.SEQ, cfg.NSEQ, cfg.NPOOL
    NT = SEQ // 128
    NQ = SEQ // 512
    NS = NSEQ * 4
    nc = bass.Bass("TRN2", target_bir_lowering=False)
    S = Sched(nc)

    def din(name, shape, dt=F32):
        return nc.dram_tensor(name, list(shape), dt, kind="ExternalInput").ap()

    def dout(name, shape, dt=F32):
        return nc.dram_tensor(name, list(shape), dt, kind="ExternalOutput").ap()

    def dscr(name, shape, dt=F32):
        kind = "ExternalOutput" if cfg.debug else "Internal"
        return nc.dram_tensor(name, list(shape), dt, kind=kind).ap()

    x_p = din("x_p", [SEQ, 1024])
    x_s = din("x_s", [NS, 1024])
    cache_lat = din("cache_lat", [NPOOL, 128 * 256])
    cache_kr = din("cache_kr", [NPOOL, 128 * 32])
    ptab = din("ptab", [128, NSEQ], I32)
    st_conv = din("st_conv", [NSEQ, 3, 1536])
    st_ssm = din("st_ssm", [NSEQ * 8, 4096])
    a_norm = din("a_norm", [128, 8])
    a_w_in = din("a_w_in", [1024, 1184])
    a_g_qa = din("a_g_qa", [128, 3])
    a_w_uq = din("a_w_uq", [384, 768])
    a_g_kv = din("a_g_kv", [1, 256])
    a_w_uk = din("a_w_uk", [256, 512])
    a_w_uv = din("a_w_uv", [256, 512])
    a_g_q = din("a_g_q", [1, 96])
    a_g_k = din("a_g_k", [1, 96])
    a_w_o = din("a_w_o", [512, 1024])
    b_norm = din("b_norm", [128, 8])
    b_w_in = din("b_w_in", [1024, 2064])
    b_w_conv = din("b_w_conv", [4, 1536])
    b_w_convT = din("b_w_convT", [128, 48])
    PHn = NSEQ * 8
    b_w_convL = din("b_w_convL", [PHn, 4 * 192])
    st_convL = din("st_convL", [PHn, 3 * 192])
    b_alogL = din("b_alogL", [PHn, 1])
    b_dtbL = din("b_dtbL", [PHn, 1])
    b_a_log = din("b_a_log", [1, 8])
    b_dt_bias = din("b_dt_bias", [1, 8])
    b_g_o = din("b_g_o", [1, 64])
    b_w_o = din("b_w_o", [512, 1024])
    c_ident = din("c_ident", [128, 128])
    c_tri = din("c_tri", [128, 128])
    c_strict = din("c_strict", [128, 128])
    c_negm = din("c_negm", [128, 128])
    c_blk = din("c_blk", [128, 128])
    c_amask = din("c_amask", [128, 4, 512])
    c_smask = din("c_smask", [NS, NSEQ * 32])
    c_rope_p = din("c_rope_p", [SEQ, 64])
    c_rope_s = din("c_rope_s", [NS, 64])

    y_p = dout("y_p", [SEQ, 1024])
    y_s = dout("y_s", [NS, 1024])
    lat_p = dout("lat_p", [SEQ, 256])
    kr_p = dout("kr_p", [SEQ, 32])
    lat_s = dout("lat_s", [NS, 256])
    kr_s = dout("kr_s", [NS, 32])
    conv_p = dout("conv_p", [3, 1536])
    ssm_p = dout("ssm_p", [8, 64, 64])
    conv_s = dout("conv_s", [NSEQ * 8, 3 * 192])
    ssm_s = dout("ssm_s", [NSEQ * 8, 4096])

    gt1 = dscr("gt1", [2, 128, 2, SEQ], BF16)
    xp1 = dscr("xp1", [SEQ, 1024])
    gt2 = dscr("gt2", [2, 128, 2, SEQ], BF16)
    hT1_scr = dscr("hT1_scr", [128, 8, SEQ], BF16)
    hT2_scr = dscr("hT2_scr", [128, 8, SEQ], BF16)
    D_h1, D_h2 = Buf("hT1"), Buf("hT2")
    qs_scr = dscr("qs_scr", [NS, 1536])
    ab_scr = dscr("ab_scr", [NS, 16])
    os_scr = dscr("os_scr", [NSEQ * 8, 256])
    D_gt1, D_xp1, D_gt2 = Buf("gt1"), Buf("xp1"), Buf("gt2")
    D_qs, D_ab, D_os = Buf("qs"), Buf("ab"), Buf("os")
    D_out = Buf("outs")
    D_lat = Buf("lat_out")

    ARENA_BYTES = 204 * 1024
    sb = nc.alloc_sbuf_tensor("arena", [128, ARENA_BYTES], U8)
    ar = Arena(sb.ap(), ARENA_BYTES)
    PSD = [nc.alloc_psum_tensor(f"psd{i}", [128, 1024], F32).ap() for i in range(4)]
    PS = [T(PSD[i // 2][:, (i % 2) * 512:(i % 2 + 1) * 512], f"ps{i}") for i in range(8)]
    for p_ in PS:
        p_.b.excl = True

    def psb(i):
        return PS[i].ap.bitcast(BF16)

    dmaq = Rot(["sp", "pool"])

    ident_f = ar.alloc([128], F32, "ident_f")
    ident = ar.alloc([128], BF16, "ident")
    S.dma("sp", ident_f.ap, c_ident, writes=[ident_f])
    S.op("dve", cp(ident.ap, ident_f.ap), [ident_f], [ident])

    xs1 = ar.alloc([1024], F32, "xs1")
    ptab_sb = ar.alloc([NSEQ], I32, "ptab_sb")
    S.dma("sp", ptab_sb.ap, ptab, writes=[ptab_sb])

    def load_bf16_rows(dst, dst_slices, src_rows, gain, tmp, ncols):
        for kc, rows in enumerate(src_rows):
            S.dma(dmaq.next(), tmp[:, 0:ncols], rows, writes=[tmp])
            if gain is None:
                S.op("dve", cp(dst_slices[kc], tmp[:, 0:ncols]), [tmp], [dst])
            else:
                S.op("dve", ts(dst_slices[kc], tmp[:, 0:ncols], gain[0][:, kc:kc + 1], ALU.mult),
                     [tmp, gain[1]], [dst])

    def rms_rstd(x_ap, xT, n, rstd, junk, parts=128, eng_sq="act"):
        S.op("act", actf(junk[0:parts, 0:n], x_ap, AF.Square, accum_out=rstd[0:parts, 1:2]),
             [xT], [rstd])
        S.op("act", actf(rstd[0:parts, 2:3], rstd[0:parts, 1:2], AF.Sqrt, scale=1.0 / n, bias=EPS),
             [rstd], [rstd])
        S.op("dve", recip(rstd[0:parts, 0:1], rstd[0:parts, 2:3]), [rstd], [rstd])

    def transpose_to(dstT, dst_ap_fn, src, src_ap_fn, nchunk, bank, parts=128, width=128, evac="act"):
        pb = psb(bank)
        for c in range(nchunk):
            S.op("pe", trp(pb[0:width, c * parts:(c + 1) * parts], src_ap_fn(c), ident[0:parts, 0:parts]),
                 [src, ident], [PS[bank]])
        fn = acp if evac == "act" else cp
        S.op(evac, fn(dst_ap_fn(), pb[0:width, 0:nchunk * parts]), [PS[bank]], [dstT])

    def rope(x_view, xT, cs, nh, parts, sw, t1):
        swv = sw[0:parts, 0:nh, :]
        t1v = t1[0:parts, 0:nh, :]
        S.op("pool", cp(swv[:, :, 0:16], x_view[:, :, 16:32]), [xT], [sw])
        S.op("pool", cp(swv[:, :, 16:32], x_view[:, :, 0:16]), [xT], [sw])
        S.op("dve", tt(t1v, swv, bc(cs[0:parts, 32:64], [parts, nh, 32], 1), ALU.mult), [sw, cs], [t1])
        S.op("dve", tt(x_view, x_view, bc(cs[0:parts, 0:32], [parts, nh, 32], 1), ALU.mult), [xT, cs], [xT])
        S.op("dve", tt(x_view, x_view, t1v, ALU.add), [xT, t1], [xT])

    ar.mark()
    an_sb = ar.alloc([8], F32, "an_sb")
    gqa_sb = ar.alloc([3], F32, "gqa_sb")
    S.dma("sp", an_sb.ap, a_norm, writes=[an_sb])
    S.dma("sp", gqa_sb.ap, a_g_qa, writes=[gqa_sb])
    W_in = ar.alloc([8, 1184], BF16, "W_in")
    W_uq = ar.alloc([3, 768], BF16, "W_uq")
    W_uk = ar.alloc([2, 512], BF16, "W_uk")
    W_uv = ar.alloc([2, 512], BF16, "W_uv")
    amask = ar.alloc([4, 512], BF16, "amask")
    gkv_b = ar.alloc([256], F32, "gkv_b")
    gg_b = ar.alloc([96], F32, "gg_b")
    gk_t = ar.alloc([96], F32, "gk_t")
    ar.mark()
    wtmp = ar.alloc([2064], F32, "wtmp")
    load_bf16_rows(W_in, [W_in[:, kc, :] for kc in range(8)],
                   [a_w_in[kc * 128:(kc + 1) * 128, :] for kc in range(8)], (an_sb, an_sb), wtmp, 1184)
    load_bf16_rows(W_uq, [W_uq[:, kc, :] for kc in range(3)],
                   [a_w_uq[kc * 128:(kc + 1) * 128, :] for kc in range(3)], (gqa_sb, gqa_sb), wtmp, 768)
    load_bf16_rows(W_uk, [W_uk[:, kc, :] for kc in range(2)],
                   [a_w_uk[kc * 128:(kc + 1) * 128, :] for kc in range(2)], None, wtmp, 512)
    load_bf16_rows(W_uv, [W_uv[:, kc, :] for kc in range(2)],
                   [a_w_uv[kc * 128:(kc + 1) * 128, :] for kc in range(2)], None, wtmp, 512)
    S.dma("sp", wtmp[:, 0:2048], c_amask.rearrange("p a q -> p (a q)"), writes=[wtmp])
    S.op("dve", cp(amask.ap.rearrange("p a q -> p (a q)"), wtmp[:, 0:2048]), [wtmp], [amask])
    S.barrier()
    ar.release()
    S.dma("sp", gkv_b.ap, a_g_kv.to_broadcast([128, 256]), writes=[gkv_b])
    S.dma("sp", gg_b.ap, a_g_q.to_broadcast([128, 96]), writes=[gg_b])
    S.dma("sp", gk_t.ap, a_g_k.to_broadcast([128, 96]), writes=[gk_t])
    S.op("dve", stt(gg_b.ap, gg_b.ap, float(96 ** -0.5), gk_t.ap, ALU.mult, ALU.mult), [gg_b, gk_t], [gg_b])

    junk = ar.alloc([1024], BF16, "junk")
    stat = Rot([ar.alloc([8], F32, f"stat{i}") for i in range(4)])

    def mla_kv_from_cn(cn_ap, cnT, kpe_ap, kpeT, parts, hsel, KTs, kt_cols, V_store_fn, banks, tmp, kcol0=0):
        col0, nh = hsel
        cnb, cT, sq, ssq, Kt = tmp
        bT, bKV, bK = banks
        S.op("act", acp(cnb[0:parts, :], cn_ap), [cnT], [cnb])
        pb = psb(bT)
        for c in range(2):
            S.op("pe", trp(pb[:, c * parts:(c + 1) * parts], cnb[0:parts, c * 128:(c + 1) * 128],
                           ident[0:parts, 0:parts]), [cnb, ident], [PS[bT]])
        S.op("act", acp(cT[:, :, 0:parts], pb[:, 0:2 * parts].rearrange("p (c t) -> p c t", c=2)),
             [PS[bT]], [cT])
        w = nh * 64
        for kc in range(2):
            S.op("pe", mm(PS[bKV][0:parts, 0:w], cT[:, kc, 0:parts], W_uk[:, kc, col0:col0 + w],
                          kc == 0, kc == 1), [cT, W_uk], [PS[bKV]])
        if V_store_fn is not None:
            for kc in range(2):
                S.op("pe", mm(PS[bKV][0:parts, 256:256 + w], cT[:, kc, 0:parts], W_uv[:, kc, col0:col0 + w],
                              kc == 0, kc == 1), [cT, W_uv], [PS[bKV]])
            V_store_fn(PS[bKV])
        S.op("act", actf(sq[0:parts, 0:w], PS[bKV][0:parts, 0:w], AF.Square), [PS[bKV]], [sq])
        S.op("dve", rsum(ssq[0:parts, 0:nh], sq[0:parts, 0:w].rearrange("p (h d) -> p h d", h=nh)), [sq], [ssq])
        S.op("act", actf(junk[0:parts, 0:32], kpe_ap, AF.Square, accum_out=ssq[0:parts, 8:9]), [kpeT], [junk, ssq])
        S.op("dve", ts(ssq[0:parts, 0:nh], ssq[0:parts, 0:nh], ssq[0:parts, 8:9], ALU.add), [ssq], [ssq])
        S.op("act", actf(ssq[0:parts, 0:nh], ssq[0:parts, 0:nh], AF.Sqrt, scale=1.0 / 96, bias=EPS), [ssq], [ssq])
        S.op("dve", recip(ssq[0:parts, 0:nh], ssq[0:parts, 0:nh]), [ssq], [ssq])
        S.op("dve", tt(Kt[0:parts, 0:nh, 0:64], PS[bKV][0:parts, 0:w].rearrange("p (h d) -> p h d", h=nh),
                       bc(ssq[0:parts, 0:nh], [parts, nh, 64], 2), ALU.mult), [PS[bKV], ssq], [Kt])
        S.op("dve", tt(Kt[0:parts, 0:nh, 64:96], bc(kpe_ap, [parts, nh, 32], 1),
                       bc(ssq[0:parts, 0:nh], [parts, nh, 32], 2), ALU.mult), [kpeT, ssq], [Kt])
        pk = psb(bK)
        for h in range(nh):
            S.op("pe", trp(pk[0:96, kcol0 + h * parts:kcol0 + (h + 1) * parts], Kt[0:parts, h, :], ident[0:parts, 0:parts]),
                 [Kt, ident], [PS[bK]])
        return pk

    def ckp_from_proj(ps_c, ps_k, psT, parts, cs, cn, kpe, st, sw, t1):
        rms_rstd(ps_c, psT, 256, st, junk, parts)
        S.op("dve", stt(cn[0:parts, :], ps_c, st[0:parts, 0:1], gkv_b[0:parts, :], ALU.mult, ALU.mult),
             [psT, st, gkv_b], [cn])
        S.op("act", acp(kpe[0:parts, :], ps_k), [psT], [kpe])
        rope(kpe[0:parts, :].rearrange("p (h d) -> p h d", h=1), kpe, cs, 1, parts, sw, t1)

    def phase_mla_prompt():
        ar.mark()
        KT = ar.alloc([4, SEQ], BF16, "KT")
        V1 = ar.alloc([NT, 2, 192], BF16, "V1")
        S.op("pool", mset(V1.ap, 1.0), [], [V1])
        xt = [ar.alloc([1024], F32, f"xt{i}") for i in range(2)]
        hb = [ar.alloc([1024], BF16, f"hb{i}") for i in range(2)]
        hT4 = ar.alloc([8, 512], BF16, "hT4")
        hTv = [T(hT4[:, :, i * 128:(i + 1) * 128], f"hTv{i}") for i in range(4)]
        cs_t = [ar.alloc([64], F32, f"cs{i}") for i in range(2)]
        sw = [ar.alloc([4, 32], F32, f"sw{i}") for i in range(2)]
        t1 = [ar.alloc([4, 32], F32, f"t1{i}") for i in range(2)]
        sq = [ar.alloc([384], F32, f"sq{i}") for i in range(2)]
        ssq = [ar.alloc([16], F32, f"ssq{i}") for i in range(2)]
        junk2 = [junk, junk]

        def load_h(tile, par, dstT, bank):
            x, h_ = xt[par], hb[par]
            S.dma("sp", x.ap, x_p[tile * 128:(tile + 1) * 128, :], writes=[x])
            st = stat.next()
            rms_rstd(x.ap, x, 1024, st, junk2[par])
            S.op("dve", ts(h_.ap, x.ap, st[:, 0:1], ALU.mult), [x, st], [h_])
            pb = psb(bank)
            for c in range(8):
                S.op("pe", trp(pb[:, c * 128:(c + 1) * 128], h_[:, c * 128:(c + 1) * 128], ident.ap),
                     [h_, ident], [PS[bank]])
            S.op("act", acp(dstT.ap, pb[:, 0:1024].rearrange("p (c t) -> p c t", c=8)), [PS[bank]], [dstT])

        for g in range(2):
            ar.mark()
            cn = [ar.alloc([256], F32, f"cn{i}") for i in range(2)]
            kpe = [ar.alloc([32], F32, f"kpe{i}") for i in range(2)]
            cnb = [ar.alloc([256], BF16, f"cnb{i}") for i in range(2)]
            cT = [ar.alloc([2, 128], BF16, f"cT{i}") for i in range(2)]
            Kt = [ar.alloc([4, 96], BF16, f"Kt{i}") for i in range(2)]
            streams = []
            for t in range(NT):
                par = t % 2
                B0 = 4 * par
                S.begin_capture()
                cn_t, kpe_t = cn[par], kpe[par]
                if g == 0:
                    load_h(t, par, hTv[par], B0)
                    S.dma("pool", hT1_scr[:, :, t * 128:(t + 1) * 128], hTv[par].ap, reads=[hTv[par]], writes=[D_h1])
                    for kc in range(8):
                        S.op("pe", mm(PS[B0 + 1][:, 0:288], hTv[par][:, kc, :], W_in[:, kc, 384:672], kc == 0, kc == 7),
                             [hTv[par], W_in], [PS[B0 + 1]])
                    cs = cs_t[par]
                    S.dma("pool", cs.ap, c_rope_p[t * 128:(t + 1) * 128, :], writes=[cs])
                    st = stat.next()
                    ckp_from_proj(PS[B0 + 1][:, 0:256], PS[B0 + 1][:, 256:288], PS[B0 + 1], 128, cs, cn_t, kpe_t, st,
                                  sw[par], t1[par])
                    S.dma("pool", lat_p[t * 128:(t + 1) * 128, :], cn_t.ap, reads=[cn_t], writes=[D_lat])
                    S.dma("pool", kr_p[t * 128:(t + 1) * 128, :], kpe_t.ap, reads=[kpe_t], writes=[D_lat])
                else:
                    S.dma("sp", cn_t.ap, lat_p[t * 128:(t + 1) * 128, :], reads=[D_lat], writes=[cn_t])
                    S.dma("pool", kpe_t.ap, kr_p[t * 128:(t + 1) * 128, :], reads=[D_lat], writes=[kpe_t])

                def vstore(psT, t=t):
                    pv = psT[:, 256:512].rearrange("p (a b d) -> p a b d", a=2, b=2)
                    S.op("act", acp(V1[:, t, :, 0:64], pv[:, :, 0, :]), [psT], [V1])
                    S.op("act", acp(V1[:, t, :, 128:192], pv[:, :, 1, :]), [psT], [V1])

                pk = mla_kv_from_cn(cn_t.ap, cn_t, kpe_t.ap, kpe_t, 128, (g * 256, 4), KT, None, vstore,
                                    (B0 + 2, B0 + 3, B0 + 2), (cnb[par], cT[par], sq[par], ssq[par], Kt[par]),
                                    kcol0=256)
                S.op("act", acp(KT[0:96, :, t * 128:(t + 1) * 128],
                                pk[0:96, 256:768].rearrange("p (h t) -> p h t", h=4)), [PS[B0 + 2]], [KT])
                streams.append(S.end_capture())
                if len(streams) == 2:
                    S.replay(streams)
                    streams = []
            S.replay(streams)
            S.barrier()
            ar.release()
            ar.mark()
            qan = [ar.alloc([384], BF16, f"qan{i}") for i in range(2)]
            qaT = [ar.alloc([3, 128], BF16, f"qaT{i}") for i in range(2)]
            qs = [ar.alloc([4, 96], F32, f"qs{i}") for i in range(2)]
            qf = [ar.alloc([4, 96], BF16, f"qf{i}") for i in range(2)]
            QT = ar.alloc([4, 512], BF16, "QT")
            QTv = [T(QT[0:96, :, i * 128:(i + 1) * 128], f"QTv{i}") for i in range(4)]
            sz = ar.alloc([2, 512], F32, "sz")
            pT = Rot([ar.alloc([512], BF16, f"pT{i}") for i in range(3)])
            rcp = ar.alloc([512], F32, "rcp")
            tmpo = ar.alloc([512], F32, "tmpo")
            GTt = Rot([ar.alloc([2, 512], BF16, f"GTt{i}") for i in range(1)])
            for qi in range(NQ):
                streams = []
                for s_ in range(4):
                    tile = qi * 4 + s_
                    par = s_ % 2
                    B0 = 4 * par
                    S.begin_capture()
                    S.dma("sp" if s_ % 2 == 0 else "pool", hTv[s_].ap, hT1_scr[:, :, tile * 128:(tile + 1) * 128],
                          reads=[D_h1], writes=[hTv[s_]])
                    for kc in range(8):
                        S.op("pe", mm(PS[B0 + 1][:, 0:384], hTv[s_][:, kc, :], W_in[:, kc, 0:384], kc == 0, kc == 7),
                             [hTv[s_], W_in], [PS[B0 + 1]])
                    st = stat.next()
                    rms_rstd(PS[B0 + 1][:, 0:384], PS[B0 + 1], 384, st, junk2[par])
                    S.op("dve", ts(qan[par].ap, PS[B0 + 1][:, 0:384], st[:, 0:1], ALU.mult), [PS[B0 + 1], st], [qan[par]])
                    pa = psb(B0 + 2)
                    for c in range(3):
                        S.op("pe", trp(pa[:, c * 128:(c + 1) * 128], qan[par][:, c * 128:(c + 1) * 128], ident.ap),
                             [qan[par], ident], [PS[B0 + 2]])
                    S.op("dve", cp(qaT[par].ap, pa[:, 0:384].rearrange("p (c t) -> p c t", c=3)), [PS[B0 + 2]], [qaT[par]])
                    for kc in range(3):
                        S.op("pe", mm(PS[B0 + 3][:, 0:384], qaT[par][:, kc, :], W_uq[:, kc, g * 384:(g + 1) * 384],
                                      kc == 0, kc == 2), [qaT[par], W_uq], [PS[B0 + 3]])
                    q_, sq_, ssq_ = qs[par], sq[par], ssq[par]
                    S.op("act", acp(q_.ap.rearrange("p h d -> p (h d)"), PS[B0 + 3][:, 0:384]), [PS[B0 + 3]], [q_])
                    S.op("act", actf(sq_[:, 0:384], q_.ap.rearrange("p h d -> p (h d)"), AF.Square), [q_], [sq_])
                    S.op("dve", rsum(ssq_[:, 0:4], sq_[:, 0:384].rearrange("p (h d) -> p h d", h=4)), [sq_], [ssq_])
                    S.op("act", actf(ssq_[:, 0:4], ssq_[:, 0:4], AF.Sqrt, scale=1.0 / 96, bias=EPS), [ssq_], [ssq_])
                    S.op("dve", recip(ssq_[:, 0:4], ssq_[:, 0:4]), [ssq_], [ssq_])
                    cs = cs_t[par]
                    S.dma("pool", cs.ap, c_rope_p[tile * 128:(tile + 1) * 128, :], writes=[cs])
                    rope(q_[:, :, 64:96], q_, cs, 4, 128, sw[par], t1[par])
                    S.op("dve", tt(q_.ap, q_.ap, bc(ssq_[:, 0:4], [128, 4, 96], 2), ALU.mult), [q_, ssq_], [q_])
                    S.op("dve", tt(qf[par].ap, q_.ap, bc(gg_b.ap, [128, 4, 96], 1), ALU.mult), [q_, gg_b], [qf[par]])
                    pq = psb(B0 + 2)
                    for h in range(4):
                        S.op("pe", trp(pq[0:96, 384 + h * 128:384 + (h + 1) * 128], qf[par][:, h, :], ident.ap),
                             [qf[par], ident], [PS[B0 + 2]])
                    S.op("dve", cp(QTv[s_].ap, pq[0:96, 384:896].rearrange("p (h t) -> p h t", h=4)), [PS[B0 + 2]], [QTv[s_]])
                    streams.append(S.end_capture())
                    if len(streams) == 2:
                        S.replay(streams)
                        streams = []
                hall = [hTv[i] for i in range(4)]
                qall = [QTv[i] for i in range(4)]
                for c in range(2):
                    col = 672 + g * 256 + c * 128
                    for kc in range(8):
                        S.op("pe", mm(PS[4 + c].ap, W_in[:, kc, col:col + 128], hT4[:, kc, :], kc == 0, kc == 7),
                             [W_in] + hall, [PS[4 + c]])
                    S.op("act", actf(sz[:, c, :], PS[4 + c].ap, AF.Exp, scale=-1.0), [PS[4 + c]], [sz])
                    S.op("dve", ts(sz[:, c, :], sz[:, c, :], 1.0, ALU.add), [sz], [sz])
                    S.op("dve", recip(sz[:, c, :], sz[:, c, :]), [sz], [sz])
                    S.op("dve", tt(sz[:, c, :], PS[4 + c].ap, sz[:, c, :], ALU.mult), [PS[4 + c], sz], [sz])
                G = GTt.next()
                nk = 4 * qi + 4
                sbank = Rot([0, 1, 2, 3])
                units = [(h, kt) for h in range(4) for kt in range(nk)]
                pend = None

                def emit_pv(u):
                    h, kt, p = u
                    ob = 6 + (h % 2)
                    pr, par = h // 2, h % 2
                    lhs = V1[:, kt, pr, 0:128] if par == 0 else V1[:, kt, pr, 64:192]
                    S.op("pe", mm(PS[ob].ap, lhs, p.ap, kt == 0, kt == nk - 1), [V1, p], [PS[ob]])
                    if kt == nk - 1:
                        if par == 0:
                            o_r, s_r = slice(0, 64), slice(64, 128)
                        else:
                            o_r, s_r = slice(64, 128), slice(0, 64)
                        S.op("dve", recip(rcp[s_r, :], PS[ob][s_r, :]), [PS[ob]], [rcp])
                        S.op("dve", tt(tmpo[o_r, :], PS[ob][o_r, :], rcp[s_r, :], ALU.mult), [PS[ob], rcp], [tmpo])
                        S.op("dve", tt(G[o_r, pr, :], tmpo[o_r, :], sz[o_r, pr, :], ALU.mult), [tmpo, sz], [G])

                for (h, kt) in units:
                    b = sbank.next()
                    S.op("pe", mm(PS[b].ap, KT[0:96, h, kt * 128:(kt + 1) * 128], QT[0:96, h, :]),
                         [KT] + qall, [PS[b]])
                    p = pT.next()
                    S.op("act", actf(p.ap, PS[b].ap, AF.Exp), [PS[b]], [p])
                    if kt >= 4 * qi:
                        S.op("pool", tt(p.ap, p.ap, amask[:, kt - 4 * qi, :], ALU.mult), [p, amask], [p])
                    if pend is not None:
                        emit_pv(pend)
                    pend = (h, kt, p)
                emit_pv(pend)
                S.dma("sp", gt1[g, :, :, qi * 512:(qi + 1) * 512], G.ap, reads=[G], writes=[D_gt1])
            S.barrier()
            ar.release()
        S.barrier()
        ar.release()

    def phase_out_proj(src_x, D_src, gt, D_gt, Wo, dst, D_dst):
        ar.mark()
        xt = Rot([ar.alloc([1024], F32, f"xo{i}") for i in range(2)])
        gT = Rot([ar.alloc([4, 128], BF16, f"gT{i}") for i in range(2)])
        for t in range(NT):
            x, g_ = xt.next(), gT.next()
            S.dma("sp", x.ap, src_x[t * 128:(t + 1) * 128, :], reads=[D_src], writes=[x])
            for grp in range(2):
                S.dma("pool", g_[:, 2 * grp:2 * grp + 2, :], gt[grp, :, :, t * 128:(t + 1) * 128],
                      reads=[D_gt], writes=[g_])
            for half in range(2):
                b = 2 * (t % 2) + half
                for c in range(4):
                    S.op("pe", mm(PS[b].ap, g_[:, c, :], Wo[:, c, half * 512:(half + 1) * 512], c == 0, c == 3),
                         [g_, Wo], [PS[b]])
                S.op("dve", tt(x[:, half * 512:(half + 1) * 512], x[:, half * 512:(half + 1) * 512],
                               PS[b].ap, ALU.add), [x, PS[b]], [x])
            S.dma("sp", dst[t * 128:(t + 1) * 128, :], x.ap, reads=[x], writes=[D_dst])
        S.barrier()
        ar.release()


    def phase_mla_sample():
        ar.mark()
        P = NS
        SC = 16
        NCH = 128 // SC
        xs = ar.alloc([1024], F32, "xs")
        hb = ar.alloc([1024], BF16, "s_hb")
        hTs = ar.alloc([8, P], BF16, "hTs")
        cs = ar.alloc([64], F32, "s_cs")
        cn_s = ar.alloc([256], F32, "cn_s")
        kpe_s = ar.alloc([32], F32, "kpe_s")
        sw = ar.alloc([8, 32], F32, "s_sw")
        t1 = ar.alloc([8, 32], F32, "s_t1")
        qan = ar.alloc([384], BF16, "s_qan")
        qaT = ar.alloc([3, P], BF16, "s_qaT")
        qs = ar.alloc([8, 96], F32, "s_qs")
        qfb = ar.alloc([8, 96], BF16, "s_qfb")
        sq = Rot([ar.alloc([768], F32, f"s_sq{i}") for i in range(2)])
        ssq = Rot([ar.alloc([16], F32, f"s_ssq{i}") for i in range(2)])
        qnT = ar.alloc([8, P], BF16, "qnT")
        qpT = ar.alloc([8, P], BF16, "qpT")
        wukT = ar.alloc([8, 256], BF16, "wukT")
        qlatT = ar.alloc([2, NSEQ, 8, 4], BF16, "qlatT")
        qpeT = ar.alloc([NSEQ, 8, 4], BF16, "qpeT")
        smask = ar.alloc([NSEQ * 32], F32, "smask")
        pTn = ar.alloc([NSEQ * 32], BF16, "pTn")
        scn = ar.alloc([NSEQ * 32], F32, "scn")
        lbn = ar.alloc([289], BF16, "lbn")
        gl = Rot([ar.alloc([SC, 256], F32, f"gl{i}") for i in range(2)])
        gk = Rot([ar.alloc([SC, 32], F32, f"gk{i}") for i in range(2)])
        lb = Rot([ar.alloc([289], BF16, f"lb{i}") for i in range(3)])
        cT = Rot([ar.alloc([2, 128], BF16, f"s_cT{i}") for i in range(2)])
        kpT = Rot([ar.alloc([128], BF16, f"kpT{i}") for i in range(2)])
        sc = Rot([ar.alloc([32], F32, f"s_sc{i}") for i in range(2)])
        pT = Rot([ar.alloc([32], BF16, f"s_pT{i}") for i in range(3)])
        ol = ar.alloc([256], BF16, "ol")
        olr = ar.alloc([4], F32, "olr")
        olatT = ar.alloc([2, 8, P], BF16, "olatT")
        ez = ar.alloc([512], F32, "s_ez")
        z_sb = ar.alloc([512], F32, "s_zsb")
        gated = ar.alloc([512], BF16, "s_gated")
        gTs = ar.alloc([4, P], BF16, "gTs")
        for l_ in lb.items + [lbn]:
            S.op("pool", mset(l_[:, 256:257], 1.0), [], [l_])
        S.dma("sp", smask[0:P, :], c_smask, writes=[smask])
        S.dma("sp", cs[0:P, :], c_rope_s, writes=[cs])
        S.dma("sp", xs[0:P, :], x_s, writes=[xs])
        st = stat.next()
        rms_rstd(xs[0:P, :], xs, 1024, st, junk, P)
        S.op("dve", ts(hb[0:P, :], xs[0:P, :], st[0:P, 0:1], ALU.mult), [xs, st], [hb])
        pb = psb(0)
        for c in range(8):
            S.op("pe", trp(pb[:, c * P:(c + 1) * P], hb[0:P, c * 128:(c + 1) * 128], ident[0:P, 0:P]), [hb, ident], [PS[0]])
        S.op("act", acp(hTs.ap, pb[:, 0:8 * P].rearrange("p (c t) -> p c t", c=8)), [PS[0]], [hTs])
        for (bank, c0, w) in ((1, 384, 288), (2, 0, 384), (3, 672, 512)):
            for kc in range(8):
                S.op("pe", mm(PS[bank][0:P, 0:w], hTs[:, kc, :], W_in[:, kc, c0:c0 + w], kc == 0, kc == 7),
                     [hTs, W_in], [PS[bank]])
        st = stat.next()
        ckp_from_proj(PS[1][0:P, 0:256], PS[1][0:P, 256:288], PS[1], P, cs, cn_s, kpe_s, st, sw, t1)
        S.dma("sp", lat_s, cn_s[0:P, :], reads=[cn_s], writes=[D_out])
        S.dma("sp", kr_s, kpe_s[0:P, :], reads=[kpe_s], writes=[D_out])
        st = stat.next()
        rms_rstd(PS[2][0:P, 0:384], PS[2], 384, st, junk, P)
        S.op("dve", ts(qan[0:P, :], PS[2][0:P, 0:384], st[0:P, 0:1], ALU.mult), [PS[2], st], [qan])
        pa = psb(0)
        for c in range(3):
            S.op("pe", trp(pa[:, c * P:(c + 1) * P], qan[0:P, c * 128:(c + 1) * 128], ident[0:P, 0:P]), [qan, ident], [PS[0]])
        S.op("dve", cp(qaT.ap, pa[:, 0:3 * P].rearrange("p (c t) -> p c t", c=3)), [PS[0]], [qaT])
        for (bank, c0, w) in ((4, 0, 384), (5, 384, 384)):
            for kc in range(3):
                S.op("pe", mm(PS[bank][0:P, 0:w], qaT[:, kc, :], W_uq[:, kc, c0:c0 + w], kc == 0, kc == 2), [qaT, W_uq], [PS[bank]])
            S.op("act", acp(qs[0:P, c0 // 96:c0 // 96 + 4, :].rearrange("p h d -> p (h d)"), PS[bank][0:P, 0:w]), [PS[bank]], [qs])
        sq_, ssq_ = sq.next(), ssq.next()
        S.op("act", actf(sq_[0:P, 0:768], qs[0:P].rearrange("p h d -> p (h d)"), AF.Square), [qs], [sq_])
        S.op("dve", rsum(ssq_[0:P, 0:8], sq_[0:P, 0:768].rearrange("p (h d) -> p h d", h=8)), [sq_], [ssq_])
        S.op("act", actf(ssq_[0:P, 0:8], ssq_[0:P, 0:8], AF.Sqrt, scale=1.0 / 96, bias=EPS), [ssq_], [ssq_])
        S.op("dve", recip(ssq_[0:P, 0:8], ssq_[0:P, 0:8]), [ssq_], [ssq_])
        rope(qs[0:P, :, 64:96], qs, cs, 8, P, sw, t1)
        S.op("dve", tt(qs[0:P], qs[0:P], bc(ssq_[0:P, 0:8], [P, 8, 96], 2), ALU.mult), [qs, ssq_], [qs])
        S.op("dve", tt(qfb[0:P], qs[0:P], bc(gg_b[0:P, :], [P, 8, 96], 1), ALU.mult), [qs, gg_b], [qfb])
        pq = psb(0)
        for h in range(8):
            S.op("pe", trp(pq[0:64, h * P:(h + 1) * P], qfb[0:P, h, 0:64], ident[0:P, 0:P]), [qfb, ident], [PS[0]])
        S.op("dve", cp(qnT[0:64], pq[0:64, 0:8 * P].rearrange("p (h t) -> p h t", h=8)), [PS[0]], [qnT])
        for h in range(8):
            S.op("pe", trp(pq[0:32, h * P:(h + 1) * P], qfb[0:P, h, 64:96], ident[0:P, 0:P]), [qfb, ident], [PS[0]])
        S.op("dve", cp(qpT[0:32], pq[0:32, 0:8 * P].rearrange("p (h t) -> p h t", h=8)), [PS[0]], [qpT])
        S.op("dve", cp(qpeT[0:32].rearrange("p n h t -> p h n t"),
                       qpT[0:32].rearrange("p h (n t) -> p h n t", t=4)), [qpT], [qpeT])
        for kc in range(2):
            pw = psb(4 + kc)
            for h in range(8):
                S.op("pe", trp(pw[0:64, h * 128:(h + 1) * 128], W_uk[:, kc, h * 64:(h + 1) * 64], ident.ap), [W_uk, ident], [PS[4 + kc]])
            S.op("act", acp(wukT[0:64, :, kc * 128:(kc + 1) * 128], pw[0:64, 0:1024].rearrange("p (h c) -> p h c", h=8)),
                 [PS[4 + kc]], [wukT])
        for ck in range(2):
            for h in range(8):
                S.op("pe", mm(PS[4 + ck][:, h * P:(h + 1) * P], wukT[0:64, h, ck * 128:(ck + 1) * 128], qnT[0:64, h, :]),
                     [wukT, qnT], [PS[4 + ck]])
            S.op("act", acp(qlatT[:, ck].rearrange("p n h t -> p h n t"),
                            PS[4 + ck][:, 0:8 * P].rearrange("p (h n t) -> p h n t", h=8, t=4)), [PS[4 + ck]], [qlatT])

        def tile_scores(lbt, parts, kp_ap, kpT_src, bank_t, bank_k, bank_s, rhs_lat, rhs_pe, ncols):
            pbt = psb(bank_t)
            cT_, kpT_ = cT.next(), kpT.next()
            for c in range(2):
                S.op("pe", trp(pbt[:, c * parts:(c + 1) * parts], lbt[0:parts, c * 128:(c + 1) * 128], ident[0:parts, 0:parts]),
                     [lbt, ident], [PS[bank_t]])
            S.op("pe", trp(pbt[0:32, 256:256 + parts], lbt[0:parts, 257:289], ident[0:parts, 0:parts]), [lbt, ident], [PS[bank_t]])
            S.op("act", acp(cT_[:, :, 0:parts], pbt[:, 0:2 * parts].rearrange("p (c t) -> p c t", c=2)), [PS[bank_t]], [cT_])
            S.op("dve", cp(kpT_[0:32, 0:parts], pbt[0:32, 256:256 + parts]), [PS[bank_t]], [kpT_])
            for kc in range(2):
                S.op("pe", mm(PS[bank_k][0:parts, :], cT_[:, kc, 0:parts], W_uk[:, kc, :], kc == 0, kc == 1), [cT_, W_uk], [PS[bank_k]])
            sq_, ssq_ = sq.next(), ssq.next()
            S.op("act", actf(sq_[0:parts, 0:512], PS[bank_k][0:parts, :], AF.Square), [PS[bank_k]], [sq_])
            S.op("dve", rsum(ssq_[0:parts, 0:8], sq_[0:parts, 0:512].rearrange("p (h d) -> p h d", h=8)), [sq_], [ssq_])
            S.op("act", actf(sq_[0:parts, 512:544], kp_ap, AF.Square, accum_out=ssq_[0:parts, 8:9]), [kpT_src], [sq_, ssq_])
            S.op("dve", ts(ssq_[0:parts, 0:8], ssq_[0:parts, 0:8], ssq_[0:parts, 8:9], ALU.add), [ssq_], [ssq_])
            S.op("act", actf(ssq_[0:parts, 0:8], ssq_[0:parts, 0:8], AF.Sqrt, scale=1.0 / 96, bias=EPS), [ssq_], [ssq_])
            S.op("dve", recip(ssq_[0:parts, 0:8], ssq_[0:parts, 0:8]), [ssq_], [ssq_])
            for kc in range(2):
                S.op("pe", mm(PS[bank_s][0:parts, 0:ncols], cT_[:, kc, 0:parts], rhs_lat(kc), kc == 0, False),
                     [cT_, qlatT], [PS[bank_s]])
            S.op("pe", mm(PS[bank_s][0:parts, 0:ncols], kpT_[0:32, 0:parts], rhs_pe, False, True), [kpT_, qpeT], [PS[bank_s]])
            return ssq_

        S.op("act", acp(lbn[0:P, 0:256], cn_s[0:P, :]), [cn_s], [lbn])
        S.op("act", acp(lbn[0:P, 257:289], kpe_s[0:P, :]), [kpe_s], [lbn])
        NC_ = NSEQ * 32
        r_ = tile_scores(lbn, P, kpe_s[0:P, :], kpe_s, 0, 1, 2,
                         lambda kc: qlatT[:, kc].rearrange("p n h t -> p (n h t)"),
                         qpeT[0:32].rearrange("p n h t -> p (n h t)"), NC_)
        S.op("dve", tt(scn[0:P, :].rearrange("p (n h t) -> p n h t", h=8, t=4),
                       PS[2][0:P, 0:NC_].rearrange("p (n h t) -> p n h t", h=8, t=4),
                       r_[0:P, 0:8].unsqueeze(1).unsqueeze(3).to_broadcast([P, NSEQ, 8, 4]), ALU.mult), [PS[2], r_], [scn])
        S.op("act", actf(scn[0:P, :], scn[0:P, :], AF.Exp), [scn], [scn])
        S.op("dve", tt(pTn[0:P, :], scn[0:P, :], smask[0:P, :], ALU.mult), [scn, smask], [pTn])

        S.op("act", acp(z_sb[0:P, :], PS[3][0:P, :]), [PS[3]], [z_sb])
        lbc = Rot([ar.alloc([SC, 289], BF16, f"lbc{i}") for i in range(2)])
        for l_ in lbc.items:
            S.op("pool", mset(l_[:, :, 256:257], 1.0), [], [l_])
        cT2 = Rot([ar.alloc([2, 2, 128], BF16, f"cT2_{i}") for i in range(2)])
        kpT2 = Rot([ar.alloc([2, 128], BF16, f"kpT2_{i}") for i in range(2)])
        sq2 = Rot([ar.alloc([1024], F32, f"sq2_{i}") for i in range(2)])
        sqk = Rot([ar.alloc([SC, 32], F32, f"sqk{i}") for i in range(2)])
        ssqc = Rot([ar.alloc([SC, 8], F32, f"ssqc{i}") for i in range(2)])
        sspc = Rot([ar.alloc([SC], F32, f"sspc{i}") for i in range(2)])
        scr = Rot([ar.alloc([SC, 32], F32, f"scr{i}") for i in range(2)])
        pTc = Rot([ar.alloc([SC, 32], BF16, f"pTc{i}") for i in range(2)])
        OLB = 7
        nb = 0

        def stage_b(n, ch, lc_, ssq_, ssp_, scr_, first_of_seq):
            p_ = pTc.next()
            S.op("dve", tt(ssq_.ap, ssq_.ap, bc(ssp_.ap, [128, SC, 8], 2), ALU.add), [ssq_, ssp_], [ssq_])
            S.op("act", actf(ssq_.ap, ssq_.ap, AF.Sqrt, scale=1.0 / 96, bias=EPS), [ssq_], [ssq_])
            S.op("dve", recip(ssq_.ap, ssq_.ap), [ssq_], [ssq_])
            S.op("dve", tt(scr_.ap.rearrange("p s (h t) -> p s h t", t=4), scr_.ap.rearrange("p s (h t) -> p s h t", t=4),
                           bc(ssq_.ap, [128, SC, 8, 4], 3), ALU.mult), [scr_, ssq_], [scr_])
            S.op("act", actf(p_.ap, scr_.ap, AF.Exp), [scr_], [p_])
            if first_of_seq:
                S.op("pe", mm(PS[OLB][0:32, 0:257], pTn[0:P, n * 32:(n + 1) * 32], lbn[0:P, 0:257], True, False),
                     [pTn, lbn], [PS[OLB]])
            for i in range(SC):
                last = (ch == NCH - 1 and i == SC - 1)
                S.op("pe", mm(PS[OLB][0:32, 0:257], p_[:, i, :], lc_[:, i, 0:257], False, last), [p_, lc_], [PS[OLB]])

        def seq_epilogue(n):
            OL = OLB
            S.op("dve", recip(olr[0:32, 0:1], PS[OL][0:32, 256:257]), [PS[OL]], [olr])
            S.op("dve", ts(ol[0:32, :], PS[OL][0:32, 0:256], olr[0:32, 0:1], ALU.mult), [PS[OL], olr], [ol])
            po = psb(6)
            for ck in range(2):
                S.op("pe", trp(po[:, 512 + ck * 32:512 + (ck + 1) * 32], ol[0:32, ck * 128:(ck + 1) * 128], ident[0:32, 0:32]),
                     [ol, ident], [PS[6]])
            S.op("act", acp(olatT[:, :, :, n * 4:(n + 1) * 4],
                            po[:, 512:576].rearrange("p (c h t) -> p c h t", c=2, t=4)), [PS[6]], [olatT])

        pending = None
        for n in range(NSEQ):
            for ch in range(NCH):
                gl_, gk_ = gl.next(), gk.next()
                S.dma_fn("pool", lambda e, gl_=gl_, n=n, ch=ch: e.indirect_dma_start(
                    out=gl_.ap.rearrange("p s c -> p (s c)"), out_offset=None, in_=cache_lat,
                    in_offset=bass.IndirectOffsetOnAxis(ap=ptab_sb[:, n:n + 1], axis=0),
                    element_offset=ch * SC * 256), [ptab_sb], [gl_])
                S.dma_fn("pool", lambda e, gk_=gk_, n=n, ch=ch: e.indirect_dma_start(
                    out=gk_.ap.rearrange("p s c -> p (s c)"), out_offset=None, in_=cache_kr,
                    in_offset=bass.IndirectOffsetOnAxis(ap=ptab_sb[:, n:n + 1], axis=0),
                    element_offset=ch * SC * 32), [ptab_sb], [gk_])
                lc_, ssq_, ssp_, scr_, qk_ = lbc.next(), ssqc.next(), sspc.next(), scr.next(), sqk.next()
                S.op("pool", cp(lc_[:, :, 257:289], gk_.ap), [gk_], [lc_])
                S.op("act", actf(qk_.ap, gk_.ap, AF.Square), [gk_], [qk_])
                S.op("dve", rsum(ssp_.ap, qk_.ap), [qk_], [ssp_])
                for j in range(SC // 2):
                    sl = slice(2 * j, 2 * j + 2)
                    S.op("pool", cp(lc_[:, sl, 0:256], gl_[:, sl, :]), [gl_], [lc_])
                    bT = nb % 2
                    bK = (2, 3) if nb % 2 == 0 else (4, 5)
                    nb += 1
                    pbt = psb(bT)
                    c2, k2 = cT2.next(), kpT2.next()
                    for g_ in range(2):
                        for ck in range(2):
                            S.op("pe", trp(pbt[:, (g_ * 2 + ck) * 128:(g_ * 2 + ck + 1) * 128],
                                           lc_[:, 2 * j + g_, ck * 128:(ck + 1) * 128], ident.ap), [lc_, ident], [PS[bT]])
                        S.op("pe", trp(pbt[0:32, 512 + g_ * 128:512 + (g_ + 1) * 128], lc_[:, 2 * j + g_, 257:289], ident.ap),
                             [lc_, ident], [PS[bT]])
                    S.op("act", acp(c2.ap.rearrange("p g c t -> p (g c t)"), pbt[:, 0:512]), [PS[bT]], [c2])
                    S.op("dve", cp(k2[0:32].rearrange("p g t -> p (g t)"), pbt[0:32, 512:768]), [PS[bT]], [k2])
                    for g_ in range(2):
                        for kc in range(2):
                            S.op("pe", mm(PS[bK[g_]].ap, c2[:, g_, kc, :], W_uk[:, kc, :], kc == 0, kc == 1), [c2, W_uk], [PS[bK[g_]]])
                    for g_ in range(2):
                        for kc in range(2):
                            S.op("pe", mm(PS[6][:, g_ * 32:(g_ + 1) * 32], c2[:, g_, kc, :],
                                          qlatT[:, kc, n].rearrange("p h t -> p (h t)"), kc == 0, False), [c2, qlatT], [PS[6]])
                        S.op("pe", mm(PS[6][:, g_ * 32:(g_ + 1) * 32], k2[0:32, g_, :],
                                      qpeT[0:32, n].rearrange("p h t -> p (h t)"), False, True), [k2, qpeT], [PS[6]])
                    q2 = sq2.next()
                    S.op("act", actf(q2.ap, PSD[bK[0] // 2], AF.Square), [PS[bK[0]], PS[bK[1]]], [q2])
                    S.op("dve", rsum(ssq_[:, sl, :].rearrange("p g h -> p (g h)"), q2.ap.rearrange("p (x d) -> p x d", d=64)),
                         [q2], [ssq_])
                    S.op("dve", cp(scr_[:, sl, :].rearrange("p g x -> p (g x)"), PS[6][:, 0:64]), [PS[6]], [scr_])
                    if j == 1 and pending is not None:
                        stage_b(*pending)
                        if pending[-1] is False and pending[1] == NCH - 1:
                            pass
                        pending = None
                        if ch == 0 and n > 0:
                            seq_epilogue(n - 1)
                pending = (n, ch, lc_, ssq_, ssp_, scr_, ch == 0)
        stage_b(*pending)
        seq_epilogue(NSEQ - 1)
        for h in range(8):
            for ck in range(2):
                S.op("pe", mm(PS[1][0:P, h * 64:(h + 1) * 64], olatT[:, ck, h, :], W_uv[:, ck, h * 64:(h + 1) * 64], ck == 0, ck == 1),
                     [olatT, W_uv], [PS[1]])
        S.op("act", actf(ez[0:P, :], z_sb[0:P, :], AF.Exp, scale=-1.0), [z_sb], [ez])
        S.op("dve", ts(ez[0:P, :], ez[0:P, :], 1.0, ALU.add), [ez], [ez])
        S.op("dve", recip(ez[0:P, :], ez[0:P, :]), [ez], [ez])
        S.op("dve", tt(ez[0:P, :], z_sb[0:P, :], ez[0:P, :], ALU.mult), [z_sb, ez], [ez])
        S.op("dve", tt(gated[0:P, :], PS[1][0:P, :], ez[0:P, :], ALU.mult), [PS[1], ez], [gated])
        pg = psb(0)
        for c in range(4):
            S.op("pe", trp(pg[:, c * P:(c + 1) * P], gated[0:P, c * 128:(c + 1) * 128], ident[0:P, 0:P]), [gated, ident], [PS[0]])
        S.op("act", acp(gTs.ap, pg[:, 0:4 * P].rearrange("p (c t) -> p c t", c=4)), [PS[0]], [gTs])
        for half in range(2):
            for c in range(4):
                S.op("pe", mm(PS[4 + half][0:P, :], gTs[:, c, :], W_o[:, c, half * 512:(half + 1) * 512], c == 0, c == 3),
                     [gTs, W_o], [PS[4 + half]])
            S.op("dve", tt(xs1[0:P, half * 512:(half + 1) * 512], xs[0:P, half * 512:(half + 1) * 512],
                           PS[4 + half][0:P, :], ALU.add), [xs, PS[4 + half]], [xs1])
        if "s2" not in phases:
            S.dma("sp", y_s, xs1[0:P, :], reads=[xs1], writes=[D_out])
        S.barrier()
        ar.release()

    def load_gdn_weights():
        W = {}
        bn_sb = ar.alloc([8], F32, "bn_sb")
        S.dma("sp", bn_sb.ap, b_norm, writes=[bn_sb])
        wtmp2 = ar.alloc([2064], F32, "wtmp2")
        W["in"] = ar.alloc([8, 2064], BF16, "Wb_in")
        load_bf16_rows(W["in"], [W["in"][:, kc, :] for kc in range(8)],
                       [b_w_in[kc * 128:(kc + 1) * 128, :] for kc in range(8)], (bn_sb, bn_sb), wtmp2, 2064)
        W["zab"] = ar.alloc([8, 2, 272], BF16, "Wzab")
        for g_ in range(2):
            S.op("pool", cp(W["zab"][:, :, g_, 0:256], W["in"][:, :, 1536 + g_ * 256:1792 + g_ * 256]), [W["in"]], [W["zab"]])
            S.op("pool", cp(W["zab"][:, :, g_, 256:272], W["in"][:, :, 2048:2064]), [W["in"]], [W["zab"]])
        W["o"] = ar.alloc([4, 1024], BF16, "Wb_o")
        load_bf16_rows(W["o"], [W["o"][:, kc, :] for kc in range(4)],
                       [b_w_o[kc * 128:(kc + 1) * 128, :] for kc in range(4)], None, wtmp2, 1024)
        W["convT"] = ar.alloc([12, 4], F32, "wconvT")
        S.dma("sp", W["convT"].ap.rearrange("p c j -> p (c j)"), b_w_convT, writes=[W["convT"]])
        W["negA"] = ar.alloc([8], F32, "negA")
        S.dma("sp", W["negA"].ap, b_a_log.to_broadcast([128, 8]), writes=[W["negA"]])
        S.op("act", actf(W["negA"].ap, W["negA"].ap, AF.Exp), [W["negA"]], [W["negA"]])
        S.op("dve", ts(W["negA"].ap, W["negA"].ap, -1.0, ALU.mult), [W["negA"]], [W["negA"]])
        W["dtb"] = ar.alloc([8], F32, "dtb")
        S.dma("sp", W["dtb"].ap, b_dt_bias.to_broadcast([128, 8]), writes=[W["dtb"]])
        W["go"] = ar.alloc([64], F32, "go")
        S.dma("sp", W["go"].ap, b_g_o.to_broadcast([128, 64]), writes=[W["go"]])
        for nm, src in (("tri", c_tri), ("strict", c_strict), ("negm", c_negm)):
            W[nm] = ar.alloc([128], F32, nm)
            S.dma("sp", W[nm].ap, src, writes=[W[nm]])
        W["ones"] = ar.alloc([128], F32, "ones_f")
        S.op("pool", mset(W["ones"].ap, 1.0), [], [W["ones"]])
        blk_f = ar.alloc([128], F32, "blk_f")
        S.dma("sp", blk_f.ap, c_blk, writes=[blk_f])
        W["blk"] = ar.alloc([128], BF16, "blk")
        S.op("dve", cp(W["blk"].ap, blk_f.ap), [blk_f], [W["blk"]])
        return W

    def gates(xa_ap, xb_ap, srcT, parts, nh, W, hcol, gt, bt, tmp):
        a1, a2 = tmp
        P = slice(0, parts)
        S.op("dve", tt(a1[P, 0:nh], xa_ap, W["dtb"][P, hcol:hcol + nh], ALU.add), [srcT, W["dtb"]], [a1])
        S.op("dve", stt(a2[P, 0:nh], a1[P, 0:nh], -1.0, a1[P, 0:nh], ALU.mult, ALU.max), [a1], [a2])
        S.op("act", actf(a2[P, 0:nh], a2[P, 0:nh], AF.Exp, scale=-1.0), [a2], [a2])
        S.op("act", actf(a2[P, 0:nh], a2[P, 0:nh], AF.Ln, bias=1.0), [a2], [a2])
        S.op("dve", stt(a1[P, 0:nh], a1[P, 0:nh], 0.0, a2[P, 0:nh], ALU.max, ALU.add), [a1, a2], [a1])
        S.op("dve", tt(gt[P, 0:nh], a1[P, 0:nh], W["negA"][P, hcol:hcol + nh], ALU.mult), [a1, W["negA"]], [gt])
        S.op("act", actf(a2[P, 0:nh], xb_ap, AF.Exp, scale=-1.0), [srcT], [a2])
        S.op("dve", ts(a2[P, 0:nh], a2[P, 0:nh], 1.0, ALU.add), [a2], [a2])
        S.op("dve", recip(bt[P, 0:nh], a2[P, 0:nh]), [a2], [bt])

    def phase_gdn_prompt(W):
        ar.mark()
        NTL = SEQ // 512
        xt = Rot([ar.alloc([1024], F32, f"gx{i}") for i in range(2)])
        hb = ar.alloc([1024], BF16, "ghb")
        hT4 = ar.alloc([8, 512], BF16, "ghT4")
        qk = [ar.alloc([6, 515], F32, f"qk{i}") for i in range(2)]
        cs = ar.alloc([6, 512], F32, "gcs")
        sqb = ar.alloc([512], BF16, "sqb")
        sd = ar.alloc([512], F32, "sd")
        QKn = ar.alloc([4, 512], BF16, "QKn")
        vb = ar.alloc([2, 512], BF16, "vb")
        KnZ = ar.alloc([2, 2, 512], BF16, "KnZ")
        SbZ = ar.alloc([2, 2, 64], BF16, "SbZ")
        S.op("pool", mset(KnZ.ap, 0.0), [], [KnZ])
        gt_, bt_ = ar.alloc([4], F32, "g_t"), ar.alloc([4], F32, "b_t")
        a1, a2 = ar.alloc([4], F32, "ga1"), ar.alloc([4], F32, "ga2")
        smp = [ar.alloc([32], F32, f"gsm{i}") for i in range(2)]
        smS = ar.alloc([8], F32, "gsmS")
        egp = [ar.alloc([2], F32, f"eg{i}") for i in range(2)]
        z_p = [ar.alloc([256], F32, f"z_p{i}") for i in range(2)]
        Z = ar.alloc([4, 128], F32, "Z")
        Dm = ar.alloc([4, 128], F32, "Dm")
        decay = ar.alloc([4, 128], F32, "decay")
        A1 = ar.alloc([4, 128], F32, "A1")
        bS = ar.alloc([4, 128], F32, "bS")
        MYr = [ar.alloc([4, 256], BF16, f"MY{i}") for i in range(2)]
        Yf = ar.alloc([4, 128], BF16, "Yf")
        intra = ar.alloc([4, 128], BF16, "intra")
        intraTp = [ar.alloc([4, 128], BF16, f"intraT{i}") for i in range(2)]
        Nr = Rot([ar.alloc([4, 128], BF16, f"N{i}") for i in range(2)])
        Mr = Rot([ar.alloc([4, 128], BF16, f"M{i}") for i in range(2)])
        ImLT = ar.alloc([4, 128], BF16, "ImLT")
        Yr = Rot([ar.alloc([4, 128], BF16, f"Y{i}") for i in range(2)])
        Ywz = ar.alloc([2, 2, 128], BF16, "Ywz")
        ktzp = [ar.alloc([2, 2, 128], BF16, f"ktz{i}") for i in range(2)]
        S.op("pool", mset(Ywz.ap, 0.0), [], [Ywz])
        for k_ in ktzp:
            S.op("pool", mset(k_.ap, 0.0), [], [k_])
        ktok = ar.alloc([4, 64], F32, "ktok")
        u_p = [ar.alloc([4, 64], F32, f"u_sb{i}") for i in range(2)]
        wT_p = [ar.alloc([2, 128], BF16, f"wT_sb{i}") for i in range(2)]
        vn = ar.alloc([4, 64], BF16, "vn")
        o_sb = ar.alloc([4, 64], F32, "o_sb")
        o_t = ar.alloc([4, 64], F32, "o_t")
        St = ar.alloc([2, 64], F32, "St")
        St_t = ar.alloc([2, 64], F32, "St_t")
        Sb = ar.alloc([2, 64], BF16, "Sb")
        sgz = ar.alloc([256], F32, "sgz")
        gated = ar.alloc([256], BF16, "gated")
        G2 = Rot([ar.alloc([2, 512], BF16, f"G2{i}") for i in range(2)])
        ident_b = ident
        crow = ar.alloc([6, 128], F32, "crow")

        def load_h(tile, col):
            x = xt.next()
            S.dma("sp", x.ap, xp1[tile * 128:(tile + 1) * 128, :], reads=[D_xp1], writes=[x])
            st = stat.next()
            rms_rstd(x.ap, x, 1024, st, junk)
            S.op("dve", ts(hb.ap, x.ap, st[:, 0:1], ALU.mult), [x, st], [hb])
            pb = psb(0)
            for c in range(8):
                S.op("pe", trp(pb[:, c * 128:(c + 1) * 128], hb[:, c * 128:(c + 1) * 128], ident.ap),
                     [hb, ident], [PS[0]])
            S.op("act", acp(hT4[:, :, col:col + 128], pb[:, 0:1024].rearrange("p (c t) -> p c t", c=8)),
                 [PS[0]], [hT4])

        for g in range(2):
            if g == 1:
                S.barrier()
            S.op("pool", mset(St.ap, 0.0), [], [St])
            S.op("pool", mset(SbZ.ap, 0.0), [], [SbZ])
            gch = [2 * g, 2 * g + 1, 4 + 2 * g, 5 + 2 * g, 8 + 2 * g, 9 + 2 * g]
            for ti in range(NTL):
                cur, prev = qk[ti % 2], qk[(ti + 1) % 2]
                for s in range(4):
                    tile_ = ti * 4 + s
                    if g == 0:
                        load_h(tile_, s * 128)
                    else:
                        S.dma("sp" if s % 2 == 0 else "pool", hT4[:, :, s * 128:(s + 1) * 128],
                              hT2_scr[:, :, tile_ * 128:(tile_ + 1) * 128], reads=[D_h2], writes=[hT4])
                if g == 0:
                    S.dma("pool", hT2_scr[:, :, ti * 512:(ti + 1) * 512], hT4.ap, reads=[hT4], writes=[D_h2])
                if ti == 0:
                    S.op("pool", mset(cur[:, :, 0:3], 0.0), [], [cur])
                else:
                    S.op("pool", cp(cur[:, :, 0:3], prev[:, :, 512:515]), [prev], [cur])
                for lc, gc_ in enumerate(gch):
                    b = 1 + (lc % 2)
                    for kc in range(8):
                        S.op("pe", mm(PS[b].ap, W["in"][:, kc, gc_ * 128:(gc_ + 1) * 128], hT4[:, kc, :],
                                      kc == 0, kc == 7), [W["in"], hT4], [PS[b]])
                    S.op("act", acp(cur[:, lc, 3:515], PS[b].ap), [PS[b]], [cur])
                    ce = "dve"
                    S.op(ce, ts(cs[:, lc, :], cur[:, lc, 0:512], W["convT"][:, gc_, 0:1], ALU.mult),
                         [cur, W["convT"]], [cs])
                    for j in range(1, 4):
                        S.op(ce, stt(cs[:, lc, :], cur[:, lc, j:j + 512], W["convT"][:, gc_, j:j + 1], cs[:, lc, :],
                                     ALU.mult, ALU.add), [cur, W["convT"], cs], [cs])
                    S.op("act", actf(cs[:, lc, :], cs[:, lc, :], AF.Silu), [cs], [cs])
                if ti == NTL - 1:
                    for lc, gc_ in enumerate(gch):
                        S.op("pe", trp(PS[3][0:3, 0:128], cur[:, lc, 512:515], ident_f.ap), [cur, ident_f], [PS[3]])
                        S.op("dve", cp(crow[0:3, lc, :], PS[3][0:3, 0:128]), [PS[3]], [crow])
                        S.dma("sp", conv_p[:, gc_ * 128:(gc_ + 1) * 128], crow[0:3, lc, :], reads=[crow], writes=[D_out])
                for lc in range(4 if cfg.cut >= 1 else 0):
                    S.op("act", actf(sqb.ap, cs[:, lc, :], AF.Square), [cs], [sqb])
                    S.op("pe", mm(PS[3].ap, W["blk"].ap, sqb.ap), [W["blk"], sqb], [PS[3]])
                    S.op("act", actf(sd.ap, PS[3].ap, AF.Sqrt, bias=EPS), [PS[3]], [sd])
                    S.op("dve", recip(sd.ap, sd.ap), [sd], [sd])
                    S.op("dve", stt(QKn[:, lc, :], cs[:, lc, :], 0.125 if lc < 2 else 1.0, sd.ap, ALU.mult, ALU.mult),
                         [cs, sd], [QKn])
                S.op("act", acp(vb.ap, cs[:, 4:6, :]), [cs], [vb])
                S.op("pool", cp(KnZ[0:64, :, 0, :], QKn[0:64, 2:4, :]), [QKn], [KnZ])
                S.op("pool", cp(KnZ[64:128, :, 1, :], QKn[64:128, 2:4, :]), [QKn], [KnZ])
                Gt = G2.next()
                def local(s):
                    CUT = cfg.cut
                    par = s % 2
                    sm, eg, ktz, intraT, u_sb, wT_sb = smp[par], egp[par], ktzp[par], intraTp[par], u_p[par], wT_p[par]
                    tc_ = slice(s * 128, (s + 1) * 128)
                    zc = 1536 + g * 256
                    for kc in range(8):
                        S.op("pe", mm(PS[0][:, 0:272], hT4[:, kc, tc_], W["zab"][:, kc, g, :], kc == 0, kc == 7),
                             [hT4, W["zab"]], [PS[0]])
                    S.op("act", acp(z_p[par].ap, PS[0][:, 0:256]), [PS[0]], [z_p[par]])
                    gates(PS[0][:, 256 + 4 * g:260 + 4 * g], PS[0][:, 264 + 4 * g:268 + 4 * g], PS[0], 128, 4, W,
                          4 * g, gt_, bt_, (a1, a2))
                    S.op("pe", mm(PS[4][:, 0:4], W["tri"].ap, gt_.ap), [W["tri"], gt_], [PS[4]])
                    S.op("dve", cp(sm[:, 0:4], PS[4][:, 0:4]), [PS[4]], [sm])
                    S.op("dve", tt(Z.ap, bc(gt_.ap, [128, 4, 128], 2), bc(W["tri"].ap, [128, 4, 128], 1), ALU.mult),
                         [gt_, W["tri"]], [Z])
                    S.op("pe", mm(PS[4].ap, W["ones"].ap, Z.ap.rearrange("p h j -> p (h j)")), [W["ones"], Z], [PS[4]])
                    gcrow = PS[4].ap.rearrange("p (h j) -> p h j", h=4)
                    S.op("dve", tt(Dm.ap, bc(sm[:, 0:4], [128, 4, 128], 2), gcrow, ALU.subtract), [sm, PS[4]], [Dm])
                    S.op("dve", stt(Dm.ap, Dm.ap, 0.0, bc(W["negm"].ap, [128, 4, 128], 1), ALU.min, ALU.add),
                         [Dm, W["negm"]], [Dm])
                    S.op("act", actf(decay.ap, Dm.ap, AF.Exp), [Dm], [decay])
                    S.op("act", actf(sm[:, 4:8], sm[:, 0:4], AF.Exp), [sm], [sm])
                    S.op("dve", tt(sm[:, 8:12], sm[:, 4:8], bt_.ap, ALU.mult), [sm, bt_], [sm])
                    S.op("dve", tt(sm[:, 20:24], gcrow[:, :, 127], sm[:, 0:4], ALU.subtract), [PS[4], sm], [sm])
                    S.op("act", actf(sm[:, 12:16], sm[:, 20:24], AF.Exp), [sm], [sm])
                    S.op("act", actf(eg[0:64, :], gcrow[0:64, 0::2, 127], AF.Exp), [PS[4]], [eg])
                    S.op("act", actf(eg[64:128, :], gcrow[64:128, 1::2, 127], AF.Exp), [PS[4]], [eg])
                    if CUT < 3:
                        return
                    pb7 = psb(7)
                    for c in range(2):
                        S.op("pe", trp(pb7[:, c * 128:(c + 1) * 128], QKn[:, 2 + c, tc_], ident.ap), [QKn, ident], [PS[7]])
                    for c in range(2):
                        S.op("pe", trp(pb7[:, 256 + c * 128:256 + (c + 1) * 128], vb[:, c, tc_], ident.ap),
                             [vb, ident], [PS[7]])
                    S.op("act", acp(ktok.ap.rearrange("p h d -> p (h d)"), pb7[:, 0:256]), [PS[7]], [ktok])
                    S.op("dve", tt(MYr[1][:, :, 128:192], pb7[:, 256:512].rearrange("p (h d) -> p h d", h=4),
                                   bc(bt_.ap, [128, 4, 64], 2), ALU.mult), [PS[7], bt_], [MYr[1]])
                    S.op("dve", tt(MYr[1][:, :, 192:256], ktok.ap, bc(sm[:, 8:12], [128, 4, 64], 2), ALU.mult), [ktok, sm], [MYr[1]])
                    kz = ktok.ap.rearrange("p (pr par) d -> p pr par d", par=2)
                    ekv = sm[:, 12:16].rearrange("p (pr par) -> p pr par", par=2)
                    for par in range(2):
                        S.op("dve", tt(ktz[:, :, par, par * 64:(par + 1) * 64], kz[:, :, par, :],
                                       bc(ekv[:, :, par], [128, 2, 64], 2), ALU.mult), [ktok, sm], [ktz])
                    if CUT < 3.2:
                        return
                    for h in range(4):
                        pr, par = h // 2, h % 2
                        kz_ = KnZ[:, pr, par, tc_]
                        S.op("pe", mm(PS[5][:, h * 128:(h + 1) * 128], kz_, QKn[:, 2 + pr, tc_]), [QKn, KnZ], [PS[5]])
                        S.op("pe", mm(PS[6][:, h * 128:(h + 1) * 128], QKn[:, pr, tc_], kz_), [QKn, KnZ], [PS[6]])
                    if CUT < 3.4:
                        return
                    S.op("dve", tt(A1.ap.rearrange("p h j -> p (h j)"), PS[5].ap, decay.ap.rearrange("p h j -> p (h j)"),
                                   ALU.mult), [PS[5], decay], [A1])
                    S.op("dve", tt(bS.ap, bc(bt_.ap, [128, 4, 128], 2), bc(W["strict"].ap, [128, 4, 128], 1), ALU.mult),
                         [bt_, W["strict"]], [bS])
                    S.op("dve", tt(MYr[0][:, :, 0:128], A1.ap, bS.ap, ALU.mult), [A1, bS], [MYr[0]])
                    S.op("dve", tt(intra.ap.rearrange("p h j -> p (h j)"), PS[6].ap, decay.ap.rearrange("p h j -> p (h j)"),
                                   ALU.mult), [PS[6], decay], [intra])
                    if CUT < 3.6:
                        return
                    for h in range(4):
                        S.op("pe", trp(pb7[:, h * 128:(h + 1) * 128], intra[:, h, :], ident.ap), [intra, ident], [PS[7]])
                    S.op("act", acp(intraT.ap.rearrange("p h j -> p (h j)"), pb7[:, 0:512]), [PS[7]], [intraT])
                    for h in range(4):
                        S.op("pe", trp(pb7[:, 512 + h * 128:512 + (h + 1) * 128], MYr[0][:, h, 0:128], ident.ap), [MYr[0], ident], [PS[7]])
                    if CUT < 3.8:
                        return
                    N = Nr.next()
                    S.op("act", acp(N.ap.rearrange("p h j -> p (h j)"), pb7[:, 512:1024]), [PS[7]], [N])
                    if CUT < 3.85:
                        return
                    S.op("dve", tt(ImLT.ap, bc(ident_b.ap, [128, 4, 128], 1), N.ap, ALU.subtract), [ident_b, N], [ImLT])
                    if CUT < 4:
                        return
                    PA = PSD[2]
                    PAv = PA.rearrange("p (h x) -> p h x", h=4)
                    for k in range(0, 7):
                        cur_, nxt_ = MYr[k % 2], MYr[(k + 1) % 2]
                        if k == 0:
                            c0_, c1_ = 0, 128
                        elif k < 6:
                            c0_, c1_ = 0, 256
                        else:
                            c0_, c1_ = 128, 256
                        for h in range(4):
                            S.op("pe", mm(PA[:, h * 256 + c0_:h * 256 + c1_], N[:, h, :], cur_[:, h, c0_:c1_]),
                                 [N, cur_], [PS[4], PS[5]])
                        if k < 6:
                            N2 = Nr.next()
                            for h in range(4):
                                S.op("pe", mm(PS[6][:, h * 128:(h + 1) * 128], cur_[:, h, 0:128], N[:, h, :]), [cur_, N], [PS[6]])
                            S.op("act", acp(nxt_[:, :, 0:128], PAv[:, :, 0:128]), [PS[4], PS[5]], [nxt_])
                        if k >= 1:
                            dstY = nxt_[:, :, 128:256] if k < 6 else Yf.ap
                            dT = nxt_ if k < 6 else Yf
                            S.op("dve", tt(dstY, PAv[:, :, 128:256], cur_[:, :, 128:256], ALU.add), [PS[4], PS[5], cur_], [dT])
                        if k < 6:
                            S.op("act", acp(N2.ap.rearrange("p h j -> p (h j)"), PS[6].ap), [PS[6]], [N2])
                            N = N2
                    Y = Yf
                    if CUT < 5:
                        return
                    yv = Y[:, :, 64:128].rearrange("p (pr par) d -> p pr par d", par=2)
                    for par in range(2):
                        S.op("pool", cp(Ywz[:, :, par, par * 64:(par + 1) * 64], yv[:, :, par, :]), [Y], [Ywz])
                    for h in range(4):
                        S.op("pe", mm(PS[1][:, h * 64:(h + 1) * 64], ImLT[:, h, :], Y[:, h, 0:64]), [ImLT, Y], [PS[1]])
                    for pr in range(2):
                        for par in range(2):
                            S.op("pe", mm(PS[1][:, 256 + pr * 128:256 + (pr + 1) * 128], Ywz[:, pr, par, :], ImLT[:, 2 * pr + par, :],
                                          par == 0, par == 1), [Ywz, ImLT], [PS[1]])
                    S.op("act", acp(u_sb.ap.rearrange("p h d -> p (h d)"), PS[1][:, 0:256]), [PS[1]], [u_sb])
                    S.op("dve", cp(wT_sb.ap.rearrange("p a t -> p (a t)"), PS[1][:, 256:512]), [PS[1]], [wT_sb])

                def scan(s):
                    CUT = cfg.cut
                    par_s = s % 2
                    sm, eg, ktz, intraT, u_sb, wT_sb = smp[par_s], egp[par_s], ktzp[par_s], intraTp[par_s], u_p[par_s], wT_p[par_s]
                    tc_ = slice(s * 128, (s + 1) * 128)
                    pb2 = psb(2)
                    for h in range(4):
                        pr, par = h // 2, h % 2
                        S.op("pe", mm(PS[2][:, h * 64:(h + 1) * 64], wT_sb[:, pr, :], SbZ[:, pr, par, :]),
                             [wT_sb, SbZ], [PS[2]])
                        S.op("pe", mm(PS[3][:, h * 64:(h + 1) * 64], QKn[:, pr, tc_], SbZ[:, pr, par, :]),
                             [QKn, SbZ], [PS[3]])
                    S.op("dve", tt(vn.ap.rearrange("p h d -> p (h d)"), u_sb.ap.rearrange("p h d -> p (h d)"),
                                   PS[2][:, 0:256], ALU.subtract), [u_sb, PS[2]], [vn])
                    for h in range(4):
                        S.op("pe", mm(PS[3][:, 256 + h * 64:256 + (h + 1) * 64], intraT[:, h, :], vn[:, h, :]),
                             [intraT, vn], [PS[3]])
                    for pr in range(2):
                        for par in range(2):
                            S.op("pe", mm(PS[2][:, 256 + pr * 64:256 + (pr + 1) * 64], ktz[:, pr, par, :], vn[:, 2 * pr + par, :],
                                          par == 0, par == 1), [ktz, vn], [PS[2]])
                    S.op("dve", tt(St_t.ap, St.ap, bc(eg.ap, [128, 2, 64], 2), ALU.mult), [St, eg], [St_t])
                    S.op("dve", tt(St.ap.rearrange("p a d -> p (a d)"), St_t.ap.rearrange("p a d -> p (a d)"),
                                   PS[2][:, 256:384], ALU.add), [St_t, PS[2]], [St])
                    S.op("act", acp(SbZ[0:64, :, 0, :], St[0:64, :, :]), [St], [SbZ])
                    S.op("act", acp(SbZ[64:128, :, 1, :], St[64:128, :, :]), [St], [SbZ])
                    S.op("dve", tt(o_t.ap, PS[3][:, 0:256].rearrange("p (h d) -> p h d", h=4),
                                   bc(sm[:, 4:8], [128, 4, 64], 2), ALU.mult), [PS[3], sm], [o_t])
                    S.op("dve", tt(o_sb.ap.rearrange("p h d -> p (h d)"), o_t.ap.rearrange("p h d -> p (h d)"),
                                   PS[3][:, 256:512], ALU.add), [o_t, PS[3]], [o_sb])
                    if CUT < 7:
                        return
                    S.op("act", actf(o_t.ap, o_sb.ap, AF.Square), [o_sb], [o_t])
                    S.op("dve", rsum(smS[:, 0:4], o_t.ap), [o_t], [smS])
                    S.op("act", actf(smS[:, 0:4], smS[:, 0:4], AF.Sqrt, scale=1.0 / 64, bias=EPS), [smS], [smS])
                    S.op("dve", recip(smS[:, 0:4], smS[:, 0:4]), [smS], [smS])
                    S.op("dve", tt(o_sb.ap, o_sb.ap, bc(smS[:, 0:4], [128, 4, 64], 2), ALU.mult), [o_sb, smS], [o_sb])
                    S.op("dve", tt(o_sb.ap, o_sb.ap, bc(W["go"].ap, [128, 4, 64], 1), ALU.mult), [o_sb, W["go"]], [o_sb])
                    S.op("act", actf(sgz.ap, z_p[par_s].ap, AF.Silu), [z_p[par_s]], [sgz])
                    S.op("dve", tt(gated.ap, o_sb.ap.rearrange("p h d -> p (h d)"), sgz.ap, ALU.mult), [o_sb, sgz], [gated])
                    for c in range(2):
                        S.op("pe", trp(pb2[:, 768 + c * 128:768 + (c + 1) * 128], gated[:, c * 128:(c + 1) * 128], ident.ap),
                             [gated, ident], [PS[2]])
                    S.op("act", acp(Gt[:, :, tc_], pb2[:, 768:1024].rearrange("p (c t) -> p c t", c=2)), [PS[2]], [Gt])
                local(0)
                for s in range(4):
                    streams = []
                    if s < 3:
                        S.begin_capture()
                        local(s + 1)
                        streams.append(S.end_capture())
                    S.begin_capture()
                    scan(s)
                    streams.append(S.end_capture())
                    S.replay(streams)
                S.dma("sp", gt2[g, :, :, ti * 512:(ti + 1) * 512], Gt.ap, reads=[Gt], writes=[D_gt2])
            if cfg.cut >= 6:
                S.dma("sp", ssm_p[4 * g:4 * g + 4].rearrange("(pr par) dk dv -> (par dk) pr dv", par=2), St.ap,
                      reads=[St], writes=[D_out])
        S.barrier()
        ar.release()


    def phase_gdn_sample(W):
        ar.mark()
        P, PH = NS, NSEQ * 8
        hb = ar.alloc([1024], BF16, "g_hb")
        hTs = ar.alloc([8, P], BF16, "g_hTs")
        qkv_sb = ar.alloc([1536], F32, "qkv_sb")
        z_sb = ar.alloc([512], F32, "z_sb")
        ab_sb = ar.alloc([16], F32, "ab_sb")
        E = ar.alloc([7, 3, 64], F32, "E")
        Wc = ar.alloc([4, 3, 64], F32, "Wc")
        AB = ar.alloc([4, 2], F32, "AB")
        alog = ar.alloc([1], F32, "alog")
        dtb = ar.alloc([1], F32, "dtbL")
        cv = ar.alloc([4, 3, 64], F32, "cv")
        cv2 = ar.alloc([4, 3, 64], F32, "cv2")
        nr = ar.alloc([4, 2], F32, "nr")
        qk = ar.alloc([4, 2, 64], F32, "qkn")
        g1, g2, gg_, eg, bet = [ar.alloc([4], F32, f"sg{i}") for i in range(5)]
        St = ar.alloc([64, 64], F32, "StS")
        tmp = ar.alloc([64, 64], F32, "tmpS")
        kS = ar.alloc([64], F32, "kS")
        dl = ar.alloc([64], F32, "dl")
        o_all = ar.alloc([4, 64], F32, "o_all")
        o_tok = ar.alloc([8, 64], F32, "o_tok")
        o_sq = ar.alloc([8, 64], F32, "o_sq")
        orr = ar.alloc([8], F32, "orr")
        gated = ar.alloc([512], BF16, "g_gated")
        gTs = ar.alloc([4, P], BF16, "g_gTs")
        yv = ar.alloc([1024], F32, "yv")
        st = stat.next()
        rms_rstd(xs1[0:P, :], xs1, 1024, st, junk, P)
        S.op("dve", ts(hb[0:P, :], xs1[0:P, :], st[0:P, 0:1], ALU.mult), [xs1, st], [hb])
        pb = psb(0)
        for c in range(8):
            S.op("pe", trp(pb[:, c * P:(c + 1) * P], hb[0:P, c * 128:(c + 1) * 128], ident[0:P, 0:P]), [hb, ident], [PS[0]])
        S.op("act", acp(hTs.ap, pb[:, 0:8 * P].rearrange("p (c t) -> p c t", c=8)), [PS[0]], [hTs])
        for (bank, c0, w, dst) in ((1, 0, 512, qkv_sb[0:P, 0:512]), (2, 512, 512, qkv_sb[0:P, 512:1024]),
                                   (3, 1024, 512, qkv_sb[0:P, 1024:1536]), (4, 1536, 512, z_sb[0:P, :]),
                                   (5, 2048, 16, ab_sb[0:P, :])):
            for kc in range(8):
                S.op("pe", mm(PS[bank][0:P, 0:w], hTs[:, kc, :], W["in"][:, kc, c0:c0 + w], kc == 0, kc == 7),
                     [hTs, W["in"]], [PS[bank]])
            dT = qkv_sb if bank <= 3 else (z_sb if bank == 4 else ab_sb)
            S.op("act", acp(dst, PS[bank][0:P, 0:w]), [PS[bank]], [dT])
        S.dma("sp", qs_scr, qkv_sb[0:P, :], reads=[qkv_sb], writes=[D_qs])
        S.dma("sp", ab_scr, ab_sb[0:P, :], reads=[ab_sb], writes=[D_ab])
        S.dma("pool", E[0:PH, 0:3].rearrange("p r s d -> p (r s d)"), st_convL, writes=[E])
        S.dma("pool", Wc[0:PH].rearrange("p j s d -> p (j s d)"), b_w_convL, writes=[Wc])
        S.dma("pool", alog[0:PH, :], b_alogL, writes=[alog])
        S.dma("pool", dtb[0:PH, :], b_dtbL, writes=[dtb])
        S.dma("pool", St[0:PH].rearrange("p k v -> p (k v)"), st_ssm, writes=[St])
        for n in range(NSEQ):
            q_ = dmaq.next()
            for t in range(4):
                S.dma(q_, E[n * 8:(n + 1) * 8, 3 + t, :, :],
                      qs_scr[n * 4 + t].rearrange("(s h d) -> h s d", s=3, h=8), reads=[D_qs], writes=[E])
            S.dma(q_, AB[n * 8:(n + 1) * 8, :, :], ab_scr[n * 4:(n + 1) * 4, :].rearrange("t (x h) -> h t x", x=2),
                  reads=[D_ab], writes=[AB], allow_slow_non_contiguous=True)
        S.dma("sp", conv_s, E[0:PH, 4:7].rearrange("p r s d -> p (r s d)"), reads=[E], writes=[D_out])
        H = slice(0, PH)
        S.op("dve", tt(cv[H], E[H, 0:4], bc(Wc[H, 0], [PH, 4, 3, 64], 1), ALU.mult), [E, Wc], [cv])
        for j_ in range(1, 4):
            S.op("dve", tt(cv2[H], E[H, j_:j_ + 4], bc(Wc[H, j_], [PH, 4, 3, 64], 1), ALU.mult), [E, Wc], [cv2])
            S.op("dve", tt(cv[H], cv[H], cv2[H], ALU.add), [cv, cv2], [cv])
        S.op("act", actf(cv[H], cv[H], AF.Silu), [cv], [cv])
        S.op("act", actf(cv2[H, :, 0:2, :], cv[H, :, 0:2, :], AF.Square), [cv], [cv2])
        S.op("dve", rsum(nr[H], cv2[H, :, 0:2, :]), [cv2], [nr])
        S.op("act", actf(nr[H], nr[H], AF.Sqrt, bias=EPS), [nr], [nr])
        S.op("dve", recip(nr[H], nr[H]), [nr], [nr])
        S.op("dve", tt(qk[H], cv[H, :, 0:2, :], bc(nr[H], [PH, 4, 2, 64], 3), ALU.mult), [cv, nr], [qk])
        S.op("dve", ts(qk[H, :, 0, :], qk[H, :, 0, :], 0.125, ALU.mult), [qk], [qk])
        S.op("dve", ts(g1[H], AB[H, :, 0], dtb[H, 0:1], ALU.add), [AB, dtb], [g1])
        S.op("dve", stt(g2[H], g1[H], -1.0, g1[H], ALU.mult, ALU.max), [g1], [g2])
        S.op("act", actf(g2[H], g2[H], AF.Exp, scale=-1.0), [g2], [g2])
        S.op("act", actf(g2[H], g2[H], AF.Ln, bias=1.0), [g2], [g2])
        S.op("dve", stt(g1[H], g1[H], 0.0, g2[H], ALU.max, ALU.add), [g1, g2], [g1])
        S.op("act", actf(alog[H], alog[H], AF.Exp), [alog], [alog])
        S.op("dve", ts(gg_[H], g1[H], alog[H, 0:1], ALU.mult, -1.0, ALU.mult), [g1, alog], [gg_])
        S.op("act", actf(eg[H], gg_[H], AF.Exp), [gg_], [eg])
        S.op("act", actf(g2[H], AB[H, :, 1], AF.Exp, scale=-1.0), [AB], [g2])
        S.op("dve", ts(g2[H], g2[H], 1.0, ALU.add), [g2], [g2])
        S.op("dve", recip(bet[H], g2[H]), [g2], [bet])
        for t in range(4):
            q_t, k_t, v_t = qk[H, t, 0, :], qk[H, t, 1, :], cv[H, t, 2, :]
            S.op("dve", ts(St[H], St[H], eg[H, t:t + 1], ALU.mult), [St, eg], [St])
            S.op("dve", tt(tmp[H], St[H], bc(k_t, [PH, 64, 64], 2), ALU.mult), [St, qk], [tmp])
            S.op("dve", rsum(kS[H], tmp[H].rearrange("p k v -> p v k")), [tmp], [kS])
            S.op("dve", tt(dl[H], v_t, kS[H], ALU.subtract), [cv, kS], [dl])
            S.op("dve", ts(dl[H], dl[H], bet[H, t:t + 1], ALU.mult), [dl, bet], [dl])
            S.op("dve", tt(tmp[H], bc(k_t, [PH, 64, 64], 2), bc(dl[H], [PH, 64, 64], 1), ALU.mult), [qk, dl], [tmp])
            S.op("dve", tt(St[H], St[H], tmp[H], ALU.add), [St, tmp], [St])
            S.op("dve", tt(tmp[H], St[H], bc(q_t, [PH, 64, 64], 2), ALU.mult), [St, qk], [tmp])
            S.op("dve", rsum(o_all[H, t, :], tmp[H].rearrange("p k v -> p v k")), [tmp], [o_all])
        S.dma("sp", ssm_s, St[0:PH].rearrange("p k v -> p (k v)"), reads=[St], writes=[D_out])
        S.dma("sp", os_scr, o_all[0:PH].rearrange("p t d -> p (t d)"), reads=[o_all], writes=[D_os])
        for n in range(NSEQ):
            S.dma(dmaq.next(), o_tok[n * 4:(n + 1) * 4], os_scr[n * 8:(n + 1) * 8, :].rearrange("h (t d) -> t h d", t=4),
                  reads=[D_os], writes=[o_tok])
        Pp = slice(0, P)
        S.op("act", actf(o_sq[Pp], o_tok[Pp], AF.Square), [o_tok], [o_sq])
        S.op("dve", rsum(orr[Pp], o_sq[Pp]), [o_sq], [orr])
        S.op("act", actf(orr[Pp], orr[Pp], AF.Sqrt, scale=1.0 / 64, bias=EPS), [orr], [orr])
        S.op("dve", recip(orr[Pp], orr[Pp]), [orr], [orr])
        S.op("dve", tt(o_tok[Pp], o_tok[Pp], bc(orr[Pp], [P, 8, 64], 2), ALU.mult), [o_tok, orr], [o_tok])
        S.op("dve", tt(o_tok[Pp], o_tok[Pp], bc(W["go"][Pp, :], [P, 8, 64], 1), ALU.mult), [o_tok, W["go"]], [o_tok])
        S.op("act", actf(z_sb[Pp, :], z_sb[Pp, :], AF.Silu), [z_sb], [z_sb])
        S.op("dve", tt(gated[Pp, :], o_tok[Pp].rearrange("p h d -> p (h d)"), z_sb[Pp, :], ALU.mult), [o_tok, z_sb], [gated])
        pg = psb(0)
        for c in range(4):
            S.op("pe", trp(pg[:, c * P:(c + 1) * P], gated[0:P, c * 128:(c + 1) * 128], ident[0:P, 0:P]), [gated, ident], [PS[0]])
        S.op("act", acp(gTs.ap, pg[:, 0:4 * P].rearrange("p (c t) -> p c t", c=4)), [PS[0]], [gTs])
        for half in range(2):
            for c in range(4):
                S.op("pe", mm(PS[1 + half][0:P, :], gTs[:, c, :], W["o"][:, c, half * 512:(half + 1) * 512], c == 0, c == 3),
                     [gTs, W["o"]], [PS[1 + half]])
            S.op("dve", tt(yv[0:P, half * 512:(half + 1) * 512], xs1[0:P, half * 512:(half + 1) * 512],
                           PS[1 + half][0:P, :], ALU.add), [xs1, PS[1 + half]], [yv])
        S.dma("sp", y_s, yv[0:P, :], reads=[yv], writes=[D_out])
        S.barrier()
        ar.release()

    phases = cfg.phases or ["mla_p", "p3", "s1", "gdn_p", "s2"]
    if "mla_p" in phases:
        phase_mla_prompt()
    W_o = ar.alloc([4, 1024], BF16, "W_o")
    ar.mark()
    wtmp = ar.alloc([2064], F32, "wtmp_o")
    load_bf16_rows(W_o, [W_o[:, kc, :] for kc in range(4)],
                   [a_w_o[kc * 128:(kc + 1) * 128, :] for kc in range(4)], None, wtmp, 1024)
    S.barrier()
    ar.release()
    if "p3" in phases:
        phase_out_proj(x_p, Buf(), gt1, D_gt1, W_o, xp1 if "gdn_p" in phases else y_p,
                       D_xp1 if "gdn_p" in phases else D_out)
    if "s1" in phases:
        phase_mla_sample()
    S.barrier()
    ar.release()
    if "gdn_p" in phases or "s2" in phases:
        ar.mark()
        junk = ar.alloc([1024], BF16, "junk2")
        stat = Rot([ar.alloc([8], F32, f"statb{i}") for i in range(4)])
        WG = load_gdn_weights()
        if "gdn_p" in phases:
            phase_gdn_prompt(WG)
            phase_out_proj(xp1, D_xp1, gt2, D_gt2, WG["o"], y_p, D_out)
        if "s2" in phases:
            phase_gdn_sample(WG)

    S.barrier(engines=("sp",))
    S.op("sp", lambda e: e.nop(), [D_out, D_lat, D_xp1, D_gt1, D_gt2, D_qs, D_ab, D_os, D_h1, D_h2], [])
    S.emit()
    return nc


def host_consts(SEQ, NSEQ, past_len):
    i = np.arange(128)
    c = {}
    c["c_ident"] = np.eye(128, dtype=np.float32)
    c["c_tri"] = (i[:, None] <= i[None, :]).astype(np.float32)
    c["c_strict"] = (i[:, None] > i[None, :]).astype(np.float32)
    c["c_negm"] = np.where(i[:, None] >= i[None, :], 0.0, NEG).astype(np.float32)
    c["c_blk"] = ((i[:, None] // 64) == (i[None, :] // 64)).astype(np.float32)
    q = np.arange(512)
    am = np.zeros((128, 4, 512), np.float32)
    for j in range(4):
        am[:, j, :] = ((128 * j + i)[:, None] <= q[None, :])
    c["c_amask"] = am
    NS = NSEQ * 4
    sm = np.zeros((NS, NSEQ, 8, 4), np.float32)
    for n in range(NSEQ):
        for tk in range(4):
            for tq in range(4):
                if tk <= tq:
                    sm[n * 4 + tk, n, :, tq] = 1.0
    c["c_smask"] = sm.reshape(NS, NSEQ * 32)

    def rope_tab(pos):
        half = 16
        inv = (np.float32(10000.0) ** (-np.arange(half, dtype=np.float32) / np.float32(half))).astype(np.float32)
        ang = pos.astype(np.float32)[:, None] * inv[None, :]
        cos = np.cos(ang).astype(np.float32)
        sin = np.sin(ang).astype(np.float32)
        return np.concatenate([cos, cos, -sin, sin], axis=1).astype(np.float32)

    c["c_rope_p"] = rope_tab(np.arange(SEQ))
    c["c_rope_s"] = rope_tab(np.tile(past_len + np.arange(4), NSEQ))
    return c


def core_inputs(inp, core, n_cores, SEQ, NSEQ, consts):
    b = core % inp["x_prompt"].shape[0]
    f = np.ascontiguousarray
    m = dict(consts)
    m["x_p"] = f(inp["x_prompt"][b])
    m["x_s"] = f(inp["x_sample"][core * NSEQ:(core + 1) * NSEQ].reshape(NSEQ * 4, 1024))
    m["cache_lat"] = inp["cache_latent"][0].reshape(-1, 128 * 256)
    m["cache_kr"] = inp["cache_krope"][0].reshape(-1, 128 * 32)
    m["ptab"] = f(inp["page_table"][core * NSEQ:(core + 1) * NSEQ].T.astype(np.int32))
    m["st_conv"] = f(inp["state_conv"][0, core * NSEQ:(core + 1) * NSEQ])
    m["st_ssm"] = f(inp["state_ssm"][0, core * NSEQ:(core + 1) * NSEQ].reshape(NSEQ * 8, 4096))
    m["a_norm"] = f(inp["a_norm"][0].reshape(8, 128).T)
    m["a_w_in"] = inp["a_w_in"][0]
    m["a_g_qa"] = f(inp["a_g_qa"][0].reshape(3, 128).T)
    m["a_w_uq"] = inp["a_w_uq"][0]
    m["a_g_kv"] = inp["a_g_kv"][0].reshape(1, 256)
    m["a_w_uk"] = inp["a_w_uk"][0].reshape(256, 512)
    m["a_w_uv"] = inp["a_w_uv"][0].reshape(256, 512)
    m["a_g_q"] = inp["a_g_q"][0].reshape(1, 96)
    m["a_g_k"] = inp["a_g_k"][0].reshape(1, 96)
    m["a_w_o"] = inp["a_w_o"][0]
    m["b_norm"] = f(inp["b_norm"][0].reshape(8, 128).T)
    m["b_w_in"] = inp["b_w_in"][0]
    m["b_w_conv"] = inp["b_w_conv"][0]
    m["b_w_convT"] = f(inp["b_w_conv"][0].reshape(4, 12, 128).transpose(2, 1, 0).reshape(128, 48))
    sc_ = inp["state_conv"][0, core * NSEQ:(core + 1) * NSEQ]
    m["st_convL"] = f(sc_.reshape(NSEQ, 3, 3, 8, 64).transpose(0, 3, 1, 2, 4).reshape(NSEQ * 8, 576))
    wc_ = inp["b_w_conv"][0].reshape(4, 3, 8, 64).transpose(2, 0, 1, 3).reshape(8, 768)
    m["b_w_convL"] = f(np.tile(wc_, (NSEQ, 1)))
    m["b_alogL"] = f(np.tile(inp["b_a_log"][0].reshape(8, 1), (NSEQ, 1)))
    m["b_dtbL"] = f(np.tile(inp["b_dt_bias"][0].reshape(8, 1), (NSEQ, 1)))
    m["b_a_log"] = inp["b_a_log"][0].reshape(1, 8)
    m["b_dt_bias"] = inp["b_dt_bias"][0].reshape(1, 8)
    m["b_g_o"] = inp["b_g_o"][0].reshape(1, 64)
    m["b_w_o"] = inp["b_w_o"][0]
    return m


def run(inp, n_cores, cfg):
    inp = {k: np.asarray(v) for k, v in inp.items()}
    past_len = inp["page_table"].shape[1] * 128
    consts = host_consts(cfg.SEQ, cfg.NSEQ, past_len)
    nc = build(cfg)
    in_maps = [core_inputs(inp, c, n_cores, cfg.SEQ, cfg.NSEQ, consts) for c in range(n_cores)]
    res = run_bass_kernel_spmd(nc, in_maps, core_ids=list(range(n_cores)))
    return res.results


def fix_out(name, a, NSEQ):
    if name == "conv_s":
        return np.ascontiguousarray(a.reshape(NSEQ, 8, 3, 3, 64).transpose(0, 2, 3, 1, 4)).reshape(NSEQ, 3, 1536)
    return a


def kernel(**inputs):
    cfg = Cfg()
    r = run(inputs, 8, cfg)
    B = 4
    y_p = np.stack([r[b]["y_p"] for b in range(B)])
    y_s = np.concatenate([r[c]["y_s"].reshape(16, 4, 1024) for c in range(8)])
    lat_p = np.stack([r[b]["lat_p"] for b in range(B)])[None]
    kr_p = np.stack([r[b]["kr_p"] for b in range(B)])[None]
    lat_s = np.concatenate([r[c]["lat_s"].reshape(16, 4, 256) for c in range(8)])[None]
    kr_s = np.concatenate([r[c]["kr_s"].reshape(16, 4, 32) for c in range(8)])[None]
    conv_p = np.stack([r[b]["conv_p"] for b in range(B)])[None]
    ssm_p = np.stack([r[b]["ssm_p"] for b in range(B)])[None]
    conv_s = np.concatenate([fix_out("conv_s", r[c]["conv_s"], 16) for c in range(8)])[None]
    ssm_s = np.concatenate([r[c]["ssm_s"].reshape(16, 8, 64, 64) for c in range(8)])[None]
    return (y_p, y_s, lat_p, kr_p, lat_s, kr_s, conv_p, ssm_p, conv_s, ssm_s)
```

```python
import numpy as np
import concourse.bass as bass
import concourse.mybir as mybir
from concourse.bass_utils import run_bass_kernel_spmd

F32 = mybir.dt.float32
BF16 = mybir.dt.bfloat16
I32 = mybir.dt.int32
U8 = mybir.dt.uint8
ALU = mybir.AluOpType
AF = mybir.ActivationFunctionType
AX = mybir.AxisListType

ENGS = ("pe", "act", "dve", "pool", "sp")
NDSEM = 16
EPS = 1e-6
NEG = -30000.0


class Buf:
    __slots__ = ("name", "w", "r", "excl")

    def __init__(self, name="", excl=False):
        self.name = name
        self.w = None
        self.r = []
        self.excl = excl


class T:
    def __init__(self, ap, name=""):
        self.ap = ap
        self.b = Buf(name)

    def __getitem__(self, k):
        return self.ap[k]


class Op:
    __slots__ = ("eng", "fn", "waits", "idx", "dma", "dsem", "dval", "signal")


def _b(x):
    return x.b if isinstance(x, T) else x


class Sched:
    def __init__(self, nc, self_sync=("act", "dve", "pool")):
        self.nc = nc
        self.ops = {e: [] for e in ENGS}
        self.seen = {e: {} for e in ENGS}
        self.ndma = {e: 0 for e in ENGS}
        self.self_sync = set(self_sync)
        self.last_dma = {}

    def _waits(self, eng, deps):
        waits = []
        seen = self.seen[eng]
        for d in deps:
            if d.dma:
                key = ("d", d.eng, d.dsem)
                val = d.dval
            else:
                if d.fn is None:
                    continue
                if d.eng == eng and eng not in self.self_sync:
                    continue
                key = ("e", d.eng)
                val = d.idx + 1
            if seen.get(key, 0) >= val:
                continue
            seen[key] = val
            waits.append(d)
            d.signal = True
        return waits

    def _mk(self, eng, fn, reads, writes, dma=False):
        if getattr(self, "cap", None) is not None:
            self.cap.append((eng, fn, list(reads), list(writes), dma))
            return None
        reads = [_b(x) for x in reads]
        writes = [_b(x) for x in writes]
        if eng != "pe":
            ex = [b for b in reads if b.excl and b not in writes]
            if ex:
                writes = writes + ex
                reads = [b for b in reads if not b.excl]
        op = Op()
        op.eng = eng
        op.fn = fn
        op.dma = dma
        op.signal = False
        op.idx = len(self.ops[eng])
        deps = []
        for b in reads:
            if b.w is not None:
                deps.append(b.w)
        for b in writes:
            if b.w is not None:
                deps.append(b.w)
            deps.extend(b.r)
        if dma:
            n = self.ndma[eng]
            self.ndma[eng] = n + 1
            op.dsem = n % NDSEM
            op.dval = 16 * (n // NDSEM + 1)
            op.signal = True
            prev = self.last_dma.get((eng, op.dsem))
            if prev is not None:
                deps.append(prev)
            self.last_dma[(eng, op.dsem)] = op
        op.waits = self._waits(eng, deps)
        self.ops[eng].append(op)
        for b in reads:
            b.r.append(op)
        for b in writes:
            b.w = op
            b.r = []
        return op

    def op(self, eng, fn, reads=(), writes=()):
        return self._mk(eng, fn, reads, writes)

    def begin_capture(self):
        self.cap = []

    def end_capture(self):
        c = self.cap
        self.cap = None
        return c

    def replay(self, streams):
        streams = [st for st in streams if st]
        pos = [0] * len(streams)
        total = sum(len(st) for st in streams)
        for _ in range(total):
            k = min(range(len(streams)), key=lambda i: (pos[i] / len(streams[i])) if pos[i] < len(streams[i]) else 2.0)
            a = streams[k][pos[k]]
            pos[k] += 1
            self._mk(*a)

    def dma(self, eng, out, in_, reads=(), writes=(), **kw):
        return self._mk(eng, lambda e: e.dma_start(out=out, in_=in_, **kw), reads, writes, dma=True)

    def dma_fn(self, eng, fn, reads=(), writes=()):
        return self._mk(eng, fn, reads, writes, dma=True)

    def barrier(self, engines=ENGS):
        last = []
        for e in ENGS:
            for o in reversed(self.ops[e]):
                if not o.dma and o.fn is not None:
                    last.append(o)
                    break
        deps = last + list(self.last_dma.values())
        for e in engines:
            op = Op()
            op.eng = e
            op.fn = None
            op.dma = False
            op.signal = False
            op.idx = len(self.ops[e])
            op.waits = self._waits(e, [d for d in deps if d.dma or d.eng != e])
            self.ops[e].append(op)

    def emit(self):
        nc = self.nc
        esem = {e: nc.alloc_semaphore(name=f"es_{e}") for e in ENGS}
        dsem = {e: [nc.alloc_semaphore(name=f"ds_{e}{i}") for i in range(NDSEM)]
                for e in ENGS if self.ndma[e] > 0}
        signum = {}
        for e in ENGS:
            n = 0
            for o in self.ops[e]:
                if o.dma or o.fn is None:
                    continue
                if o.signal:
                    n += 1
                    signum[id(o)] = n

        def run(e, eng):
            for o in self.ops[e]:
                for d in o.waits:
                    if d.dma:
                        eng.wait_ge(dsem[d.eng][d.dsem], d.dval)
                    else:
                        eng.wait_ge(esem[d.eng], signum[id(d)])
                if o.fn is None:
                    continue
                ins = o.fn(eng)
                if o.dma:
                    ins.then_inc(dsem[e][o.dsem], 16)
                elif o.signal:
                    ins.then_inc(esem[e], 1)

        with nc.Block() as block:
            @block.tensor
            def _(eng):
                run("pe", eng)

            @block.scalar
            def _(eng):
                run("act", eng)

            @block.vector
            def _(eng):
                run("dve", eng)

            @block.gpsimd
            def _(eng):
                run("pool", eng)

            @block.sync
            def _(eng):
                run("sp", eng)


class Arena:
    def __init__(self, base, nbytes):
        self.base = base
        self.nbytes = nbytes
        self.off = 0
        self.marks = []
        self.n = 0

    def alloc(self, shape_free, dtype, name=None):
        esz = {F32: 4, BF16: 2, I32: 4}[dtype]
        n = int(np.prod(shape_free))
        nb = (n * esz + 63) // 64 * 64
        assert self.off + nb <= self.nbytes, ("SBUF arena overflow", name, self.off, nb, self.nbytes)
        ap = self.base[:, self.off:self.off + n * esz].bitcast(dtype)
        self.off += nb
        if len(shape_free) > 1:
            names = " ".join(f"a{i}" for i in range(len(shape_free)))
            kw = {f"a{i}": int(s) for i, s in enumerate(shape_free)}
            ap = ap.rearrange(f"p ({names}) -> p {names}", **kw)
        self.n += 1
        return T(ap, name or f"t{self.n}")

    def mark(self):
        self.marks.append(self.off)

    def release(self):
        self.off = self.marks.pop()


class Rot:
    def __init__(self, items):
        self.items = items
        self.i = 0

    def next(self):
        t = self.items[self.i % len(self.items)]
        self.i += 1
        return t


def mm(out, lhsT, rhs, start=True, stop=True):
    return lambda e: e.matmul(out, lhsT=lhsT, rhs=rhs, start=start, stop=stop)


def trp(out, in_, ident):
    return lambda e: e.transpose(out=out, in_=in_, identity=ident)


def actf(out, in_, func, **kw):
    return lambda e: e.activation(out=out, in_=in_, func=func, **kw)


def tt(out, a, b, op):
    return lambda e: e.tensor_tensor(out=out, in0=a, in1=b, op=op)


def ts(out, a, s1, op0, s2=None, op1=None):
    if op1 is None:
        return lambda e: e.tensor_scalar(out=out, in0=a, scalar1=s1, scalar2=None, op0=op0)
    return lambda e: e.tensor_scalar(out=out, in0=a, scalar1=s1, scalar2=s2, op0=op0, op1=op1)


def stt(out, a, s, b, op0, op1):
    return lambda e: e.scalar_tensor_tensor(out=out, in0=a, scalar=s, in1=b, op0=op0, op1=op1)


def cp(out, in_):
    return lambda e: e.tensor_copy(out=out, in_=in_)


def acp(out, in_):
    return lambda e: e.copy(out=out, in_=in_)


def rsum(out, in_):
    return lambda e: e.reduce_sum(out=out, in_=in_, axis=AX.X)


def recip(out, in_):
    return lambda e: e.reciprocal(out=out, in_=in_)


def mset(ap, v):
    return lambda e: e.memset(ap, v)


def bc(ap, shape, axis):
    return ap.unsqueeze(axis).to_broadcast(list(shape))


class Cfg:
    def __init__(self, SEQ=8192, NSEQ=16, NPOOL=20480, debug=False, phases=None, cut=99):
        self.cut = cut
        self.SEQ = SEQ
        self.NSEQ = NSEQ
        self.NPOOL = NPOOL
        self.debug = debug
        self.phases = phases


def build(cfg):
    SEQ, NSEQ, NPOOL = cfg.SEQ, cfg.NSEQ, cfg.NPOOL
    NT = SEQ // 128
    NQ = SEQ // 512
    NS = NSEQ * 4
    nc = bass.Bass("TRN2", target_bir_lowering=False)
    S = Sched(nc)

    def din(name, shape, dt=F32):
        return nc.dram_tensor(name, list(shape), dt, kind="ExternalInput").ap()

    def dout(name, shape, dt=F32):
        return nc.dram_tensor(name, list(shape), dt, kind="ExternalOutput").ap()

    def dscr(name, shape, dt=F32):
        kind = "ExternalOutput" if cfg.debug else "Internal"
        return nc.dram_tensor(name, list(shape), dt, kind=kind).ap()

    x_p = din("x_p", [SEQ, 1024])
    x_s = din("x_s", [NS, 1024])
    cache_lat = din("cache_lat", [NPOOL, 128 * 256])
    cache_kr = din("cache_kr", [NPOOL, 128 * 32])
    ptab = din("ptab", [128, NSEQ], I32)
    st_conv = din("st_conv", [NSEQ, 3, 1536])
    st_ssm = din("st_ssm", [NSEQ * 8, 4096])
    a_norm = din("a_norm", [128, 8])
    a_w_in = din("a_w_in", [1024, 1184])
    a_g_qa = din("a_g_qa", [128, 3])
    a_w_uq = din("a_w_uq", [384, 768])
    a_g_kv = din("a_g_kv", [1, 256])
    a_w_uk = din("a_w_uk", [256, 512])
    a_w_uv = din("a_w_uv", [256, 512])
    a_g_q = din("a_g_q", [1, 96])
    a_g_k = din("a_g_k", [1, 96])
    a_w_o = din("a_w_o", [512, 1024])
    b_norm = din("b_norm", [128, 8])
    b_w_in = din("b_w_in", [1024, 2064])
    b_w_conv = din("b_w_conv", [4, 1536])
    b_w_convT = din("b_w_convT", [128, 48])
    PHn = NSEQ * 8
    b_w_convL = din("b_w_convL", [PHn, 4 * 192])
    st_convL = din("st_convL", [PHn, 3 * 192])
    b_alogL = din("b_alogL", [PHn, 1])
    b_dtbL = din("b_dtbL", [PHn, 1])
    b_a_log = din("b_a_log", [1, 8])
    b_dt_bias = din("b_dt_bias", [1, 8])
    b_g_o = din("b_g_o", [1, 64])
    b_w_o = din("b_w_o", [512, 1024])
    c_ident = din("c_ident", [128, 128])
    c_tri = din("c_tri", [128, 128])
    c_strict = din("c_strict", [128, 128])
    c_negm = din("c_negm", [128, 128])
    c_blk = din("c_blk", [128, 128])
    c_amask = din("c_amask", [128, 4, 512])
    c_smask = din("c_smask", [NS, NSEQ * 32])
    c_rope_p = din("c_rope_p", [SEQ, 64])
    c_rope_s = din("c_rope_s", [NS, 64])

    y_p = dout("y_p", [SEQ, 1024])
    y_s = dout("y_s", [NS, 1024])
    lat_p = dout("lat_p", [SEQ, 256])
    kr_p = dout("kr_p", [SEQ, 32])
    lat_s = dout("lat_s", [NS, 256])
    kr_s = dout("kr_s", [NS, 32])
    conv_p = dout("conv_p", [3, 1536])
    ssm_p = dout("ssm_p", [8, 64, 64])
    conv_s = dout("conv_s", [NSEQ * 8, 3 * 192])
    ssm_s = dout("ssm_s", [NSEQ * 8, 4096])

    gt1 = dscr("gt1", [2, 128, 2, SEQ], BF16)
    xp1 = dscr("xp1", [SEQ, 1024])
    gt2 = dscr("gt2", [2, 128, 2, SEQ], BF16)
    hT1_scr = dscr("hT1_scr", [128, 8, SEQ], BF16)
    hT2_scr = dscr("hT2_scr", [128, 8, SEQ], BF16)
    D_h1, D_h2 = Buf("hT1"), Buf("hT2")
    qs_scr = dscr("qs_scr", [NS, 1536])
    ab_scr = dscr("ab_scr", [NS, 16])
    os_scr = dscr("os_scr", [NSEQ * 8, 256])
    D_gt1, D_xp1, D_gt2 = Buf("gt1"), Buf("xp1"), Buf("gt2")
    D_qs, D_ab, D_os = Buf("qs"), Buf("ab"), Buf("os")
    D_out = Buf("outs")
    D_lat = Buf("lat_out")

    ARENA_BYTES = 204 * 1024
    sb = nc.alloc_sbuf_tensor("arena", [128, ARENA_BYTES], U8)
    ar = Arena(sb.ap(), ARENA_BYTES)
    PSD = [nc.alloc_psum_tensor(f"psd{i}", [128, 1024], F32).ap() for i in range(4)]
    PS = [T(PSD[i // 2][:, (i % 2) * 512:(i % 2 + 1) * 512], f"ps{i}") for i in range(8)]
    for p_ in PS:
        p_.b.excl = True

    def psb(i):
        return PS[i].ap.bitcast(BF16)

    dmaq = Rot(["sp", "pool"])

    ident_f = ar.alloc([128], F32, "ident_f")
    ident = ar.alloc([128], BF16, "ident")
    S.dma("sp", ident_f.ap, c_ident, writes=[ident_f])
    S.op("dve", cp(ident.ap, ident_f.ap), [ident_f], [ident])

    xs1 = ar.alloc([1024], F32, "xs1")
    ptab_sb = ar.alloc([NSEQ], I32, "ptab_sb")
    S.dma("sp", ptab_sb.ap, ptab, writes=[ptab_sb])

    def load_bf16_rows(dst, dst_slices, src_rows, gain, tmp, ncols):
        for kc, rows in enumerate(src_rows):
            S.dma(dmaq.next(), tmp[:, 0:ncols], rows, writes=[tmp])
            if gain is None:
                S.op("dve", cp(dst_slices[kc], tmp[:, 0:ncols]), [tmp], [dst])
            else:
                S.op("dve", ts(dst_slices[kc], tmp[:, 0:ncols], gain[0][:, kc:kc + 1], ALU.mult),
                     [tmp, gain[1]], [dst])

    def rms_rstd(x_ap, xT, n, rstd, junk, parts=128, eng_sq="act"):
        S.op("act", actf(junk[0:parts, 0:n], x_ap, AF.Square, accum_out=rstd[0:parts, 1:2]),
             [xT], [rstd])
        S.op("act", actf(rstd[0:parts, 2:3], rstd[0:parts, 1:2], AF.Sqrt, scale=1.0 / n, bias=EPS),
             [rstd], [rstd])
        S.op("dve", recip(rstd[0:parts, 0:1], rstd[0:parts, 2:3]), [rstd], [rstd])

    def transpose_to(dstT, dst_ap_fn, src, src_ap_fn, nchunk, bank, parts=128, width=128, evac="act"):
        pb = psb(bank)
        for c in range(nchunk):
            S.op("pe", trp(pb[0:width, c * parts:(c + 1) * parts], src_ap_fn(c), ident[0:parts, 0:parts]),
                 [src, ident], [PS[bank]])
        fn = acp if evac == "act" else cp
        S.op(evac, fn(dst_ap_fn(), pb[0:width, 0:nchunk * parts]), [PS[bank]], [dstT])

    def rope(x_view, xT, cs, nh, parts, sw, t1):
        swv = sw[0:parts, 0:nh, :]
        t1v = t1[0:parts, 0:nh, :]
        S.op("dve", cp(swv[:, :, 0:16], x_view[:, :, 16:32]), [xT], [sw])
        S.op("dve", cp(swv[:, :, 16:32], x_view[:, :, 0:16]), [xT], [sw])
        S.op("dve", tt(t1v, swv, bc(cs[0:parts, 32:64], [parts, nh, 32], 1), ALU.mult), [sw, cs], [t1])
        S.op("dve", tt(x_view, x_view, bc(cs[0:parts, 0:32], [parts, nh, 32], 1), ALU.mult), [xT, cs], [xT])
        S.op("dve", tt(x_view, x_view, t1v, ALU.add), [xT, t1], [xT])

    ar.mark()
    an_sb = ar.alloc([8], F32, "an_sb")
    gqa_sb = ar.alloc([3], F32, "gqa_sb")
    S.dma("sp", an_sb.ap, a_norm, writes=[an_sb])
    S.dma("sp", gqa_sb.ap, a_g_qa, writes=[gqa_sb])
    W_in = ar.alloc([8, 1184], BF16, "W_in")
    W_uq = ar.alloc([3, 768], BF16, "W_uq")
    W_uk = ar.alloc([2, 512], BF16, "W_uk")
    W_uv = ar.alloc([2, 512], BF16, "W_uv")
    amask = ar.alloc([4, 512], BF16, "amask")
    gkv_b = ar.alloc([256], F32, "gkv_b")
    gg_b = ar.alloc([96], F32, "gg_b")
    gk_t = ar.alloc([96], F32, "gk_t")
    ar.mark()
    wtmp = ar.alloc([2064], F32, "wtmp")
    load_bf16_rows(W_in, [W_in[:, kc, :] for kc in range(8)],
                   [a_w_in[kc * 128:(kc + 1) * 128, :] for kc in range(8)], (an_sb, an_sb), wtmp, 1184)
    load_bf16_rows(W_uq, [W_uq[:, kc, :] for kc in range(3)],
                   [a_w_uq[kc * 128:(kc + 1) * 128, :] for kc in range(3)], (gqa_sb, gqa_sb), wtmp, 768)
    load_bf16_rows(W_uk, [W_uk[:, kc, :] for kc in range(2)],
                   [a_w_uk[kc * 128:(kc + 1) * 128, :] for kc in range(2)], None, wtmp, 512)
    load_bf16_rows(W_uv, [W_uv[:, kc, :] for kc in range(2)],
                   [a_w_uv[kc * 128:(kc + 1) * 128, :] for kc in range(2)], None, wtmp, 512)
    S.dma("sp", wtmp[:, 0:2048], c_amask.rearrange("p a q -> p (a q)"), writes=[wtmp])
    S.op("dve", cp(amask.ap.rearrange("p a q -> p (a q)"), wtmp[:, 0:2048]), [wtmp], [amask])
    S.barrier()
    ar.release()
    S.dma("sp", gkv_b.ap, a_g_kv.to_broadcast([128, 256]), writes=[gkv_b])
    S.dma("sp", gg_b.ap, a_g_q.to_broadcast([128, 96]), writes=[gg_b])
    S.dma("sp", gk_t.ap, a_g_k.to_broadcast([128, 96]), writes=[gk_t])
    S.op("dve", stt(gg_b.ap, gg_b.ap, float(96 ** -0.5), gk_t.ap, ALU.mult, ALU.mult), [gg_b, gk_t], [gg_b])

    junk = ar.alloc([1024], BF16, "junk")
    stat = Rot([ar.alloc([8], F32, f"stat{i}") for i in range(4)])

    def mla_kv_from_cn(cn_ap, cnT, kpe_ap, kpeT, parts, hsel, KTs, kt_cols, V_store_fn, banks, tmp, kcol0=0):
        col0, nh = hsel
        cnb, cT, sq, ssq, Kt = tmp
        bT, bKV, bK = banks
        S.op("act", acp(cnb[0:parts, :], cn_ap), [cnT], [cnb])
        pb = psb(bT)
        for c in range(2):
            S.op("pe", trp(pb[:, c * parts:(c + 1) * parts], cnb[0:parts, c * 128:(c + 1) * 128],
                           ident[0:parts, 0:parts]), [cnb, ident], [PS[bT]])
        S.op("act", acp(cT[:, :, 0:parts], pb[:, 0:2 * parts].rearrange("p (c t) -> p c t", c=2)),
             [PS[bT]], [cT])
        w = nh * 64
        for kc in range(2):
            S.op("pe", mm(PS[bKV][0:parts, 0:w], cT[:, kc, 0:parts], W_uk[:, kc, col0:col0 + w],
                          kc == 0, kc == 1), [cT, W_uk], [PS[bKV]])
        if V_store_fn is not None:
            for kc in range(2):
                S.op("pe", mm(PS[bKV][0:parts, 256:256 + w], cT[:, kc, 0:parts], W_uv[:, kc, col0:col0 + w],
                              kc == 0, kc == 1), [cT, W_uv], [PS[bKV]])
            V_store_fn(PS[bKV])
        S.op("act", actf(sq[0:parts, 0:w], PS[bKV][0:parts, 0:w], AF.Square), [PS[bKV]], [sq])
        S.op("dve", rsum(ssq[0:parts, 0:nh], sq[0:parts, 0:w].rearrange("p (h d) -> p h d", h=nh)), [sq], [ssq])
        S.op("act", actf(junk[0:parts, 0:32], kpe_ap, AF.Square, accum_out=ssq[0:parts, 8:9]), [kpeT], [junk, ssq])
        S.op("dve", ts(ssq[0:parts, 0:nh], ssq[0:parts, 0:nh], ssq[0:parts, 8:9], ALU.add), [ssq], [ssq])
        S.op("act", actf(ssq[0:parts, 0:nh], ssq[0:parts, 0:nh], AF.Sqrt, scale=1.0 / 96, bias=EPS), [ssq], [ssq])
        S.op("dve", recip(ssq[0:parts, 0:nh], ssq[0:parts, 0:nh]), [ssq], [ssq])
        S.op("dve", tt(Kt[0:parts, 0:nh, 0:64], PS[bKV][0:parts, 0:w].rearrange("p (h d) -> p h d", h=nh),
                       bc(ssq[0:parts, 0:nh], [parts, nh, 64], 2), ALU.mult), [PS[bKV], ssq], [Kt])
        S.op("dve", tt(Kt[0:parts, 0:nh, 64:96], bc(kpe_ap, [parts, nh, 32], 1),
                       bc(ssq[0:parts, 0:nh], [parts, nh, 32], 2), ALU.mult), [kpeT, ssq], [Kt])
        pk = psb(bK)
        for h in range(nh):
            S.op("pe", trp(pk[0:96, kcol0 + h * parts:kcol0 + (h + 1) * parts], Kt[0:parts, h, :], ident[0:parts, 0:parts]),
                 [Kt, ident], [PS[bK]])
        return pk

    def ckp_from_proj(ps_c, ps_k, psT, parts, cs, cn, kpe, st, sw, t1):
        rms_rstd(ps_c, psT, 256, st, junk, parts)
        S.op("dve", stt(cn[0:parts, :], ps_c, st[0:parts, 0:1], gkv_b[0:parts, :], ALU.mult, ALU.mult),
             [psT, st, gkv_b], [cn])
        S.op("act", acp(kpe[0:parts, :], ps_k), [psT], [kpe])
        rope(kpe[0:parts, :].rearrange("p (h d) -> p h d", h=1), kpe, cs, 1, parts, sw, t1)

    def phase_mla_prompt():
        ar.mark()
        KT = ar.alloc([4, SEQ], BF16, "KT")
        V1 = ar.alloc([NT, 2, 192], BF16, "V1")
        S.op("pool", mset(V1.ap, 1.0), [], [V1])
        xt = [ar.alloc([1024], F32, f"xt{i}") for i in range(2)]
        hb = [ar.alloc([1024], BF16, f"hb{i}") for i in range(2)]
        hT4 = ar.alloc([8, 512], BF16, "hT4")
        hTv = [T(hT4[:, :, i * 128:(i + 1) * 128], f"hTv{i}") for i in range(4)]
        cs_t = [ar.alloc([64], F32, f"cs{i}") for i in range(2)]
        sw = [ar.alloc([4, 32], F32, f"sw{i}") for i in range(2)]
        t1 = [ar.alloc([4, 32], F32, f"t1{i}") for i in range(2)]
        sq = [ar.alloc([384], F32, f"sq{i}") for i in range(2)]
        ssq = [ar.alloc([16], F32, f"ssq{i}") for i in range(2)]
        junk2 = [junk, junk]

        def load_h(tile, par, dstT, bank):
            x, h_ = xt[par], hb[par]
            S.dma("sp", x.ap, x_p[tile * 128:(tile + 1) * 128, :], writes=[x])
            st = stat.next()
            rms_rstd(x.ap, x, 1024, st, junk2[par])
            S.op("dve", ts(h_.ap, x.ap, st[:, 0:1], ALU.mult), [x, st], [h_])
            pb = psb(bank)
            for c in range(8):
                S.op("pe", trp(pb[:, c * 128:(c + 1) * 128], h_[:, c * 128:(c + 1) * 128], ident.ap),
                     [h_, ident], [PS[bank]])
            S.op("act", acp(dstT.ap, pb[:, 0:1024].rearrange("p (c t) -> p c t", c=8)), [PS[bank]], [dstT])

        for g in range(2):
            ar.mark()
            cn = [ar.alloc([256], F32, f"cn{i}") for i in range(2)]
            kpe = [ar.alloc([32], F32, f"kpe{i}") for i in range(2)]
            cnb = [ar.alloc([256], BF16, f"cnb{i}") for i in range(2)]
            cT = [ar.alloc([2, 128], BF16, f"cT{i}") for i in range(2)]
            Kt = [ar.alloc([4, 96], BF16, f"Kt{i}") for i in range(2)]
            streams = []
            for t in range(NT):
                par = t % 2
                B0 = 4 * par
                S.begin_capture()
                cn_t, kpe_t = cn[par], kpe[par]
                if g == 0:
                    load_h(t, par, hTv[par], B0)
                    S.dma("pool", hT1_scr[:, :, t * 128:(t + 1) * 128], hTv[par].ap, reads=[hTv[par]], writes=[D_h1])
                    for kc in range(8):
                        S.op("pe", mm(PS[B0 + 1][:, 0:288], hTv[par][:, kc, :], W_in[:, kc, 384:672], kc == 0, kc == 7),
                             [hTv[par], W_in], [PS[B0 + 1]])
                    cs = cs_t[par]
                    S.dma("pool", cs.ap, c_rope_p[t * 128:(t + 1) * 128, :], writes=[cs])
                    st = stat.next()
                    ckp_from_proj(PS[B0 + 1][:, 0:256], PS[B0 + 1][:, 256:288], PS[B0 + 1], 128, cs, cn_t, kpe_t, st,
                                  sw[par], t1[par])
                    S.dma("pool", lat_p[t * 128:(t + 1) * 128, :], cn_t.ap, reads=[cn_t], writes=[D_lat])
                    S.dma("pool", kr_p[t * 128:(t + 1) * 128, :], kpe_t.ap, reads=[kpe_t], writes=[D_lat])
                else:
                    S.dma("sp", cn_t.ap, lat_p[t * 128:(t + 1) * 128, :], reads=[D_lat], writes=[cn_t])
                    S.dma("pool", kpe_t.ap, kr_p[t * 128:(t + 1) * 128, :], reads=[D_lat], writes=[kpe_t])

                def vstore(psT, t=t):
                    pv = psT[:, 256:512].rearrange("p (a b d) -> p a b d", a=2, b=2)
                    S.op("act", acp(V1[:, t, :, 0:64], pv[:, :, 0, :]), [psT], [V1])
                    S.op("act", acp(V1[:, t, :, 128:192], pv[:, :, 1, :]), [psT], [V1])

                pk = mla_kv_from_cn(cn_t.ap, cn_t, kpe_t.ap, kpe_t, 128, (g * 256, 4), KT, None, vstore,
                                    (B0 + 2, B0 + 3, B0 + 2), (cnb[par], cT[par], sq[par], ssq[par], Kt[par]),
                                    kcol0=256)
                S.op("act", acp(KT[0:96, :, t * 128:(t + 1) * 128],
                                pk[0:96, 256:768].rearrange("p (h t) -> p h t", h=4)), [PS[B0 + 2]], [KT])
                streams.append(S.end_capture())
                if len(streams) == 2:
                    S.replay(streams)
                    streams = []
            S.replay(streams)
            S.barrier()
            ar.release()
            ar.mark()
            qan = [ar.alloc([384], BF16, f"qan{i}") for i in range(2)]
            qaT = [ar.alloc([3, 128], BF16, f"qaT{i}") for i in range(2)]
            qs = [ar.alloc([4, 96], F32, f"qs{i}") for i in range(2)]
            qf = [ar.alloc([4, 96], BF16, f"qf{i}") for i in range(2)]
            QT = ar.alloc([4, 512], BF16, "QT")
            QTv = [T(QT[0:96, :, i * 128:(i + 1) * 128], f"QTv{i}") for i in range(4)]
            sz = ar.alloc([2, 512], F32, "sz")
            pT = Rot([ar.alloc([512], BF16, f"pT{i}") for i in range(3)])
            rcp = ar.alloc([512], F32, "rcp")
            tmpo = ar.alloc([512], F32, "tmpo")
            GTt = Rot([ar.alloc([2, 512], BF16, f"GTt{i}") for i in range(1)])
            for qi in range(NQ):
                streams = []
                for s_ in range(4):
                    tile = qi * 4 + s_
                    par = s_ % 2
                    B0 = 4 * par
                    S.begin_capture()
                    S.dma("sp" if s_ % 2 == 0 else "pool", hTv[s_].ap, hT1_scr[:, :, tile * 128:(tile + 1) * 128],
                          reads=[D_h1], writes=[hTv[s_]])
                    for kc in range(8):
                        S.op("pe", mm(PS[B0 + 1][:, 0:384], hTv[s_][:, kc, :], W_in[:, kc, 0:384], kc == 0, kc == 7),
                             [hTv[s_], W_in], [PS[B0 + 1]])
                    st = stat.next()
                    rms_rstd(PS[B0 + 1][:, 0:384], PS[B0 + 1], 384, st, junk2[par])
                    S.op("dve", ts(qan[par].ap, PS[B0 + 1][:, 0:384], st[:, 0:1], ALU.mult), [PS[B0 + 1], st], [qan[par]])
                    pa = psb(B0 + 2)
                    for c in range(3):
                        S.op("pe", trp(pa[:, c * 128:(c + 1) * 128], qan[par][:, c * 128:(c + 1) * 128], ident.ap),
                             [qan[par], ident], [PS[B0 + 2]])
                    S.op("dve", cp(qaT[par].ap, pa[:, 0:384].rearrange("p (c t) -> p c t", c=3)), [PS[B0 + 2]], [qaT[par]])
                    for kc in range(3):
                        S.op("pe", mm(PS[B0 + 3][:, 0:384], qaT[par][:, kc, :], W_uq[:, kc, g * 384:(g + 1) * 384],
                                      kc == 0, kc == 2), [qaT[par], W_uq], [PS[B0 + 3]])
                    q_, sq_, ssq_ = qs[par], sq[par], ssq[par]
                    S.op("act", acp(q_.ap.rearrange("p h d -> p (h d)"), PS[B0 + 3][:, 0:384]), [PS[B0 + 3]], [q_])
                    S.op("act", actf(sq_[:, 0:384], q_.ap.rearrange("p h d -> p (h d)"), AF.Square), [q_], [sq_])
                    S.op("dve", rsum(ssq_[:, 0:4], sq_[:, 0:384].rearrange("p (h d) -> p h d", h=4)), [sq_], [ssq_])
                    S.op("act", actf(ssq_[:, 0:4], ssq_[:, 0:4], AF.Sqrt, scale=1.0 / 96, bias=EPS), [ssq_], [ssq_])
                    S.op("dve", recip(ssq_[:, 0:4], ssq_[:, 0:4]), [ssq_], [ssq_])
                    cs = cs_t[par]
                    S.dma("pool", cs.ap, c_rope_p[tile * 128:(tile + 1) * 128, :], writes=[cs])
                    rope(q_[:, :, 64:96], q_, cs, 4, 128, sw[par], t1[par])
                    S.op("dve", tt(q_.ap, q_.ap, bc(ssq_[:, 0:4], [128, 4, 96], 2), ALU.mult), [q_, ssq_], [q_])
                    S.op("dve", tt(qf[par].ap, q_.ap, bc(gg_b.ap, [128, 4, 96], 1), ALU.mult), [q_, gg_b], [qf[par]])
                    pq = psb(B0 + 2)
                    for h in range(4):
                        S.op("pe", trp(pq[0:96, 384 + h * 128:384 + (h + 1) * 128], qf[par][:, h, :], ident.ap),
                             [qf[par], ident], [PS[B0 + 2]])
                    S.op("dve", cp(QTv[s_].ap, pq[0:96, 384:896].rearrange("p (h t) -> p h t", h=4)), [PS[B0 + 2]], [QTv[s_]])
                    streams.append(S.end_capture())
                    if len(streams) == 2:
                        S.replay(streams)
                        streams = []
                hall = [hTv[i] for i in range(4)]
                qall = [QTv[i] for i in range(4)]
                for c in range(2):
                    col = 672 + g * 256 + c * 128
                    for kc in range(8):
                        S.op("pe", mm(PS[4 + c].ap, W_in[:, kc, col:col + 128], hT4[:, kc, :], kc == 0, kc == 7),
                             [W_in] + hall, [PS[4 + c]])
                    S.op("act", actf(sz[:, c, :], PS[4 + c].ap, AF.Exp, scale=-1.0), [PS[4 + c]], [sz])
                    S.op("dve", ts(sz[:, c, :], sz[:, c, :], 1.0, ALU.add), [sz], [sz])
                    S.op("dve", recip(sz[:, c, :], sz[:, c, :]), [sz], [sz])
                    S.op("dve", tt(sz[:, c, :], PS[4 + c].ap, sz[:, c, :], ALU.mult), [PS[4 + c], sz], [sz])
                G = GTt.next()
                nk = 4 * qi + 4
                sbank = Rot([0, 1, 2, 3])
                units = [(h, kt) for h in range(4) for kt in range(nk)]
                pend = None

                def emit_pv(u):
                    h, kt, p = u
                    ob = 6 + (h % 2)
                    pr, par = h // 2, h % 2
                    lhs = V1[:, kt, pr, 0:128] if par == 0 else V1[:, kt, pr, 64:192]
                    S.op("pe", mm(PS[ob].ap, lhs, p.ap, kt == 0, kt == nk - 1), [V1, p], [PS[ob]])
                    if kt == nk - 1:
                        if par == 0:
                            o_r, s_r = slice(0, 64), slice(64, 128)
                        else:
                            o_r, s_r = slice(64, 128), slice(0, 64)
                        S.op("dve", recip(rcp[s_r, :], PS[ob][s_r, :]), [PS[ob]], [rcp])
                        S.op("dve", tt(tmpo[o_r, :], PS[ob][o_r, :], rcp[s_r, :], ALU.mult), [PS[ob], rcp], [tmpo])
                        S.op("dve", tt(G[o_r, pr, :], tmpo[o_r, :], sz[o_r, pr, :], ALU.mult), [tmpo, sz], [G])

                for (h, kt) in units:
                    b = sbank.next()
                    S.op("pe", mm(PS[b].ap, KT[0:96, h, kt * 128:(kt + 1) * 128], QT[0:96, h, :]),
                         [KT] + qall, [PS[b]])
                    p = pT.next()
                    S.op("act", actf(p.ap, PS[b].ap, AF.Exp), [PS[b]], [p])
                    if kt >= 4 * qi:
                        S.op("pool", tt(p.ap, p.ap, amask[:, kt - 4 * qi, :], ALU.mult), [p, amask], [p])
                    if pend is not None:
                        emit_pv(pend)
                    pend = (h, kt, p)
                emit_pv(pend)
                S.dma("sp", gt1[g, :, :, qi * 512:(qi + 1) * 512], G.ap, reads=[G], writes=[D_gt1])
            S.barrier()
            ar.release()
        S.barrier()
        ar.release()

    def phase_out_proj(src_x, D_src, gt, D_gt, Wo, dst, D_dst):
        ar.mark()
        xt = Rot([ar.alloc([1024], F32, f"xo{i}") for i in range(2)])
        gT = Rot([ar.alloc([4, 128], BF16, f"gT{i}") for i in range(2)])
        for t in range(NT):
            x, g_ = xt.next(), gT.next()
            S.dma("sp", x.ap, src_x[t * 128:(t + 1) * 128, :], reads=[D_src], writes=[x])
            for grp in range(2):
                S.dma("pool", g_[:, 2 * grp:2 * grp + 2, :], gt[grp, :, :, t * 128:(t + 1) * 128],
                      reads=[D_gt], writes=[g_])
            for half in range(2):
                b = 2 * (t % 2) + half
                for c in range(4):
                    S.op("pe", mm(PS[b].ap, g_[:, c, :], Wo[:, c, half * 512:(half + 1) * 512], c == 0, c == 3),
                         [g_, Wo], [PS[b]])
                S.op("dve", tt(x[:, half * 512:(half + 1) * 512], x[:, half * 512:(half + 1) * 512],
                               PS[b].ap, ALU.add), [x, PS[b]], [x])
            S.dma("sp", dst[t * 128:(t + 1) * 128, :], x.ap, reads=[x], writes=[D_dst])
        S.barrier()
        ar.release()


    def phase_mla_sample():
        ar.mark()
        P = NS
        SC = 16
        NCH = 128 // SC
        xs = ar.alloc([1024], F32, "xs")
        hb = ar.alloc([1024], BF16, "s_hb")
        hTs = ar.alloc([8, P], BF16, "hTs")
        cs = ar.alloc([64], F32, "s_cs")
        cn_s = ar.alloc([256], F32, "cn_s")
        kpe_s = ar.alloc([32], F32, "kpe_s")
        sw = ar.alloc([8, 32], F32, "s_sw")
        t1 = ar.alloc([8, 32], F32, "s_t1")
        qan = ar.alloc([384], BF16, "s_qan")
        qaT = ar.alloc([3, P], BF16, "s_qaT")
        qs = ar.alloc([8, 96], F32, "s_qs")
        qfb = ar.alloc([8, 96], BF16, "s_qfb")
        sq = Rot([ar.alloc([768], F32, f"s_sq{i}") for i in range(2)])
        ssq = Rot([ar.alloc([16], F32, f"s_ssq{i}") for i in range(2)])
        qnT = ar.alloc([8, P], BF16, "qnT")
        qpT = ar.alloc([8, P], BF16, "qpT")
        wukT = ar.alloc([8, 256], BF16, "wukT")
        qlatT = ar.alloc([2, NSEQ, 8, 4], BF16, "qlatT")
        qpeT = ar.alloc([NSEQ, 8, 4], BF16, "qpeT")
        smask = ar.alloc([NSEQ * 32], F32, "smask")
        pTn = ar.alloc([NSEQ * 32], BF16, "pTn")
        scn = ar.alloc([NSEQ * 32], F32, "scn")
        lbn = ar.alloc([289], BF16, "lbn")
        gl = Rot([ar.alloc([SC, 256], F32, f"gl{i}") for i in range(2)])
        gk = Rot([ar.alloc([SC, 32], F32, f"gk{i}") for i in range(2)])
        lb = Rot([ar.alloc([289], BF16, f"lb{i}") for i in range(3)])
        cT = Rot([ar.alloc([2, 128], BF16, f"s_cT{i}") for i in range(2)])
        kpT = Rot([ar.alloc([128], BF16, f"kpT{i}") for i in range(2)])
        sc = Rot([ar.alloc([32], F32, f"s_sc{i}") for i in range(2)])
        pT = Rot([ar.alloc([32], BF16, f"s_pT{i}") for i in range(3)])
        ol = ar.alloc([256], BF16, "ol")
        olr = ar.alloc([4], F32, "olr")
        olatT = ar.alloc([2, 8, P], BF16, "olatT")
        ez = ar.alloc([512], F32, "s_ez")
        z_sb = ar.alloc([512], F32, "s_zsb")
        gated = ar.alloc([512], BF16, "s_gated")
        gTs = ar.alloc([4, P], BF16, "gTs")
        for l_ in lb.items + [lbn]:
            S.op("pool", mset(l_[:, 256:257], 1.0), [], [l_])
        S.dma("sp", smask[0:P, :], c_smask, writes=[smask])
        S.dma("sp", cs[0:P, :], c_rope_s, writes=[cs])
        S.dma("sp", xs[0:P, :], x_s, writes=[xs])
        st = stat.next()
        rms_rstd(xs[0:P, :], xs, 1024, st, junk, P)
        S.op("dve", ts(hb[0:P, :], xs[0:P, :], st[0:P, 0:1], ALU.mult), [xs, st], [hb])
        pb = psb(0)
        for c in range(8):
            S.op("pe", trp(pb[:, c * P:(c + 1) * P], hb[0:P, c * 128:(c + 1) * 128], ident[0:P, 0:P]), [hb, ident], [PS[0]])
        S.op("act", acp(hTs.ap, pb[:, 0:8 * P].rearrange("p (c t) -> p c t", c=8)), [PS[0]], [hTs])
        for (bank, c0, w) in ((1, 384, 288), (2, 0, 384), (3, 672, 512)):
            for kc in range(8):
                S.op("pe", mm(PS[bank][0:P, 0:w], hTs[:, kc, :], W_in[:, kc, c0:c0 + w], kc == 0, kc == 7),
                     [hTs, W_in], [PS[bank]])
        st = stat.next()
        ckp_from_proj(PS[1][0:P, 0:256], PS[1][0:P, 256:288], PS[1], P, cs, cn_s, kpe_s, st, sw, t1)
        S.dma("sp", lat_s, cn_s[0:P, :], reads=[cn_s], writes=[D_out])
        S.dma("sp", kr_s, kpe_s[0:P, :], reads=[kpe_s], writes=[D_out])
        st = stat.next()
        rms_rstd(PS[2][0:P, 0:384], PS[2], 384, st, junk, P)
        S.op("dve", ts(qan[0:P, :], PS[2][0:P, 0:384], st[0:P, 0:1], ALU.mult), [PS[2], st], [qan])
        pa = psb(0)
        for c in range(3):
            S.op("pe", trp(pa[:, c * P:(c + 1) * P], qan[0:P, c * 128:(c + 1) * 128], ident[0:P, 0:P]), [qan, ident], [PS[0]])
        S.op("dve", cp(qaT.ap, pa[:, 0:3 * P].rearrange("p (c t) -> p c t", c=3)), [PS[0]], [qaT])
        for (bank, c0, w) in ((4, 0, 384), (5, 384, 384)):
            for kc in range(3):
                S.op("pe", mm(PS[bank][0:P, 0:w], qaT[:, kc, :], W_uq[:, kc, c0:c0 + w], kc == 0, kc == 2), [qaT, W_uq], [PS[bank]])
            S.op("act", acp(qs[0:P, c0 // 96:c0 // 96 + 4, :].rearrange("p h d -> p (h d)"), PS[bank][0:P, 0:w]), [PS[bank]], [qs])
        sq_, ssq_ = sq.next(), ssq.next()
        S.op("act", actf(sq_[0:P, 0:768], qs[0:P].rearrange("p h d -> p (h d)"), AF.Square), [qs], [sq_])
        S.op("dve", rsum(ssq_[0:P, 0:8], sq_[0:P, 0:768].rearrange("p (h d) -> p h d", h=8)), [sq_], [ssq_])
        S.op("act", actf(ssq_[0:P, 0:8], ssq_[0:P, 0:8], AF.Sqrt, scale=1.0 / 96, bias=EPS), [ssq_], [ssq_])
        S.op("dve", recip(ssq_[0:P, 0:8], ssq_[0:P, 0:8]), [ssq_], [ssq_])
        rope(qs[0:P, :, 64:96], qs, cs, 8, P, sw, t1)
        S.op("dve", tt(qs[0:P], qs[0:P], bc(ssq_[0:P, 0:8], [P, 8, 96], 2), ALU.mult), [qs, ssq_], [qs])
        S.op("dve", tt(qfb[0:P], qs[0:P], bc(gg_b[0:P, :], [P, 8, 96], 1), ALU.mult), [qs, gg_b], [qfb])
        pq = psb(0)
        for h in range(8):
            S.op("pe", trp(pq[0:64, h * P:(h + 1) * P], qfb[0:P, h, 0:64], ident[0:P, 0:P]), [qfb, ident], [PS[0]])
        S.op("dve", cp(qnT[0:64], pq[0:64, 0:8 * P].rearrange("p (h t) -> p h t", h=8)), [PS[0]], [qnT])
        for h in range(8):
            S.op("pe", trp(pq[0:32, h * P:(h + 1) * P], qfb[0:P, h, 64:96], ident[0:P, 0:P]), [qfb, ident], [PS[0]])
        S.op("dve", cp(qpT[0:32], pq[0:32, 0:8 * P].rearrange("p (h t) -> p h t", h=8)), [PS[0]], [qpT])
        S.op("dve", cp(qpeT[0:32].rearrange("p n h t -> p h n t"),
                       qpT[0:32].rearrange("p h (n t) -> p h n t", t=4)), [qpT], [qpeT])
        for kc in range(2):
            pw = psb(4 + kc)
            for h in range(8):
                S.op("pe", trp(pw[0:64, h * 128:(h + 1) * 128], W_uk[:, kc, h * 64:(h + 1) * 64], ident.ap), [W_uk, ident], [PS[4 + kc]])
            S.op("act", acp(wukT[0:64, :, kc * 128:(kc + 1) * 128], pw[0:64, 0:1024].rearrange("p (h c) -> p h c", h=8)),
                 [PS[4 + kc]], [wukT])
        for ck in range(2):
            for h in range(8):
                S.op("pe", mm(PS[4 + ck][:, h * P:(h + 1) * P], wukT[0:64, h, ck * 128:(ck + 1) * 128], qnT[0:64, h, :]),
                     [wukT, qnT], [PS[4 + ck]])
            S.op("act", acp(qlatT[:, ck].rearrange("p n h t -> p h n t"),
                            PS[4 + ck][:, 0:8 * P].rearrange("p (h n t) -> p h n t", h=8, t=4)), [PS[4 + ck]], [qlatT])

        def tile_scores(lbt, parts, kp_ap, kpT_src, bank_t, bank_k, bank_s, rhs_lat, rhs_pe, ncols):
            pbt = psb(bank_t)
            cT_, kpT_ = cT.next(), kpT.next()
            for c in range(2):
                S.op("pe", trp(pbt[:, c * parts:(c + 1) * parts], lbt[0:parts, c * 128:(c + 1) * 128], ident[0:parts, 0:parts]),
                     [lbt, ident], [PS[bank_t]])
            S.op("pe", trp(pbt[0:32, 256:256 + parts], lbt[0:parts, 257:289], ident[0:parts, 0:parts]), [lbt, ident], [PS[bank_t]])
            S.op("act", acp(cT_[:, :, 0:parts], pbt[:, 0:2 * parts].rearrange("p (c t) -> p c t", c=2)), [PS[bank_t]], [cT_])
            S.op("dve", cp(kpT_[0:32, 0:parts], pbt[0:32, 256:256 + parts]), [PS[bank_t]], [kpT_])
            for kc in range(2):
                S.op("pe", mm(PS[bank_k][0:parts, :], cT_[:, kc, 0:parts], W_uk[:, kc, :], kc == 0, kc == 1), [cT_, W_uk], [PS[bank_k]])
            sq_, ssq_ = sq.next(), ssq.next()
            S.op("act", actf(sq_[0:parts, 0:512], PS[bank_k][0:parts, :], AF.Square), [PS[bank_k]], [sq_])
            S.op("dve", rsum(ssq_[0:parts, 0:8], sq_[0:parts, 0:512].rearrange("p (h d) -> p h d", h=8)), [sq_], [ssq_])
            S.op("act", actf(sq_[0:parts, 512:544], kp_ap, AF.Square, accum_out=ssq_[0:parts, 8:9]), [kpT_src], [sq_, ssq_])
            S.op("dve", ts(ssq_[0:parts, 0:8], ssq_[0:parts, 0:8], ssq_[0:parts, 8:9], ALU.add), [ssq_], [ssq_])
            S.op("act", actf(ssq_[0:parts, 0:8], ssq_[0:parts, 0:8], AF.Sqrt, scale=1.0 / 96, bias=EPS), [ssq_], [ssq_])
            S.op("dve", recip(ssq_[0:parts, 0:8], ssq_[0:parts, 0:8]), [ssq_], [ssq_])
            for kc in range(2):
                S.op("pe", mm(PS[bank_s][0:parts, 0:ncols], cT_[:, kc, 0:parts], rhs_lat(kc), kc == 0, False),
                     [cT_, qlatT], [PS[bank_s]])
            S.op("pe", mm(PS[bank_s][0:parts, 0:ncols], kpT_[0:32, 0:parts], rhs_pe, False, True), [kpT_, qpeT], [PS[bank_s]])
            return ssq_

        S.op("act", acp(lbn[0:P, 0:256], cn_s[0:P, :]), [cn_s], [lbn])
        S.op("act", acp(lbn[0:P, 257:289], kpe_s[0:P, :]), [kpe_s], [lbn])
        NC_ = NSEQ * 32
        r_ = tile_scores(lbn, P, kpe_s[0:P, :], kpe_s, 0, 1, 2,
                         lambda kc: qlatT[:, kc].rearrange("p n h t -> p (n h t)"),
                         qpeT[0:32].rearrange("p n h t -> p (n h t)"), NC_)
        S.op("dve", tt(scn[0:P, :].rearrange("p (n h t) -> p n h t", h=8, t=4),
                       PS[2][0:P, 0:NC_].rearrange("p (n h t) -> p n h t", h=8, t=4),
                       r_[0:P, 0:8].unsqueeze(1).unsqueeze(3).to_broadcast([P, NSEQ, 8, 4]), ALU.mult), [PS[2], r_], [scn])
        S.op("act", actf(scn[0:P, :], scn[0:P, :], AF.Exp), [scn], [scn])
        S.op("dve", tt(pTn[0:P, :], scn[0:P, :], smask[0:P, :], ALU.mult), [scn, smask], [pTn])

        S.op("act", acp(z_sb[0:P, :], PS[3][0:P, :]), [PS[3]], [z_sb])
        lbc = Rot([ar.alloc([SC, 289], BF16, f"lbc{i}") for i in range(2)])
        for l_ in lbc.items:
            S.op("pool", mset(l_[:, :, 256:257], 1.0), [], [l_])
        cT2 = Rot([ar.alloc([2, 2, 128], BF16, f"cT2_{i}") for i in range(2)])
        kpT2 = Rot([ar.alloc([2, 128], BF16, f"kpT2_{i}") for i in range(2)])
        sq2 = Rot([ar.alloc([1024], F32, f"sq2_{i}") for i in range(2)])
        sqk = Rot([ar.alloc([SC, 32], F32, f"sqk{i}") for i in range(2)])
        ssqc = Rot([ar.alloc([SC, 8], F32, f"ssqc{i}") for i in range(2)])
        sspc = Rot([ar.alloc([SC], F32, f"sspc{i}") for i in range(2)])
        scr = Rot([ar.alloc([SC, 32], F32, f"scr{i}") for i in range(2)])
        pTc = Rot([ar.alloc([SC, 32], BF16, f"pTc{i}") for i in range(2)])
        OLB = 7
        nb = 0

        def stage_b(n, ch, lc_, ssq_, ssp_, scr_, first_of_seq):
            p_ = pTc.next()
            S.op("dve", tt(ssq_.ap, ssq_.ap, bc(ssp_.ap, [128, SC, 8], 2), ALU.add), [ssq_, ssp_], [ssq_])
            S.op("act", actf(ssq_.ap, ssq_.ap, AF.Sqrt, scale=1.0 / 96, bias=EPS), [ssq_], [ssq_])
            S.op("dve", recip(ssq_.ap, ssq_.ap), [ssq_], [ssq_])
            S.op("dve", tt(scr_.ap.rearrange("p s (h t) -> p s h t", t=4), scr_.ap.rearrange("p s (h t) -> p s h t", t=4),
                           bc(ssq_.ap, [128, SC, 8, 4], 3), ALU.mult), [scr_, ssq_], [scr_])
            S.op("act", actf(p_.ap, scr_.ap, AF.Exp), [scr_], [p_])
            if first_of_seq:
                S.op("pe", mm(PS[OLB][0:32, 0:257], pTn[0:P, n * 32:(n + 1) * 32], lbn[0:P, 0:257], True, False),
                     [pTn, lbn], [PS[OLB]])
            for i in range(SC):
                last = (ch == NCH - 1 and i == SC - 1)
                S.op("pe", mm(PS[OLB][0:32, 0:257], p_[:, i, :], lc_[:, i, 0:257], False, last), [p_, lc_], [PS[OLB]])

        def seq_epilogue(n):
            OL = OLB
            S.op("dve", recip(olr[0:32, 0:1], PS[OL][0:32, 256:257]), [PS[OL]], [olr])
            S.op("dve", ts(ol[0:32, :], PS[OL][0:32, 0:256], olr[0:32, 0:1], ALU.mult), [PS[OL], olr], [ol])
            po = psb(6)
            for ck in range(2):
                S.op("pe", trp(po[:, 512 + ck * 32:512 + (ck + 1) * 32], ol[0:32, ck * 128:(ck + 1) * 128], ident[0:32, 0:32]),
                     [ol, ident], [PS[6]])
            S.op("act", acp(olatT[:, :, :, n * 4:(n + 1) * 4],
                            po[:, 512:576].rearrange("p (c h t) -> p c h t", c=2, t=4)), [PS[6]], [olatT])

        pending = None
        for n in range(NSEQ):
            for ch in range(NCH):
                gl_, gk_ = gl.next(), gk.next()
                S.dma_fn("pool", lambda e, gl_=gl_, n=n, ch=ch: e.indirect_dma_start(
                    out=gl_.ap.rearrange("p s c -> p (s c)"), out_offset=None, in_=cache_lat,
                    in_offset=bass.IndirectOffsetOnAxis(ap=ptab_sb[:, n:n + 1], axis=0),
                    element_offset=ch * SC * 256), [ptab_sb], [gl_])
                S.dma_fn("pool", lambda e, gk_=gk_, n=n, ch=ch: e.indirect_dma_start(
                    out=gk_.ap.rearrange("p s c -> p (s c)"), out_offset=None, in_=cache_kr,
                    in_offset=bass.IndirectOffsetOnAxis(ap=ptab_sb[:, n:n + 1], axis=0),
                    element_offset=ch * SC * 32), [ptab_sb], [gk_])
                lc_, ssq_, ssp_, scr_, qk_ = lbc.next(), ssqc.next(), sspc.next(), scr.next(), sqk.next()
                S.op("pool", cp(lc_[:, :, 257:289], gk_.ap), [gk_], [lc_])
                S.op("act", actf(qk_.ap, gk_.ap, AF.Square), [gk_], [qk_])
                S.op("dve", rsum(ssp_.ap, qk_.ap), [qk_], [ssp_])
                for j in range(SC // 2):
                    sl = slice(2 * j, 2 * j + 2)
                    S.op("pool", cp(lc_[:, sl, 0:256], gl_[:, sl, :]), [gl_], [lc_])
                    bT = nb % 2
                    bK = (2, 3) if nb % 2 == 0 else (4, 5)
                    nb += 1
                    pbt = psb(bT)
                    c2, k2 = cT2.next(), kpT2.next()
                    for g_ in range(2):
                        for ck in range(2):
                            S.op("pe", trp(pbt[:, (g_ * 2 + ck) * 128:(g_ * 2 + ck + 1) * 128],
                                           lc_[:, 2 * j + g_, ck * 128:(ck + 1) * 128], ident.ap), [lc_, ident], [PS[bT]])
                        S.op("pe", trp(pbt[0:32, 512 + g_ * 128:512 + (g_ + 1) * 128], lc_[:, 2 * j + g_, 257:289], ident.ap),
                             [lc_, ident], [PS[bT]])
                    S.op("act", acp(c2.ap.rearrange("p g c t -> p (g c t)"), pbt[:, 0:512]), [PS[bT]], [c2])
                    S.op("dve", cp(k2[0:32].rearrange("p g t -> p (g t)"), pbt[0:32, 512:768]), [PS[bT]], [k2])
                    for g_ in range(2):
                        for kc in range(2):
                            S.op("pe", mm(PS[bK[g_]].ap, c2[:, g_, kc, :], W_uk[:, kc, :], kc == 0, kc == 1), [c2, W_uk], [PS[bK[g_]]])
                    for g_ in range(2):
                        for kc in range(2):
                            S.op("pe", mm(PS[6][:, g_ * 32:(g_ + 1) * 32], c2[:, g_, kc, :],
                                          qlatT[:, kc, n].rearrange("p h t -> p (h t)"), kc == 0, False), [c2, qlatT], [PS[6]])
                        S.op("pe", mm(PS[6][:, g_ * 32:(g_ + 1) * 32], k2[0:32, g_, :],
                                      qpeT[0:32, n].rearrange("p h t -> p (h t)"), False, True), [k2, qpeT], [PS[6]])
                    q2 = sq2.next()
                    S.op("act", actf(q2.ap, PSD[bK[0] // 2], AF.Square), [PS[bK[0]], PS[bK[1]]], [q2])
                    S.op("dve", rsum(ssq_[:, sl, :].rearrange("p g h -> p (g h)"), q2.ap.rearrange("p (x d) -> p x d", d=64)),
                         [q2], [ssq_])
                    S.op("dve", cp(scr_[:, sl, :].rearrange("p g x -> p (g x)"), PS[6][:, 0:64]), [PS[6]], [scr_])
                    if j == 1 and pending is not None:
                        stage_b(*pending)
                        if pending[-1] is False and pending[1] == NCH - 1:
                            pass
                        pending = None
                        if ch == 0 and n > 0:
                            seq_epilogue(n - 1)
                pending = (n, ch, lc_, ssq_, ssp_, scr_, ch == 0)
        stage_b(*pending)
        seq_epilogue(NSEQ - 1)
        for h in range(8):
            for ck in range(2):
                S.op("pe", mm(PS[1][0:P, h * 64:(h + 1) * 64], olatT[:, ck, h, :], W_uv[:, ck, h * 64:(h + 1) * 64], ck == 0, ck == 1),
                     [olatT, W_uv], [PS[1]])
        S.op("act", actf(ez[0:P, :], z_sb[0:P, :], AF.Exp, scale=-1.0), [z_sb], [ez])
        S.op("dve", ts(ez[0:P, :], ez[0:P, :], 1.0, ALU.add), [ez], [ez])
        S.op("dve", recip(ez[0:P, :], ez[0:P, :]), [ez], [ez])
        S.op("dve", tt(ez[0:P, :], z_sb[0:P, :], ez[0:P, :], ALU.mult), [z_sb, ez], [ez])
        S.op("dve", tt(gated[0:P, :], PS[1][0:P, :], ez[0:P, :], ALU.mult), [PS[1], ez], [gated])
        pg = psb(0)
        for c in range(4):
            S.op("pe", trp(pg[:, c * P:(c + 1) * P], gated[0:P, c * 128:(c + 1) * 128], ident[0:P, 0:P]), [gated, ident], [PS[0]])
        S.op("act", acp(gTs.ap, pg[:, 0:4 * P].rearrange("p (c t) -> p c t", c=4)), [PS[0]], [gTs])
        for half in range(2):
            for c in range(4):
                S.op("pe", mm(PS[4 + half][0:P, :], gTs[:, c, :], W_o[:, c, half * 512:(half + 1) * 512], c == 0, c == 3),
                     [gTs, W_o], [PS[4 + half]])
            S.op("dve", tt(xs1[0:P, half * 512:(half + 1) * 512], xs[0:P, half * 512:(half + 1) * 512],
                           PS[4 + half][0:P, :], ALU.add), [xs, PS[4 + half]], [xs1])
        if "s2" not in phases:
            S.dma("sp", y_s, xs1[0:P, :], reads=[xs1], writes=[D_out])
        S.barrier()
        ar.release()

    def load_gdn_weights():
        W = {}
        bn_sb = ar.alloc([8], F32, "bn_sb")
        S.dma("sp", bn_sb.ap, b_norm, writes=[bn_sb])
        wtmp2 = ar.alloc([2064], F32, "wtmp2")
        W["in"] = ar.alloc([8, 2064], BF16, "Wb_in")
        load_bf16_rows(W["in"], [W["in"][:, kc, :] for kc in range(8)],
                       [b_w_in[kc * 128:(kc + 1) * 128, :] for kc in range(8)], (bn_sb, bn_sb), wtmp2, 2064)
        W["zab"] = ar.alloc([8, 2, 272], BF16, "Wzab")
        for g_ in range(2):
            S.op("pool", cp(W["zab"][:, :, g_, 0:256], W["in"][:, :, 1536 + g_ * 256:1792 + g_ * 256]), [W["in"]], [W["zab"]])
            S.op("pool", cp(W["zab"][:, :, g_, 256:272], W["in"][:, :, 2048:2064]), [W["in"]], [W["zab"]])
        W["o"] = ar.alloc([4, 1024], BF16, "Wb_o")
        load_bf16_rows(W["o"], [W["o"][:, kc, :] for kc in range(4)],
                       [b_w_o[kc * 128:(kc + 1) * 128, :] for kc in range(4)], None, wtmp2, 1024)
        W["convT"] = ar.alloc([12, 4], F32, "wconvT")
        S.dma("sp", W["convT"].ap.rearrange("p c j -> p (c j)"), b_w_convT, writes=[W["convT"]])
        W["negA"] = ar.alloc([8], F32, "negA")
        S.dma("sp", W["negA"].ap, b_a_log.to_broadcast([128, 8]), writes=[W["negA"]])
        S.op("act", actf(W["negA"].ap, W["negA"].ap, AF.Exp), [W["negA"]], [W["negA"]])
        S.op("dve", ts(W["negA"].ap, W["negA"].ap, -1.0, ALU.mult), [W["negA"]], [W["negA"]])
        W["dtb"] = ar.alloc([8], F32, "dtb")
        S.dma("sp", W["dtb"].ap, b_dt_bias.to_broadcast([128, 8]), writes=[W["dtb"]])
        W["go"] = ar.alloc([64], F32, "go")
        S.dma("sp", W["go"].ap, b_g_o.to_broadcast([128, 64]), writes=[W["go"]])
        for nm, src in (("tri", c_tri), ("strict", c_strict), ("negm", c_negm)):
            W[nm] = ar.alloc([128], F32, nm)
            S.dma("sp", W[nm].ap, src, writes=[W[nm]])
        W["ones"] = ar.alloc([128], F32, "ones_f")
        S.op("pool", mset(W["ones"].ap, 1.0), [], [W["ones"]])
        blk_f = ar.alloc([128], F32, "blk_f")
        S.dma("sp", blk_f.ap, c_blk, writes=[blk_f])
        W["blk"] = ar.alloc([128], BF16, "blk")
        S.op("dve", cp(W["blk"].ap, blk_f.ap), [blk_f], [W["blk"]])
        return W

    def gates(xa_ap, xb_ap, srcT, parts, nh, W, hcol, gt, bt, tmp):
        a1, a2 = tmp
        P = slice(0, parts)
        S.op("dve", tt(a1[P, 0:nh], xa_ap, W["dtb"][P, hcol:hcol + nh], ALU.add), [srcT, W["dtb"]], [a1])
        S.op("dve", stt(a2[P, 0:nh], a1[P, 0:nh], -1.0, a1[P, 0:nh], ALU.mult, ALU.max), [a1], [a2])
        S.op("act", actf(a2[P, 0:nh], a2[P, 0:nh], AF.Exp, scale=-1.0), [a2], [a2])
        S.op("act", actf(a2[P, 0:nh], a2[P, 0:nh], AF.Ln, bias=1.0), [a2], [a2])
        S.op("dve", stt(a1[P, 0:nh], a1[P, 0:nh], 0.0, a2[P, 0:nh], ALU.max, ALU.add), [a1, a2], [a1])
        S.op("dve", tt(gt[P, 0:nh], a1[P, 0:nh], W["negA"][P, hcol:hcol + nh], ALU.mult), [a1, W["negA"]], [gt])
        S.op("act", actf(a2[P, 0:nh], xb_ap, AF.Exp, scale=-1.0), [srcT], [a2])
        S.op("dve", ts(a2[P, 0:nh], a2[P, 0:nh], 1.0, ALU.add), [a2], [a2])
        S.op("dve", recip(bt[P, 0:nh], a2[P, 0:nh]), [a2], [bt])

    def phase_gdn_prompt(W):
        ar.mark()
        NTL = SEQ // 512
        xt = Rot([ar.alloc([1024], F32, f"gx{i}") for i in range(2)])
        hb = ar.alloc([1024], BF16, "ghb")
        hT4 = ar.alloc([8, 512], BF16, "ghT4")
        qk = [ar.alloc([6, 515], F32, f"qk{i}") for i in range(2)]
        cs = ar.alloc([6, 512], F32, "gcs")
        sqb = ar.alloc([512], BF16, "sqb")
        sd = ar.alloc([512], F32, "sd")
        QKn = ar.alloc([4, 512], BF16, "QKn")
        vb = ar.alloc([2, 512], BF16, "vb")
        KnZ = ar.alloc([2, 2, 512], BF16, "KnZ")
        SbZ = ar.alloc([2, 2, 64], BF16, "SbZ")
        S.op("pool", mset(KnZ.ap, 0.0), [], [KnZ])
        gt_, bt_ = ar.alloc([4], F32, "g_t"), ar.alloc([4], F32, "b_t")
        a1, a2 = ar.alloc([4], F32, "ga1"), ar.alloc([4], F32, "ga2")
        smp = [ar.alloc([32], F32, f"gsm{i}") for i in range(2)]
        smS = ar.alloc([8], F32, "gsmS")
        egp = [ar.alloc([2], F32, f"eg{i}") for i in range(2)]
        z_all = ar.alloc([4, 256], F32, "z_all")
        o_all = ar.alloc([4, 4, 64], F32, "o_all")
        osq = ar.alloc([1024], F32, "osq")
        sgz_all = ar.alloc([1024], F32, "sgz_all")
        gated_all = ar.alloc([4, 256], BF16, "gated_all")
        smO = ar.alloc([16], F32, "smO")
        Z = ar.alloc([4, 128], F32, "Z")
        Dm = ar.alloc([4, 128], F32, "Dm")
        decay = ar.alloc([4, 128], F32, "decay")
        A1 = ar.alloc([4, 128], F32, "A1")
        bS = ar.alloc([4, 128], F32, "bS")
        MYr = [ar.alloc([4, 256], BF16, f"MY{i}") for i in range(2)]
        Yf = ar.alloc([4, 128], BF16, "Yf")
        intra = ar.alloc([4, 128], BF16, "intra")
        intraTp = [ar.alloc([4, 128], BF16, f"intraT{i}") for i in range(2)]
        Nr = Rot([ar.alloc([4, 128], BF16, f"N{i}") for i in range(2)])
        Mr = Rot([ar.alloc([4, 128], BF16, f"M{i}") for i in range(2)])
        ImLT = ar.alloc([4, 128], BF16, "ImLT")
        Yr = Rot([ar.alloc([4, 128], BF16, f"Y{i}") for i in range(2)])
        Ywz = ar.alloc([2, 2, 128], BF16, "Ywz")
        ktzp = [ar.alloc([2, 2, 128], BF16, f"ktz{i}") for i in range(2)]
        S.op("pool", mset(Ywz.ap, 0.0), [], [Ywz])
        for k_ in ktzp:
            S.op("pool", mset(k_.ap, 0.0), [], [k_])
        ktok = ar.alloc([4, 64], F32, "ktok")
        u_p = [ar.alloc([4, 64], F32, f"u_sb{i}") for i in range(2)]
        wT_p = [ar.alloc([2, 128], BF16, f"wT_sb{i}") for i in range(2)]
        vn = ar.alloc([4, 64], BF16, "vn")
        o_sb = ar.alloc([4, 64], F32, "o_sb")
        o_t = ar.alloc([4, 64], F32, "o_t")
        St = ar.alloc([2, 64], F32, "St")
        St_t = ar.alloc([2, 64], F32, "St_t")
        Sb = ar.alloc([2, 64], BF16, "Sb")
        sgz = ar.alloc([256], F32, "sgz")
        gated = ar.alloc([256], BF16, "gated")
        G2 = Rot([ar.alloc([2, 512], BF16, f"G2{i}") for i in range(2)])
        ident_b = ident
        crow = ar.alloc([6, 128], F32, "crow")

        def load_h(tile, col):
            x = xt.next()
            S.dma("sp", x.ap, xp1[tile * 128:(tile + 1) * 128, :], reads=[D_xp1], writes=[x])
            st = stat.next()
            rms_rstd(x.ap, x, 1024, st, junk)
            S.op("dve", ts(hb.ap, x.ap, st[:, 0:1], ALU.mult), [x, st], [hb])
            pb = psb(0)
            for c in range(8):
                S.op("pe", trp(pb[:, c * 128:(c + 1) * 128], hb[:, c * 128:(c + 1) * 128], ident.ap),
                     [hb, ident], [PS[0]])
            S.op("act", acp(hT4[:, :, col:col + 128], pb[:, 0:1024].rearrange("p (c t) -> p c t", c=8)),
                 [PS[0]], [hT4])

        for g in range(2):
            if g == 1:
                S.barrier()
            S.op("pool", mset(St.ap, 0.0), [], [St])
            S.op("pool", mset(SbZ.ap, 0.0), [], [SbZ])
            gch = [2 * g, 2 * g + 1, 4 + 2 * g, 5 + 2 * g, 8 + 2 * g, 9 + 2 * g]
            for ti in range(NTL):
                cur, prev = qk[ti % 2], qk[(ti + 1) % 2]
                for s in range(4):
                    tile_ = ti * 4 + s
                    if g == 0:
                        load_h(tile_, s * 128)
                    else:
                        S.dma("sp" if s % 2 == 0 else "pool", hT4[:, :, s * 128:(s + 1) * 128],
                              hT2_scr[:, :, tile_ * 128:(tile_ + 1) * 128], reads=[D_h2], writes=[hT4])
                if g == 0:
                    S.dma("pool", hT2_scr[:, :, ti * 512:(ti + 1) * 512], hT4.ap, reads=[hT4], writes=[D_h2])
                if ti == 0:
                    S.op("pool", mset(cur[:, :, 0:3], 0.0), [], [cur])
                else:
                    S.op("pool", cp(cur[:, :, 0:3], prev[:, :, 512:515]), [prev], [cur])
                for lc, gc_ in enumerate(gch):
                    b = 1 + (lc % 2)
                    for kc in range(8):
                        S.op("pe", mm(PS[b].ap, W["in"][:, kc, gc_ * 128:(gc_ + 1) * 128], hT4[:, kc, :],
                                      kc == 0, kc == 7), [W["in"], hT4], [PS[b]])
                    S.op("act", acp(cur[:, lc, 3:515], PS[b].ap), [PS[b]], [cur])
                    ce = "dve"
                    S.op(ce, ts(cs[:, lc, :], cur[:, lc, 0:512], W["convT"][:, gc_, 0:1], ALU.mult),
                         [cur, W["convT"]], [cs])
                    for j in range(1, 4):
                        S.op(ce, stt(cs[:, lc, :], cur[:, lc, j:j + 512], W["convT"][:, gc_, j:j + 1], cs[:, lc, :],
                                     ALU.mult, ALU.add), [cur, W["convT"], cs], [cs])
                    S.op("act", actf(cs[:, lc, :], cs[:, lc, :], AF.Silu), [cs], [cs])
                if ti == NTL - 1:
                    for lc, gc_ in enumerate(gch):
                        S.op("pe", trp(PS[3][0:3, 0:128], cur[:, lc, 512:515], ident_f.ap), [cur, ident_f], [PS[3]])
                        S.op("dve", cp(crow[0:3, lc, :], PS[3][0:3, 0:128]), [PS[3]], [crow])
                        S.dma("sp", conv_p[:, gc_ * 128:(gc_ + 1) * 128], crow[0:3, lc, :], reads=[crow], writes=[D_out])
                for lc in range(4 if cfg.cut >= 1 else 0):
                    S.op("act", actf(sqb.ap, cs[:, lc, :], AF.Square), [cs], [sqb])
                    S.op("pe", mm(PS[3].ap, W["blk"].ap, sqb.ap), [W["blk"], sqb], [PS[3]])
                    S.op("act", actf(sd.ap, PS[3].ap, AF.Sqrt, bias=EPS), [PS[3]], [sd])
                    S.op("dve", recip(sd.ap, sd.ap), [sd], [sd])
                    S.op("dve", stt(QKn[:, lc, :], cs[:, lc, :], 0.125 if lc < 2 else 1.0, sd.ap, ALU.mult, ALU.mult),
                         [cs, sd], [QKn])
                S.op("act", acp(vb.ap, cs[:, 4:6, :]), [cs], [vb])
                S.op("pool", cp(KnZ[0:64, :, 0, :], QKn[0:64, 2:4, :]), [QKn], [KnZ])
                S.op("pool", cp(KnZ[64:128, :, 1, :], QKn[64:128, 2:4, :]), [QKn], [KnZ])
                Gt = G2.next()
                def local(s):
                    CUT = cfg.cut
                    par = s % 2
                    sm, eg, ktz, intraT, u_sb, wT_sb = smp[par], egp[par], ktzp[par], intraTp[par], u_p[par], wT_p[par]
                    tc_ = slice(s * 128, (s + 1) * 128)
                    zc = 1536 + g * 256
                    for kc in range(8):
                        S.op("pe", mm(PS[0][:, 0:272], hT4[:, kc, tc_], W["zab"][:, kc, g, :], kc == 0, kc == 7),
                             [hT4, W["zab"]], [PS[0]])
                    S.op("act", acp(z_all[:, s, :], PS[0][:, 0:256]), [PS[0]], [z_all])
                    gates(PS[0][:, 256 + 4 * g:260 + 4 * g], PS[0][:, 264 + 4 * g:268 + 4 * g], PS[0], 128, 4, W,
                          4 * g, gt_, bt_, (a1, a2))
                    S.op("pe", mm(PS[4][:, 0:4], W["tri"].ap, gt_.ap), [W["tri"], gt_], [PS[4]])
                    S.op("dve", cp(sm[:, 0:4], PS[4][:, 0:4]), [PS[4]], [sm])
                    S.op("dve", tt(Z.ap, bc(gt_.ap, [128, 4, 128], 2), bc(W["tri"].ap, [128, 4, 128], 1), ALU.mult),
                         [gt_, W["tri"]], [Z])
                    S.op("pe", mm(PS[4].ap, W["ones"].ap, Z.ap.rearrange("p h j -> p (h j)")), [W["ones"], Z], [PS[4]])
                    gcrow = PS[4].ap.rearrange("p (h j) -> p h j", h=4)
                    S.op("dve", tt(Dm.ap, bc(sm[:, 0:4], [128, 4, 128], 2), gcrow, ALU.subtract), [sm, PS[4]], [Dm])
                    S.op("dve", stt(Dm.ap, Dm.ap, 0.0, bc(W["negm"].ap, [128, 4, 128], 1), ALU.min, ALU.add),
                         [Dm, W["negm"]], [Dm])
                    S.op("act", actf(decay.ap, Dm.ap, AF.Exp), [Dm], [decay])
                    S.op("act", actf(sm[:, 4:8], sm[:, 0:4], AF.Exp), [sm], [sm])
                    S.op("dve", tt(sm[:, 8:12], sm[:, 4:8], bt_.ap, ALU.mult), [sm, bt_], [sm])
                    S.op("dve", tt(sm[:, 20:24], gcrow[:, :, 127], sm[:, 0:4], ALU.subtract), [PS[4], sm], [sm])
                    S.op("act", actf(sm[:, 12:16], sm[:, 20:24], AF.Exp), [sm], [sm])
                    S.op("act", actf(eg[0:64, :], gcrow[0:64, 0::2, 127], AF.Exp), [PS[4]], [eg])
                    S.op("act", actf(eg[64:128, :], gcrow[64:128, 1::2, 127], AF.Exp), [PS[4]], [eg])
                    if CUT < 3:
                        return
                    pb7 = psb(7)
                    for c in range(2):
                        S.op("pe", trp(pb7[:, c * 128:(c + 1) * 128], QKn[:, 2 + c, tc_], ident.ap), [QKn, ident], [PS[7]])
                    for c in range(2):
                        S.op("pe", trp(pb7[:, 256 + c * 128:256 + (c + 1) * 128], vb[:, c, tc_], ident.ap),
                             [vb, ident], [PS[7]])
                    S.op("act", acp(ktok.ap.rearrange("p h d -> p (h d)"), pb7[:, 0:256]), [PS[7]], [ktok])
                    S.op("dve", tt(MYr[1][:, :, 128:192], pb7[:, 256:512].rearrange("p (h d) -> p h d", h=4),
                                   bc(bt_.ap, [128, 4, 64], 2), ALU.mult), [PS[7], bt_], [MYr[1]])
                    S.op("dve", tt(MYr[1][:, :, 192:256], ktok.ap, bc(sm[:, 8:12], [128, 4, 64], 2), ALU.mult), [ktok, sm], [MYr[1]])
                    kz = ktok.ap.rearrange("p (pr par) d -> p pr par d", par=2)
                    ekv = sm[:, 12:16].rearrange("p (pr par) -> p pr par", par=2)
                    for par in range(2):
                        S.op("dve", tt(ktz[:, :, par, par * 64:(par + 1) * 64], kz[:, :, par, :],
                                       bc(ekv[:, :, par], [128, 2, 64], 2), ALU.mult), [ktok, sm], [ktz])
                    if CUT < 3.2:
                        return
                    for h in range(4):
                        pr, par = h // 2, h % 2
                        kz_ = KnZ[:, pr, par, tc_]
                        S.op("pe", mm(PS[5][:, h * 128:(h + 1) * 128], kz_, QKn[:, 2 + pr, tc_]), [QKn, KnZ], [PS[5]])
                        S.op("pe", mm(PS[6][:, h * 128:(h + 1) * 128], QKn[:, pr, tc_], kz_), [QKn, KnZ], [PS[6]])
                    if CUT < 3.4:
                        return
                    S.op("dve", tt(A1.ap.rearrange("p h j -> p (h j)"), PS[5].ap, decay.ap.rearrange("p h j -> p (h j)"),
                                   ALU.mult), [PS[5], decay], [A1])
                    S.op("dve", tt(bS.ap, bc(bt_.ap, [128, 4, 128], 2), bc(W["strict"].ap, [128, 4, 128], 1), ALU.mult),
                         [bt_, W["strict"]], [bS])
                    S.op("dve", tt(MYr[0][:, :, 0:128], A1.ap, bS.ap, ALU.mult), [A1, bS], [MYr[0]])
                    S.op("dve", tt(intra.ap.rearrange("p h j -> p (h j)"), PS[6].ap, decay.ap.rearrange("p h j -> p (h j)"),
                                   ALU.mult), [PS[6], decay], [intra])
                    if CUT < 3.6:
                        return
                    for h in range(4):
                        S.op("pe", trp(pb7[:, h * 128:(h + 1) * 128], intra[:, h, :], ident.ap), [intra, ident], [PS[7]])
                    S.op("act", acp(intraT.ap.rearrange("p h j -> p (h j)"), pb7[:, 0:512]), [PS[7]], [intraT])
                    for h in range(4):
                        S.op("pe", trp(pb7[:, 512 + h * 128:512 + (h + 1) * 128], MYr[0][:, h, 0:128], ident.ap), [MYr[0], ident], [PS[7]])
                    if CUT < 3.8:
                        return
                    N = Nr.next()
                    S.op("act", acp(N.ap.rearrange("p h j -> p (h j)"), pb7[:, 512:1024]), [PS[7]], [N])
                    if CUT < 3.85:
                        return
                    S.op("dve", tt(ImLT.ap, bc(ident_b.ap, [128, 4, 128], 1), N.ap, ALU.subtract), [ident_b, N], [ImLT])
                    if CUT < 4:
                        return
                    PA = PSD[2]
                    PAv = PA.rearrange("p (h x) -> p h x", h=4)
                    for k in range(0, 7):
                        cur_, nxt_ = MYr[k % 2], MYr[(k + 1) % 2]
                        if k == 0:
                            c0_, c1_ = 0, 128
                        elif k < 6:
                            c0_, c1_ = 0, 256
                        else:
                            c0_, c1_ = 128, 256
                        for h in range(4):
                            S.op("pe", mm(PA[:, h * 256 + c0_:h * 256 + c1_], N[:, h, :], cur_[:, h, c0_:c1_]),
                                 [N, cur_], [PS[4], PS[5]])
                        if k < 6:
                            N2 = Nr.next()
                            for h in range(4):
                                S.op("pe", mm(PS[6][:, h * 128:(h + 1) * 128], cur_[:, h, 0:128], N[:, h, :]), [cur_, N], [PS[6]])
                            S.op("act", acp(nxt_[:, :, 0:128], PAv[:, :, 0:128]), [PS[4], PS[5]], [nxt_])
                        if k >= 1:
                            dstY = nxt_[:, :, 128:256] if k < 6 else Yf.ap
                            dT = nxt_ if k < 6 else Yf
                            S.op("dve", tt(dstY, PAv[:, :, 128:256], cur_[:, :, 128:256], ALU.add), [PS[4], PS[5], cur_], [dT])
                        if k < 6:
                            S.op("act", acp(N2.ap.rearrange("p h j -> p (h j)"), PS[6].ap), [PS[6]], [N2])
                            N = N2
                    Y = Yf
                    if CUT < 5:
                        return
                    yv = Y[:, :, 64:128].rearrange("p (pr par) d -> p pr par d", par=2)
                    for par in range(2):
                        S.op("dve" if par == 0 else "act", (cp if par == 0 else acp)(Ywz[:, :, par, par * 64:(par + 1) * 64], yv[:, :, par, :]), [Y], [Ywz])
                    for h in range(4):
                        S.op("pe", mm(PS[1][:, h * 64:(h + 1) * 64], ImLT[:, h, :], Y[:, h, 0:64]), [ImLT, Y], [PS[1]])
                    for pr in range(2):
                        for par in range(2):
                            S.op("pe", mm(PS[1][:, 256 + pr * 128:256 + (pr + 1) * 128], Ywz[:, pr, par, :], ImLT[:, 2 * pr + par, :],
                                          par == 0, par == 1), [Ywz, ImLT], [PS[1]])
                    S.op("act", acp(u_sb.ap.rearrange("p h d -> p (h d)"), PS[1][:, 0:256]), [PS[1]], [u_sb])
                    S.op("dve", cp(wT_sb.ap.rearrange("p a t -> p (a t)"), PS[1][:, 256:512]), [PS[1]], [wT_sb])

                def scan(s):
                    CUT = cfg.cut
                    par_s = s % 2
                    sm, eg, ktz, intraT, u_sb, wT_sb = smp[par_s], egp[par_s], ktzp[par_s], intraTp[par_s], u_p[par_s], wT_p[par_s]
                    tc_ = slice(s * 128, (s + 1) * 128)
                    pb2 = psb(2)
                    for h in range(4):
                        pr, par = h // 2, h % 2
                        S.op("pe", mm(PS[2][:, h * 64:(h + 1) * 64], wT_sb[:, pr, :], SbZ[:, pr, par, :]),
                             [wT_sb, SbZ], [PS[2]])
                        S.op("pe", mm(PS[3][:, h * 64:(h + 1) * 64], QKn[:, pr, tc_], SbZ[:, pr, par, :]),
                             [QKn, SbZ], [PS[3]])
                    S.op("dve", tt(vn.ap.rearrange("p h d -> p (h d)"), u_sb.ap.rearrange("p h d -> p (h d)"),
                                   PS[2][:, 0:256], ALU.subtract), [u_sb, PS[2]], [vn])
                    for h in range(4):
                        S.op("pe", mm(PS[3][:, 256 + h * 64:256 + (h + 1) * 64], intraT[:, h, :], vn[:, h, :]),
                             [intraT, vn], [PS[3]])
                    for pr in range(2):
                        for par in range(2):
                            S.op("pe", mm(PS[2][:, 256 + pr * 64:256 + (pr + 1) * 64], ktz[:, pr, par, :], vn[:, 2 * pr + par, :],
                                          par == 0, par == 1), [ktz, vn], [PS[2]])
                    S.op("dve", tt(St_t.ap, St.ap, bc(eg.ap, [128, 2, 64], 2), ALU.mult), [St, eg], [St_t])
                    S.op("dve", tt(St.ap.rearrange("p a d -> p (a d)"), St_t.ap.rearrange("p a d -> p (a d)"),
                                   PS[2][:, 256:384], ALU.add), [St_t, PS[2]], [St])
                    S.op("act", acp(SbZ[0:64, :, 0, :], St[0:64, :, :]), [St], [SbZ])
                    S.op("act", acp(SbZ[64:128, :, 1, :], St[64:128, :, :]), [St], [SbZ])
                    S.op("dve", tt(o_t.ap, PS[3][:, 0:256].rearrange("p (h d) -> p h d", h=4),
                                   bc(sm[:, 4:8], [128, 4, 64], 2), ALU.mult), [PS[3], sm], [o_t])
                    S.op("dve", tt(o_all[:, s].rearrange("p h d -> p (h d)"), o_t.ap.rearrange("p h d -> p (h d)"),
                                   PS[3][:, 256:512], ALU.add), [o_t, PS[3]], [o_all])

                def output_block():
                    o16 = o_all.ap.rearrange("p s h d -> p (s h) d")
                    S.op("act", actf(osq.ap.rearrange("p (x d) -> p x d", d=64), o16, AF.Square), [o_all], [osq])
                    S.op("dve", rsum(smO.ap, osq.ap.rearrange("p (x d) -> p x d", d=64)), [osq], [smO])
                    S.op("act", actf(smO.ap, smO.ap, AF.Sqrt, scale=1.0 / 64, bias=EPS), [smO], [smO])
                    S.op("dve", recip(smO.ap, smO.ap), [smO], [smO])
                    S.op("dve", tt(o16, o16, bc(smO.ap, [128, 16, 64], 2), ALU.mult), [o_all, smO], [o_all])
                    S.op("dve", tt(o16, o16, bc(W["go"].ap, [128, 16, 64], 1), ALU.mult), [o_all, W["go"]], [o_all])
                    S.op("act", actf(sgz_all.ap, z_all.ap.rearrange("p s z -> p (s z)"), AF.Silu), [z_all], [sgz_all])
                    S.op("dve", tt(gated_all.ap.rearrange("p s z -> p (s z)"), o_all.ap.rearrange("p s h d -> p (s h d)"),
                                   sgz_all.ap, ALU.mult), [o_all, sgz_all], [gated_all])
                    pbo = psb(2)
                    for s_ in range(4):
                        for c in range(2):
                            S.op("pe", trp(pbo[:, (s_ * 2 + c) * 128:(s_ * 2 + c + 1) * 128], gated_all[:, s_, c * 128:(c + 1) * 128],
                                           ident.ap), [gated_all, ident], [PS[2]])
                    S.op("act", acp(Gt.ap.rearrange("p c (s t) -> p s c t", s=4),
                                    pbo[:, 0:1024].rearrange("p (s c t) -> p s c t", s=4, c=2)), [PS[2]], [Gt])

                local(0)
                for s in range(4):
                    streams = []
                    if s < 3:
                        S.begin_capture()
                        local(s + 1)
                        streams.append(S.end_capture())
                    S.begin_capture()
                    scan(s)
                    streams.append(S.end_capture())
                    S.replay(streams)
                output_block()
                S.dma("sp", gt2[g, :, :, ti * 512:(ti + 1) * 512], Gt.ap, reads=[Gt], writes=[D_gt2])
            if cfg.cut >= 6:
                S.dma("sp", ssm_p[4 * g:4 * g + 4].rearrange("(pr par) dk dv -> (par dk) pr dv", par=2), St.ap,
                      reads=[St], writes=[D_out])
        S.barrier()
        ar.release()


    def phase_gdn_sample(W):
        ar.mark()
        P, PH = NS, NSEQ * 8
        hb = ar.alloc([1024], BF16, "g_hb")
        hTs = ar.alloc([8, P], BF16, "g_hTs")
        qkv_sb = ar.alloc([1536], F32, "qkv_sb")
        z_sb = ar.alloc([512], F32, "z_sb")
        ab_sb = ar.alloc([16], F32, "ab_sb")
        E = ar.alloc([7, 3, 64], F32, "E")
        Wc = ar.alloc([4, 3, 64], F32, "Wc")
        AB = ar.alloc([4, 2], F32, "AB")
        alog = ar.alloc([1], F32, "alog")
        dtb = ar.alloc([1], F32, "dtbL")
        cv = ar.alloc([4, 3, 64], F32, "cv")
        cv2 = ar.alloc([4, 3, 64], F32, "cv2")
        nr = ar.alloc([4, 2], F32, "nr")
        qk = ar.alloc([4, 2, 64], F32, "qkn")
        g1, g2, gg_, eg, bet = [ar.alloc([4], F32, f"sg{i}") for i in range(5)]
        St = ar.alloc([64, 64], F32, "StS")
        tmp = ar.alloc([64, 64], F32, "tmpS")
        kS = ar.alloc([64], F32, "kS")
        dl = ar.alloc([64], F32, "dl")
        o_all = ar.alloc([4, 64], F32, "o_all")
        o_tok = ar.alloc([8, 64], F32, "o_tok")
        o_sq = ar.alloc([8, 64], F32, "o_sq")
        orr = ar.alloc([8], F32, "orr")
        gated = ar.alloc([512], BF16, "g_gated")
        gTs = ar.alloc([4, P], BF16, "g_gTs")
        yv = ar.alloc([1024], F32, "yv")
        st = stat.next()
        rms_rstd(xs1[0:P, :], xs1, 1024, st, junk, P)
        S.op("dve", ts(hb[0:P, :], xs1[0:P, :], st[0:P, 0:1], ALU.mult), [xs1, st], [hb])
        pb = psb(0)
        for c in range(8):
            S.op("pe", trp(pb[:, c * P:(c + 1) * P], hb[0:P, c * 128:(c + 1) * 128], ident[0:P, 0:P]), [hb, ident], [PS[0]])
        S.op("act", acp(hTs.ap, pb[:, 0:8 * P].rearrange("p (c t) -> p c t", c=8)), [PS[0]], [hTs])
        for (bank, c0, w, dst) in ((1, 0, 512, qkv_sb[0:P, 0:512]), (2, 512, 512, qkv_sb[0:P, 512:1024]),
                                   (3, 1024, 512, qkv_sb[0:P, 1024:1536]), (4, 1536, 512, z_sb[0:P, :]),
                                   (5, 2048, 16, ab_sb[0:P, :])):
            for kc in range(8):
                S.op("pe", mm(PS[bank][0:P, 0:w], hTs[:, kc, :], W["in"][:, kc, c0:c0 + w], kc == 0, kc == 7),
                     [hTs, W["in"]], [PS[bank]])
            dT = qkv_sb if bank <= 3 else (z_sb if bank == 4 else ab_sb)
            S.op("act", acp(dst, PS[bank][0:P, 0:w]), [PS[bank]], [dT])
        S.dma("sp", qs_scr, qkv_sb[0:P, :], reads=[qkv_sb], writes=[D_qs])
        S.dma("sp", ab_scr, ab_sb[0:P, :], reads=[ab_sb], writes=[D_ab])
        S.dma("pool", E[0:PH, 0:3].rearrange("p r s d -> p (r s d)"), st_convL, writes=[E])
        S.dma("pool", Wc[0:PH].rearrange("p j s d -> p (j s d)"), b_w_convL, writes=[Wc])
        S.dma("pool", alog[0:PH, :], b_alogL, writes=[alog])
        S.dma("pool", dtb[0:PH, :], b_dtbL, writes=[dtb])
        S.dma("pool", St[0:PH].rearrange("p k v -> p (k v)"), st_ssm, writes=[St])
        for n in range(NSEQ):
            q_ = dmaq.next()
            for t in range(4):
                S.dma(q_, E[n * 8:(n + 1) * 8, 3 + t, :, :],
                      qs_scr[n * 4 + t].rearrange("(s h d) -> h s d", s=3, h=8), reads=[D_qs], writes=[E])
            S.dma(q_, AB[n * 8:(n + 1) * 8, :, :], ab_scr[n * 4:(n + 1) * 4, :].rearrange("t (x h) -> h t x", x=2),
                  reads=[D_ab], writes=[AB], allow_slow_non_contiguous=True)
        S.dma("sp", conv_s, E[0:PH, 4:7].rearrange("p r s d -> p (r s d)"), reads=[E], writes=[D_out])
        H = slice(0, PH)
        S.op("dve", tt(cv[H], E[H, 0:4], bc(Wc[H, 0], [PH, 4, 3, 64], 1), ALU.mult), [E, Wc], [cv])
        for j_ in range(1, 4):
            S.op("dve", tt(cv2[H], E[H, j_:j_ + 4], bc(Wc[H, j_], [PH, 4, 3, 64], 1), ALU.mult), [E, Wc], [cv2])
            S.op("dve", tt(cv[H], cv[H], cv2[H], ALU.add), [cv, cv2], [cv])
        S.op("act", actf(cv[H], cv[H], AF.Silu), [cv], [cv])
        S.op("act", actf(cv2[H, :, 0:2, :], cv[H, :, 0:2, :], AF.Square), [cv], [cv2])
        S.op("dve", rsum(nr[H], cv2[H, :, 0:2, :]), [cv2], [nr])
        S.op("act", actf(nr[H], nr[H], AF.Sqrt, bias=EPS), [nr], [nr])
        S.op("dve", recip(nr[H], nr[H]), [nr], [nr])
        S.op("dve", tt(qk[H], cv[H, :, 0:2, :], bc(nr[H], [PH, 4, 2, 64], 3), ALU.mult), [cv, nr], [qk])
        S.op("dve", ts(qk[H, :, 0, :], qk[H, :, 0, :], 0.125, ALU.mult), [qk], [qk])
        S.op("dve", ts(g1[H], AB[H, :, 0], dtb[H, 0:1], ALU.add), [AB, dtb], [g1])
        S.op("dve", stt(g2[H], g1[H], -1.0, g1[H], ALU.mult, ALU.max), [g1], [g2])
        S.op("act", actf(g2[H], g2[H], AF.Exp, scale=-1.0), [g2], [g2])
        S.op("act", actf(g2[H], g2[H], AF.Ln, bias=1.0), [g2], [g2])
        S.op("dve", stt(g1[H], g1[H], 0.0, g2[H], ALU.max, ALU.add), [g1, g2], [g1])
        S.op("act", actf(alog[H], alog[H], AF.Exp), [alog], [alog])
        S.op("dve", ts(gg_[H], g1[H], alog[H, 0:1], ALU.mult, -1.0, ALU.mult), [g1, alog], [gg_])
        S.op("act", actf(eg[H], gg_[H], AF.Exp), [gg_], [eg])
        S.op("act", actf(g2[H], AB[H, :, 1], AF.Exp, scale=-1.0), [AB], [g2])
        S.op("dve", ts(g2[H], g2[H], 1.0, ALU.add), [g2], [g2])
        S.op("dve", recip(bet[H], g2[H]), [g2], [bet])
        for t in range(4):
            q_t, k_t, v_t = qk[H, t, 0, :], qk[H, t, 1, :], cv[H, t, 2, :]
            S.op("dve", ts(St[H], St[H], eg[H, t:t + 1], ALU.mult), [St, eg], [St])
            S.op("dve", tt(tmp[H], St[H], bc(k_t, [PH, 64, 64], 2), ALU.mult), [St, qk], [tmp])
            S.op("dve", rsum(kS[H], tmp[H].rearrange("p k v -> p v k")), [tmp], [kS])
            S.op("dve", tt(dl[H], v_t, kS[H], ALU.subtract), [cv, kS], [dl])
            S.op("dve", ts(dl[H], dl[H], bet[H, t:t + 1], ALU.mult), [dl, bet], [dl])
            S.op("dve", tt(tmp[H], bc(k_t, [PH, 64, 64], 2), bc(dl[H], [PH, 64, 64], 1), ALU.mult), [qk, dl], [tmp])
            S.op("dve", tt(St[H], St[H], tmp[H], ALU.add), [St, tmp], [St])
            S.op("dve", tt(tmp[H], St[H], bc(q_t, [PH, 64, 64], 2), ALU.mult), [St, qk], [tmp])
            S.op("dve", rsum(o_all[H, t, :], tmp[H].rearrange("p k v -> p v k")), [tmp], [o_all])
        S.dma("sp", ssm_s, St[0:PH].rearrange("p k v -> p (k v)"), reads=[St], writes=[D_out])
        S.dma("sp", os_scr, o_all[0:PH].rearrange("p t d -> p (t d)"), reads=[o_all], writes=[D_os])
        for n in range(NSEQ):
            S.dma(dmaq.next(), o_tok[n * 4:(n + 1) * 4], os_scr[n * 8:(n + 1) * 8, :].rearrange("h (t d) -> t h d", t=4),
                  reads=[D_os], writes=[o_tok])
        Pp = slice(0, P)
        S.op("act", actf(o_sq[Pp], o_tok[Pp], AF.Square), [o_tok], [o_sq])
        S.op("dve", rsum(orr[Pp], o_sq[Pp]), [o_sq], [orr])
        S.op("act", actf(orr[Pp], orr[Pp], AF.Sqrt, scale=1.0 / 64, bias=EPS), [orr], [orr])
        S.op("dve", recip(orr[Pp], orr[Pp]), [orr], [orr])
        S.op("dve", tt(o_tok[Pp], o_tok[Pp], bc(orr[Pp], [P, 8, 64], 2), ALU.mult), [o_tok, orr], [o_tok])
        S.op("dve", tt(o_tok[Pp], o_tok[Pp], bc(W["go"][Pp, :], [P, 8, 64], 1), ALU.mult), [o_tok, W["go"]], [o_tok])
        S.op("act", actf(z_sb[Pp, :], z_sb[Pp, :], AF.Silu), [z_sb], [z_sb])
        S.op("dve", tt(gated[Pp, :], o_tok[Pp].rearrange("p h d -> p (h d)"), z_sb[Pp, :], ALU.mult), [o_tok, z_sb], [gated])
        pg = psb(0)
        for c in range(4):
            S.op("pe", trp(pg[:, c * P:(c + 1) * P], gated[0:P, c * 128:(c + 1) * 128], ident[0:P, 0:P]), [gated, ident], [PS[0]])
        S.op("act", acp(gTs.ap, pg[:, 0:4 * P].rearrange("p (c t) -> p c t", c=4)), [PS[0]], [gTs])
        for half in range(2):
            for c in range(4):
                S.op("pe", mm(PS[1 + half][0:P, :], gTs[:, c, :], W["o"][:, c, half * 512:(half + 1) * 512], c == 0, c == 3),
                     [gTs, W["o"]], [PS[1 + half]])
            S.op("dve", tt(yv[0:P, half * 512:(half + 1) * 512], xs1[0:P, half * 512:(half + 1) * 512],
                           PS[1 + half][0:P, :], ALU.add), [xs1, PS[1 + half]], [yv])
        S.dma("sp", y_s, yv[0:P, :], reads=[yv], writes=[D_out])
        S.barrier()
        ar.release()

    phases = cfg.phases or ["mla_p", "p3", "s1", "gdn_p", "s2"]
    if "mla_p" in phases:
        phase_mla_prompt()
    W_o = ar.alloc([4, 1024], BF16, "W_o")
    ar.mark()
    wtmp = ar.alloc([2064], F32, "wtmp_o")
    load_bf16_rows(W_o, [W_o[:, kc, :] for kc in range(4)],
                   [a_w_o[kc * 128:(kc + 1) * 128, :] for kc in range(4)], None, wtmp, 1024)
    S.barrier()
    ar.release()
    if "p3" in phases:
        phase_out_proj(x_p, Buf(), gt1, D_gt1, W_o, xp1 if "gdn_p" in phases else y_p,
                       D_xp1 if "gdn_p" in phases else D_out)
    if "s1" in phases:
        phase_mla_sample()
    S.barrier()
    ar.release()
    if "gdn_p" in phases or "s2" in phases:
        ar.mark()
        junk = ar.alloc([1024], BF16, "junk2")
        stat = Rot([ar.alloc([8], F32, f"statb{i}") for i in range(4)])
        WG = load_gdn_weights()
        if "gdn_p" in phases:
            phase_gdn_prompt(WG)
            phase_out_proj(xp1, D_xp1, gt2, D_gt2, WG["o"], y_p, D_out)
        if "s2" in phases:
            phase_gdn_sample(WG)

    S.barrier(engines=("sp",))
    S.op("sp", lambda e: e.nop(), [D_out, D_lat, D_xp1, D_gt1, D_gt2, D_qs, D_ab, D_os, D_h1, D_h2], [])
    S.emit()
    return nc


def host_consts(SEQ, NSEQ, past_len):
    i = np.arange(128)
    c = {}
    c["c_ident"] = np.eye(128, dtype=np.float32)
    c["c_tri"] = (i[:, None] <= i[None, :]).astype(np.float32)
    c["c_strict"] = (i[:, None] > i[None, :]).astype(np.float32)
    c["c_negm"] = np.where(i[:, None] >= i[None, :], 0.0, NEG).astype(np.float32)
    c["c_blk"] = ((i[:, None] // 64) == (i[None, :] // 64)).astype(np.float32)
    q = np.arange(512)
    am = np.zeros((128, 4, 512), np.float32)
    for j in range(4):
        am[:, j, :] = ((128 * j + i)[:, None] <= q[None, :])
    c["c_amask"] = am
    NS = NSEQ * 4
    sm = np.zeros((NS, NSEQ, 8, 4), np.float32)
    for n in range(NSEQ):
        for tk in range(4):
            for tq in range(4):
                if tk <= tq:
                    sm[n * 4 + tk, n, :, tq] = 1.0
    c["c_smask"] = sm.reshape(NS, NSEQ * 32)

    def rope_tab(pos):
        half = 16
        inv = (np.float32(10000.0) ** (-np.arange(half, dtype=np.float32) / np.float32(half))).astype(np.float32)
        ang = pos.astype(np.float32)[:, None] * inv[None, :]
        cos = np.cos(ang).astype(np.float32)
        sin = np.sin(ang).astype(np.float32)
        return np.concatenate([cos, cos, -sin, sin], axis=1).astype(np.float32)

    c["c_rope_p"] = rope_tab(np.arange(SEQ))
    c["c_rope_s"] = rope_tab(np.tile(past_len + np.arange(4), NSEQ))
    return c


def core_inputs(inp, core, n_cores, SEQ, NSEQ, consts):
    b = core % inp["x_prompt"].shape[0]
    f = np.ascontiguousarray
    m = dict(consts)
    m["x_p"] = f(inp["x_prompt"][b])
    m["x_s"] = f(inp["x_sample"][core * NSEQ:(core + 1) * NSEQ].reshape(NSEQ * 4, 1024))
    m["cache_lat"] = inp["cache_latent"][0].reshape(-1, 128 * 256)
    m["cache_kr"] = inp["cache_krope"][0].reshape(-1, 128 * 32)
    m["ptab"] = f(inp["page_table"][core * NSEQ:(core + 1) * NSEQ].T.astype(np.int32))
    m["st_conv"] = f(inp["state_conv"][0, core * NSEQ:(core + 1) * NSEQ])
    m["st_ssm"] = f(inp["state_ssm"][0, core * NSEQ:(core + 1) * NSEQ].reshape(NSEQ * 8, 4096))
    m["a_norm"] = f(inp["a_norm"][0].reshape(8, 128).T)
    m["a_w_in"] = inp["a_w_in"][0]
    m["a_g_qa"] = f(inp["a_g_qa"][0].reshape(3, 128).T)
    m["a_w_uq"] = inp["a_w_uq"][0]
    m["a_g_kv"] = inp["a_g_kv"][0].reshape(1, 256)
    m["a_w_uk"] = inp["a_w_uk"][0].reshape(256, 512)
    m["a_w_uv"] = inp["a_w_uv"][0].reshape(256, 512)
    m["a_g_q"] = inp["a_g_q"][0].reshape(1, 96)
    m["a_g_k"] = inp["a_g_k"][0].reshape(1, 96)
    m["a_w_o"] = inp["a_w_o"][0]
    m["b_norm"] = f(inp["b_norm"][0].reshape(8, 128).T)
    m["b_w_in"] = inp["b_w_in"][0]
    m["b_w_conv"] = inp["b_w_conv"][0]
    m["b_w_convT"] = f(inp["b_w_conv"][0].reshape(4, 12, 128).transpose(2, 1, 0).reshape(128, 48))
    sc_ = inp["state_conv"][0, core * NSEQ:(core + 1) * NSEQ]
    m["st_convL"] = f(sc_.reshape(NSEQ, 3, 3, 8, 64).transpose(0, 3, 1, 2, 4).reshape(NSEQ * 8, 576))
    wc_ = inp["b_w_conv"][0].reshape(4, 3, 8, 64).transpose(2, 0, 1, 3).reshape(8, 768)
    m["b_w_convL"] = f(np.tile(wc_, (NSEQ, 1)))
    m["b_alogL"] = f(np.tile(inp["b_a_log"][0].reshape(8, 1), (NSEQ, 1)))
    m["b_dtbL"] = f(np.tile(inp["b_dt_bias"][0].reshape(8, 1), (NSEQ, 1)))
    m["b_a_log"] = inp["b_a_log"][0].reshape(1, 8)
    m["b_dt_bias"] = inp["b_dt_bias"][0].reshape(1, 8)
    m["b_g_o"] = inp["b_g_o"][0].reshape(1, 64)
    m["b_w_o"] = inp["b_w_o"][0]
    return m


def run(inp, n_cores, cfg):
    inp = {k: np.asarray(v) for k, v in inp.items()}
    past_len = inp["page_table"].shape[1] * 128
    consts = host_consts(cfg.SEQ, cfg.NSEQ, past_len)
    nc = build(cfg)
    in_maps = [core_inputs(inp, c, n_cores, cfg.SEQ, cfg.NSEQ, consts) for c in range(n_cores)]
    res = run_bass_kernel_spmd(nc, in_maps, core_ids=list(range(n_cores)))
    return res.results


def fix_out(name, a, NSEQ):
    if name == "conv_s":
        return np.ascontiguousarray(a.reshape(NSEQ, 8, 3, 3, 64).transpose(0, 2, 3, 1, 4)).reshape(NSEQ, 3, 1536)
    return a


def kernel(**inputs):
    cfg = Cfg()
    r = run(inputs, 8, cfg)
    B = 4
    y_p = np.stack([r[b]["y_p"] for b in range(B)])
    y_s = np.concatenate([r[c]["y_s"].reshape(16, 4, 1024) for c in range(8)])
    lat_p = np.stack([r[b]["lat_p"] for b in range(B)])[None]
    kr_p = np.stack([r[b]["kr_p"] for b in range(B)])[None]
    lat_s = np.concatenate([r[c]["lat_s"].reshape(16, 4, 256) for c in range(8)])[None]
    kr_s = np.concatenate([r[c]["kr_s"].reshape(16, 4, 32) for c in range(8)])[None]
    conv_p = np.stack([r[b]["conv_p"] for b in range(B)])[None]
    ssm_p = np.stack([r[b]["ssm_p"] for b in range(B)])[None]
    conv_s = np.concatenate([fix_out("conv_s", r[c]["conv_s"], 16) for c in range(8)])[None]
    ssm_s = np.concatenate([r[c]["ssm_s"].reshape(16, 8, 64, 64) for c in range(8)])[None]
    return (y_p, y_s, lat_p, kr_p, lat_s, kr_s, conv_p, ssm_p, conv_s, ssm_s)
```

```python
import numpy as np
import concourse.bass as bass
import concourse.mybir as mybir
from concourse.bass_utils import run_bass_kernel_spmd

F32 = mybir.dt.float32
BF16 = mybir.dt.bfloat16
I32 = mybir.dt.int32
U8 = mybir.dt.uint8
ALU = mybir.AluOpType
AF = mybir.ActivationFunctionType
AX = mybir.AxisListType

ENGS = ("pe", "act", "dve", "pool", "sp")
NDSEM = 16
EPS = 1e-6
NEG = -30000.0


class Buf:
    __slots__ = ("name", "w", "r", "excl")

    def __init__(self, name="", excl=False):
        self.name = name
        self.w = None
        self.r = []
        self.excl = excl


class T:
    def __init__(self, ap, name=""):
        self.ap = ap
        self.b = Buf(name)

    def __getitem__(self, k):
        return self.ap[k]


class Op:
    __slots__ = ("eng", "fn", "waits", "idx", "dma", "dsem", "dval", "signal")


def _b(x):
    return x.b if isinstance(x, T) else x


class Sched:
    def __init__(self, nc, self_sync=("act", "dve", "pool")):
        self.nc = nc
        self.ops = {e: [] for e in ENGS}
        self.seen = {e: {} for e in ENGS}
        self.ndma = {e: 0 for e in ENGS}
        self.self_sync = set(self_sync)
        self.last_dma = {}

    def _waits(self, eng, deps):
        waits = []
        seen = self.seen[eng]
        for d in deps:
            if d.dma:
                key = ("d", d.eng, d.dsem)
                val = d.dval
            else:
                if d.fn is None:
                    continue
                if d.eng == eng and eng not in self.self_sync:
                    continue
                key = ("e", d.eng)
                val = d.idx + 1
            if seen.get(key, 0) >= val:
                continue
            seen[key] = val
            waits.append(d)
            d.signal = True
        return waits

    def _mk(self, eng, fn, reads, writes, dma=False):
        if getattr(self, "cap", None) is not None:
            self.cap.append((eng, fn, list(reads), list(writes), dma))
            return None
        reads = [_b(x) for x in reads]
        writes = [_b(x) for x in writes]
        if eng != "pe":
            ex = [b for b in reads if b.excl and b not in writes]
            if ex:
                writes = writes + ex
                reads = [b for b in reads if not b.excl]
        op = Op()
        op.eng = eng
        op.fn = fn
        op.dma = dma
        op.signal = False
        op.idx = len(self.ops[eng])
        deps = []
        for b in reads:
            if b.w is not None:
                deps.append(b.w)
        for b in writes:
            if b.w is not None:
                deps.append(b.w)
            deps.extend(b.r)
        if dma:
            n = self.ndma[eng]
            self.ndma[eng] = n + 1
            op.dsem = n % NDSEM
            op.dval = 16 * (n // NDSEM + 1)
            op.signal = True
            prev = self.last_dma.get((eng, op.dsem))
            if prev is not None:
                deps.append(prev)
            self.last_dma[(eng, op.dsem)] = op
        op.waits = self._waits(eng, deps)
        self.ops[eng].append(op)
        for b in reads:
            b.r.append(op)
        for b in writes:
            b.w = op
            b.r = []
        return op

    def op(self, eng, fn, reads=(), writes=()):
        return self._mk(eng, fn, reads, writes)

    def begin_capture(self):
        self.cap = []

    def end_capture(self):
        c = self.cap
        self.cap = None
        return c

    def replay(self, streams):
        streams = [st for st in streams if st]
        pos = [0] * len(streams)
        total = sum(len(st) for st in streams)
        for _ in range(total):
            k = min(range(len(streams)), key=lambda i: (pos[i] / len(streams[i])) if pos[i] < len(streams[i]) else 2.0)
            a = streams[k][pos[k]]
            pos[k] += 1
            self._mk(*a)

    def dma(self, eng, out, in_, reads=(), writes=(), **kw):
        return self._mk(eng, lambda e: e.dma_start(out=out, in_=in_, **kw), reads, writes, dma=True)

    def dma_fn(self, eng, fn, reads=(), writes=()):
        return self._mk(eng, fn, reads, writes, dma=True)

    def barrier(self, engines=ENGS):
        last = []
        for e in ENGS:
            for o in reversed(self.ops[e]):
                if not o.dma and o.fn is not None:
                    last.append(o)
                    break
        deps = last + list(self.last_dma.values())
        for e in engines:
            op = Op()
            op.eng = e
            op.fn = None
            op.dma = False
            op.signal = False
            op.idx = len(self.ops[e])
            op.waits = self._waits(e, [d for d in deps if d.dma or d.eng != e])
            self.ops[e].append(op)

    def emit(self):
        nc = self.nc
        esem = {e: nc.alloc_semaphore(name=f"es_{e}") for e in ENGS}
        dsem = {e: [nc.alloc_semaphore(name=f"ds_{e}{i}") for i in range(NDSEM)]
                for e in ENGS if self.ndma[e] > 0}
        signum = {}
        for e in ENGS:
            n = 0
            for o in self.ops[e]:
                if o.dma or o.fn is None:
                    continue
                if o.signal:
                    n += 1
                    signum[id(o)] = n

        def run(e, eng):
            for o in self.ops[e]:
                for d in o.waits:
                    if d.dma:
                        eng.wait_ge(dsem[d.eng][d.dsem], d.dval)
                    else:
                        eng.wait_ge(esem[d.eng], signum[id(d)])
                if o.fn is None:
                    continue
                ins = o.fn(eng)
                if o.dma:
                    ins.then_inc(dsem[e][o.dsem], 16)
                elif o.signal:
                    ins.then_inc(esem[e], 1)

        with nc.Block() as block:
            @block.tensor
            def _(eng):
                run("pe", eng)

            @block.scalar
            def _(eng):
                run("act", eng)

            @block.vector
            def _(eng):
                run("dve", eng)

            @block.gpsimd
            def _(eng):
                run("pool", eng)

            @block.sync
            def _(eng):
                run("sp", eng)


class Arena:
    def __init__(self, base, nbytes):
        self.base = base
        self.nbytes = nbytes
        self.off = 0
        self.marks = []
        self.n = 0

    def alloc(self, shape_free, dtype, name=None):
        esz = {F32: 4, BF16: 2, I32: 4}[dtype]
        n = int(np.prod(shape_free))
        nb = (n * esz + 63) // 64 * 64
        assert self.off + nb <= self.nbytes, ("SBUF arena overflow", name, self.off, nb, self.nbytes)
        ap = self.base[:, self.off:self.off + n * esz].bitcast(dtype)
        self.off += nb
        if len(shape_free) > 1:
            names = " ".join(f"a{i}" for i in range(len(shape_free)))
            kw = {f"a{i}": int(s) for i, s in enumerate(shape_free)}
            ap = ap.rearrange(f"p ({names}) -> p {names}", **kw)
        self.n += 1
        return T(ap, name or f"t{self.n}")

    def mark(self):
        self.marks.append(self.off)

    def release(self):
        self.off = self.marks.pop()


class Rot:
    def __init__(self, items):
        self.items = items
        self.i = 0

    def next(self):
        t = self.items[self.i % len(self.items)]
        self.i += 1
        return t


def mm(out, lhsT, rhs, start=True, stop=True):
    return lambda e: e.matmul(out, lhsT=lhsT, rhs=rhs, start=start, stop=stop)


def trp(out, in_, ident):
    return lambda e: e.transpose(out=out, in_=in_, identity=ident)


def actf(out, in_, func, **kw):
    return lambda e: e.activation(out=out, in_=in_, func=func, **kw)


def tt(out, a, b, op):
    return lambda e: e.tensor_tensor(out=out, in0=a, in1=b, op=op)


def ts(out, a, s1, op0, s2=None, op1=None):
    if op1 is None:
        return lambda e: e.tensor_scalar(out=out, in0=a, scalar1=s1, scalar2=None, op0=op0)
    return lambda e: e.tensor_scalar(out=out, in0=a, scalar1=s1, scalar2=s2, op0=op0, op1=op1)


def stt(out, a, s, b, op0, op1):
    return lambda e: e.scalar_tensor_tensor(out=out, in0=a, scalar=s, in1=b, op0=op0, op1=op1)


def cp(out, in_):
    return lambda e: e.tensor_copy(out=out, in_=in_)


def acp(out, in_):
    return lambda e: e.copy(out=out, in_=in_)


def rsum(out, in_):
    return lambda e: e.reduce_sum(out=out, in_=in_, axis=AX.X)


def recip(out, in_):
    return lambda e: e.reciprocal(out=out, in_=in_)


def mset(ap, v):
    return lambda e: e.memset(ap, v)


def bc(ap, shape, axis):
    return ap.unsqueeze(axis).to_broadcast(list(shape))


class Cfg:
    def __init__(self, SEQ=8192, NSEQ=16, NPOOL=20480, debug=False, phases=None, cut=99):
        self.cut = cut
        self.SEQ = SEQ
        self.NSEQ = NSEQ
        self.NPOOL = NPOOL
        self.debug = debug
        self.phases = phases


def build(cfg):
    SEQ, NSEQ, NPOOL = cfg.SEQ, cfg.NSEQ, cfg.NPOOL
    NT = SEQ // 128
    NQ = SEQ // 512
    NS = NSEQ * 4
    nc = bass.Bass("TRN2", target_bir_lowering=False)
    S = Sched(nc)

    def din(name, shape, dt=F32):
        return nc.dram_tensor(name, list(shape), dt, kind="ExternalInput").ap()

    def dout(name, shape, dt=F32):
        return nc.dram_tensor(name, list(shape), dt, kind="ExternalOutput").ap()

    def dscr(name, shape, dt=F32):
        kind = "ExternalOutput" if cfg.debug else "Internal"
        return nc.dram_tensor(name, list(shape), dt, kind=kind).ap()

    x_p = din("x_p", [SEQ, 1024])
    x_s = din("x_s", [NS, 1024])
    cache_lat = din("cache_lat", [NPOOL, 128 * 256])
    cache_kr = din("cache_kr", [NPOOL, 128 * 32])
    ptab = din("ptab", [128, NSEQ], I32)
    st_conv = din("st_conv", [NSEQ, 3, 1536])
    st_ssm = din("st_ssm", [NSEQ * 8, 4096])
    a_norm = din("a_norm", [128, 8])
    a_w_in = din("a_w_in", [1024, 1184])
    a_g_qa = din("a_g_qa", [128, 3])
    a_w_uq = din("a_w_uq", [384, 768])
    a_g_kv = din("a_g_kv", [1, 256])
    a_w_uk = din("a_w_uk", [256, 512])
    a_w_uv = din("a_w_uv", [256, 512])
    a_g_q = din("a_g_q", [1, 96])
    a_g_k = din("a_g_k", [1, 96])
    a_w_o = din("a_w_o", [512, 1024])
    b_norm = din("b_norm", [128, 8])
    b_w_in = din("b_w_in", [1024, 2064])
    b_w_conv = din("b_w_conv", [4, 1536])
    b_w_convT = din("b_w_convT", [128, 48])
    PHn = NSEQ * 8
    b_w_convL = din("b_w_convL", [PHn, 4 * 192])
    st_convL = din("st_convL", [PHn, 3 * 192])
    b_alogL = din("b_alogL", [PHn, 1])
    b_dtbL = din("b_dtbL", [PHn, 1])
    b_a_log = din("b_a_log", [1, 8])
    b_dt_bias = din("b_dt_bias", [1, 8])
    b_g_o = din("b_g_o", [1, 64])
    b_w_o = din("b_w_o", [512, 1024])
    c_ident = din("c_ident", [128, 128])
    c_tri = din("c_tri", [128, 128])
    c_strict = din("c_strict", [128, 128])
    c_negm = din("c_negm", [128, 128])
    c_blk = din("c_blk", [128, 128])
    c_amask = din("c_amask", [128, 4, 512])
    c_smask = din("c_smask", [NS, NSEQ * 32])
    c_rope_p = din("c_rope_p", [SEQ, 64])
    c_rope_s = din("c_rope_s", [NS, 64])

    y_p = dout("y_p", [SEQ, 1024])
    y_s = dout("y_s", [NS, 1024])
    lat_p = dout("lat_p", [SEQ, 256])
    kr_p = dout("kr_p", [SEQ, 32])
    lat_s = dout("lat_s", [NS, 256])
    kr_s = dout("kr_s", [NS, 32])
    conv_p = dout("conv_p", [3, 1536])
    ssm_p = dout("ssm_p", [8, 64, 64])
    conv_s = dout("conv_s", [NSEQ * 8, 3 * 192])
    ssm_s = dout("ssm_s", [NSEQ * 8, 4096])

    gt1 = dscr("gt1", [2, 128, 2, SEQ], BF16)
    xp1 = dscr("xp1", [SEQ, 1024])
    gt2 = dscr("gt2", [2, 128, 2, SEQ], BF16)
    hT1_scr = dscr("hT1_scr", [128, 8, SEQ], BF16)
    hT2_scr = dscr("hT2_scr", [128, 8, SEQ], BF16)
    D_h1, D_h2 = Buf("hT1"), Buf("hT2")
    qs_scr = dscr("qs_scr", [NS, 1536])
    ab_scr = dscr("ab_scr", [NS, 16])
    os_scr = dscr("os_scr", [NSEQ * 8, 256])
    D_gt1, D_xp1, D_gt2 = Buf("gt1"), Buf("xp1"), Buf("gt2")
    D_qs, D_ab, D_os = Buf("qs"), Buf("ab"), Buf("os")
    D_out = Buf("outs")
    D_lat = Buf("lat_out")

    ARENA_BYTES = 204 * 1024
    sb = nc.alloc_sbuf_tensor("arena", [128, ARENA_BYTES], U8)
    ar = Arena(sb.ap(), ARENA_BYTES)
    PSD = [nc.alloc_psum_tensor(f"psd{i}", [128, 1024], F32).ap() for i in range(4)]
    PS = [T(PSD[i // 2][:, (i % 2) * 512:(i % 2 + 1) * 512], f"ps{i}") for i in range(8)]
    for p_ in PS:
        p_.b.excl = True

    def psb(i):
        return PS[i].ap.bitcast(BF16)

    dmaq = Rot(["sp", "pool"])

    ident_f = ar.alloc([128], F32, "ident_f")
    ident = ar.alloc([128], BF16, "ident")
    S.dma("sp", ident_f.ap, c_ident, writes=[ident_f])
    S.op("dve", cp(ident.ap, ident_f.ap), [ident_f], [ident])

    xs1 = ar.alloc([1024], F32, "xs1")
    ptab_sb = ar.alloc([NSEQ], I32, "ptab_sb")
    S.dma("sp", ptab_sb.ap, ptab, writes=[ptab_sb])

    def load_bf16_rows(dst, dst_slices, src_rows, gain, tmp, ncols):
        for kc, rows in enumerate(src_rows):
            S.dma(dmaq.next(), tmp[:, 0:ncols], rows, writes=[tmp])
            if gain is None:
                S.op("dve", cp(dst_slices[kc], tmp[:, 0:ncols]), [tmp], [dst])
            else:
                S.op("dve", ts(dst_slices[kc], tmp[:, 0:ncols], gain[0][:, kc:kc + 1], ALU.mult),
                     [tmp, gain[1]], [dst])

    def rms_rstd(x_ap, xT, n, rstd, junk, parts=128, eng_sq="act"):
        S.op("act", actf(junk[0:parts, 0:n], x_ap, AF.Square, accum_out=rstd[0:parts, 1:2]),
             [xT], [rstd])
        S.op("act", actf(rstd[0:parts, 2:3], rstd[0:parts, 1:2], AF.Sqrt, scale=1.0 / n, bias=EPS),
             [rstd], [rstd])
        S.op("dve", recip(rstd[0:parts, 0:1], rstd[0:parts, 2:3]), [rstd], [rstd])

    def transpose_to(dstT, dst_ap_fn, src, src_ap_fn, nchunk, bank, parts=128, width=128, evac="act"):
        pb = psb(bank)
        for c in range(nchunk):
            S.op("pe", trp(pb[0:width, c * parts:(c + 1) * parts], src_ap_fn(c), ident[0:parts, 0:parts]),
                 [src, ident], [PS[bank]])
        fn = acp if evac == "act" else cp
        S.op(evac, fn(dst_ap_fn(), pb[0:width, 0:nchunk * parts]), [PS[bank]], [dstT])

    def rope(x_view, xT, cs, nh, parts, sw, t1):
        swv = sw[0:parts, 0:nh, :]
        t1v = t1[0:parts, 0:nh, :]
        S.op("pool", cp(swv[:, :, 0:16], x_view[:, :, 16:32]), [xT], [sw])
        S.op("pool", cp(swv[:, :, 16:32], x_view[:, :, 0:16]), [xT], [sw])
        S.op("dve", tt(t1v, swv, bc(cs[0:parts, 32:64], [parts, nh, 32], 1), ALU.mult), [sw, cs], [t1])
        S.op("dve", tt(x_view, x_view, bc(cs[0:parts, 0:32], [parts, nh, 32], 1), ALU.mult), [xT, cs], [xT])
        S.op("dve", tt(x_view, x_view, t1v, ALU.add), [xT, t1], [xT])

    ar.mark()
    an_sb = ar.alloc([8], F32, "an_sb")
    gqa_sb = ar.alloc([3], F32, "gqa_sb")
    S.dma("sp", an_sb.ap, a_norm, writes=[an_sb])
    S.dma("sp", gqa_sb.ap, a_g_qa, writes=[gqa_sb])
    W_in = ar.alloc([8, 1184], BF16, "W_in")
    W_uq = ar.alloc([3, 768], BF16, "W_uq")
    W_uk = ar.alloc([2, 512], BF16, "W_uk")
    W_uv = ar.alloc([2, 512], BF16, "W_uv")
    amask = ar.alloc([4, 512], BF16, "amask")
    gkv_b = ar.alloc([256], F32, "gkv_b")
    gg_b = ar.alloc([96], F32, "gg_b")
    gk_t = ar.alloc([96], F32, "gk_t")
    ar.mark()
    wtmp = ar.alloc([2064], F32, "wtmp")
    load_bf16_rows(W_in, [W_in[:, kc, :] for kc in range(8)],
                   [a_w_in[kc * 128:(kc + 1) * 128, :] for kc in range(8)], (an_sb, an_sb), wtmp, 1184)
    load_bf16_rows(W_uq, [W_uq[:, kc, :] for kc in range(3)],
                   [a_w_uq[kc * 128:(kc + 1) * 128, :] for kc in range(3)], (gqa_sb, gqa_sb), wtmp, 768)
    load_bf16_rows(W_uk, [W_uk[:, kc, :] for kc in range(2)],
                   [a_w_uk[kc * 128:(kc + 1) * 128, :] for kc in range(2)], None, wtmp, 512)
    load_bf16_rows(W_uv, [W_uv[:, kc, :] for kc in range(2)],
                   [a_w_uv[kc * 128:(kc + 1) * 128, :] for kc in range(2)], None, wtmp, 512)
    S.dma("sp", wtmp[:, 0:2048], c_amask.rearrange("p a q -> p (a q)"), writes=[wtmp])
    S.op("dve", cp(amask.ap.rearrange("p a q -> p (a q)"), wtmp[:, 0:2048]), [wtmp], [amask])
    S.barrier()
    ar.release()
    S.dma("sp", gkv_b.ap, a_g_kv.to_broadcast([128, 256]), writes=[gkv_b])
    S.dma("sp", gg_b.ap, a_g_q.to_broadcast([128, 96]), writes=[gg_b])
    S.dma("sp", gk_t.ap, a_g_k.to_broadcast([128, 96]), writes=[gk_t])
    S.op("dve", stt(gg_b.ap, gg_b.ap, float(96 ** -0.5), gk_t.ap, ALU.mult, ALU.mult), [gg_b, gk_t], [gg_b])

    junk = ar.alloc([1024], BF16, "junk")
    stat = Rot([ar.alloc([8], F32, f"stat{i}") for i in range(4)])

    def mla_kv_from_cn(cn_ap, cnT, kpe_ap, kpeT, parts, hsel, KTs, kt_cols, V_store_fn, banks, tmp, kcol0=0):
        col0, nh = hsel
        cnb, cT, sq, ssq, Kt = tmp
        bT, bKV, bK = banks
        S.op("act", acp(cnb[0:parts, :], cn_ap), [cnT], [cnb])
        pb = psb(bT)
        for c in range(2):
            S.op("pe", trp(pb[:, c * parts:(c + 1) * parts], cnb[0:parts, c * 128:(c + 1) * 128],
                           ident[0:parts, 0:parts]), [cnb, ident], [PS[bT]])
        S.op("act", acp(cT[:, :, 0:parts], pb[:, 0:2 * parts].rearrange("p (c t) -> p c t", c=2)),
             [PS[bT]], [cT])
        w = nh * 64
        for kc in range(2):
            S.op("pe", mm(PS[bKV][0:parts, 0:w], cT[:, kc, 0:parts], W_uk[:, kc, col0:col0 + w],
                          kc == 0, kc == 1), [cT, W_uk], [PS[bKV]])
        if V_store_fn is not None:
            for kc in range(2):
                S.op("pe", mm(PS[bKV][0:parts, 256:256 + w], cT[:, kc, 0:parts], W_uv[:, kc, col0:col0 + w],
                              kc == 0, kc == 1), [cT, W_uv], [PS[bKV]])
            V_store_fn(PS[bKV])
        S.op("act", actf(sq[0:parts, 0:w], PS[bKV][0:parts, 0:w], AF.Square), [PS[bKV]], [sq])
        S.op("dve", rsum(ssq[0:parts, 0:nh], sq[0:parts, 0:w].rearrange("p (h d) -> p h d", h=nh)), [sq], [ssq])
        S.op("act", actf(junk[0:parts, 0:32], kpe_ap, AF.Square, accum_out=ssq[0:parts, 8:9]), [kpeT], [junk, ssq])
        S.op("dve", ts(ssq[0:parts, 0:nh], ssq[0:parts, 0:nh], ssq[0:parts, 8:9], ALU.add), [ssq], [ssq])
        S.op("act", actf(ssq[0:parts, 0:nh], ssq[0:parts, 0:nh], AF.Sqrt, scale=1.0 / 96, bias=EPS), [ssq], [ssq])
        S.op("dve", recip(ssq[0:parts, 0:nh], ssq[0:parts, 0:nh]), [ssq], [ssq])
        S.op("dve", tt(Kt[0:parts, 0:nh, 0:64], PS[bKV][0:parts, 0:w].rearrange("p (h d) -> p h d", h=nh),
                       bc(ssq[0:parts, 0:nh], [parts, nh, 64], 2), ALU.mult), [PS[bKV], ssq], [Kt])
        S.op("dve", tt(Kt[0:parts, 0:nh, 64:96], bc(kpe_ap, [parts, nh, 32], 1),
                       bc(ssq[0:parts, 0:nh], [parts, nh, 32], 2), ALU.mult), [kpeT, ssq], [Kt])
        pk = psb(bK)
        for h in range(nh):
            S.op("pe", trp(pk[0:96, kcol0 + h * parts:kcol0 + (h + 1) * parts], Kt[0:parts, h, :], ident[0:parts, 0:parts]),
                 [Kt, ident], [PS[bK]])
        return pk

    def ckp_from_proj(ps_c, ps_k, psT, parts, cs, cn, kpe, st, sw, t1):
        rms_rstd(ps_c, psT, 256, st, junk, parts)
        S.op("dve", stt(cn[0:parts, :], ps_c, st[0:parts, 0:1], gkv_b[0:parts, :], ALU.mult, ALU.mult),
             [psT, st, gkv_b], [cn])
        S.op("act", acp(kpe[0:parts, :], ps_k), [psT], [kpe])
        rope(kpe[0:parts, :].rearrange("p (h d) -> p h d", h=1), kpe, cs, 1, parts, sw, t1)

    def phase_mla_prompt():
        ar.mark()
        KT = ar.alloc([4, SEQ], BF16, "KT")
        V1 = ar.alloc([NT, 2, 192], BF16, "V1")
        S.op("pool", mset(V1.ap, 1.0), [], [V1])
        xt = [ar.alloc([1024], F32, f"xt{i}") for i in range(2)]
        hb = [ar.alloc([1024], BF16, f"hb{i}") for i in range(2)]
        hT4 = ar.alloc([8, 512], BF16, "hT4")
        hTv = [T(hT4[:, :, i * 128:(i + 1) * 128], f"hTv{i}") for i in range(4)]
        cs_t = [ar.alloc([64], F32, f"cs{i}") for i in range(2)]
        sw = [ar.alloc([4, 32], F32, f"sw{i}") for i in range(2)]
        t1 = [ar.alloc([4, 32], F32, f"t1{i}") for i in range(2)]
        sq = [ar.alloc([384], F32, f"sq{i}") for i in range(2)]
        ssq = [ar.alloc([16], F32, f"ssq{i}") for i in range(2)]
        junk2 = [junk, junk]

        def load_h(tile, par, dstT, bank):
            x, h_ = xt[par], hb[par]
            S.dma("sp", x.ap, x_p[tile * 128:(tile + 1) * 128, :], writes=[x])
            st = stat.next()
            rms_rstd(x.ap, x, 1024, st, junk2[par])
            S.op("dve", ts(h_.ap, x.ap, st[:, 0:1], ALU.mult), [x, st], [h_])
            pb = psb(bank)
            for c in range(8):
                S.op("pe", trp(pb[:, c * 128:(c + 1) * 128], h_[:, c * 128:(c + 1) * 128], ident.ap),
                     [h_, ident], [PS[bank]])
            S.op("act", acp(dstT.ap, pb[:, 0:1024].rearrange("p (c t) -> p c t", c=8)), [PS[bank]], [dstT])

        for g in range(2):
            ar.mark()
            cn = [ar.alloc([256], F32, f"cn{i}") for i in range(2)]
            kpe = [ar.alloc([32], F32, f"kpe{i}") for i in range(2)]
            cnb = [ar.alloc([256], BF16, f"cnb{i}") for i in range(2)]
            cT = [ar.alloc([2, 128], BF16, f"cT{i}") for i in range(2)]
            Kt = [ar.alloc([4, 96], BF16, f"Kt{i}") for i in range(2)]
            streams = []
            for t in range(NT):
                par = t % 2
                B0 = 4 * par
                S.begin_capture()
                cn_t, kpe_t = cn[par], kpe[par]
                if g == 0:
                    load_h(t, par, hTv[par], B0)
                    S.dma("pool", hT1_scr[:, :, t * 128:(t + 1) * 128], hTv[par].ap, reads=[hTv[par]], writes=[D_h1])
                    for kc in range(8):
                        S.op("pe", mm(PS[B0 + 1][:, 0:288], hTv[par][:, kc, :], W_in[:, kc, 384:672], kc == 0, kc == 7),
                             [hTv[par], W_in], [PS[B0 + 1]])
                    cs = cs_t[par]
                    S.dma("pool", cs.ap, c_rope_p[t * 128:(t + 1) * 128, :], writes=[cs])
                    st = stat.next()
                    ckp_from_proj(PS[B0 + 1][:, 0:256], PS[B0 + 1][:, 256:288], PS[B0 + 1], 128, cs, cn_t, kpe_t, st,
                                  sw[par], t1[par])
                    S.dma("pool", lat_p[t * 128:(t + 1) * 128, :], cn_t.ap, reads=[cn_t], writes=[D_lat])
                    S.dma("pool", kr_p[t * 128:(t + 1) * 128, :], kpe_t.ap, reads=[kpe_t], writes=[D_lat])
                else:
                    S.dma("sp", cn_t.ap, lat_p[t * 128:(t + 1) * 128, :], reads=[D_lat], writes=[cn_t])
                    S.dma("pool", kpe_t.ap, kr_p[t * 128:(t + 1) * 128, :], reads=[D_lat], writes=[kpe_t])

                def vstore(psT, t=t):
                    pv = psT[:, 256:512].rearrange("p (a b d) -> p a b d", a=2, b=2)
                    S.op("act", acp(V1[:, t, :, 0:64], pv[:, :, 0, :]), [psT], [V1])
                    S.op("act", acp(V1[:, t, :, 128:192], pv[:, :, 1, :]), [psT], [V1])

                pk = mla_kv_from_cn(cn_t.ap, cn_t, kpe_t.ap, kpe_t, 128, (g * 256, 4), KT, None, vstore,
                                    (B0 + 2, B0 + 3, B0 + 2), (cnb[par], cT[par], sq[par], ssq[par], Kt[par]),
                                    kcol0=256)
                S.op("act", acp(KT[0:96, :, t * 128:(t + 1) * 128],
                                pk[0:96, 256:768].rearrange("p (h t) -> p h t", h=4)), [PS[B0 + 2]], [KT])
                streams.append(S.end_capture())
                if len(streams) == 2:
                    S.replay(streams)
                    streams = []
            S.replay(streams)
            S.barrier()
            ar.release()
            ar.mark()
            qan = [ar.alloc([384], BF16, f"qan{i}") for i in range(2)]
            qaT = [ar.alloc([3, 128], BF16, f"qaT{i}") for i in range(2)]
            qs = [ar.alloc([4, 96], F32, f"qs{i}") for i in range(2)]
            qf = [ar.alloc([4, 96], BF16, f"qf{i}") for i in range(2)]
            QT = ar.alloc([4, 512], BF16, "QT")
            QTv = [T(QT[0:96, :, i * 128:(i + 1) * 128], f"QTv{i}") for i in range(4)]
            sz = ar.alloc([2, 512], F32, "sz")
            pT = Rot([ar.alloc([512], BF16, f"pT{i}") for i in range(3)])
            rcp = ar.alloc([512], F32, "rcp")
            tmpo = ar.alloc([512], F32, "tmpo")
            GTt = Rot([ar.alloc([2, 512], BF16, f"GTt{i}") for i in range(1)])
            for qi in range(NQ):
                streams = []
                for s_ in range(4):
                    tile = qi * 4 + s_
                    par = s_ % 2
                    B0 = 4 * par
                    S.begin_capture()
                    S.dma("sp" if s_ % 2 == 0 else "pool", hTv[s_].ap, hT1_scr[:, :, tile * 128:(tile + 1) * 128],
                          reads=[D_h1], writes=[hTv[s_]])
                    for kc in range(8):
                        S.op("pe", mm(PS[B0 + 1][:, 0:384], hTv[s_][:, kc, :], W_in[:, kc, 0:384], kc == 0, kc == 7),
                             [hTv[s_], W_in], [PS[B0 + 1]])
                    st = stat.next()
                    rms_rstd(PS[B0 + 1][:, 0:384], PS[B0 + 1], 384, st, junk2[par])
                    S.op("dve", ts(qan[par].ap, PS[B0 + 1][:, 0:384], st[:, 0:1], ALU.mult), [PS[B0 + 1], st], [qan[par]])
                    pa = psb(B0 + 2)
                    for c in range(3):
                        S.op("pe", trp(pa[:, c * 128:(c + 1) * 128], qan[par][:, c * 128:(c + 1) * 128], ident.ap),
                             [qan[par], ident], [PS[B0 + 2]])
                    S.op("dve", cp(qaT[par].ap, pa[:, 0:384].rearrange("p (c t) -> p c t", c=3)), [PS[B0 + 2]], [qaT[par]])
                    for kc in range(3):
                        S.op("pe", mm(PS[B0 + 3][:, 0:384], qaT[par][:, kc, :], W_uq[:, kc, g * 384:(g + 1) * 384],
                                      kc == 0, kc == 2), [qaT[par], W_uq], [PS[B0 + 3]])
                    q_, sq_, ssq_ = qs[par], sq[par], ssq[par]
                    S.op("act", acp(q_.ap.rearrange("p h d -> p (h d)"), PS[B0 + 3][:, 0:384]), [PS[B0 + 3]], [q_])
                    S.op("act", actf(sq_[:, 0:384], q_.ap.rearrange("p h d -> p (h d)"), AF.Square), [q_], [sq_])
                    S.op("dve", rsum(ssq_[:, 0:4], sq_[:, 0:384].rearrange("p (h d) -> p h d", h=4)), [sq_], [ssq_])
                    S.op("act", actf(ssq_[:, 0:4], ssq_[:, 0:4], AF.Sqrt, scale=1.0 / 96, bias=EPS), [ssq_], [ssq_])
                    S.op("dve", recip(ssq_[:, 0:4], ssq_[:, 0:4]), [ssq_], [ssq_])
                    cs = cs_t[par]
                    S.dma("pool", cs.ap, c_rope_p[tile * 128:(tile + 1) * 128, :], writes=[cs])
                    rope(q_[:, :, 64:96], q_, cs, 4, 128, sw[par], t1[par])
                    S.op("dve", tt(q_.ap, q_.ap, bc(ssq_[:, 0:4], [128, 4, 96], 2), ALU.mult), [q_, ssq_], [q_])
                    S.op("dve", tt(qf[par].ap, q_.ap, bc(gg_b.ap, [128, 4, 96], 1), ALU.mult), [q_, gg_b], [qf[par]])
                    pq = psb(B0 + 2)
                    for h in range(4):
                        S.op("pe", trp(pq[0:96, 384 + h * 128:384 + (h + 1) * 128], qf[par][:, h, :], ident.ap),
                             [qf[par], ident], [PS[B0 + 2]])
                    S.op("dve", cp(QTv[s_].ap, pq[0:96, 384:896].rearrange("p (h t) -> p h t", h=4)), [PS[B0 + 2]], [QTv[s_]])
                    streams.append(S.end_capture())
                    if len(streams) == 2:
                        S.replay(streams)
                        streams = []
                hall = [hTv[i] for i in range(4)]
                qall = [QTv[i] for i in range(4)]
                for c in range(2):
                    col = 672 + g * 256 + c * 128
                    for kc in range(8):
                        S.op("pe", mm(PS[4 + c].ap, W_in[:, kc, col:col + 128], hT4[:, kc, :], kc == 0, kc == 7),
                             [W_in] + hall, [PS[4 + c]])
                    S.op("act", actf(sz[:, c, :], PS[4 + c].ap, AF.Exp, scale=-1.0), [PS[4 + c]], [sz])
                    S.op("dve", ts(sz[:, c, :], sz[:, c, :], 1.0, ALU.add), [sz], [sz])
                    S.op("dve", recip(sz[:, c, :], sz[:, c, :]), [sz], [sz])
                    S.op("dve", tt(sz[:, c, :], PS[4 + c].ap, sz[:, c, :], ALU.mult), [PS[4 + c], sz], [sz])
                G = GTt.next()
                nk = 4 * qi + 4
                sbank = Rot([0, 1, 2, 3])
                units = [(h, kt) for h in range(4) for kt in range(nk)]
                pend = None

                def emit_pv(u):
                    h, kt, p = u
                    ob = 6 + (h % 2)
                    pr, par = h // 2, h % 2
                    lhs = V1[:, kt, pr, 0:128] if par == 0 else V1[:, kt, pr, 64:192]
                    S.op("pe", mm(PS[ob].ap, lhs, p.ap, kt == 0, kt == nk - 1), [V1, p], [PS[ob]])
                    if kt == nk - 1:
                        if par == 0:
                            o_r, s_r = slice(0, 64), slice(64, 128)
                        else:
                            o_r, s_r = slice(64, 128), slice(0, 64)
                        S.op("dve", recip(rcp[s_r, :], PS[ob][s_r, :]), [PS[ob]], [rcp])
                        S.op("dve", tt(tmpo[o_r, :], PS[ob][o_r, :], rcp[s_r, :], ALU.mult), [PS[ob], rcp], [tmpo])
                        S.op("dve", tt(G[o_r, pr, :], tmpo[o_r, :], sz[o_r, pr, :], ALU.mult), [tmpo, sz], [G])

                for (h, kt) in units:
                    b = sbank.next()
                    S.op("pe", mm(PS[b].ap, KT[0:96, h, kt * 128:(kt + 1) * 128], QT[0:96, h, :]),
                         [KT] + qall, [PS[b]])
                    p = pT.next()
                    S.op("act", actf(p.ap, PS[b].ap, AF.Exp), [PS[b]], [p])
                    if kt >= 4 * qi:
                        S.op("dve", tt(p.ap, p.ap, amask[:, kt - 4 * qi, :], ALU.mult), [p, amask], [p])
                    if pend is not None:
                        emit_pv(pend)
                    pend = (h, kt, p)
                emit_pv(pend)
                S.dma("sp", gt1[g, :, :, qi * 512:(qi + 1) * 512], G.ap, reads=[G], writes=[D_gt1])
            S.barrier()
            ar.release()
        S.barrier()
        ar.release()

    def phase_out_proj(src_x, D_src, gt, D_gt, Wo, dst, D_dst):
        ar.mark()
        xt = Rot([ar.alloc([1024], F32, f"xo{i}") for i in range(2)])
        gT = Rot([ar.alloc([4, 128], BF16, f"gT{i}") for i in range(2)])
        for t in range(NT):
            x, g_ = xt.next(), gT.next()
            S.dma("sp", x.ap, src_x[t * 128:(t + 1) * 128, :], reads=[D_src], writes=[x])
            for grp in range(2):
                S.dma("pool", g_[:, 2 * grp:2 * grp + 2, :], gt[grp, :, :, t * 128:(t + 1) * 128],
                      reads=[D_gt], writes=[g_])
            for half in range(2):
                b = 2 * (t % 2) + half
                for c in range(4):
                    S.op("pe", mm(PS[b].ap, g_[:, c, :], Wo[:, c, half * 512:(half + 1) * 512], c == 0, c == 3),
                         [g_, Wo], [PS[b]])
                S.op("dve", tt(x[:, half * 512:(half + 1) * 512], x[:, half * 512:(half + 1) * 512],
                               PS[b].ap, ALU.add), [x, PS[b]], [x])
            S.dma("sp", dst[t * 128:(t + 1) * 128, :], x.ap, reads=[x], writes=[D_dst])
        S.barrier()
        ar.release()


    def phase_mla_sample():
        ar.mark()
        P = NS
        SC = 16
        NCH = 128 // SC
        xs = ar.alloc([1024], F32, "xs")
        hb = ar.alloc([1024], BF16, "s_hb")
        hTs = ar.alloc([8, P], BF16, "hTs")
        cs = ar.alloc([64], F32, "s_cs")
        cn_s = ar.alloc([256], F32, "cn_s")
        kpe_s = ar.alloc([32], F32, "kpe_s")
        sw = ar.alloc([8, 32], F32, "s_sw")
        t1 = ar.alloc([8, 32], F32, "s_t1")
        qan = ar.alloc([384], BF16, "s_qan")
        qaT = ar.alloc([3, P], BF16, "s_qaT")
        qs = ar.alloc([8, 96], F32, "s_qs")
        qfb = ar.alloc([8, 96], BF16, "s_qfb")
        sq = Rot([ar.alloc([768], F32, f"s_sq{i}") for i in range(2)])
        ssq = Rot([ar.alloc([16], F32, f"s_ssq{i}") for i in range(2)])
        qnT = ar.alloc([8, P], BF16, "qnT")
        qpT = ar.alloc([8, P], BF16, "qpT")
        wukT = ar.alloc([8, 256], BF16, "wukT")
        qlatT = ar.alloc([2, NSEQ, 8, 4], BF16, "qlatT")
        qpeT = ar.alloc([NSEQ, 8, 4], BF16, "qpeT")
        smask = ar.alloc([NSEQ * 32], F32, "smask")
        pTn = ar.alloc([NSEQ * 32], BF16, "pTn")
        scn = ar.alloc([NSEQ * 32], F32, "scn")
        lbn = ar.alloc([289], BF16, "lbn")
        gl = Rot([ar.alloc([SC, 256], F32, f"gl{i}") for i in range(2)])
        gk = Rot([ar.alloc([SC, 32], F32, f"gk{i}") for i in range(2)])
        lb = Rot([ar.alloc([289], BF16, f"lb{i}") for i in range(3)])
        cT = Rot([ar.alloc([2, 128], BF16, f"s_cT{i}") for i in range(2)])
        kpT = Rot([ar.alloc([128], BF16, f"kpT{i}") for i in range(2)])
        sc = Rot([ar.alloc([32], F32, f"s_sc{i}") for i in range(2)])
        pT = Rot([ar.alloc([32], BF16, f"s_pT{i}") for i in range(3)])
        ol = ar.alloc([256], BF16, "ol")
        olr = ar.alloc([4], F32, "olr")
        olatT = ar.alloc([2, 8, P], BF16, "olatT")
        ez = ar.alloc([512], F32, "s_ez")
        z_sb = ar.alloc([512], F32, "s_zsb")
        gated = ar.alloc([512], BF16, "s_gated")
        gTs = ar.alloc([4, P], BF16, "gTs")
        for l_ in lb.items + [lbn]:
            S.op("pool", mset(l_[:, 256:257], 1.0), [], [l_])
        S.dma("sp", smask[0:P, :], c_smask, writes=[smask])
        S.dma("sp", cs[0:P, :], c_rope_s, writes=[cs])
        S.dma("sp", xs[0:P, :], x_s, writes=[xs])
        st = stat.next()
        rms_rstd(xs[0:P, :], xs, 1024, st, junk, P)
        S.op("dve", ts(hb[0:P, :], xs[0:P, :], st[0:P, 0:1], ALU.mult), [xs, st], [hb])
        pb = psb(0)
        for c in range(8):
            S.op("pe", trp(pb[:, c * P:(c + 1) * P], hb[0:P, c * 128:(c + 1) * 128], ident[0:P, 0:P]), [hb, ident], [PS[0]])
        S.op("act", acp(hTs.ap, pb[:, 0:8 * P].rearrange("p (c t) -> p c t", c=8)), [PS[0]], [hTs])
        for (bank, c0, w) in ((1, 384, 288), (2, 0, 384), (3, 672, 512)):
            for kc in range(8):
                S.op("pe", mm(PS[bank][0:P, 0:w], hTs[:, kc, :], W_in[:, kc, c0:c0 + w], kc == 0, kc == 7),
                     [hTs, W_in], [PS[bank]])
        st = stat.next()
        ckp_from_proj(PS[1][0:P, 0:256], PS[1][0:P, 256:288], PS[1], P, cs, cn_s, kpe_s, st, sw, t1)
        S.dma("sp", lat_s, cn_s[0:P, :], reads=[cn_s], writes=[D_out])
        S.dma("sp", kr_s, kpe_s[0:P, :], reads=[kpe_s], writes=[D_out])
        st = stat.next()
        rms_rstd(PS[2][0:P, 0:384], PS[2], 384, st, junk, P)
        S.op("dve", ts(qan[0:P, :], PS[2][0:P, 0:384], st[0:P, 0:1], ALU.mult), [PS[2], st], [qan])
        pa = psb(0)
        for c in range(3):
            S.op("pe", trp(pa[:, c * P:(c + 1) * P], qan[0:P, c * 128:(c + 1) * 128], ident[0:P, 0:P]), [qan, ident], [PS[0]])
        S.op("dve", cp(qaT.ap, pa[:, 0:3 * P].rearrange("p (c t) -> p c t", c=3)), [PS[0]], [qaT])
        for (bank, c0, w) in ((4, 0, 384), (5, 384, 384)):
            for kc in range(3):
                S.op("pe", mm(PS[bank][0:P, 0:w], qaT[:, kc, :], W_uq[:, kc, c0:c0 + w], kc == 0, kc == 2), [qaT, W_uq], [PS[bank]])
            S.op("act", acp(qs[0:P, c0 // 96:c0 // 96 + 4, :].rearrange("p h d -> p (h d)"), PS[bank][0:P, 0:w]), [PS[bank]], [qs])
        sq_, ssq_ = sq.next(), ssq.next()
        S.op("act", actf(sq_[0:P, 0:768], qs[0:P].rearrange("p h d -> p (h d)"), AF.Square), [qs], [sq_])
        S.op("dve", rsum(ssq_[0:P, 0:8], sq_[0:P, 0:768].rearrange("p (h d) -> p h d", h=8)), [sq_], [ssq_])
        S.op("act", actf(ssq_[0:P, 0:8], ssq_[0:P, 0:8], AF.Sqrt, scale=1.0 / 96, bias=EPS), [ssq_], [ssq_])
        S.op("dve", recip(ssq_[0:P, 0:8], ssq_[0:P, 0:8]), [ssq_], [ssq_])
        rope(qs[0:P, :, 64:96], qs, cs, 8, P, sw, t1)
        S.op("dve", tt(qs[0:P], qs[0:P], bc(ssq_[0:P, 0:8], [P, 8, 96], 2), ALU.mult), [qs, ssq_], [qs])
        S.op("dve", tt(qfb[0:P], qs[0:P], bc(gg_b[0:P, :], [P, 8, 96], 1), ALU.mult), [qs, gg_b], [qfb])
        pq = psb(0)
        for h in range(8):
            S.op("pe", trp(pq[0:64, h * P:(h + 1) * P], qfb[0:P, h, 0:64], ident[0:P, 0:P]), [qfb, ident], [PS[0]])
        S.op("dve", cp(qnT[0:64], pq[0:64, 0:8 * P].rearrange("p (h t) -> p h t", h=8)), [PS[0]], [qnT])
        for h in range(8):
            S.op("pe", trp(pq[0:32, h * P:(h + 1) * P], qfb[0:P, h, 64:96], ident[0:P, 0:P]), [qfb, ident], [PS[0]])
        S.op("dve", cp(qpT[0:32], pq[0:32, 0:8 * P].rearrange("p (h t) -> p h t", h=8)), [PS[0]], [qpT])
        S.op("dve", cp(qpeT[0:32].rearrange("p n h t -> p h n t"),
                       qpT[0:32].rearrange("p h (n t) -> p h n t", t=4)), [qpT], [qpeT])
        for kc in range(2):
            pw = psb(4 + kc)
            for h in range(8):
                S.op("pe", trp(pw[0:64, h * 128:(h + 1) * 128], W_uk[:, kc, h * 64:(h + 1) * 64], ident.ap), [W_uk, ident], [PS[4 + kc]])
            S.op("act", acp(wukT[0:64, :, kc * 128:(kc + 1) * 128], pw[0:64, 0:1024].rearrange("p (h c) -> p h c", h=8)),
                 [PS[4 + kc]], [wukT])
        for ck in range(2):
            for h in range(8):
                S.op("pe", mm(PS[4 + ck][:, h * P:(h + 1) * P], wukT[0:64, h, ck * 128:(ck + 1) * 128], qnT[0:64, h, :]),
                     [wukT, qnT], [PS[4 + ck]])
            S.op("act", acp(qlatT[:, ck].rearrange("p n h t -> p h n t"),
                            PS[4 + ck][:, 0:8 * P].rearrange("p (h n t) -> p h n t", h=8, t=4)), [PS[4 + ck]], [qlatT])

        def tile_scores(lbt, parts, kp_ap, kpT_src, bank_t, bank_k, bank_s, rhs_lat, rhs_pe, ncols):
            pbt = psb(bank_t)
            cT_, kpT_ = cT.next(), kpT.next()
            for c in range(2):
                S.op("pe", trp(pbt[:, c * parts:(c + 1) * parts], lbt[0:parts, c * 128:(c + 1) * 128], ident[0:parts, 0:parts]),
                     [lbt, ident], [PS[bank_t]])
            S.op("pe", trp(pbt[0:32, 256:256 + parts], lbt[0:parts, 257:289], ident[0:parts, 0:parts]), [lbt, ident], [PS[bank_t]])
            S.op("act", acp(cT_[:, :, 0:parts], pbt[:, 0:2 * parts].rearrange("p (c t) -> p c t", c=2)), [PS[bank_t]], [cT_])
            S.op("dve", cp(kpT_[0:32, 0:parts], pbt[0:32, 256:256 + parts]), [PS[bank_t]], [kpT_])
            for kc in range(2):
                S.op("pe", mm(PS[bank_k][0:parts, :], cT_[:, kc, 0:parts], W_uk[:, kc, :], kc == 0, kc == 1), [cT_, W_uk], [PS[bank_k]])
            sq_, ssq_ = sq.next(), ssq.next()
            S.op("act", actf(sq_[0:parts, 0:512], PS[bank_k][0:parts, :], AF.Square), [PS[bank_k]], [sq_])
            S.op("dve", rsum(ssq_[0:parts, 0:8], sq_[0:parts, 0:512].rearrange("p (h d) -> p h d", h=8)), [sq_], [ssq_])
            S.op("act", actf(sq_[0:parts, 512:544], kp_ap, AF.Square, accum_out=ssq_[0:parts, 8:9]), [kpT_src], [sq_, ssq_])
            S.op("dve", ts(ssq_[0:parts, 0:8], ssq_[0:parts, 0:8], ssq_[0:parts, 8:9], ALU.add), [ssq_], [ssq_])
            S.op("act", actf(ssq_[0:parts, 0:8], ssq_[0:parts, 0:8], AF.Sqrt, scale=1.0 / 96, bias=EPS), [ssq_], [ssq_])
            S.op("dve", recip(ssq_[0:parts, 0:8], ssq_[0:parts, 0:8]), [ssq_], [ssq_])
            for kc in range(2):
                S.op("pe", mm(PS[bank_s][0:parts, 0:ncols], cT_[:, kc, 0:parts], rhs_lat(kc), kc == 0, False),
                     [cT_, qlatT], [PS[bank_s]])
            S.op("pe", mm(PS[bank_s][0:parts, 0:ncols], kpT_[0:32, 0:parts], rhs_pe, False, True), [kpT_, qpeT], [PS[bank_s]])
            return ssq_

        S.op("act", acp(lbn[0:P, 0:256], cn_s[0:P, :]), [cn_s], [lbn])
        S.op("act", acp(lbn[0:P, 257:289], kpe_s[0:P, :]), [kpe_s], [lbn])
        NC_ = NSEQ * 32
        r_ = tile_scores(lbn, P, kpe_s[0:P, :], kpe_s, 0, 1, 2,
                         lambda kc: qlatT[:, kc].rearrange("p n h t -> p (n h t)"),
                         qpeT[0:32].rearrange("p n h t -> p (n h t)"), NC_)
        S.op("dve", tt(scn[0:P, :].rearrange("p (n h t) -> p n h t", h=8, t=4),
                       PS[2][0:P, 0:NC_].rearrange("p (n h t) -> p n h t", h=8, t=4),
                       r_[0:P, 0:8].unsqueeze(1).unsqueeze(3).to_broadcast([P, NSEQ, 8, 4]), ALU.mult), [PS[2], r_], [scn])
        S.op("act", actf(scn[0:P, :], scn[0:P, :], AF.Exp), [scn], [scn])
        S.op("dve", tt(pTn[0:P, :], scn[0:P, :], smask[0:P, :], ALU.mult), [scn, smask], [pTn])

        S.op("act", acp(z_sb[0:P, :], PS[3][0:P, :]), [PS[3]], [z_sb])
        lbc = Rot([ar.alloc([SC, 289], BF16, f"lbc{i}") for i in range(2)])
        for l_ in lbc.items:
            S.op("pool", mset(l_[:, :, 256:257], 1.0), [], [l_])
        cT2 = Rot([ar.alloc([2, 2, 128], BF16, f"cT2_{i}") for i in range(2)])
        kpT2 = Rot([ar.alloc([2, 128], BF16, f"kpT2_{i}") for i in range(2)])
        sq2 = Rot([ar.alloc([1024], F32, f"sq2_{i}") for i in range(2)])
        sqk = Rot([ar.alloc([SC, 32], F32, f"sqk{i}") for i in range(2)])
        ssqc = Rot([ar.alloc([SC, 8], F32, f"ssqc{i}") for i in range(2)])
        sspc = Rot([ar.alloc([SC], F32, f"sspc{i}") for i in range(2)])
        scr = Rot([ar.alloc([SC, 32], F32, f"scr{i}") for i in range(2)])
        pTc = Rot([ar.alloc([SC, 32], BF16, f"pTc{i}") for i in range(2)])
        OLB = 7
        nb = 0

        def stage_b(n, ch, lc_, ssq_, ssp_, scr_, first_of_seq):
            p_ = pTc.next()
            S.op("dve", tt(ssq_.ap, ssq_.ap, bc(ssp_.ap, [128, SC, 8], 2), ALU.add), [ssq_, ssp_], [ssq_])
            S.op("act", actf(ssq_.ap, ssq_.ap, AF.Sqrt, scale=1.0 / 96, bias=EPS), [ssq_], [ssq_])
            S.op("dve", recip(ssq_.ap, ssq_.ap), [ssq_], [ssq_])
            S.op("dve", tt(scr_.ap.rearrange("p s (h t) -> p s h t", t=4), scr_.ap.rearrange("p s (h t) -> p s h t", t=4),
                           bc(ssq_.ap, [128, SC, 8, 4], 3), ALU.mult), [scr_, ssq_], [scr_])
            S.op("act", actf(p_.ap, scr_.ap, AF.Exp), [scr_], [p_])
            if first_of_seq:
                S.op("pe", mm(PS[OLB][0:32, 0:257], pTn[0:P, n * 32:(n + 1) * 32], lbn[0:P, 0:257], True, False),
                     [pTn, lbn], [PS[OLB]])
            for i in range(SC):
                last = (ch == NCH - 1 and i == SC - 1)
                S.op("pe", mm(PS[OLB][0:32, 0:257], p_[:, i, :], lc_[:, i, 0:257], False, last), [p_, lc_], [PS[OLB]])

        def seq_epilogue(n):
            OL = OLB
            S.op("dve", recip(olr[0:32, 0:1], PS[OL][0:32, 256:257]), [PS[OL]], [olr])
            S.op("dve", ts(ol[0:32, :], PS[OL][0:32, 0:256], olr[0:32, 0:1], ALU.mult), [PS[OL], olr], [ol])
            po = psb(6)
            for ck in range(2):
                S.op("pe", trp(po[:, 512 + ck * 32:512 + (ck + 1) * 32], ol[0:32, ck * 128:(ck + 1) * 128], ident[0:32, 0:32]),
                     [ol, ident], [PS[6]])
            S.op("act", acp(olatT[:, :, :, n * 4:(n + 1) * 4],
                            po[:, 512:576].rearrange("p (c h t) -> p c h t", c=2, t=4)), [PS[6]], [olatT])

        pending = None
        for n in range(NSEQ):
            for ch in range(NCH):
                gl_, gk_ = gl.next(), gk.next()
                S.dma_fn("pool", lambda e, gl_=gl_, n=n, ch=ch: e.indirect_dma_start(
                    out=gl_.ap.rearrange("p s c -> p (s c)"), out_offset=None, in_=cache_lat,
                    in_offset=bass.IndirectOffsetOnAxis(ap=ptab_sb[:, n:n + 1], axis=0),
                    element_offset=ch * SC * 256), [ptab_sb], [gl_])
                S.dma_fn("pool", lambda e, gk_=gk_, n=n, ch=ch: e.indirect_dma_start(
                    out=gk_.ap.rearrange("p s c -> p (s c)"), out_offset=None, in_=cache_kr,
                    in_offset=bass.IndirectOffsetOnAxis(ap=ptab_sb[:, n:n + 1], axis=0),
                    element_offset=ch * SC * 32), [ptab_sb], [gk_])
                lc_, ssq_, ssp_, scr_, qk_ = lbc.next(), ssqc.next(), sspc.next(), scr.next(), sqk.next()
                S.op("pool", cp(lc_[:, :, 257:289], gk_.ap), [gk_], [lc_])
                S.op("act", actf(qk_.ap, gk_.ap, AF.Square), [gk_], [qk_])
                S.op("dve", rsum(ssp_.ap, qk_.ap), [qk_], [ssp_])
                for j in range(SC // 2):
                    sl = slice(2 * j, 2 * j + 2)
                    S.op("pool", cp(lc_[:, sl, 0:256], gl_[:, sl, :]), [gl_], [lc_])
                    bT = nb % 2
                    bK = (2, 3) if nb % 2 == 0 else (4, 5)
                    nb += 1
                    pbt = psb(bT)
                    c2, k2 = cT2.next(), kpT2.next()
                    for g_ in range(2):
                        for ck in range(2):
                            S.op("pe", trp(pbt[:, (g_ * 2 + ck) * 128:(g_ * 2 + ck + 1) * 128],
                                           lc_[:, 2 * j + g_, ck * 128:(ck + 1) * 128], ident.ap), [lc_, ident], [PS[bT]])
                        S.op("pe", trp(pbt[0:32, 512 + g_ * 128:512 + (g_ + 1) * 128], lc_[:, 2 * j + g_, 257:289], ident.ap),
                             [lc_, ident], [PS[bT]])
                    S.op("act", acp(c2.ap.rearrange("p g c t -> p (g c t)"), pbt[:, 0:512]), [PS[bT]], [c2])
                    S.op("dve", cp(k2[0:32].rearrange("p g t -> p (g t)"), pbt[0:32, 512:768]), [PS[bT]], [k2])
                    for g_ in range(2):
                        for kc in range(2):
                            S.op("pe", mm(PS[bK[g_]].ap, c2[:, g_, kc, :], W_uk[:, kc, :], kc == 0, kc == 1), [c2, W_uk], [PS[bK[g_]]])
                    for g_ in range(2):
                        for kc in range(2):
                            S.op("pe", mm(PS[6][:, g_ * 32:(g_ + 1) * 32], c2[:, g_, kc, :],
                                          qlatT[:, kc, n].rearrange("p h t -> p (h t)"), kc == 0, False), [c2, qlatT], [PS[6]])
                        S.op("pe", mm(PS[6][:, g_ * 32:(g_ + 1) * 32], k2[0:32, g_, :],
                                      qpeT[0:32, n].rearrange("p h t -> p (h t)"), False, True), [k2, qpeT], [PS[6]])
                    q2 = sq2.next()
                    S.op("act", actf(q2.ap, PSD[bK[0] // 2], AF.Square), [PS[bK[0]], PS[bK[1]]], [q2])
                    S.op("dve", rsum(ssq_[:, sl, :].rearrange("p g h -> p (g h)"), q2.ap.rearrange("p (x d) -> p x d", d=64)),
                         [q2], [ssq_])
                    S.op("dve", cp(scr_[:, sl, :].rearrange("p g x -> p (g x)"), PS[6][:, 0:64]), [PS[6]], [scr_])
                    if j == 1 and pending is not None:
                        stage_b(*pending)
                        if pending[-1] is False and pending[1] == NCH - 1:
                            pass
                        pending = None
                        if ch == 0 and n > 0:
                            seq_epilogue(n - 1)
                pending = (n, ch, lc_, ssq_, ssp_, scr_, ch == 0)
        stage_b(*pending)
        seq_epilogue(NSEQ - 1)
        for h in range(8):
            for ck in range(2):
                S.op("pe", mm(PS[1][0:P, h * 64:(h + 1) * 64], olatT[:, ck, h, :], W_uv[:, ck, h * 64:(h + 1) * 64], ck == 0, ck == 1),
                     [olatT, W_uv], [PS[1]])
        S.op("act", actf(ez[0:P, :], z_sb[0:P, :], AF.Exp, scale=-1.0), [z_sb], [ez])
        S.op("dve", ts(ez[0:P, :], ez[0:P, :], 1.0, ALU.add), [ez], [ez])
        S.op("dve", recip(ez[0:P, :], ez[0:P, :]), [ez], [ez])
        S.op("dve", tt(ez[0:P, :], z_sb[0:P, :], ez[0:P, :], ALU.mult), [z_sb, ez], [ez])
        S.op("dve", tt(gated[0:P, :], PS[1][0:P, :], ez[0:P, :], ALU.mult), [PS[1], ez], [gated])
        pg = psb(0)
        for c in range(4):
            S.op("pe", trp(pg[:, c * P:(c + 1) * P], gated[0:P, c * 128:(c + 1) * 128], ident[0:P, 0:P]), [gated, ident], [PS[0]])
        S.op("act", acp(gTs.ap, pg[:, 0:4 * P].rearrange("p (c t) -> p c t", c=4)), [PS[0]], [gTs])
        for half in range(2):
            for c in range(4):
                S.op("pe", mm(PS[4 + half][0:P, :], gTs[:, c, :], W_o[:, c, half * 512:(half + 1) * 512], c == 0, c == 3),
                     [gTs, W_o], [PS[4 + half]])
            S.op("dve", tt(xs1[0:P, half * 512:(half + 1) * 512], xs[0:P, half * 512:(half + 1) * 512],
                           PS[4 + half][0:P, :], ALU.add), [xs, PS[4 + half]], [xs1])
        if "s2" not in phases:
            S.dma("sp", y_s, xs1[0:P, :], reads=[xs1], writes=[D_out])
        S.barrier()
        ar.release()

    def load_gdn_weights():
        W = {}
        bn_sb = ar.alloc([8], F32, "bn_sb")
        S.dma("sp", bn_sb.ap, b_norm, writes=[bn_sb])
        wtmp2 = ar.alloc([2064], F32, "wtmp2")
        W["in"] = ar.alloc([8, 2064], BF16, "Wb_in")
        load_bf16_rows(W["in"], [W["in"][:, kc, :] for kc in range(8)],
                       [b_w_in[kc * 128:(kc + 1) * 128, :] for kc in range(8)], (bn_sb, bn_sb), wtmp2, 2064)
        W["zab"] = ar.alloc([8, 2, 272], BF16, "Wzab")
        for g_ in range(2):
            S.op("pool", cp(W["zab"][:, :, g_, 0:256], W["in"][:, :, 1536 + g_ * 256:1792 + g_ * 256]), [W["in"]], [W["zab"]])
            S.op("pool", cp(W["zab"][:, :, g_, 256:272], W["in"][:, :, 2048:2064]), [W["in"]], [W["zab"]])
        W["o"] = ar.alloc([4, 1024], BF16, "Wb_o")
        load_bf16_rows(W["o"], [W["o"][:, kc, :] for kc in range(4)],
                       [b_w_o[kc * 128:(kc + 1) * 128, :] for kc in range(4)], None, wtmp2, 1024)
        W["convT"] = ar.alloc([12, 4], F32, "wconvT")
        S.dma("sp", W["convT"].ap.rearrange("p c j -> p (c j)"), b_w_convT, writes=[W["convT"]])
        W["negA"] = ar.alloc([8], F32, "negA")
        S.dma("sp", W["negA"].ap, b_a_log.to_broadcast([128, 8]), writes=[W["negA"]])
        S.op("act", actf(W["negA"].ap, W["negA"].ap, AF.Exp), [W["negA"]], [W["negA"]])
        S.op("dve", ts(W["negA"].ap, W["negA"].ap, -1.0, ALU.mult), [W["negA"]], [W["negA"]])
        W["dtb"] = ar.alloc([8], F32, "dtb")
        S.dma("sp", W["dtb"].ap, b_dt_bias.to_broadcast([128, 8]), writes=[W["dtb"]])
        W["go"] = ar.alloc([64], F32, "go")
        S.dma("sp", W["go"].ap, b_g_o.to_broadcast([128, 64]), writes=[W["go"]])
        for nm, src in (("tri", c_tri), ("strict", c_strict), ("negm", c_negm)):
            W[nm] = ar.alloc([128], F32, nm)
            S.dma("sp", W[nm].ap, src, writes=[W[nm]])
        W["ones"] = ar.alloc([128], F32, "ones_f")
        S.op("pool", mset(W["ones"].ap, 1.0), [], [W["ones"]])
        blk_f = ar.alloc([128], F32, "blk_f")
        S.dma("sp", blk_f.ap, c_blk, writes=[blk_f])
        W["blk"] = ar.alloc([128], BF16, "blk")
        S.op("dve", cp(W["blk"].ap, blk_f.ap), [blk_f], [W["blk"]])
        return W

    def gates(xa_ap, xb_ap, srcT, parts, nh, W, hcol, gt, bt, tmp):
        a1, a2 = tmp
        P = slice(0, parts)
        S.op("dve", tt(a1[P, 0:nh], xa_ap, W["dtb"][P, hcol:hcol + nh], ALU.add), [srcT, W["dtb"]], [a1])
        S.op("dve", stt(a2[P, 0:nh], a1[P, 0:nh], -1.0, a1[P, 0:nh], ALU.mult, ALU.max), [a1], [a2])
        S.op("act", actf(a2[P, 0:nh], a2[P, 0:nh], AF.Exp, scale=-1.0), [a2], [a2])
        S.op("act", actf(a2[P, 0:nh], a2[P, 0:nh], AF.Ln, bias=1.0), [a2], [a2])
        S.op("dve", stt(a1[P, 0:nh], a1[P, 0:nh], 0.0, a2[P, 0:nh], ALU.max, ALU.add), [a1, a2], [a1])
        S.op("dve", tt(gt[P, 0:nh], a1[P, 0:nh], W["negA"][P, hcol:hcol + nh], ALU.mult), [a1, W["negA"]], [gt])
        S.op("act", actf(a2[P, 0:nh], xb_ap, AF.Exp, scale=-1.0), [srcT], [a2])
        S.op("dve", ts(a2[P, 0:nh], a2[P, 0:nh], 1.0, ALU.add), [a2], [a2])
        S.op("dve", recip(bt[P, 0:nh], a2[P, 0:nh]), [a2], [bt])

    def phase_gdn_prompt(W):
        ar.mark()
        NTL = SEQ // 512
        xt = Rot([ar.alloc([1024], F32, f"gx{i}") for i in range(2)])
        hb = ar.alloc([1024], BF16, "ghb")
        hT4 = ar.alloc([8, 512], BF16, "ghT4")
        qk = [ar.alloc([6, 515], F32, f"qk{i}") for i in range(2)]
        cs = ar.alloc([6, 512], F32, "gcs")
        sqb = ar.alloc([512], BF16, "sqb")
        sd = ar.alloc([512], F32, "sd")
        QKn = ar.alloc([4, 512], BF16, "QKn")
        vb = ar.alloc([2, 512], BF16, "vb")
        KnZ = ar.alloc([2, 2, 512], BF16, "KnZ")
        SbZ = ar.alloc([2, 2, 64], BF16, "SbZ")
        S.op("pool", mset(KnZ.ap, 0.0), [], [KnZ])
        gt_, bt_ = ar.alloc([4], F32, "g_t"), ar.alloc([4], F32, "b_t")
        a1, a2 = ar.alloc([4], F32, "ga1"), ar.alloc([4], F32, "ga2")
        smp = [ar.alloc([32], F32, f"gsm{i}") for i in range(2)]
        smS = ar.alloc([8], F32, "gsmS")
        egp = [ar.alloc([2], F32, f"eg{i}") for i in range(2)]
        z_p = [ar.alloc([256], F32, f"z_p{i}") for i in range(2)]
        Z = ar.alloc([4, 128], F32, "Z")
        Dm = ar.alloc([4, 128], F32, "Dm")
        decay = ar.alloc([4, 128], F32, "decay")
        A1 = ar.alloc([4, 128], F32, "A1")
        bS = ar.alloc([4, 128], F32, "bS")
        MYr = [ar.alloc([4, 256], BF16, f"MY{i}") for i in range(2)]
        Yf = ar.alloc([4, 128], BF16, "Yf")
        intra = ar.alloc([4, 128], BF16, "intra")
        intraTp = [ar.alloc([4, 128], BF16, f"intraT{i}") for i in range(2)]
        Nr = Rot([ar.alloc([4, 128], BF16, f"N{i}") for i in range(2)])
        Mr = Rot([ar.alloc([4, 128], BF16, f"M{i}") for i in range(2)])
        ImLT = ar.alloc([4, 128], BF16, "ImLT")
        Yr = Rot([ar.alloc([4, 128], BF16, f"Y{i}") for i in range(2)])
        Ywz = ar.alloc([2, 2, 128], BF16, "Ywz")
        ktzp = [ar.alloc([2, 2, 128], BF16, f"ktz{i}") for i in range(2)]
        S.op("pool", mset(Ywz.ap, 0.0), [], [Ywz])
        for k_ in ktzp:
            S.op("pool", mset(k_.ap, 0.0), [], [k_])
        ktok = ar.alloc([4, 64], F32, "ktok")
        u_p = [ar.alloc([4, 64], F32, f"u_sb{i}") for i in range(2)]
        wT_p = [ar.alloc([2, 128], BF16, f"wT_sb{i}") for i in range(2)]
        vn = ar.alloc([4, 64], BF16, "vn")
        o_sb = ar.alloc([4, 64], F32, "o_sb")
        o_t = ar.alloc([4, 64], F32, "o_t")
        St = ar.alloc([2, 64], F32, "St")
        St_t = ar.alloc([2, 64], F32, "St_t")
        Sb = ar.alloc([2, 64], BF16, "Sb")
        sgz = ar.alloc([256], F32, "sgz")
        gated = ar.alloc([256], BF16, "gated")
        G2 = Rot([ar.alloc([2, 512], BF16, f"G2{i}") for i in range(2)])
        ident_b = ident
        crow = ar.alloc([6, 128], F32, "crow")

        def load_h(tile, col):
            x = xt.next()
            S.dma("sp", x.ap, xp1[tile * 128:(tile + 1) * 128, :], reads=[D_xp1], writes=[x])
            st = stat.next()
            rms_rstd(x.ap, x, 1024, st, junk)
            S.op("dve", ts(hb.ap, x.ap, st[:, 0:1], ALU.mult), [x, st], [hb])
            pb = psb(0)
            for c in range(8):
                S.op("pe", trp(pb[:, c * 128:(c + 1) * 128], hb[:, c * 128:(c + 1) * 128], ident.ap),
                     [hb, ident], [PS[0]])
            S.op("act", acp(hT4[:, :, col:col + 128], pb[:, 0:1024].rearrange("p (c t) -> p c t", c=8)),
                 [PS[0]], [hT4])

        for g in range(2):
            if g == 1:
                S.barrier()
            S.op("pool", mset(St.ap, 0.0), [], [St])
            S.op("pool", mset(SbZ.ap, 0.0), [], [SbZ])
            gch = [2 * g, 2 * g + 1, 4 + 2 * g, 5 + 2 * g, 8 + 2 * g, 9 + 2 * g]
            for ti in range(NTL):
                cur, prev = qk[ti % 2], qk[(ti + 1) % 2]
                for s in range(4):
                    tile_ = ti * 4 + s
                    if g == 0:
                        load_h(tile_, s * 128)
                    else:
                        S.dma("sp" if s % 2 == 0 else "pool", hT4[:, :, s * 128:(s + 1) * 128],
                              hT2_scr[:, :, tile_ * 128:(tile_ + 1) * 128], reads=[D_h2], writes=[hT4])
                if g == 0:
                    S.dma("pool", hT2_scr[:, :, ti * 512:(ti + 1) * 512], hT4.ap, reads=[hT4], writes=[D_h2])
                if ti == 0:
                    S.op("pool", mset(cur[:, :, 0:3], 0.0), [], [cur])
                else:
                    S.op("pool", cp(cur[:, :, 0:3], prev[:, :, 512:515]), [prev], [cur])
                for lc, gc_ in enumerate(gch):
                    b = 1 + (lc % 2)
                    for kc in range(8):
                        S.op("pe", mm(PS[b].ap, W["in"][:, kc, gc_ * 128:(gc_ + 1) * 128], hT4[:, kc, :],
                                      kc == 0, kc == 7), [W["in"], hT4], [PS[b]])
                    S.op("act", acp(cur[:, lc, 3:515], PS[b].ap), [PS[b]], [cur])
                    ce = "dve"
                    S.op(ce, ts(cs[:, lc, :], cur[:, lc, 0:512], W["convT"][:, gc_, 0:1], ALU.mult),
                         [cur, W["convT"]], [cs])
                    for j in range(1, 4):
                        S.op(ce, stt(cs[:, lc, :], cur[:, lc, j:j + 512], W["convT"][:, gc_, j:j + 1], cs[:, lc, :],
                                     ALU.mult, ALU.add), [cur, W["convT"], cs], [cs])
                    S.op("act", actf(cs[:, lc, :], cs[:, lc, :], AF.Silu), [cs], [cs])
                if ti == NTL - 1:
                    for lc, gc_ in enumerate(gch):
                        S.op("pe", trp(PS[3][0:3, 0:128], cur[:, lc, 512:515], ident_f.ap), [cur, ident_f], [PS[3]])
                        S.op("dve", cp(crow[0:3, lc, :], PS[3][0:3, 0:128]), [PS[3]], [crow])
                        S.dma("sp", conv_p[:, gc_ * 128:(gc_ + 1) * 128], crow[0:3, lc, :], reads=[crow], writes=[D_out])
                for lc in range(4 if cfg.cut >= 1 else 0):
                    S.op("act", actf(sqb.ap, cs[:, lc, :], AF.Square), [cs], [sqb])
                    S.op("pe", mm(PS[3].ap, W["blk"].ap, sqb.ap), [W["blk"], sqb], [PS[3]])
                    S.op("act", actf(sd.ap, PS[3].ap, AF.Sqrt, bias=EPS), [PS[3]], [sd])
                    S.op("dve", recip(sd.ap, sd.ap), [sd], [sd])
                    S.op("dve", stt(QKn[:, lc, :], cs[:, lc, :], 0.125 if lc < 2 else 1.0, sd.ap, ALU.mult, ALU.mult),
                         [cs, sd], [QKn])
                S.op("act", acp(vb.ap, cs[:, 4:6, :]), [cs], [vb])
                S.op("pool", cp(KnZ[0:64, :, 0, :], QKn[0:64, 2:4, :]), [QKn], [KnZ])
                S.op("pool", cp(KnZ[64:128, :, 1, :], QKn[64:128, 2:4, :]), [QKn], [KnZ])
                Gt = G2.next()
                def local(s):
                    CUT = cfg.cut
                    par = s % 2
                    sm, eg, ktz, intraT, u_sb, wT_sb = smp[par], egp[par], ktzp[par], intraTp[par], u_p[par], wT_p[par]
                    tc_ = slice(s * 128, (s + 1) * 128)
                    zc = 1536 + g * 256
                    for kc in range(8):
                        S.op("pe", mm(PS[0][:, 0:272], hT4[:, kc, tc_], W["zab"][:, kc, g, :], kc == 0, kc == 7),
                             [hT4, W["zab"]], [PS[0]])
                    S.op("act", acp(z_p[par].ap, PS[0][:, 0:256]), [PS[0]], [z_p[par]])
                    gates(PS[0][:, 256 + 4 * g:260 + 4 * g], PS[0][:, 264 + 4 * g:268 + 4 * g], PS[0], 128, 4, W,
                          4 * g, gt_, bt_, (a1, a2))
                    S.op("pe", mm(PS[4][:, 0:4], W["tri"].ap, gt_.ap), [W["tri"], gt_], [PS[4]])
                    S.op("dve", cp(sm[:, 0:4], PS[4][:, 0:4]), [PS[4]], [sm])
                    S.op("dve", tt(Z.ap, bc(gt_.ap, [128, 4, 128], 2), bc(W["tri"].ap, [128, 4, 128], 1), ALU.mult),
                         [gt_, W["tri"]], [Z])
                    S.op("pe", mm(PS[4].ap, W["ones"].ap, Z.ap.rearrange("p h j -> p (h j)")), [W["ones"], Z], [PS[4]])
                    gcrow = PS[4].ap.rearrange("p (h j) -> p h j", h=4)
                    S.op("dve", tt(Dm.ap, bc(sm[:, 0:4], [128, 4, 128], 2), gcrow, ALU.subtract), [sm, PS[4]], [Dm])
                    S.op("dve", stt(Dm.ap, Dm.ap, 0.0, bc(W["negm"].ap, [128, 4, 128], 1), ALU.min, ALU.add),
                         [Dm, W["negm"]], [Dm])
                    S.op("act", actf(decay.ap, Dm.ap, AF.Exp), [Dm], [decay])
                    S.op("act", actf(sm[:, 4:8], sm[:, 0:4], AF.Exp), [sm], [sm])
                    S.op("dve", tt(sm[:, 8:12], sm[:, 4:8], bt_.ap, ALU.mult), [sm, bt_], [sm])
                    S.op("dve", tt(sm[:, 20:24], gcrow[:, :, 127], sm[:, 0:4], ALU.subtract), [PS[4], sm], [sm])
                    S.op("act", actf(sm[:, 12:16], sm[:, 20:24], AF.Exp), [sm], [sm])
                    S.op("act", actf(eg[0:64, :], gcrow[0:64, 0::2, 127], AF.Exp), [PS[4]], [eg])
                    S.op("act", actf(eg[64:128, :], gcrow[64:128, 1::2, 127], AF.Exp), [PS[4]], [eg])
                    if CUT < 3:
                        return
                    pb7 = psb(7)
                    for c in range(2):
                        S.op("pe", trp(pb7[:, c * 128:(c + 1) * 128], QKn[:, 2 + c, tc_], ident.ap), [QKn, ident], [PS[7]])
                    for c in range(2):
                        S.op("pe", trp(pb7[:, 256 + c * 128:256 + (c + 1) * 128], vb[:, c, tc_], ident.ap),
                             [vb, ident], [PS[7]])
                    S.op("act", acp(ktok.ap.rearrange("p h d -> p (h d)"), pb7[:, 0:256]), [PS[7]], [ktok])
                    S.op("dve", tt(MYr[1][:, :, 128:192], pb7[:, 256:512].rearrange("p (h d) -> p h d", h=4),
                                   bc(bt_.ap, [128, 4, 64], 2), ALU.mult), [PS[7], bt_], [MYr[1]])
                    S.op("dve", tt(MYr[1][:, :, 192:256], ktok.ap, bc(sm[:, 8:12], [128, 4, 64], 2), ALU.mult), [ktok, sm], [MYr[1]])
                    kz = ktok.ap.rearrange("p (pr par) d -> p pr par d", par=2)
                    ekv = sm[:, 12:16].rearrange("p (pr par) -> p pr par", par=2)
                    for par in range(2):
                        S.op("dve", tt(ktz[:, :, par, par * 64:(par + 1) * 64], kz[:, :, par, :],
                                       bc(ekv[:, :, par], [128, 2, 64], 2), ALU.mult), [ktok, sm], [ktz])
                    if CUT < 3.2:
                        return
                    for h in range(4):
                        pr, par = h // 2, h % 2
                        kz_ = KnZ[:, pr, par, tc_]
                        S.op("pe", mm(PS[5][:, h * 128:(h + 1) * 128], kz_, QKn[:, 2 + pr, tc_]), [QKn, KnZ], [PS[5]])
                        S.op("pe", mm(PS[6][:, h * 128:(h + 1) * 128], QKn[:, pr, tc_], kz_), [QKn, KnZ], [PS[6]])
                    if CUT < 3.4:
                        return
                    S.op("dve", tt(A1.ap.rearrange("p h j -> p (h j)"), PS[5].ap, decay.ap.rearrange("p h j -> p (h j)"),
                                   ALU.mult), [PS[5], decay], [A1])
                    S.op("dve", tt(bS.ap, bc(bt_.ap, [128, 4, 128], 2), bc(W["strict"].ap, [128, 4, 128], 1), ALU.mult),
                         [bt_, W["strict"]], [bS])
                    S.op("dve", tt(MYr[0][:, :, 0:128], A1.ap, bS.ap, ALU.mult), [A1, bS], [MYr[0]])
                    S.op("dve", tt(intra.ap.rearrange("p h j -> p (h j)"), PS[6].ap, decay.ap.rearrange("p h j -> p (h j)"),
                                   ALU.mult), [PS[6], decay], [intra])
                    if CUT < 3.6:
                        return
                    for h in range(4):
                        S.op("pe", trp(pb7[:, h * 128:(h + 1) * 128], intra[:, h, :], ident.ap), [intra, ident], [PS[7]])
                    S.op("act", acp(intraT.ap.rearrange("p h j -> p (h j)"), pb7[:, 0:512]), [PS[7]], [intraT])
                    for h in range(4):
                        S.op("pe", trp(pb7[:, 512 + h * 128:512 + (h + 1) * 128], MYr[0][:, h, 0:128], ident.ap), [MYr[0], ident], [PS[7]])
                    if CUT < 3.8:
                        return
                    N = Nr.next()
                    S.op("act", acp(N.ap.rearrange("p h j -> p (h j)"), pb7[:, 512:1024]), [PS[7]], [N])
                    if CUT < 3.85:
                        return
                    S.op("dve", tt(ImLT.ap, bc(ident_b.ap, [128, 4, 128], 1), N.ap, ALU.subtract), [ident_b, N], [ImLT])
                    if CUT < 4:
                        return
                    PA = PSD[2]
                    PAv = PA.rearrange("p (h x) -> p h x", h=4)
                    for k in range(0, 7):
                        cur_, nxt_ = MYr[k % 2], MYr[(k + 1) % 2]
                        if k == 0:
                            c0_, c1_ = 0, 128
                        elif k < 6:
                            c0_, c1_ = 0, 256
                        else:
                            c0_, c1_ = 128, 256
                        for h in range(4):
                            S.op("pe", mm(PA[:, h * 256 + c0_:h * 256 + c1_], N[:, h, :], cur_[:, h, c0_:c1_]),
                                 [N, cur_], [PS[4], PS[5]])
                        if k < 6:
                            N2 = Nr.next()
                            for h in range(4):
                                S.op("pe", mm(PS[6][:, h * 128:(h + 1) * 128], cur_[:, h, 0:128], N[:, h, :]), [cur_, N], [PS[6]])
                            S.op("act", acp(nxt_[:, :, 0:128], PAv[:, :, 0:128]), [PS[4], PS[5]], [nxt_])
                        if k >= 1:
                            dstY = nxt_[:, :, 128:256] if k < 6 else Yf.ap
                            dT = nxt_ if k < 6 else Yf
                            S.op("dve", tt(dstY, PAv[:, :, 128:256], cur_[:, :, 128:256], ALU.add), [PS[4], PS[5], cur_], [dT])
                        if k < 6:
                            S.op("act", acp(N2.ap.rearrange("p h j -> p (h j)"), PS[6].ap), [PS[6]], [N2])
                            N = N2
                    Y = Yf
                    if CUT < 5:
                        return
                    yv = Y[:, :, 64:128].rearrange("p (pr par) d -> p pr par d", par=2)
                    for par in range(2):
                        S.op("pool", cp(Ywz[:, :, par, par * 64:(par + 1) * 64], yv[:, :, par, :]), [Y], [Ywz])
                    for h in range(4):
                        S.op("pe", mm(PS[1][:, h * 64:(h + 1) * 64], ImLT[:, h, :], Y[:, h, 0:64]), [ImLT, Y], [PS[1]])
                    for pr in range(2):
                        for par in range(2):
                            S.op("pe", mm(PS[1][:, 256 + pr * 128:256 + (pr + 1) * 128], Ywz[:, pr, par, :], ImLT[:, 2 * pr + par, :],
                                          par == 0, par == 1), [Ywz, ImLT], [PS[1]])
                    S.op("act", acp(u_sb.ap.rearrange("p h d -> p (h d)"), PS[1][:, 0:256]), [PS[1]], [u_sb])
                    S.op("dve", cp(wT_sb.ap.rearrange("p a t -> p (a t)"), PS[1][:, 256:512]), [PS[1]], [wT_sb])

                def scan(s):
                    CUT = cfg.cut
                    par_s = s % 2
                    sm, eg, ktz, intraT, u_sb, wT_sb = smp[par_s], egp[par_s], ktzp[par_s], intraTp[par_s], u_p[par_s], wT_p[par_s]
                    tc_ = slice(s * 128, (s + 1) * 128)
                    pb2 = psb(2)
                    for h in range(4):
                        pr, par = h // 2, h % 2
                        S.op("pe", mm(PS[2][:, h * 64:(h + 1) * 64], wT_sb[:, pr, :], SbZ[:, pr, par, :]),
                             [wT_sb, SbZ], [PS[2]])
                        S.op("pe", mm(PS[3][:, h * 64:(h + 1) * 64], QKn[:, pr, tc_], SbZ[:, pr, par, :]),
                             [QKn, SbZ], [PS[3]])
                    S.op("dve", tt(vn.ap.rearrange("p h d -> p (h d)"), u_sb.ap.rearrange("p h d -> p (h d)"),
                                   PS[2][:, 0:256], ALU.subtract), [u_sb, PS[2]], [vn])
                    for h in range(4):
                        S.op("pe", mm(PS[3][:, 256 + h * 64:256 + (h + 1) * 64], intraT[:, h, :], vn[:, h, :]),
                             [intraT, vn], [PS[3]])
                    for pr in range(2):
                        for par in range(2):
                            S.op("pe", mm(PS[2][:, 256 + pr * 64:256 + (pr + 1) * 64], ktz[:, pr, par, :], vn[:, 2 * pr + par, :],
                                          par == 0, par == 1), [ktz, vn], [PS[2]])
                    S.op("dve", tt(St_t.ap, St.ap, bc(eg.ap, [128, 2, 64], 2), ALU.mult), [St, eg], [St_t])
                    S.op("dve", tt(St.ap.rearrange("p a d -> p (a d)"), St_t.ap.rearrange("p a d -> p (a d)"),
                                   PS[2][:, 256:384], ALU.add), [St_t, PS[2]], [St])
                    S.op("act", acp(SbZ[0:64, :, 0, :], St[0:64, :, :]), [St], [SbZ])
                    S.op("act", acp(SbZ[64:128, :, 1, :], St[64:128, :, :]), [St], [SbZ])
                    S.op("dve", tt(o_t.ap, PS[3][:, 0:256].rearrange("p (h d) -> p h d", h=4),
                                   bc(sm[:, 4:8], [128, 4, 64], 2), ALU.mult), [PS[3], sm], [o_t])
                    S.op("dve", tt(o_sb.ap.rearrange("p h d -> p (h d)"), o_t.ap.rearrange("p h d -> p (h d)"),
                                   PS[3][:, 256:512], ALU.add), [o_t, PS[3]], [o_sb])
                    if CUT < 7:
                        return
                    S.op("act", actf(o_t.ap, o_sb.ap, AF.Square), [o_sb], [o_t])
                    S.op("dve", rsum(smS[:, 0:4], o_t.ap), [o_t], [smS])
                    S.op("act", actf(smS[:, 0:4], smS[:, 0:4], AF.Sqrt, scale=1.0 / 64, bias=EPS), [smS], [smS])
                    S.op("dve", recip(smS[:, 0:4], smS[:, 0:4]), [smS], [smS])
                    S.op("dve", tt(o_sb.ap, o_sb.ap, bc(smS[:, 0:4], [128, 4, 64], 2), ALU.mult), [o_sb, smS], [o_sb])
                    S.op("dve", tt(o_sb.ap, o_sb.ap, bc(W["go"].ap, [128, 4, 64], 1), ALU.mult), [o_sb, W["go"]], [o_sb])
                    S.op("act", actf(sgz.ap, z_p[par_s].ap, AF.Silu), [z_p[par_s]], [sgz])
                    S.op("dve", tt(gated.ap, o_sb.ap.rearrange("p h d -> p (h d)"), sgz.ap, ALU.mult), [o_sb, sgz], [gated])
                    for c in range(2):
                        S.op("pe", trp(pb2[:, 768 + c * 128:768 + (c + 1) * 128], gated[:, c * 128:(c + 1) * 128], ident.ap),
                             [gated, ident], [PS[2]])
                    S.op("act", acp(Gt[:, :, tc_], pb2[:, 768:1024].rearrange("p (c t) -> p c t", c=2)), [PS[2]], [Gt])
                local(0)
                for s in range(4):
                    streams = []
                    if s < 3:
                        S.begin_capture()
                        local(s + 1)
                        streams.append(S.end_capture())
                    S.begin_capture()
                    scan(s)
                    streams.append(S.end_capture())
                    S.replay(streams)
                S.dma("sp", gt2[g, :, :, ti * 512:(ti + 1) * 512], Gt.ap, reads=[Gt], writes=[D_gt2])
            if cfg.cut >= 6:
                S.dma("sp", ssm_p[4 * g:4 * g + 4].rearrange("(pr par) dk dv -> (par dk) pr dv", par=2), St.ap,
                      reads=[St], writes=[D_out])
        S.barrier()
        ar.release()


    def phase_gdn_sample(W):
        ar.mark()
        P, PH = NS, NSEQ * 8
        hb = ar.alloc([1024], BF16, "g_hb")
        hTs = ar.alloc([8, P], BF16, "g_hTs")
        qkv_sb = ar.alloc([1536], F32, "qkv_sb")
        z_sb = ar.alloc([512], F32, "z_sb")
        ab_sb = ar.alloc([16], F32, "ab_sb")
        E = ar.alloc([7, 3, 64], F32, "E")
        Wc = ar.alloc([4, 3, 64], F32, "Wc")
        AB = ar.alloc([4, 2], F32, "AB")
        alog = ar.alloc([1], F32, "alog")
        dtb = ar.alloc([1], F32, "dtbL")
        cv = ar.alloc([4, 3, 64], F32, "cv")
        cv2 = ar.alloc([4, 3, 64], F32, "cv2")
        nr = ar.alloc([4, 2], F32, "nr")
        qk = ar.alloc([4, 2, 64], F32, "qkn")
        g1, g2, gg_, eg, bet = [ar.alloc([4], F32, f"sg{i}") for i in range(5)]
        St = ar.alloc([64, 64], F32, "StS")
        tmp = ar.alloc([64, 64], F32, "tmpS")
        kS = ar.alloc([64], F32, "kS")
        dl = ar.alloc([64], F32, "dl")
        o_all = ar.alloc([4, 64], F32, "o_all")
        o_tok = ar.alloc([8, 64], F32, "o_tok")
        o_sq = ar.alloc([8, 64], F32, "o_sq")
        orr = ar.alloc([8], F32, "orr")
        gated = ar.alloc([512], BF16, "g_gated")
        gTs = ar.alloc([4, P], BF16, "g_gTs")
        yv = ar.alloc([1024], F32, "yv")
        st = stat.next()
        rms_rstd(xs1[0:P, :], xs1, 1024, st, junk, P)
        S.op("dve", ts(hb[0:P, :], xs1[0:P, :], st[0:P, 0:1], ALU.mult), [xs1, st], [hb])
        pb = psb(0)
        for c in range(8):
            S.op("pe", trp(pb[:, c * P:(c + 1) * P], hb[0:P, c * 128:(c + 1) * 128], ident[0:P, 0:P]), [hb, ident], [PS[0]])
        S.op("act", acp(hTs.ap, pb[:, 0:8 * P].rearrange("p (c t) -> p c t", c=8)), [PS[0]], [hTs])
        for (bank, c0, w, dst) in ((1, 0, 512, qkv_sb[0:P, 0:512]), (2, 512, 512, qkv_sb[0:P, 512:1024]),
                                   (3, 1024, 512, qkv_sb[0:P, 1024:1536]), (4, 1536, 512, z_sb[0:P, :]),
                                   (5, 2048, 16, ab_sb[0:P, :])):
            for kc in range(8):
                S.op("pe", mm(PS[bank][0:P, 0:w], hTs[:, kc, :], W["in"][:, kc, c0:c0 + w], kc == 0, kc == 7),
                     [hTs, W["in"]], [PS[bank]])
            dT = qkv_sb if bank <= 3 else (z_sb if bank == 4 else ab_sb)
            S.op("act", acp(dst, PS[bank][0:P, 0:w]), [PS[bank]], [dT])
        S.dma("sp", qs_scr, qkv_sb[0:P, :], reads=[qkv_sb], writes=[D_qs])
        S.dma("sp", ab_scr, ab_sb[0:P, :], reads=[ab_sb], writes=[D_ab])
        S.dma("pool", E[0:PH, 0:3].rearrange("p r s d -> p (r s d)"), st_convL, writes=[E])
        S.dma("pool", Wc[0:PH].rearrange("p j s d -> p (j s d)"), b_w_convL, writes=[Wc])
        S.dma("pool", alog[0:PH, :], b_alogL, writes=[alog])
        S.dma("pool", dtb[0:PH, :], b_dtbL, writes=[dtb])
        S.dma("pool", St[0:PH].rearrange("p k v -> p (k v)"), st_ssm, writes=[St])
        for n in range(NSEQ):
            q_ = dmaq.next()
            for t in range(4):
                S.dma(q_, E[n * 8:(n + 1) * 8, 3 + t, :, :],
                      qs_scr[n * 4 + t].rearrange("(s h d) -> h s d", s=3, h=8), reads=[D_qs], writes=[E])
            S.dma(q_, AB[n * 8:(n + 1) * 8, :, :], ab_scr[n * 4:(n + 1) * 4, :].rearrange("t (x h) -> h t x", x=2),
                  reads=[D_ab], writes=[AB], allow_slow_non_contiguous=True)
        S.dma("sp", conv_s, E[0:PH, 4:7].rearrange("p r s d -> p (r s d)"), reads=[E], writes=[D_out])
        H = slice(0, PH)
        S.op("dve", tt(cv[H], E[H, 0:4], bc(Wc[H, 0], [PH, 4, 3, 64], 1), ALU.mult), [E, Wc], [cv])
        for j_ in range(1, 4):
            S.op("dve", tt(cv2[H], E[H, j_:j_ + 4], bc(Wc[H, j_], [PH, 4, 3, 64], 1), ALU.mult), [E, Wc], [cv2])
            S.op("dve", tt(cv[H], cv[H], cv2[H], ALU.add), [cv, cv2], [cv])
        S.op("act", actf(cv[H], cv[H], AF.Silu), [cv], [cv])
        S.op("act", actf(cv2[H, :, 0:2, :], cv[H, :, 0:2, :], AF.Square), [cv], [cv2])
        S.op("dve", rsum(nr[H], cv2[H, :, 0:2, :]), [cv2], [nr])
        S.op("act", actf(nr[H], nr[H], AF.Sqrt, bias=EPS), [nr], [nr])
        S.op("dve", recip(nr[H], nr[H]), [nr], [nr])
        S.op("dve", tt(qk[H], cv[H, :, 0:2, :], bc(nr[H], [PH, 4, 2, 64], 3), ALU.mult), [cv, nr], [qk])
        S.op("dve", ts(qk[H, :, 0, :], qk[H, :, 0, :], 0.125, ALU.mult), [qk], [qk])
        S.op("dve", ts(g1[H], AB[H, :, 0], dtb[H, 0:1], ALU.add), [AB, dtb], [g1])
        S.op("dve", stt(g2[H], g1[H], -1.0, g1[H], ALU.mult, ALU.max), [g1], [g2])
        S.op("act", actf(g2[H], g2[H], AF.Exp, scale=-1.0), [g2], [g2])
        S.op("act", actf(g2[H], g2[H], AF.Ln, bias=1.0), [g2], [g2])
        S.op("dve", stt(g1[H], g1[H], 0.0, g2[H], ALU.max, ALU.add), [g1, g2], [g1])
        S.op("act", actf(alog[H], alog[H], AF.Exp), [alog], [alog])
        S.op("dve", ts(gg_[H], g1[H], alog[H, 0:1], ALU.mult, -1.0, ALU.mult), [g1, alog], [gg_])
        S.op("act", actf(eg[H], gg_[H], AF.Exp), [gg_], [eg])
        S.op("act", actf(g2[H], AB[H, :, 1], AF.Exp, scale=-1.0), [AB], [g2])
        S.op("dve", ts(g2[H], g2[H], 1.0, ALU.add), [g2], [g2])
        S.op("dve", recip(bet[H], g2[H]), [g2], [bet])
        for t in range(4):
            q_t, k_t, v_t = qk[H, t, 0, :], qk[H, t, 1, :], cv[H, t, 2, :]
            S.op("dve", ts(St[H], St[H], eg[H, t:t + 1], ALU.mult), [St, eg], [St])
            S.op("dve", tt(tmp[H], St[H], bc(k_t, [PH, 64, 64], 2), ALU.mult), [St, qk], [tmp])
            S.op("dve", rsum(kS[H], tmp[H].rearrange("p k v -> p v k")), [tmp], [kS])
            S.op("dve", tt(dl[H], v_t, kS[H], ALU.subtract), [cv, kS], [dl])
            S.op("dve", ts(dl[H], dl[H], bet[H, t:t + 1], ALU.mult), [dl, bet], [dl])
            S.op("dve", tt(tmp[H], bc(k_t, [PH, 64, 64], 2), bc(dl[H], [PH, 64, 64], 1), ALU.mult), [qk, dl], [tmp])
            S.op("dve", tt(St[H], St[H], tmp[H], ALU.add), [St, tmp], [St])
            S.op("dve", tt(tmp[H], St[H], bc(q_t, [PH, 64, 64], 2), ALU.mult), [St, qk], [tmp])
            S.op("dve", rsum(o_all[H, t, :], tmp[H].rearrange("p k v -> p v k")), [tmp], [o_all])
        S.dma("sp", ssm_s, St[0:PH].rearrange("p k v -> p (k v)"), reads=[St], writes=[D_out])
        S.dma("sp", os_scr, o_all[0:PH].rearrange("p t d -> p (t d)"), reads=[o_all], writes=[D_os])
        for n in range(NSEQ):
            S.dma(dmaq.next(), o_tok[n * 4:(n + 1) * 4], os_scr[n * 8:(n + 1) * 8, :].rearrange("h (t d) -> t h d", t=4),
                  reads=[D_os], writes=[o_tok])
        Pp = slice(0, P)
        S.op("act", actf(o_sq[Pp], o_tok[Pp], AF.Square), [o_tok], [o_sq])
        S.op("dve", rsum(orr[Pp], o_sq[Pp]), [o_sq], [orr])
        S.op("act", actf(orr[Pp], orr[Pp], AF.Sqrt, scale=1.0 / 64, bias=EPS), [orr], [orr])
        S.op("dve", recip(orr[Pp], orr[Pp]), [orr], [orr])
        S.op("dve", tt(o_tok[Pp], o_tok[Pp], bc(orr[Pp], [P, 8, 64], 2), ALU.mult), [o_tok, orr], [o_tok])
        S.op("dve", tt(o_tok[Pp], o_tok[Pp], bc(W["go"][Pp, :], [P, 8, 64], 1), ALU.mult), [o_tok, W["go"]], [o_tok])
        S.op("act", actf(z_sb[Pp, :], z_sb[Pp, :], AF.Silu), [z_sb], [z_sb])
        S.op("dve", tt(gated[Pp, :], o_tok[Pp].rearrange("p h d -> p (h d)"), z_sb[Pp, :], ALU.mult), [o_tok, z_sb], [gated])
        pg = psb(0)
        for c in range(4):
            S.op("pe", trp(pg[:, c * P:(c + 1) * P], gated[0:P, c * 128:(c + 1) * 128], ident[0:P, 0:P]), [gated, ident], [PS[0]])
        S.op("act", acp(gTs.ap, pg[:, 0:4 * P].rearrange("p (c t) -> p c t", c=4)), [PS[0]], [gTs])
        for half in range(2):
            for c in range(4):
                S.op("pe", mm(PS[1 + half][0:P, :], gTs[:, c, :], W["o"][:, c, half * 512:(half + 1) * 512], c == 0, c == 3),
                     [gTs, W["o"]], [PS[1 + half]])
            S.op("dve", tt(yv[0:P, half * 512:(half + 1) * 512], xs1[0:P, half * 512:(half + 1) * 512],
                           PS[1 + half][0:P, :], ALU.add), [xs1, PS[1 + half]], [yv])
        S.dma("sp", y_s, yv[0:P, :], reads=[yv], writes=[D_out])
        S.barrier()
        ar.release()

    phases = cfg.phases or ["mla_p", "p3", "s1", "gdn_p", "s2"]
    if "mla_p" in phases:
        phase_mla_prompt()
    W_o = ar.alloc([4, 1024], BF16, "W_o")
    ar.mark()
    wtmp = ar.alloc([2064], F32, "wtmp_o")
    load_bf16_rows(W_o, [W_o[:, kc, :] for kc in range(4)],
                   [a_w_o[kc * 128:(kc + 1) * 128, :] for kc in range(4)], None, wtmp, 1024)
    S.barrier()
    ar.release()
    if "p3" in phases:
        phase_out_proj(x_p, Buf(), gt1, D_gt1, W_o, xp1 if "gdn_p" in phases else y_p,
                       D_xp1 if "gdn_p" in phases else D_out)
    if "s1" in phases:
        phase_mla_sample()
    S.barrier()
    ar.release()
    if "gdn_p" in phases or "s2" in phases:
        ar.mark()
        junk = ar.alloc([1024], BF16, "junk2")
        stat = Rot([ar.alloc([8], F32, f"statb{i}") for i in range(4)])
        WG = load_gdn_weights()
        if "gdn_p" in phases:
            phase_gdn_prompt(WG)
            phase_out_proj(xp1, D_xp1, gt2, D_gt2, WG["o"], y_p, D_out)
        if "s2" in phases:
            phase_gdn_sample(WG)

    S.barrier(engines=("sp",))
    S.op("sp", lambda e: e.nop(), [D_out, D_lat, D_xp1, D_gt1, D_gt2, D_qs, D_ab, D_os, D_h1, D_h2], [])
    S.emit()
    return nc


def host_consts(SEQ, NSEQ, past_len):
    i = np.arange(128)
    c = {}
    c["c_ident"] = np.eye(128, dtype=np.float32)
    c["c_tri"] = (i[:, None] <= i[None, :]).astype(np.float32)
    c["c_strict"] = (i[:, None] > i[None, :]).astype(np.float32)
    c["c_negm"] = np.where(i[:, None] >= i[None, :], 0.0, NEG).astype(np.float32)
    c["c_blk"] = ((i[:, None] // 64) == (i[None, :] // 64)).astype(np.float32)
    q = np.arange(512)
    am = np.zeros((128, 4, 512), np.float32)
    for j in range(4):
        am[:, j, :] = ((128 * j + i)[:, None] <= q[None, :])
    c["c_amask"] = am
    NS = NSEQ * 4
    sm = np.zeros((NS, NSEQ, 8, 4), np.float32)
    for n in range(NSEQ):
        for tk in range(4):
            for tq in range(4):
                if tk <= tq:
                    sm[n * 4 + tk, n, :, tq] = 1.0
    c["c_smask"] = sm.reshape(NS, NSEQ * 32)

    def rope_tab(pos):
        half = 16
        inv = (np.float32(10000.0) ** (-np.arange(half, dtype=np.float32) / np.float32(half))).astype(np.float32)
        ang = pos.astype(np.float32)[:, None] * inv[None, :]
        cos = np.cos(ang).astype(np.float32)
        sin = np.sin(ang).astype(np.float32)
        return np.concatenate([cos, cos, -sin, sin], axis=1).astype(np.float32)

    c["c_rope_p"] = rope_tab(np.arange(SEQ))
    c["c_rope_s"] = rope_tab(np.tile(past_len + np.arange(4), NSEQ))
    return c


def core_inputs(inp, core, n_cores, SEQ, NSEQ, consts):
    b = core % inp["x_prompt"].shape[0]
    f = np.ascontiguousarray
    m = dict(consts)
    m["x_p"] = f(inp["x_prompt"][b])
    m["x_s"] = f(inp["x_sample"][core * NSEQ:(core + 1) * NSEQ].reshape(NSEQ * 4, 1024))
    m["cache_lat"] = inp["cache_latent"][0].reshape(-1, 128 * 256)
    m["cache_kr"] = inp["cache_krope"][0].reshape(-1, 128 * 32)
    m["ptab"] = f(inp["page_table"][core * NSEQ:(core + 1) * NSEQ].T.astype(np.int32))
    m["st_conv"] = f(inp["state_conv"][0, core * NSEQ:(core + 1) * NSEQ])
    m["st_ssm"] = f(inp["state_ssm"][0, core * NSEQ:(core + 1) * NSEQ].reshape(NSEQ * 8, 4096))
    m["a_norm"] = f(inp["a_norm"][0].reshape(8, 128).T)
    m["a_w_in"] = inp["a_w_in"][0]
    m["a_g_qa"] = f(inp["a_g_qa"][0].reshape(3, 128).T)
    m["a_w_uq"] = inp["a_w_uq"][0]
    m["a_g_kv"] = inp["a_g_kv"][0].reshape(1, 256)
    m["a_w_uk"] = inp["a_w_uk"][0].reshape(256, 512)
    m["a_w_uv"] = inp["a_w_uv"][0].reshape(256, 512)
    m["a_g_q"] = inp["a_g_q"][0].reshape(1, 96)
    m["a_g_k"] = inp["a_g_k"][0].reshape(1, 96)
    m["a_w_o"] = inp["a_w_o"][0]
    m["b_norm"] = f(inp["b_norm"][0].reshape(8, 128).T)
    m["b_w_in"] = inp["b_w_in"][0]
    m["b_w_conv"] = inp["b_w_conv"][0]
    m["b_w_convT"] = f(inp["b_w_conv"][0].reshape(4, 12, 128).transpose(2, 1, 0).reshape(128, 48))
    sc_ = inp["state_conv"][0, core * NSEQ:(core + 1) * NSEQ]
    m["st_convL"] = f(sc_.reshape(NSEQ, 3, 3, 8, 64).transpose(0, 3, 1, 2, 4).reshape(NSEQ * 8, 576))
    wc_ = inp["b_w_conv"][0].reshape(4, 3, 8, 64).transpose(2, 0, 1, 3).reshape(8, 768)
    m["b_w_convL"] = f(np.tile(wc_, (NSEQ, 1)))
    m["b_alogL"] = f(np.tile(inp["b_a_log"][0].reshape(8, 1), (NSEQ, 1)))
    m["b_dtbL"] = f(np.tile(inp["b_dt_bias"][0].reshape(8, 1), (NSEQ, 1)))
    m["b_a_log"] = inp["b_a_log"][0].reshape(1, 8)
    m["b_dt_bias"] = inp["b_dt_bias"][0].reshape(1, 8)
    m["b_g_o"] = inp["b_g_o"][0].reshape(1, 64)
    m["b_w_o"] = inp["b_w_o"][0]
    return m


def run(inp, n_cores, cfg):
    inp = {k: np.asarray(v) for k, v in inp.items()}
    past_len = inp["page_table"].shape[1] * 128
    consts = host_consts(cfg.SEQ, cfg.NSEQ, past_len)
    nc = build(cfg)
    in_maps = [core_inputs(inp, c, n_cores, cfg.SEQ, cfg.NSEQ, consts) for c in range(n_cores)]
    res = run_bass_kernel_spmd(nc, in_maps, core_ids=list(range(n_cores)))
    return res.results


def fix_out(name, a, NSEQ):
    if name == "conv_s":
        return np.ascontiguousarray(a.reshape(NSEQ, 8, 3, 3, 64).transpose(0, 2, 3, 1, 4)).reshape(NSEQ, 3, 1536)
    return a


def kernel(**inputs):
    cfg = Cfg()
    r = run(inputs, 8, cfg)
    B = 4
    y_p = np.stack([r[b]["y_p"] for b in range(B)])
    y_s = np.concatenate([r[c]["y_s"].reshape(16, 4, 1024) for c in range(8)])
    lat_p = np.stack([r[b]["lat_p"] for b in range(B)])[None]
    kr_p = np.stack([r[b]["kr_p"] for b in range(B)])[None]
    lat_s = np.concatenate([r[c]["lat_s"].reshape(16, 4, 256) for c in range(8)])[None]
    kr_s = np.concatenate([r[c]["kr_s"].reshape(16, 4, 32) for c in range(8)])[None]
    conv_p = np.stack([r[b]["conv_p"] for b in range(B)])[None]
    ssm_p = np.stack([r[b]["ssm_p"] for b in range(B)])[None]
    conv_s = np.concatenate([fix_out("conv_s", r[c]["conv_s"], 16) for c in range(8)])[None]
    ssm_s = np.concatenate([r[c]["ssm_s"].reshape(16, 8, 64, 64) for c in range(8)])[None]
    return (y_p, y_s, lat_p, kr_p, lat_s, kr_s, conv_p, ssm_p, conv_s, ssm_s)
```
